# Optimizing a Trainium2 kernel written in Bass

```python
import math
import jax
import jax.numpy as jnp
from jax import lax
import numpy as np

D_MODEL = 1024
BATCH = 32
SEQ = 256
DEPTH = 4
DEC_BATCH = 2
DEC_SEQ = 1024
PAST_LEN = 256

GRID_W = 64
N_MIXERS = 4
N_RET = (DEPTH + 3) // 4
N_RWKV = (DEPTH + 2) // 4
N_DIFF = (DEPTH + 1) // 4
N_NA = DEPTH // 4

RET_HEADS = 4
RET_DK = D_MODEL // RET_HEADS
RET_DV = 2 * RET_DK
RET_QK = RET_HEADS * RET_DK
RET_V = RET_HEADS * RET_DV
RET_CHUNK = 64

RWKV_HD = 64
RWKV_HEADS = D_MODEL // RWKV_HD
RWKV_DECAY_RANK = 64
RWKV_A_RANK = 64

DIFF_HEADS = 8
DIFF_HD = D_MODEL // (2 * DIFF_HEADS)
DIFF_W = DIFF_HEADS * 2 * DIFF_HD

NA_HEADS = 16
NA_HD = D_MODEL // NA_HEADS
NA_WIN_R = 8
NA_WIN_C = 16

ROPE_BASE = 10000.0
Q_BLOCK = 128
EPS = 1e-6
GN_EPS = 1e-5

kernel_name = 'hybrid_ret_rwkv_diff_na_diffusion_step'


def rms_norm(x, w, eps=EPS):
    xf = x.astype(jnp.float32)
    y = xf * lax.rsqrt(jnp.mean(xf * xf, axis=-1, keepdims=True) + eps)
    return (y * w.astype(jnp.float32)).astype(x.dtype)


def head_layer_norm(x, w, eps=GN_EPS):
    xf = x.astype(jnp.float32)
    xc = xf - jnp.mean(xf, axis=-1, keepdims=True)
    return xc * lax.rsqrt(jnp.mean(xc * xc, axis=-1, keepdims=True) + eps) * w.astype(jnp.float32)


def split_heads(t, n_heads):
    b, l, _ = t.shape
    return t.reshape(b, l, n_heads, -1).transpose(0, 2, 1, 3)


def merge_heads(t):
    b, h, l, d = t.shape
    return t.transpose(0, 2, 1, 3).reshape(b, l, h * d)


def ada_modulation(cond, w, b):
    m = jax.nn.silu(cond) @ w + b
    shift, scale, gate = jnp.split(m[:, None, :], 3, axis=-1)
    return shift, scale, gate


def _rope_1d(x, pos):
    d = x.shape[-1]
    inv = ROPE_BASE ** (-jnp.arange(0, d, 2, dtype=jnp.float32) / d)
    ang = pos[:, None] * inv[None, :]
    cos, sin = jnp.cos(ang), jnp.sin(ang)
    x1, x2 = jnp.split(x, 2, axis=-1)
    return jnp.concatenate([x1 * cos - x2 * sin, x1 * sin + x2 * cos], axis=-1)


def axial_rope(x):
    L = x.shape[-2]
    t = jnp.arange(L)
    row = (t // GRID_W).astype(jnp.float32)
    col = (t % GRID_W).astype(jnp.float32)
    xr, xc = jnp.split(x.astype(jnp.float32), 2, axis=-1)
    return jnp.concatenate([_rope_1d(xr, row), _rope_1d(xc, col)], axis=-1).astype(x.dtype)


def centred_shift(x):
    p = jnp.pad(x, ((0, 0), (1, 1), (0, 0)))
    return 0.5 * (p[:, :-2] + p[:, 2:])


def query_blocks(fn, q):
    b, h, l, d = q.shape
    n = l // Q_BLOCK
    qb = q.reshape(b, h, n, Q_BLOCK, d).transpose(2, 0, 1, 3, 4)
    o = lax.map(fn, qb)
    return o.transpose(1, 2, 0, 3, 4).reshape(b, h, l, o.shape[-1])


def dense_attend(q, k, v):
    scale = q.shape[-1] ** -0.5
    def block(qb):
        s = jnp.einsum('bhqd,bhkd->bhqk', qb, k).astype(jnp.float32) * scale
        p = jax.nn.softmax(s, axis=-1).astype(v.dtype)
        return jnp.einsum('bhqk,bhkd->bhqd', p, v)
    return query_blocks(block, q)


def retention_chunkwise(q, k, v, log_g, s0):
    b, h, l, dk = q.shape
    dv = v.shape[-1]
    n = l // RET_CHUNK
    def chunks(t):
        return t.astype(jnp.float32).reshape(b, h, n, RET_CHUNK, t.shape[-1]).transpose(2, 0, 1, 3, 4)
    pos = jnp.arange(RET_CHUNK, dtype=jnp.float32)
    lg = log_g[:, None]
    gap = pos[:, None] - pos[None, :]
    inner_decay = jnp.where(gap >= 0, jnp.exp(lg[:, :, None] * jnp.maximum(gap, 0.0)), 0.0)
    q_decay = jnp.exp(lg * (pos + 1.0))[:, :, None]
    k_decay = jnp.exp(lg * (RET_CHUNK - 1.0 - pos))[:, :, None]
    chunk_decay = jnp.exp(log_g * RET_CHUNK)[:, None, None]
    def step(s, qkv):
        qc, kc, vc = qkv
        att = jnp.einsum('bhqd,bhkd->bhqk', qc, kc) * inner_decay
        o = jnp.einsum('bhqk,bhkv->bhqv', att, vc) + jnp.einsum('bhqd,bhdv->bhqv', qc * q_decay, s)
        s = s * chunk_decay + jnp.einsum('bhkd,bhkv->bhdv', kc * k_decay, vc)
        return s, o
    s, o = lax.scan(step, s0.astype(jnp.float32), (chunks(q), chunks(k), chunks(v)))
    return o.transpose(1, 2, 0, 3, 4).reshape(b, h, l, dv), s


def retention_mixer(h, s0, w_in, decay_logit, gn_w, w_out, latent):
    q, k, v, g = jnp.split(h @ w_in, [RET_QK, 2 * RET_QK, 2 * RET_QK + RET_V], axis=-1)
    q = split_heads(q, RET_HEADS)
    k = split_heads(k, RET_HEADS)
    v = split_heads(v, RET_HEADS)
    if latent:
        q, k = axial_rope(q), axial_rope(k)
    k = k * (RET_DK ** -0.5)
    log_g = jax.nn.log_sigmoid(decay_logit.astype(jnp.float32))
    o_f, s_f = retention_chunkwise(q, k, v, log_g[0], s0[:, 0])
    flip = lambda t: jnp.flip(t, axis=2)
    o_b, s_b = retention_chunkwise(flip(q), flip(k), flip(v), log_g[1], s0[:, 1])
    o = head_layer_norm(o_f + flip(o_b), gn_w.reshape(RET_HEADS, 1, RET_DV))
    o = merge_heads(o).astype(h.dtype) * jax.nn.silu(g)
    return o @ w_out, jnp.stack([s_f, s_b], axis=1)


def rwkv7_scan(r, w, k, v, kk, a, s0):
    def step(s, inp):
        r_t, w_t, k_t, v_t, kk_t, a_t = inp
        sa = jnp.einsum('bhvk,bhk->bhv', s, -kk_t)
        s = (s * w_t[:, :, None, :] + sa[..., None] * (kk_t * a_t)[:, :, None, :]
             + v_t[..., None] * k_t[:, :, None, :])
        return s, jnp.einsum('bhvk,bhk->bhv', s, r_t)
    seq = tuple(jnp.moveaxis(t, 1, 0) for t in (r, w, k, v, kk, a))
    s, y = lax.scan(step, s0.astype(jnp.float32), seq)
    return jnp.moveaxis(y, 0, 1), s


def rwkv7_mixer(h, s0, mu, w_in, w0, wA, wB, a0, aA, aB, k_k, k_a, r_k, gn_w, w_out):
    b, l, d_model = h.shape
    H, hd = RWKV_HEADS, RWKV_HD
    xx = centred_shift(h) - h
    x_r, x_w, x_k, x_v, x_a, x_g = [h + xx * mu[n] for n in range(6)]
    r, k, v, g = jnp.einsum('nbld,dne->nble', jnp.stack([x_r, x_k, x_v, x_g]), w_in.reshape(d_model, 4, d_model))
    to_heads = lambda t: t.astype(jnp.float32).reshape(b, l, H, hd)
    r, k, v = to_heads(r), to_heads(k), to_heads(v)
    kk = k * k_k.astype(jnp.float32).reshape(H, hd)
    kk = kk * lax.rsqrt(jnp.maximum(jnp.sum(kk * kk, axis=-1, keepdims=True), 1e-12))
    ys, bonuses, states = [], [], []
    for dr in range(2):
        wlog = -jax.nn.softplus(-(w0[dr] + jnp.tanh(x_w @ wA[dr]) @ wB[dr])) - 0.5
        decay = to_heads(jnp.exp(-jnp.exp(wlog.astype(jnp.float32))))
        a = to_heads(jax.nn.sigmoid(a0[dr] + (x_a @ aA[dr]) @ aB[dr]))
        kd = k * (1.0 + (a - 1.0) * k_a.astype(jnp.float32).reshape(H, hd))
        seq = (r, decay, kd, v, kk, a)
        if dr == 1:
            seq = tuple(jnp.flip(t, axis=1) for t in seq)
        y_d, s_d = rwkv7_scan(*seq, s0[:, dr])
        if dr == 1:
            y_d = jnp.flip(y_d, axis=1)
        ys.append(y_d)
        bonuses.append(jnp.sum(r * kd * r_k.astype(jnp.float32), axis=-1, keepdims=True) * v)
        states.append(s_d)
    o = head_layer_norm(ys[0] + ys[1], gn_w.reshape(H, hd)) + bonuses[0] + bonuses[1]
    o = o.reshape(b, l, d_model).astype(h.dtype) * jax.nn.silu(g)
    return o @ w_out, jnp.stack(states, axis=1)


def diff_project(h, w_in):
    q, k, v, g = jnp.split(h @ w_in, 4, axis=-1)
    return split_heads(q, DIFF_HEADS), split_heads(k, DIFF_HEADS), split_heads(v, DIFF_HEADS), g


def diff_lambda_value(lam_p, lam_init):
    lp = lam_p.astype(jnp.float32)
    return jnp.exp(jnp.sum(lp[0] * lp[1])) - jnp.exp(jnp.sum(lp[2] * lp[3])) + lam_init


def rope_pair(t):
    t1, t2 = jnp.split(t, 2, axis=-1)
    return jnp.concatenate([axial_rope(t1), axial_rope(t2)], axis=-1)


def diff_attend(q, k, v, lam):
    scale = DIFF_HD ** -0.5
    k1, k2 = jnp.split(k, 2, axis=-1)
    def block(qb):
        q1, q2 = jnp.split(qb, 2, axis=-1)
        p1 = jax.nn.softmax(jnp.einsum('bhqd,bhkd->bhqk', q1, k1).astype(jnp.float32) * scale, axis=-1)
        p2 = jax.nn.softmax(jnp.einsum('bhqd,bhkd->bhqk', q2, k2).astype(jnp.float32) * scale, axis=-1)
        return jnp.einsum('bhqk,bhkv->bhqv', (p1 - lam * p2).astype(v.dtype), v)
    return query_blocks(block, q)


def diff_finish(o, g, gn_w, lam_init, w_out):
    o = rms_norm(o, gn_w.reshape(DIFF_HEADS, 1, 2 * DIFF_HD)) * (1.0 - lam_init)
    return (merge_heads(o) * jax.nn.silu(g)) @ w_out


def na_project(h, w_in):
    q, k, v, g = jnp.split(h @ w_in, 4, axis=-1)
    return split_heads(q, NA_HEADS), split_heads(k, NA_HEADS), split_heads(v, NA_HEADS), g


def na_finish(o, g, w_out):
    return (merge_heads(o) * jax.nn.silu(g)) @ w_out


def neighbourhood_attend(q, k, v, k_ctx, v_ctx, bias_table):
    b, h, l, d = q.shape
    rows = l // GRID_W
    wr = min(NA_WIN_R, rows)
    wc = NA_WIN_C
    scale = d ** -0.5
    r = jnp.arange(rows)
    row_idx = jnp.clip(r - wr // 2, 0, rows - wr)[:, None] + jnp.arange(wr)[None, :]
    c = jnp.arange(GRID_W)
    cs = jnp.clip(c - wc // 2, 0, GRID_W - wc)
    col_ok = (c[None, :] >= cs[:, None]) & (c[None, :] < cs[:, None] + wc)
    row_off = row_idx - r[:, None] + (NA_WIN_R - 1)
    col_off = jnp.clip(c[None, :] - c[:, None], -(wc - 1), wc - 1) + (NA_WIN_C - 1)
    bias = bias_table.astype(jnp.float32)[:, row_off[:, None, :, None], col_off[None, :, None, :]]
    qg = q.reshape(b, h, rows, GRID_W, d)
    kg = k.reshape(b, h, rows, GRID_W, d)[:, :, row_idx]
    vg = v.reshape(b, h, rows, GRID_W, d)[:, :, row_idx]
    s_loc = jnp.einsum('bhrqd,bhrwkd->bhrqwk', qg, kg).astype(jnp.float32) * scale + bias
    s_loc = jnp.where(col_ok[:, None, :], s_loc, -jnp.inf)
    k_ctx = k_ctx.astype(q.dtype)
    v_ctx = v_ctx.astype(v.dtype)
    s_ctx = jnp.einsum('bhrqd,bhcd->bhrqc', qg, k_ctx).astype(jnp.float32) * scale
    n_loc = wr * GRID_W
    p = jax.nn.softmax(jnp.concatenate([s_loc.reshape(b, h, rows, GRID_W, n_loc), s_ctx], axis=-1), axis=-1).astype(v.dtype)
    p_loc = p[..., :n_loc].reshape(b, h, rows, GRID_W, wr, GRID_W)
    p_ctx = p[..., n_loc:]
    o = jnp.einsum('bhrqwk,bhrwkd->bhrqd', p_loc, vg) + jnp.einsum('bhrqc,bhcd->bhrqd', p_ctx, v_ctx)
    return o.reshape(b, h, l, d)


def setup_inputs(seed: int = 0) -> dict:
    key = jax.random.key(seed)
    keys = iter(jax.random.split(key, 48))
    def nrm(shape, scale=1.0):
        return jax.random.normal(next(keys), shape, jnp.float32) * scale
    def gain(shape):
        return 1.0 + nrm(shape, 0.02)
    D = D_MODEL
    ret_gamma_logit = jnp.log(2.0 ** (5.0 + jnp.arange(RET_HEADS, dtype=jnp.float32)) - 1.0)
    rwkv_w0_base = jnp.linspace(-6.0, 1.0, D, dtype=jnp.float32)
    return {
        'x_prompt': nrm((BATCH, SEQ, D)),
        'x_sample': nrm((DEC_BATCH, DEC_SEQ, D)),
        'state_ret': nrm((DEC_BATCH, N_RET, 2, RET_HEADS, RET_DK, RET_DV), 0.1),
        'state_rwkv': nrm((DEC_BATCH, N_RWKV, 2, RWKV_HEADS, RWKV_HD, RWKV_HD), 0.1),
        'cache_diff_k': nrm((DEC_BATCH, N_DIFF, DIFF_HEADS, PAST_LEN, 2 * DIFF_HD)),
        'cache_diff_v': nrm((DEC_BATCH, N_DIFF, DIFF_HEADS, PAST_LEN, 2 * DIFF_HD)),
        'cache_na_k': nrm((DEC_BATCH, N_NA, NA_HEADS, PAST_LEN, NA_HD)),
        'cache_na_v': nrm((DEC_BATCH, N_NA, NA_HEADS, PAST_LEN, NA_HD)),
        'c': nrm((DEC_BATCH, D)),
        'c_ctx': nrm((D,)),
        'norm_w': gain((DEPTH, D)),
        'w_mod': nrm((DEPTH, D, 3 * D), 0.5 * D ** -0.5),
        'b_mod': nrm((DEPTH, 3 * D), 0.1),
        'final_norm_w': gain((D,)),
        'ret_w_in': nrm((N_RET, D, 2 * RET_QK + 2 * RET_V), D ** -0.5),
        'ret_decay': ret_gamma_logit + nrm((N_RET, 2, RET_HEADS), 0.1),
        'ret_gn': gain((N_RET, RET_V)),
        'ret_w_out': nrm((N_RET, RET_V, D), RET_V ** -0.5),
        'rwkv_mu': jax.random.uniform(next(keys), (N_RWKV, 6, D), jnp.float32),
        'rwkv_w_in': nrm((N_RWKV, D, 4 * D), D ** -0.5),
        'rwkv_w0': rwkv_w0_base + nrm((N_RWKV, 2, D), 0.1),
        'rwkv_wA': nrm((N_RWKV, 2, D, RWKV_DECAY_RANK), D ** -0.5),
        'rwkv_wB': nrm((N_RWKV, 2, RWKV_DECAY_RANK, D), 0.5 * RWKV_DECAY_RANK ** -0.5),
        'rwkv_a0': nrm((N_RWKV, 2, D), 0.1),
        'rwkv_aA': nrm((N_RWKV, 2, D, RWKV_A_RANK), D ** -0.5),
        'rwkv_aB': nrm((N_RWKV, 2, RWKV_A_RANK, D), 0.5 * RWKV_A_RANK ** -0.5),
        'rwkv_kk': 0.85 + nrm((N_RWKV, D), 0.02),
        'rwkv_ka': gain((N_RWKV, D)),
        'rwkv_rk': nrm((N_RWKV, RWKV_HEADS, RWKV_HD), 0.1),
        'rwkv_gn': gain((N_RWKV, D)),
        'rwkv_w_out': nrm((N_RWKV, D, D), D ** -0.5),
        'diff_w_in': nrm((N_DIFF, D, 4 * DIFF_W), D ** -0.5),
        'diff_lambda': nrm((N_DIFF, 4, DIFF_HD), 0.1),
        'diff_gn': gain((N_DIFF, DIFF_W)),
        'diff_w_out': nrm((N_DIFF, DIFF_W, D), DIFF_W ** -0.5),
        'na_w_in': nrm((N_NA, D, 4 * D), D ** -0.5),
        'na_bias': nrm((N_NA, NA_HEADS, 2 * NA_WIN_R - 1, 2 * NA_WIN_C - 1), 0.1),
        'na_w_out': nrm((N_NA, D, D), D ** -0.5),
    }


def reference(x_prompt, x_sample, state_ret, state_rwkv, cache_diff_k, cache_diff_v, cache_na_k, cache_na_v,
              c, c_ctx, norm_w, w_mod, b_mod, final_norm_w,
              ret_w_in, ret_decay, ret_gn, ret_w_out,
              rwkv_mu, rwkv_w_in, rwkv_w0, rwkv_wA, rwkv_wB, rwkv_a0, rwkv_aA, rwkv_aB,
              rwkv_kk, rwkv_ka, rwkv_rk, rwkv_gn, rwkv_w_out,
              diff_w_in, diff_lambda, diff_gn, diff_w_out,
              na_w_in, na_bias, na_w_out):
    xp, xs = x_prompt, x_sample
    bp = xp.shape[0]
    new_ret, new_rwkv, new_dk, new_dv, new_nk, new_nv = [], [], [], [], [], []
    for i in range(DEPTH):
        kind, j = i % N_MIXERS, i // N_MIXERS
        sh_p, sc_p, g_p = ada_modulation(c_ctx[None, :], w_mod[i], b_mod[i])
        sh_s, sc_s, g_s = ada_modulation(c, w_mod[i], b_mod[i])
        hp = rms_norm(xp, norm_w[i]) * (1.0 + sc_p) + sh_p
        hs = rms_norm(xs, norm_w[i]) * (1.0 + sc_s) + sh_s
        if kind == 0:
            s0 = jnp.zeros((bp, 2, RET_HEADS, RET_DK, RET_DV), jnp.float32)
            yp, st = retention_mixer(hp, s0, ret_w_in[j], ret_decay[j], ret_gn[j], ret_w_out[j], False)
            ys, _ = retention_mixer(hs, state_ret[:, j], ret_w_in[j], ret_decay[j], ret_gn[j], ret_w_out[j], True)
            new_ret.append(st)
        elif kind == 1:
            rw = (rwkv_mu[j], rwkv_w_in[j], rwkv_w0[j], rwkv_wA[j], rwkv_wB[j], rwkv_a0[j], rwkv_aA[j],
                  rwkv_aB[j], rwkv_kk[j], rwkv_ka[j], rwkv_rk[j], rwkv_gn[j], rwkv_w_out[j])
            s0 = jnp.zeros((bp, 2, RWKV_HEADS, RWKV_HD, RWKV_HD), jnp.float32)
            yp, st = rwkv7_mixer(hp, s0, *rw)
            ys, _ = rwkv7_mixer(hs, state_rwkv[:, j], *rw)
            new_rwkv.append(st)
        elif kind == 2:
            lam_init = 0.8 - 0.6 * math.exp(-0.3 * i)
            lam = diff_lambda_value(diff_lambda[j], lam_init)
            qp, kp, vp, gp = diff_project(hp, diff_w_in[j])
            yp = diff_finish(diff_attend(qp, kp, vp, lam), gp, diff_gn[j], lam_init, diff_w_out[j])
            qs, ks_, vs, gs = diff_project(hs, diff_w_in[j])
            k_all = jnp.concatenate([rope_pair(ks_), cache_diff_k[:, j].astype(ks_.dtype)], axis=2)
            v_all = jnp.concatenate([vs, cache_diff_v[:, j].astype(vs.dtype)], axis=2)
            ys = diff_finish(diff_attend(rope_pair(qs), k_all, v_all, lam), gs, diff_gn[j], lam_init, diff_w_out[j])
            new_dk.append(kp)
            new_dv.append(vp)
        else:
            qp, kp, vp, gp = na_project(hp, na_w_in[j])
            yp = na_finish(dense_attend(qp, kp, vp), gp, na_w_out[j])
            qs, ks_, vs, gs = na_project(hs, na_w_in[j])
            o_s = neighbourhood_attend(qs, ks_, vs, cache_na_k[:, j], cache_na_v[:, j], na_bias[j])
            ys = na_finish(o_s, gs, na_w_out[j])
            new_nk.append(kp)
            new_nv.append(vp)
        xp = xp + g_p * yp
        xs = xs + g_s * ys
    y_prompt = rms_norm(xp, final_norm_w)
    y_sample = rms_norm(xs, final_norm_w)
    return (y_prompt, y_sample, jnp.stack(new_ret, axis=1), jnp.stack(new_rwkv, axis=1),
            jnp.stack(new_dk, axis=1), jnp.stack(new_dv, axis=1),
            jnp.stack(new_nk, axis=1), jnp.stack(new_nv, axis=1))
```

```python
import math
from contextlib import ExitStack

import numpy as np
import concourse.bass as bass
import concourse.mybir as mybir
from concourse.bass_utils import run_bass_kernel_spmd

F32 = mybir.dt.float32
BF16 = mybir.dt.bfloat16
AF = mybir.ActivationFunctionType
ALU = mybir.AluOpType
AX = mybir.AxisListType

EPS = 1e-6
GN_EPS = 1e-5
NEG = -30000.0


class Buf:
    __slots__ = ("t", "name", "lw", "rd", "dsem", "dcnt", "excl")

    def __init__(self, t, name, excl=False):
        self.excl = excl
        self.t = t
        self.name = name
        self.lw = None
        self.rd = {}
        self.dsem = None
        self.dcnt = 0

    def __getitem__(self, k):
        return self.t[k]


class Sched:
    CE = ("pe", "act", "dve", "pool")
    ALLQ = ("pe", "act", "dve", "pool", "sp")

    def __init__(self, nc, stack):
        self.nc = nc
        self.stack = stack
        self.sems = {}
        self.ecnt = {}
        for e in self.CE:
            self.sems[e] = stack.enter_context(nc.semaphore("es_" + e))
            self.ecnt[e] = 0
        self.q = {e: [] for e in self.ALLQ}
        self.seen = {e: {} for e in self.ALLQ}
        self.nbuf = 0
        self.ndsem = 0
        self.n_ops = 0
        self.n_waits = 0
        self.dma_bufs = {}
        self.CONST = Buf(None, "const")
        self.const_bufs = []

    def sb(self, shape, dtype=F32, name="b"):
        self.nbuf += 1
        t = self.stack.enter_context(self.nc.sbuf_tensor(f"{name}_{self.nbuf}", list(shape), dtype))
        return Buf(t, name)

    def _waits(self, eng, reads, writes):
        ev = {}
        for b in reads:
            if b.lw is not None:
                k, v = b.lw
                if ev.get(k, 0) < v:
                    ev[k] = v
            if b.excl:
                for k, v in b.rd.items():
                    if k != eng and ev.get(k, 0) < v:
                        ev[k] = v
        for b in writes:
            if b.lw is not None:
                k, v = b.lw
                if ev.get(k, 0) < v:
                    ev[k] = v
            for k, v in b.rd.items():
                if ev.get(k, 0) < v:
                    ev[k] = v
        waits = []
        seen = self.seen[eng]
        for k, v in ev.items():
            if eng == "pe" and k == "pe":
                continue
            if seen.get(k, 0) >= v:
                continue
            seen[k] = v
            waits.append((k, v))
        self.n_waits += len(waits)
        return waits

    def _mark(self, me, reads, writes):
        k, v = me
        for b in reads:
            if b.rd.get(k, 0) < v:
                b.rd[k] = v
        for b in writes:
            b.lw = me
            b.rd = {}

    def op(self, eng, fn, reads=(), writes=()):
        waits = self._waits(eng, reads, writes)
        self.ecnt[eng] += 1
        me = (eng, self.ecnt[eng])
        self.q[eng].append((waits, fn, eng, 1))
        self._mark(me, reads, writes)
        self.n_ops += 1

    def dma(self, out_ap, in_ap, reads=(), writes=(), owner=None, queue="sp", **kw):
        if owner is self.CONST:
            waits = []
        else:
            waits = self._waits(queue, reads, writes)
        if owner is None:
            owner = writes[0] if writes else reads[0]
        if owner.dsem is None:
            self.ndsem += 1
            key = f"d{self.ndsem}"
            self.sems[key] = self.stack.enter_context(self.nc.semaphore("ds_" + key))
            owner.dsem = key
            self.dma_bufs[key] = owner
        owner.dcnt += 16
        me = (owner.dsem, owner.dcnt)
        self.q[queue].append((waits, (lambda e: e.dma_start(out=out_ap, in_=in_ap, **kw)), owner.dsem, 16))
        if owner is self.CONST:
            self.const_bufs.extend(writes)
        else:
            self._mark(me, reads, writes)
        self.n_ops += 1

    def consts_done(self):
        for b in self.const_bufs:
            b.lw = (self.CONST.dsem, self.CONST.dcnt)
        self.const_bufs = []

    def barrier(self):
        tot = [(e, self.ecnt[e]) for e in self.CE] + [(k, b.dcnt) for k, b in self.dma_bufs.items()]
        for eng in self.ALLQ:
            waits = []
            for k, v in tot:
                if k == eng or v == 0:
                    continue
                if self.seen[eng].get(k, 0) >= v:
                    continue
                self.seen[eng][k] = v
                waits.append((k, v))
            if waits:
                self.q[eng].append((waits, None, None, 0))

    def emit(self):
        nc = self.nc
        self.barrier()
        with nc.Block() as block:
            def run(engobj, name):
                for waits, fn, semkey, inc in self.q[name]:
                    for k, v in waits:
                        engobj.wait_ge(self.sems[k], v)
                    if fn is not None:
                        fn(engobj).then_inc(self.sems[semkey], inc)

            @block.tensor
            def _(e):
                run(e, "pe")

            @block.scalar
            def _(e):
                run(e, "act")

            @block.vector
            def _(e):
                run(e, "dve")

            @block.gpsimd
            def _(e):
                run(e, "pool")

            @block.sync
            def _(e):
                run(e, "sp")


class Rot:
    def __init__(self, bufs):
        self.bufs = bufs
        self.i = 0

    def next(self):
        b = self.bufs[self.i % len(self.bufs)]
        self.i += 1
        return b


def _rope_tables(d):
    t = np.arange(1024)
    row = (t // 64).astype(np.float32)
    col = (t % 64).astype(np.float32)
    inv = (np.float32(10000.0) ** (-np.arange(0, d, 2, dtype=np.float32) / np.float32(d))).astype(np.float32)
    ang = np.stack([row[:, None] * inv[None, :], col[:, None] * inv[None, :]], axis=1).astype(np.float32)
    return np.cos(ang).astype(np.float32), np.sin(ang).astype(np.float32)


def _consts():
    c = {}
    c0, s0 = _rope_tables(128)
    c2, s2 = _rope_tables(32)
    c["rope0"] = np.ascontiguousarray(np.stack([c0, s0], 0))
    c["rope2"] = np.ascontiguousarray(np.stack([c2, s2], 0))
    kk = np.arange(128)[:, None]
    cols = np.arange(15 * 128)[None, :]
    m = cols // 128 - 7
    qq = cols % 128
    gap = (128 * m + qq - kk).astype(np.float32)
    c["gpn"] = np.ascontiguousarray(np.stack([np.maximum(gap, 0), np.maximum(-gap, 0)], 0))
    p = np.arange(128, dtype=np.float32)
    c["stexp"] = np.ascontiguousarray(np.stack([255.0 - p, 127.0 - p, p, 128.0 + p], 1))
    a_ = np.arange(128)
    c["tri"] = np.ascontiguousarray(((a_[:, None] <= a_[None, :]) & (a_[:, None] // 64 == a_[None, :] // 64)).astype(np.float32))
    j_ = np.arange(64)[:, None]
    t_ = np.arange(64)[None, :]
    su = (j_ < t_).astype(np.float32)
    iu = (j_ <= t_).astype(np.float32)
    sl = (t_ < j_).astype(np.float32)
    cm = np.zeros((64, 3, 128), np.float32)
    cm[:, 0, 0:64] = -su
    cm[:, 0, 64:128] = iu
    cm[:, 1, 0:64] = su
    cm[:, 1, 64:128] = iu
    cm[:, 2, 0:64] = -sl
    c["cmask"] = cm
    c["iota1k"] = np.ascontiguousarray(np.broadcast_to(np.arange(1024, dtype=np.float32)[None, :], (128, 1024)))
    return c


def _na_bias_expand(na_bias):
    out = np.full((16, 5, 128, 576), NEG, np.float32)
    jt = [0, 1, 2, 6, 7]
    qc = np.arange(64)[:, None]
    kc = np.arange(64)[None, :]
    cs = np.clip(qc - 8, 0, 48)
    col_ok = (kc >= cs) & (kc < cs + 16)
    cidx = np.clip(kc - qc, -15, 15) + 15
    for ti, j in enumerate(jt):
        r0 = min(max(2 * j - 4, 0), 8)
        nr = min(9, 16 - r0)
        for a in range(2):
            qr = 2 * j + a
            st = min(max(qr - 4, 0), 8)
            for i in range(nr):
                kr = r0 + i
                if st <= kr < st + 8:
                    dr = kr - qr + 7
                    blk = na_bias[:, dr][:, cidx]
                    blk = np.where(col_ok[None], blk, np.float32(NEG))
                    out[:, ti, a * 64:(a + 1) * 64, i * 64:(i + 1) * 64] = blk
    return out


NA_JT = {0: 0, 1: 1, 2: 2, 3: 2, 4: 2, 5: 2, 6: 3, 7: 4}


class _Stop(Exception):
    pass


def build(layers=(0, 1, 2, 3), final=True, stop=None):
    nc = bass.Bass("TRN2", target_bir_lowering=False)

    def CK(name):
        if stop == name:
            raise _Stop()

    def din(name, shape):
        return nc.dram_tensor(name, list(shape), F32, kind="ExternalInput").ap()

    def dout(name, shape):
        return nc.dram_tensor(name, list(shape), F32, kind="ExternalOutput").ap()

    def dscr(name, shape):
        return nc.dram_tensor(name, list(shape), F32).ap()

    I = {}
    for name, shape in [
        ("xp", (1024, 1024)), ("xs", (1024, 1024)), ("cond", (2, 1024)),
        ("state_ret", (2, 4, 256, 512)), ("state_rwkv", (2, 16, 64, 64)),
        ("cache_diff_k", (8, 256, 128)), ("cache_diff_v", (8, 256, 128)),
        ("cache_na_k", (16, 256, 64)), ("cache_na_v", (16, 256, 64)),
        ("norm_w", (4, 1024)), ("w_mod", (4, 1024, 3072)), ("b_mod", (4, 3072)), ("final_norm_w", (1024,)),
        ("ret_w_in", (1024, 6144)), ("ret_decay", (8,)), ("ret_gn", (2048,)), ("ret_w_out", (2048, 1024)),
        ("rwkv_mu", (6, 1024)), ("rwkv_w_in", (1024, 4096)), ("rwkv_w0", (2, 1024)), ("rwkv_wA", (2, 1024, 64)),
        ("rwkv_wB", (2, 64, 1024)), ("rwkv_a0", (2, 1024)), ("rwkv_aA", (2, 1024, 64)), ("rwkv_aB", (2, 64, 1024)),
        ("rwkv_kk", (1024,)), ("rwkv_ka", (1024,)), ("rwkv_rk", (1024,)), ("rwkv_gn", (1024,)),
        ("rwkv_w_out", (1024, 1024)),
        ("diff_w_in", (1024, 4096)), ("diff_lambda", (256,)), ("diff_gn", (1024,)), ("diff_w_out", (1024, 1024)),
        ("na_w_in", (1024, 4096)), ("na_bias_x", (16, 5, 128, 576)), ("na_w_out", (1024, 1024)),
        ("rope0", (2, 1024, 2, 64)), ("rope2", (2, 1024, 2, 16)), ("gpn", (2, 128, 1920)), ("stexp", (128, 4)),
        ("iota1k", (128, 1024)), ("tri", (128, 128)), ("cmask", (64, 3, 128)),
    ]:
        I[name] = din(name, shape)
    O = {}
    for name, shape in [
        ("yp", (1024, 1024)), ("ys", (1024, 1024)), ("st_ret", (4, 2, 4, 256, 512)),
        ("st_rwkv", (4, 2, 16, 64, 64)), ("dk", (4, 8, 256, 128)), ("dv", (4, 8, 256, 128)),
        ("nk", (4, 16, 256, 64)), ("nv", (4, 16, 256, 64)), ("xd", (2048, 1024)),
    ]:
        O[name] = dout(name, shape)
    XD = O["xd"]

    with ExitStack() as st:
        S = Sched(nc, st)
        OP = S.op

        ident = S.sb([128, 128], F32, "ident")
        PSt = st.enter_context(nc.psum_tensor("ps", [128, 8, 512], F32))
        PB = [Buf(PSt[:, i, :], f"ps{i}", excl=True) for i in range(8)]
        pS_rot = Rot(PB[0:4])
        pP_rot = Rot(PB[4:8])

        wf = Rot([S.sb([128, 8, 256], F32, "wf") for _ in range(2)])
        wb = Rot([S.sb([128, 8, 256], BF16, "wb") for _ in range(2)])
        xt_rot = Rot([S.sb([128, 1024], F32, "xt") for _ in range(2)])
        xn_rot = Rot([S.sb([128, 1024], F32, "xn") for _ in range(1)])
        st1 = Rot([S.sb([128, 8], F32, "st") for _ in range(6)])
        BIGN = 27648
        BIG = st.enter_context(nc.sbuf_tensor("big", [128, BIGN], F32))
        R_H = Buf(BIG[:, 0:4096], "RH")
        R_G = Buf(BIG[:, 4096:12288], "RG")
        ATTN = 15360
        ATT = BIG[:, 12288:27648]
        Jm = S.sb([128, 128], F32, "J")
        bsum = S.sb([128, 16, 16], F32, "bsum")
        muF = S.sb([128, 6, 8], F32, "muF")

        def bsub(off, n, name):
            assert off + n <= BIGN, (off, n)
            return Buf(BIG[:, off:off + n], name)
        scond = S.sb([128, 8, 2], F32, "scond")
        condF = S.sb([128, 8, 2], F32, "condF")
        normwF = S.sb([128, 4, 8], F32, "normwF")
        bmodF = S.sb([128, 24], F32, "bmodF")
        modF = S.sb([128, 24, 2], F32, "modF")
        scaleF = S.sb([128, 8, 2], F32, "scaleF")
        Gb = [S.sb([128, 1024], F32, "G") for _ in range(2)]
        gbt = Rot([S.sb([128, 128], F32, "gbt") for _ in range(2)])
        gnw = S.sb([128, 2048], F32, "gnw")
        rope0 = S.sb([128, 2, 8, 128], F32, "rope0")
        rope2 = S.sb([128, 2, 8, 32], F32, "rope2")
        rtmp = Rot([S.sb([128, 128], F32, "rtmp") for _ in range(2)])
        tmpA = Rot([S.sb([128, 512], F32, "tmpA") for _ in range(3)])
        tmpB = Rot([S.sb([128, 512], F32, "tmpB") for _ in range(2)])
        lamc = S.sb([128, 8], F32, "lamc")
        dlb = S.sb([128, 256], F32, "dlb")
        lgt = S.sb([128, 16], F32, "lgt")
        stx = S.sb([128, 4], F32, "stx")
        dsc = S.sb([128, 4, 4], F32, "dsc")
        strip = Rot([S.sb([128, 1920], BF16, "strip") for _ in range(2)])
        osb = Rot([S.sb([128, 512], F32, "osb") for _ in range(2)])

        def sub(off, n, name):
            assert off + n <= ATTN, (off, n)
            return Buf(ATT[:, off:off + n], name)

        def bview(buf, off_f, shape):
            n = int(np.prod(shape))
            ap = buf.t[:, off_f:off_f + n // 2].bitcast(BF16)
            if len(shape) == 1:
                return ap
            names = " ".join(f"a{i}" for i in range(len(shape)))
            kw = {f"a{i}": shape[i] for i in range(len(shape) - 1)}
            return ap.rearrange(f"p ({names}) -> p {names}", **kw)

        def fview(buf, off_f, shape):
            n = int(np.prod(shape))
            ap = buf.t[:, off_f:off_f + n]
            if len(shape) == 1:
                return ap
            names = " ".join(f"a{i}" for i in range(len(shape)))
            kw = {f"a{i}": shape[i] for i in range(len(shape) - 1)}
            return ap.rearrange(f"p ({names}) -> p {names}", **kw)

        XB = [Buf(XD[t * 128:(t + 1) * 128, :], f"xd{t}") for t in range(16)]
        first_layer = layers[0]

        def x_src(layer, t):
            if layer == first_layer:
                src = I["xp"] if t < 8 else I["xs"]
                tt = t % 8
                return src[tt * 128:(tt + 1) * 128, :], []
            return XB[t].t, [XB[t]]

        C = S.CONST
        for c2 in range(2):
            S.dma(condF[:, :, c2], I["cond"][c2].rearrange("(k p) -> p k", p=128), writes=[condF], owner=C,
                  allow_slow_non_contiguous=True)
        for l2 in range(4):
            S.dma(normwF[:, l2, :], I["norm_w"][l2].rearrange("(k p) -> p k", p=128), writes=[normwF], owner=C,
                  allow_slow_non_contiguous=True)
        for cs in range(2):
            for d, (rt, hw) in enumerate(((rope0, 64), (rope2, 16))):
                src = I["rope0" if d == 0 else "rope2"][cs].rearrange("(t p) a f -> p t (a f)", p=128)
                S.dma(rt[:, cs, :, :], src, writes=[rt], owner=C)
        S.dma(stx[:], I["stexp"], writes=[stx], owner=C)
        S.consts_done()
        OP("pool", lambda e: e.memset(ident[:], 0.0), [], [ident])
        OP("pool", lambda e: e.affine_select(out=ident[:], in_=ident[:], pattern=[[-1, 128]], compare_op=ALU.not_equal,
                                             fill=1.0, base=0, channel_multiplier=1), [ident], [ident])
        OP("act", lambda e: e.activation(out=scond[:], in_=condF[:], func=AF.Silu), [condF], [scond])
        OP("pool", lambda e: e.memset(Jm[:], 0.0), [], [Jm])
        OP("pool", lambda e: e.affine_select(out=Jm[:], in_=Jm[:], pattern=[[1, 128]], compare_op=ALU.not_equal,
                                             fill=1.0, base=-127, channel_multiplier=1), [Jm], [Jm])

        wctr = [0]

        def load_w(W, k0, c0, ncols=256, cast=True):
            f = wf.next()
            src = W[k0:k0 + 1024, c0:c0 + ncols].rearrange("(k p) n -> p k n", p=128)
            S.dma(f[:, :, 0:ncols], src, writes=[f])
            if not cast:
                return f
            b = wb.next()
            OP("pool", lambda e: e.tensor_copy(out=b[:, :, 0:ncols], in_=f[:, :, 0:ncols]), [f], [b])
            return b

        def mod(layer):
            S.dma(bmodF[:], I["b_mod"][layer].rearrange("(m p) -> p m", p=128), writes=[bmodF],
                  allow_slow_non_contiguous=True)
            psm = pP_rot.next()
            for blk in range(12):
                w = load_w(I["w_mod"][layer], 0, blk * 256, cast=False)
                for mm in range(2):
                    m = blk * 2 + mm
                    for k in range(8):
                        OP("pe", lambda e, m=m, mm=mm, k=k, w=w: e.matmul(
                            out=psm[:, m * 2:m * 2 + 2], lhsT=w[:, k, mm * 128:(mm + 1) * 128], rhs=scond[:, k, :],
                            start=(k == 0), stop=(k == 7)), [w, scond], [psm])
            OP("dve", lambda e: e.tensor_tensor(out=modF[:], in0=psm[:, 0:48].rearrange("p (m c) -> p m c", c=2),
                                                in1=bmodF[:].unsqueeze(2).to_broadcast([128, 24, 2]), op=ALU.add),
               [psm, bmodF], [modF])
            OP("dve", lambda e: e.tensor_scalar(out=scaleF[:], in0=modF[:, 8:16, :], scalar1=1.0, scalar2=None,
                                                op0=ALU.add), [modF], [scaleF])
            OP("dve", lambda e: e.tensor_tensor(out=scaleF[:], in0=scaleF[:],
                                                in1=normwF[:, layer, :].unsqueeze(2).to_broadcast([128, 8, 2]),
                                                op=ALU.mult), [scaleF, normwF], [scaleF])
            for c in range(2):
                pg = [pP_rot.next(), pP_rot.next()]
                for k in range(8):
                    g = gbt.next()
                    OP("dve", lambda e, g=g, k=k, c=c: e.tensor_copy(
                        out=g[:], in_=modF[:, 16 + k, c:c + 1].to_broadcast([128, 128])), [modF], [g])
                    OP("pe", lambda e, g=g, k=k, pg=pg: e.matmul(
                        out=pg[k // 4][:, (k % 4) * 128:(k % 4 + 1) * 128], lhsT=g[:], rhs=ident[:],
                        start=True, stop=True), [g, ident], [pg[k // 4]])
                for hlf in range(2):
                    OP("act", lambda e, hlf=hlf, c=c, pg=pg: e.copy(out=Gb[c][:, hlf * 512:(hlf + 1) * 512],
                                                                    in_=pg[hlf][:, :]), [pg[hlf]], [Gb[c]])

        def front(layer, tiles, c, hT_of):
            for ti, t in enumerate(tiles):
                xt = xt_rot.next()
                src, rb = x_src(layer, t)
                S.dma(xt[:], src, reads=rb, writes=[xt])
                s = st1.next()
                xn = xn_rot.next()
                OP("act", lambda e, xt=xt, xn=xn, s=s: e.activation(out=xn[:], in_=xt[:], func=AF.Square,
                                                                    accum_out=s[:, 0:1]), [xt], [xn, s])
                OP("act", lambda e, s=s: e.activation(out=s[:, 1:2], in_=s[:, 0:1], func=AF.Sqrt, scale=1.0 / 1024,
                                                      bias=EPS), [s], [s])
                OP("dve", lambda e, s=s: e.reciprocal(out=s[:, 2:3], in_=s[:, 1:2]), [s], [s])
                OP("dve", lambda e, xt=xt, xn=xn, s=s: e.tensor_scalar(out=xn[:], in0=xt[:], scalar1=s[:, 2:3],
                                                                       scalar2=None, op0=ALU.mult), [xt, s], [xn])
                pp = [pP_rot.next(), pP_rot.next()]
                for k in range(8):
                    OP("pe", lambda e, k=k, xn=xn, pp=pp: e.transpose(
                        out=pp[k // 4][:, (k % 4) * 128:(k % 4 + 1) * 128], in_=xn[:, k * 128:(k + 1) * 128],
                        identity=ident[:]), [xn, ident], [pp[k // 4]])
                for k in range(8):
                    dst, db = hT_of(k, ti)
                    OP("act", lambda e, k=k, dst=dst, pp=pp: e.activation(
                        out=dst, in_=pp[k // 4][:, (k % 4) * 128:(k % 4 + 1) * 128], func=AF.Identity,
                        scale=scaleF[:, k, c:c + 1], bias=modF[:, k, c:c + 1]), [pp[k // 4], scaleF, modF], [db])

        def proj(hT, hbuf, ti, w, ncols, ps):
            for k in range(8):
                OP("pe", lambda e, k=k: e.matmul(out=ps[:, 0:ncols], lhsT=hT[:, k, ti * 128:(ti + 1) * 128],
                                                 rhs=w[:, k, 0:ncols], start=(k == 0), stop=(k == 7)),
                   [hbuf, w], [ps])

        def transpose_blocks(src_ap_of, srcbuf, nblk, dst_of, evac="act"):
            i = 0
            while i < nblk:
                n = min(4, nblk - i)
                ps = pP_rot.next()
                for j in range(n):
                    OP("pe", lambda e, i=i, j=j, ps=ps: e.transpose(out=ps[:, j * 128:(j + 1) * 128],
                                                                    in_=src_ap_of(i + j), identity=ident[:]),
                       [srcbuf, ident], [ps])
                dst, db = dst_of(i, n)
                if evac == "act":
                    OP("act", lambda e, dst=dst, ps=ps, n=n: e.copy(
                        out=dst, in_=ps[:, 0:n * 128].rearrange("p (a b) -> p a b", a=n)), [ps], [db])
                else:
                    OP("dve", lambda e, dst=dst, ps=ps, n=n: e.tensor_copy(
                        out=dst, in_=ps[:, 0:n * 128].rearrange("p (a b) -> p a b", a=n)), [ps], [db])
                i += n

        def rope(src, dst, table, half, tile, ngrp):
            n = ngrp * 4 * half
            sv = src[:, 0:n].rearrange("p (g a two f) -> p g a two f", g=ngrp, a=2, two=2)
            dv = dst[:, 0:n].rearrange("p (g a two f) -> p g a two f", g=ngrp, a=2, two=2)
            cos = table[:, 0, tile, :].rearrange("p (a f) -> p a f", a=2).unsqueeze(1).to_broadcast([128, ngrp, 2, half])
            sin = table[:, 1, tile, :].rearrange("p (a f) -> p a f", a=2).unsqueeze(1).to_broadcast([128, ngrp, 2, half])
            x1, x2 = sv[:, :, :, 0, :], sv[:, :, :, 1, :]
            o1, o2 = dv[:, :, :, 0, :], dv[:, :, :, 1, :]
            t1 = rtmp.next()
            t2 = rtmp.next()
            m = ngrp * 2 * half
            t1v = t1[:, 0:m].rearrange("p (g a f) -> p g a f", g=ngrp, a=2)
            t2v = t2[:, 0:m].rearrange("p (g a f) -> p g a f", g=ngrp, a=2)
            OP("dve", lambda e: e.tensor_tensor(out=o1, in0=x1, in1=cos, op=ALU.mult), [src, table], [dst])
            OP("pool", lambda e: e.tensor_tensor(out=t1v, in0=x2, in1=sin, op=ALU.mult), [src, table], [t1])
            OP("dve", lambda e: e.tensor_tensor(out=o1, in0=o1, in1=t1v, op=ALU.subtract), [dst, t1], [dst])
            OP("pool", lambda e: e.tensor_tensor(out=t2v, in0=x1, in1=sin, op=ALU.mult), [src, table], [t2])
            OP("dve", lambda e: e.tensor_tensor(out=o2, in0=x2, in1=cos, op=ALU.mult), [src, table], [dst])
            OP("dve", lambda e: e.tensor_tensor(out=o2, in0=o2, in1=t2v, op=ALU.add), [dst, t2], [dst])

        def tail(layer, tiles, c, ogT, ogbuf, KC, Wout, ybufs):
            nt = len(tiles)
            yacc = ATT[:, 0:nt * 1024].rearrange("p (t n) -> p t n", t=nt)
            for cb in range(4):
                ws = [load_w(Wout, kk * 1024, cb * 256) for kk in range(KC // 8)]
                for ti in range(nt):
                    ps = pP_rot.next()
                    for kc in range(KC):
                        w = ws[kc // 8]
                        OP("pe", lambda e, kc=kc, w=w, ti=ti, ps=ps: e.matmul(
                            out=ps[:, 0:256], lhsT=ogT[:, kc, ti * 128:(ti + 1) * 128], rhs=w[:, kc % 8, :],
                            start=(kc == 0), stop=(kc == KC - 1)), [ogbuf, w], [ps])
                    OP("dve", lambda e, ti=ti, cb=cb, ps=ps: e.tensor_tensor(
                        out=yacc[:, ti, cb * 256:(cb + 1) * 256], in0=ps[:, 0:256],
                        in1=Gb[c][:, cb * 256:(cb + 1) * 256], op=ALU.mult), [ps, Gb[c]], ybufs)
            for ti, t in enumerate(tiles):
                xt = xt_rot.next()
                src, rb = x_src(layer, t)
                S.dma(xt[:], src, reads=rb, writes=[xt])
                OP("dve", lambda e, xt=xt, ti=ti: e.tensor_tensor(out=xt[:], in0=xt[:], in1=yacc[:, ti, :], op=ALU.add),
                   [xt] + ybufs, [xt])
                S.dma(XB[t].t, xt[:], reads=[xt], writes=[XB[t]], owner=xt, queue="act")

        def softmax_un(src, srcbufs, scale, Pout, Pbuf):
            s = st1.next()
            ax = AX.XY if len(src.shape) == 3 else AX.X
            OP("dve", lambda e: e.tensor_reduce(out=s[:, 0:1], in_=src, axis=ax, op=ALU.max), srcbufs, [s])
            OP("dve", lambda e: e.tensor_scalar(out=s[:, 1:2], in0=s[:, 0:1], scalar1=-scale, scalar2=None,
                                                op0=ALU.mult), [s], [s])
            OP("act", lambda e: e.activation(out=Pout, in_=src, func=AF.Exp, scale=scale, bias=s[:, 1:2],
                                             accum_out=s[:, 2:3]), srcbufs + [s], [Pbuf, s])
            OP("dve", lambda e: e.reciprocal(out=s[:, 3:4], in_=s[:, 2:3]), [s], [s])
            return s

        def pv(pc, pcbuf, kblocks, pcT, pcTbuf, vof, N, pso):
            nb = len(kblocks)
            i = 0
            while i < nb:
                n = min(4, nb - i)
                ps = pP_rot.next()
                for j in range(n):
                    off, nk = kblocks[i + j]
                    OP("pe", lambda e, j=j, off=off, nk=nk, ps=ps: e.transpose(
                        out=ps[0:nk, j * 128:(j + 1) * 128], in_=pc[:, off:off + nk], identity=ident[:]),
                       [pcbuf, ident], [ps])
                full = all(kblocks[i + j][1] == 128 for j in range(n))
                if full:
                    OP("act", lambda e, i=i, n=n, ps=ps: e.copy(
                        out=pcT[:, i:i + n, :], in_=ps[:, 0:n * 128].rearrange("p (a b) -> p a b", a=n)),
                       [ps], [pcTbuf])
                else:
                    for j in range(n):
                        nk = kblocks[i + j][1]
                        OP("act", lambda e, i=i, j=j, nk=nk, ps=ps: e.copy(
                            out=pcT[0:nk, i + j, :], in_=ps[0:nk, j * 128:(j + 1) * 128]), [ps], [pcTbuf])
                i += n
            for i, (off, nk) in enumerate(kblocks):
                vap, vbuf = vof(i)
                OP("pe", lambda e, i=i, nk=nk, vap=vap: e.matmul(out=pso[:, 0:N], lhsT=pcT[0:nk, i, :], rhs=vap,
                                                               start=(i == 0), stop=(i == nb - 1)),
                   [pcTbuf, vbuf], [pso])

        groups = [
            dict(tiles=[0, 1, 2, 3], c=0, seqs=[(0, 2, 0), (2, 2, 1)], sample=False, pair=0),
            dict(tiles=[4, 5, 6, 7], c=0, seqs=[(0, 2, 2), (2, 2, 3)], sample=False, pair=1),
            dict(tiles=list(range(8, 16)), c=1, seqs=[(0, 8, -1)], sample=True, pair=-1),
        ]

        def layer_diff(layer):
            lam_init = 0.8 - 0.6 * math.exp(-0.3 * layer)
            Win, Wout = I["diff_w_in"], I["diff_w_out"]
            S.dma(gnw[:, 0:1024], I["diff_gn"].partition_broadcast(128), writes=[gnw])
            OP("dve", lambda e: e.tensor_scalar(out=gnw[:, 0:1024], in0=gnw[:, 0:1024], scalar1=1.0 - lam_init,
                                                scalar2=None, op0=ALU.mult), [gnw], [gnw])
            S.dma(dlb[:], I["diff_lambda"].partition_broadcast(128), writes=[dlb])
            dl = dlb[:].rearrange("p (a f) -> p a f", a=4)
            t = tmpA.next()
            for i2 in range(2):
                OP("dve", lambda e, i2=i2: e.tensor_tensor(out=t[:, i2 * 64:(i2 + 1) * 64], in0=dl[:, 2 * i2, :],
                                                           in1=dl[:, 2 * i2 + 1, :], op=ALU.mult), [dlb], [t])
            OP("dve", lambda e: e.tensor_reduce(out=lamc[:, 0:2], in_=t[:, 0:128].rearrange("p (a f) -> p a f", a=2),
                                                axis=AX.X, op=ALU.add), [t], [lamc])
            OP("act", lambda e: e.activation(out=lamc[:, 2:4], in_=lamc[:, 0:2], func=AF.Exp), [lamc], [lamc])
            OP("dve", lambda e: e.tensor_tensor(out=lamc[:, 4:5], in0=lamc[:, 2:3], in1=lamc[:, 3:4], op=ALU.subtract),
               [lamc], [lamc])
            OP("dve", lambda e: e.tensor_scalar(out=lamc[:, 5:6], in0=lamc[:, 4:5], scalar1=lam_init, scalar2=None,
                                                op0=ALU.add), [lamc], [lamc])
            scale = 64 ** -0.5

            def do_group(g):
                S.barrier()
                tiles, c, smp = g["tiles"], g["c"], g["sample"]
                nt = len(tiles)
                T = nt * 128
                NK = T + 256 if smp else 256
                hT = bview(R_H, 0, [8, T])
                ogT = bview(R_G, 0, [8, T])
                qTb = sub(0, 1024, "qT")
                kTb = sub(1024, 1280, "kT")
                vbb = sub(2304, 1280, "vb")
                P1b = sub(3584, 1280, "P1")
                P2b = sub(4864, 1280, "P2")
                pcb = sub(6144, 1280, "pc")
                pcTb = sub(7424, 640, "pcT")
                obb = sub(8064, 2048, "ob")
                ckb = sub(10112, 1024, "ck")
                qT = bview(qTb, 0, [2, 1024])
                kT = bview(kTb, 0, [2, 1280])
                vb = bview(vbb, 0, [10, 256])
                P1 = P1b.t
                P2 = P2b.t
                pc = pcb.t
                pcT = bview(pcTb, 0, [10, 128])
                ob = fview(obb, 0, [8, 256])
                ck = fview(ckb, 0, [2, 512])
                front(layer, tiles, c, lambda k, ti: (hT[:, k, ti * 128:(ti + 1) * 128], R_H))
                CK("front")
                for hb in range(4):
                    for which, dstT, dstb in ((0, qT, qTb), (1, kT, kTb)):
                        w = load_w(Win, 0, which * 1024 + hb * 256)
                        for ti in range(nt):
                            ps = pP_rot.next()
                            proj(hT, R_H, ti, w, 256, ps)
                            ta = tmpA.next()
                            OP("act", lambda e, ta=ta, ps=ps: e.copy(out=ta[:, 0:256], in_=ps[:, 0:256]), [ps], [ta])
                            srcb = ta
                            if smp:
                                tb = tmpB.next()
                                rope(ta, tb, rope2, 16, ti, 4)
                                srcb = tb
                            elif which == 1:
                                sq = g["seqs"][ti // 2][2]
                                tt = ti % 2
                                S.dma(O["dk"][sq, 2 * hb:2 * hb + 2, tt * 128:(tt + 1) * 128, :].rearrange("h t d -> t h d"),
                                      ta[:, 0:256].rearrange("p (h d) -> p h d", h=2), reads=[ta], queue="act")
                            transpose_blocks(lambda i, srcb=srcb: srcb[:, i * 128:(i + 1) * 128], srcb, 2,
                                             lambda i0, n, dstT=dstT, dstb=dstb, ti=ti: (dstT[:, i0:i0 + n, ti * 128:(ti + 1) * 128], dstb))
                    CK("qk")
                    w = load_w(Win, 0, 2048 + hb * 256)
                    for ti in range(nt):
                        ps = pP_rot.next()
                        proj(hT, R_H, ti, w, 256, ps)
                        ta = tmpA.next()
                        OP("act", lambda e, ta=ta, ps=ps: e.copy(out=ta[:, 0:256], in_=ps[:, 0:256]), [ps], [ta])
                        OP("dve", lambda e, ta=ta, ti=ti: e.tensor_copy(out=vb[:, ti, :], in_=ta[:, 0:256]), [ta], [vbb])
                        if not smp:
                            sq = g["seqs"][ti // 2][2]
                            tt = ti % 2
                            S.dma(O["dv"][sq, 2 * hb:2 * hb + 2, tt * 128:(tt + 1) * 128, :].rearrange("h t d -> t h d"),
                                  ta[:, 0:256].rearrange("p (h d) -> p h d", h=2), reads=[ta], queue="act")
                    if smp:
                        for tt in range(2):
                            S.dma(ck[:, 0, 0:256].rearrange("p (h d) -> p h d", h=2),
                                  I["cache_diff_k"][2 * hb:2 * hb + 2, tt * 128:(tt + 1) * 128, :].rearrange("h t d -> t h d"),
                                  writes=[ckb])
                            transpose_blocks(lambda i: ck[:, 0, i * 128:(i + 1) * 128], ckb, 2,
                                             lambda i0, n, tt=tt: (kT[:, i0:i0 + n, 1024 + tt * 128:1024 + (tt + 1) * 128], kTb))
                            S.dma(ck[:, 1, 0:256].rearrange("p (h d) -> p h d", h=2),
                                  I["cache_diff_v"][2 * hb:2 * hb + 2, tt * 128:(tt + 1) * 128, :].rearrange("h t d -> t h d"),
                                  writes=[ckb])
                            OP("dve", lambda e, tt=tt: e.tensor_copy(out=vb[:, 8 + tt, :], in_=ck[:, 1, 0:256]), [ckb], [vbb])
                    CK("v")
                    for (t0, ntq, sq) in g["seqs"]:
                        nkt = NK // 128
                        if NK == 256:
                            nblk, blk = 1, 256
                        else:
                            nblk, blk = 4, 320
                        for hh in range(2):
                            h = 2 * hb + hh
                            for qi in range(ntq):
                                tq = t0 + qi
                                stats = []
                                for comp, (Pb, Pap) in enumerate(((P1b, P1), (P2b, P2))):
                                    pr = slice(comp * 64, (comp + 1) * 64)
                                    for b in range(nblk):
                                        k0 = (t0 * 128 if not smp else 0) + b * blk
                                        OP("pe", lambda e, pr=pr, hh=hh, tq=tq, k0=k0, b=b, blk=blk: e.matmul(
                                            out=PB[b][:, 0:blk], lhsT=qT[pr, hh, tq * 128:(tq + 1) * 128],
                                            rhs=kT[pr, hh, k0:k0 + blk], start=True, stop=True), [qTb, kTb], [PB[b]])
                                    src = PSt[:, 0:nblk, 0:blk]
                                    s = softmax_un(src, PB[0:nblk], scale, Pap[:, 0:NK].rearrange("p (a b) -> p a b", a=nblk), Pb)
                                    stats.append(s)
                                s1, s2 = stats
                                OP("dve", lambda e, s2=s2: e.tensor_tensor(out=s2[:, 4:5], in0=s2[:, 3:4], in1=lamc[:, 5:6],
                                                                           op=ALU.mult), [s2, lamc], [s2])
                                OP("dve", lambda e, s2=s2: e.tensor_scalar(out=P2[:, 0:NK], in0=P2[:, 0:NK], scalar1=s2[:, 4:5],
                                                                           scalar2=None, op0=ALU.mult), [P2b, s2], [P2b])
                                OP("dve", lambda e, s1=s1: e.scalar_tensor_tensor(out=pc[:, 0:NK], in0=P1[:, 0:NK], scalar=s1[:, 3:4],
                                                                                  in1=P2[:, 0:NK], op0=ALU.mult, op1=ALU.subtract),
                                   [P1b, P2b, s1], [pcb])
                                pso = pP_rot.next()
                                vt0 = 0 if smp else t0
                                pv(pc, pcb, [(i * 128, 128) for i in range(nkt)], pcT, pcTb,
                                   lambda i, hh=hh, vt0=vt0: (vb[:, vt0 + i, hh * 128:(hh + 1) * 128], vbb), 128, pso)
                                s = st1.next()
                                ta = tmpA.next()
                                OP("act", lambda e, s=s, ta=ta, pso=pso: e.activation(out=ta[:, 0:128], in_=pso[:, 0:128], func=AF.Square,
                                                                                     accum_out=s[:, 0:1]), [pso], [ta, s])
                                OP("act", lambda e, s=s: e.activation(out=s[:, 1:2], in_=s[:, 0:1], func=AF.Sqrt, scale=1.0 / 128,
                                                                      bias=EPS), [s], [s])
                                OP("dve", lambda e, s=s: e.reciprocal(out=s[:, 2:3], in_=s[:, 1:2]), [s], [s])
                                OP("dve", lambda e, s=s, pso=pso, tq=tq, hh=hh, h=h: e.scalar_tensor_tensor(
                                    out=ob[:, tq, hh * 128:(hh + 1) * 128], in0=pso[:, 0:128], scalar=s[:, 2:3],
                                    in1=gnw[:, h * 128:(h + 1) * 128], op0=ALU.mult, op1=ALU.mult), [pso, s, gnw], [obb])
                    CK("attn")
                    w = load_w(Win, 0, 3072 + hb * 256)
                    for ti in range(nt):
                        ps = pP_rot.next()
                        proj(hT, R_H, ti, w, 256, ps)
                        ta = tmpA.next()
                        OP("act", lambda e, ta=ta, ps=ps: e.activation(out=ta[:, 0:256], in_=ps[:, 0:256], func=AF.Silu), [ps], [ta])
                        OP("dve", lambda e, ta=ta, ti=ti: e.tensor_tensor(out=ob[:, ti, :], in0=ob[:, ti, :], in1=ta[:, 0:256],
                                                                          op=ALU.mult), [obb, ta], [obb])
                        transpose_blocks(lambda i, ti=ti: ob[:, ti, i * 128:(i + 1) * 128], obb, 2,
                                         lambda i0, n, ti=ti, hb=hb: (ogT[:, 2 * hb + i0:2 * hb + i0 + n, ti * 128:(ti + 1) * 128], R_G))
                S.barrier()
                ybufs = [Buf(ATT[:, 0:nt * 1024], "yacc")]
                tail(layer, tiles, c, ogT, R_G, 8, Wout, ybufs)

            for g in groups:
                do_group(g)
            S.barrier()

        nab_rot = Rot([S.sb([128, 576], F32, "nab") for _ in range(2)])

        def layer_na(layer):
            Win, Wout = I["na_w_in"], I["na_w_out"]
            scale = 64 ** -0.5

            def do_group(g):
                S.barrier()
                tiles, c, smp = g["tiles"], g["c"], g["sample"]
                nt = len(tiles)
                T = nt * 128
                hT = bview(R_H, 0, [8, T])
                ogT = bview(R_G, 0, [8, T])
                qTb = sub(0, 1024, "qT")
                kTb = sub(1024, 1280, "kT")
                vbb = sub(2304, 1280, "vb")
                P1b = sub(3584, 1280, "P1")
                pcb = sub(6144, 1280, "pc")
                pcTb = sub(7424, 640, "pcT")
                obb = sub(8064, 2048, "ob")
                ckb = sub(10112, 1024, "ck")
                qT = bview(qTb, 0, [2, 1024])
                kT = bview(kTb, 0, [2, 1280])
                vb = bview(vbb, 0, [10, 256])
                P1 = P1b.t
                pc = pcb.t
                pcT = bview(pcTb, 0, [10, 128])
                ob = fview(obb, 0, [8, 256])
                ck = fview(ckb, 0, [2, 512])
                front(layer, tiles, c, lambda k, ti: (hT[:, k, ti * 128:(ti + 1) * 128], R_H))
                for hb in range(4):
                    for which, dstT, dstb in ((0, qT, qTb), (1, kT, kTb)):
                        w = load_w(Win, 0, which * 1024 + hb * 256)
                        for ti in range(nt):
                            ps = pP_rot.next()
                            proj(hT, R_H, ti, w, 256, ps)
                            ta = tmpA.next()
                            OP("act", lambda e, ta=ta, ps=ps: e.copy(out=ta[:, 0:256], in_=ps[:, 0:256]), [ps], [ta])
                            if which == 1 and not smp:
                                sq = g["seqs"][ti // 2][2]
                                tt = ti % 2
                                S.dma(O["nk"][sq, 4 * hb:4 * hb + 4, tt * 128:(tt + 1) * 128, :].rearrange("h t d -> t h d"),
                                      ta[:, 0:256].rearrange("p (h d) -> p h d", h=4), reads=[ta], queue="act")
                            transpose_blocks(lambda i, ta=ta: ta[:, i * 128:(i + 1) * 128], ta, 2,
                                             lambda i0, n, dstT=dstT, dstb=dstb, ti=ti: (dstT[:, i0:i0 + n, ti * 128:(ti + 1) * 128], dstb))
                    w = load_w(Win, 0, 2048 + hb * 256)
                    for ti in range(nt):
                        ps = pP_rot.next()
                        proj(hT, R_H, ti, w, 256, ps)
                        ta = tmpA.next()
                        OP("act", lambda e, ta=ta, ps=ps: e.copy(out=ta[:, 0:256], in_=ps[:, 0:256]), [ps], [ta])
                        OP("dve", lambda e, ta=ta, ti=ti: e.tensor_copy(out=vb[:, ti, :], in_=ta[:, 0:256]), [ta], [vbb])
                        if not smp:
                            sq = g["seqs"][ti // 2][2]
                            tt = ti % 2
                            S.dma(O["nv"][sq, 4 * hb:4 * hb + 4, tt * 128:(tt + 1) * 128, :].rearrange("h t d -> t h d"),
                                  ta[:, 0:256].rearrange("p (h d) -> p h d", h=4), reads=[ta], queue="act")
                    if smp:
                        for tt in range(2):
                            S.dma(ck[:, 0, 0:256].rearrange("p (h d) -> p h d", h=4),
                                  I["cache_na_k"][4 * hb:4 * hb + 4, tt * 128:(tt + 1) * 128, :].rearrange("h t d -> t h d"),
                                  writes=[ckb])
                            transpose_blocks(lambda i: ck[:, 0, i * 128:(i + 1) * 128], ckb, 2,
                                             lambda i0, n, tt=tt: (kT[:, i0:i0 + n, 1024 + tt * 128:1024 + (tt + 1) * 128], kTb))
                            S.dma(ck[:, 1, 0:256].rearrange("p (h d) -> p h d", h=4),
                                  I["cache_na_v"][4 * hb:4 * hb + 4, tt * 128:(tt + 1) * 128, :].rearrange("h t d -> t h d"),
                                  writes=[ckb])
                            OP("dve", lambda e, tt=tt: e.tensor_copy(out=vb[:, 8 + tt, :], in_=ck[:, 1, 0:256]), [ckb], [vbb])
                    for (t0, ntq, sq) in g["seqs"]:
                        for hh in range(4):
                            h = 4 * hb + hh
                            cc = hh // 2
                            pr = slice((hh % 2) * 64, (hh % 2) * 64 + 64)
                            for qi in range(ntq):
                                tq = t0 + qi
                                if not smp:
                                    OP("pe", lambda e, pr=pr, cc=cc, tq=tq, t0=t0: e.matmul(
                                        out=PB[0][:, 0:256], lhsT=qT[pr, cc, tq * 128:(tq + 1) * 128],
                                        rhs=kT[pr, cc, t0 * 128:t0 * 128 + 256], start=True, stop=True), [qTb, kTb], [PB[0]])
                                    s1 = softmax_un(PB[0][:, 0:256], [PB[0]], scale, P1[:, 0:256], P1b)
                                    NKs = 256
                                    kblocks = [(0, 128), (128, 128)]
                                    vof = lambda i, hh=hh, t0=t0: (vb[:, t0 + i, hh * 64:(hh + 1) * 64], vbb)
                                else:
                                    j = qi
                                    r0 = min(max(2 * j - 4, 0), 8)
                                    nr = min(9, 16 - r0)
                                    nloc = nr * 64
                                    NKs = nloc + 256
                                    blk = NKs // 2
                                    nab = nab_rot.next()
                                    S.dma(nab[:], I["na_bias_x"][h, NA_JT[j]], writes=[nab])
                                    segs = [(r0 * 64, nloc, 0, True), (1024, 256, nloc, False)]
                                    pieces = []
                                    for key0, n, col0, biased in segs:
                                        done = 0
                                        while done < n:
                                            col = col0 + done
                                            b = col // blk
                                            m = min(n - done, (b + 1) * blk - col)
                                            pieces.append((key0 + done, m, b, col - b * blk, col, biased, col0 + done - col0 + (0 if not biased else 0)))
                                            done += m
                                    for (k0, m, b, bc, col, biased, _) in pieces:
                                        OP("pe", lambda e, pr=pr, cc=cc, tq=tq, k0=k0, m=m, b=b, bc=bc: e.matmul(
                                            out=PB[b][:, bc:bc + m], lhsT=qT[pr, cc, tq * 128:(tq + 1) * 128],
                                            rhs=kT[pr, cc, k0:k0 + m], start=True, stop=True), [qTb, kTb], [PB[b]])
                                    for (k0, m, b, bc, col, biased, _) in pieces:
                                        if biased:
                                            OP("dve", lambda e, m=m, b=b, bc=bc, col=col, nab=nab: e.scalar_tensor_tensor(
                                                out=P1[:, col:col + m], in0=PB[b][:, bc:bc + m], scalar=scale,
                                                in1=nab[:, col:col + m], op0=ALU.mult, op1=ALU.add), [PB[b], nab], [P1b])
                                        else:
                                            OP("dve", lambda e, m=m, b=b, bc=bc, col=col: e.tensor_scalar(
                                                out=P1[:, col:col + m], in0=PB[b][:, bc:bc + m], scalar1=scale, scalar2=None,
                                                op0=ALU.mult), [PB[b]], [P1b])
                                    s1 = softmax_un(P1[:, 0:NKs], [P1b], 1.0, P1[:, 0:NKs], P1b)
                                    kblocks = [(i * 128, 128) for i in range(nloc // 128)]
                                    if nloc % 128:
                                        kblocks.append((nloc - 64, 64))
                                    nlb = len(kblocks)
                                    kblocks += [(nloc, 128), (nloc + 128, 128)]

                                    def vof(i, hh=hh, r0=r0, nlb=nlb, kblocks=kblocks):
                                        if i < nlb:
                                            nk = kblocks[i][1]
                                            return vb[0:nk, r0 // 2 + i, hh * 64:(hh + 1) * 64], vbb
                                        return vb[:, 8 + (i - nlb), hh * 64:(hh + 1) * 64], vbb
                                OP("dve", lambda e, s1=s1, NKs=NKs: e.tensor_scalar(out=pc[:, 0:NKs], in0=P1[:, 0:NKs], scalar1=s1[:, 3:4],
                                                                                scalar2=None, op0=ALU.mult), [P1b, s1], [pcb])
                                pso = pP_rot.next()
                                pv(pc, pcb, kblocks, pcT, pcTb, vof, 64, pso)
                                OP("act", lambda e, pso=pso, tq=tq, hh=hh: e.copy(out=ob[:, tq, hh * 64:(hh + 1) * 64], in_=pso[:, 0:64]),
                                   [pso], [obb])
                    w = load_w(Win, 0, 3072 + hb * 256)
                    for ti in range(nt):
                        ps = pP_rot.next()
                        proj(hT, R_H, ti, w, 256, ps)
                        ta = tmpA.next()
                        OP("act", lambda e, ta=ta, ps=ps: e.activation(out=ta[:, 0:256], in_=ps[:, 0:256], func=AF.Silu), [ps], [ta])
                        OP("dve", lambda e, ta=ta, ti=ti: e.tensor_tensor(out=ob[:, ti, :], in0=ob[:, ti, :], in1=ta[:, 0:256],
                                                                          op=ALU.mult), [obb, ta], [obb])
                        transpose_blocks(lambda i, ti=ti: ob[:, ti, i * 128:(i + 1) * 128], obb, 2,
                                         lambda i0, n, ti=ti, hb=hb: (ogT[:, 2 * hb + i0:2 * hb + i0 + n, ti * 128:(ti + 1) * 128], R_G))
                S.barrier()
                ybufs = [Buf(ATT[:, 0:nt * 1024], "yacc")]
                tail(layer, tiles, c, ogT, R_G, 8, Wout, ybufs)

            for g in groups:
                do_group(g)
            S.barrier()

        def layer_ret(layer):
            Win, Wout = I["ret_w_in"], I["ret_w_out"]
            S.dma(gnw[:, 0:2048], I["ret_gn"].partition_broadcast(128), writes=[gnw])
            S.dma(lgt[:, 0:8], I["ret_decay"].partition_broadcast(128), writes=[lgt])
            OP("act", lambda e: e.activation(out=lgt[:, 0:8], in_=lgt[:, 0:8], func=AF.Exp, scale=-1.0), [lgt], [lgt])
            OP("act", lambda e: e.activation(out=lgt[:, 0:8], in_=lgt[:, 0:8], func=AF.Ln, bias=1.0), [lgt], [lgt])
            OP("dve", lambda e: e.tensor_scalar(out=lgt[:, 0:8], in0=lgt[:, 0:8], scalar1=-1.0, scalar2=None, op0=ALU.mult), [lgt], [lgt])
            OP("dve", lambda e: e.tensor_scalar(out=lgt[:, 8:12], in0=lgt[:, 4:8], scalar1=-1.0, scalar2=None, op0=ALU.mult), [lgt], [lgt])
            OP("dve", lambda e: e.tensor_scalar(out=lgt[:, 12:16], in0=lgt[:, 4:8], scalar1=1024.0, scalar2=None, op0=ALU.mult), [lgt], [lgt])
            for h in range(4):
                OP("act", lambda e, h=h: e.activation(out=dsc[:, h, 0:2], in_=stx[:, 0:2], func=AF.Exp, scale=lgt[:, h:h + 1]), [stx, lgt], [dsc])
                OP("act", lambda e, h=h: e.activation(out=dsc[:, h, 2:4], in_=stx[:, 2:4], func=AF.Exp, scale=lgt[:, 4 + h:5 + h]), [stx, lgt], [dsc])
            OP("dve", lambda e: e.tensor_scalar(out=dsc[:], in0=dsc[:], scalar1=1.0 / 16, scalar2=None, op0=ALU.mult), [dsc], [dsc])

            def do_group(g):
                S.barrier()
                tiles, c, smp = g["tiles"], g["c"], g["sample"]
                nt = len(tiles)
                T = nt * 128
                hT = bview(R_H, 0, [8, T])
                ogT = bview(R_G, 0, [16, T])
                qTb = sub(0, 1024, "qT")
                kTb = sub(1024, 1024, "kT")
                vbb = sub(2048, 2048, "vb")
                atb = sub(4096, 4096, "attT")
                obb = sub(8192, 4096, "ob")
                qT = bview(qTb, 0, [2, 1024])
                kT = bview(kTb, 0, [2, 1024])
                vb = bview(vbb, 0, [8, 512])
                attT = bview(atb, 0, [8, 1024])
                gpn = fview(atb, 0, [2, 1920])
                ob = fview(obb, 0, [8, 512])
                if smp:
                    s0b = sub(12288, 1024, "S0b")
                    qdb = sub(13312, 2048, "qTd")
                    S0 = bview(s0b, 0, [2, 2, 512])
                    qTd = bview(qdb, 0, [2, 2, 1024])
                else:
                    kdb = sub(12288, 1024, "kdec")
                    kdec = bview(kdb, 0, [2, 4, 256])
                front(layer, tiles, c, lambda k, ti: (hT[:, k, ti * 128:(ti + 1) * 128], R_H))
                for h in range(4):
                    stp = strip.next()
                    for i2 in range(2):
                        S.dma(gpn[:, i2, :], I["gpn"][i2], writes=[atb])
                    OP("act", lambda e, h=h: e.activation(out=gpn[:, 0, :], in_=gpn[:, 0, :], func=AF.Exp, scale=lgt[:, h:h + 1]), [atb, lgt], [atb])
                    OP("act", lambda e, h=h: e.activation(out=gpn[:, 1, :], in_=gpn[:, 1, :], func=AF.Exp, scale=lgt[:, 4 + h:5 + h]), [atb, lgt], [atb])
                    OP("dve", lambda e: e.scalar_tensor_tensor(out=gpn[:, 0, :], in0=gpn[:, 0, :], scalar=-1.0, in1=gpn[:, 1, :],
                                                               op0=ALU.add, op1=ALU.add), [atb], [atb])
                    OP("dve", lambda e: e.tensor_tensor(out=gpn[:, 0, 896:1024], in0=gpn[:, 0, 896:1024], in1=ident[:], op=ALU.add),
                       [atb, ident], [atb])
                    OP("dve", lambda e, stp=stp: e.tensor_scalar(out=stp[:], in0=gpn[:, 0, :], scalar1=1.0 / 16, scalar2=None, op0=ALU.mult),
                       [atb], [stp])
                    for which, dstT, dstb in ((0, qT, qTb), (1, kT, kTb)):
                        w = load_w(Win, 0, which * 1024 + h * 256)
                        for ti in range(nt):
                            ps = pP_rot.next()
                            proj(hT, R_H, ti, w, 256, ps)
                            ta = tmpA.next()
                            OP("act", lambda e, ta=ta, ps=ps: e.copy(out=ta[:, 0:256], in_=ps[:, 0:256]), [ps], [ta])
                            srcb = ta
                            if smp:
                                tb = tmpB.next()
                                rope(ta, tb, rope0, 64, ti, 1)
                                srcb = tb
                            elif which == 1:
                                tt = ti % 2
                                for d in range(2):
                                    OP("dve", lambda e, ta=ta, d=d, ti=ti, tt=tt, h=h: e.tensor_scalar(
                                        out=kdec[:, d, ti, :], in0=ta[:, 0:256], scalar1=dsc[:, h, 2 * d + tt:2 * d + tt + 1], scalar2=None,
                                        op0=ALU.mult), [ta, dsc], [kdb])
                            transpose_blocks(lambda i, srcb=srcb: srcb[:, i * 128:(i + 1) * 128], srcb, 2,
                                             lambda i0, n, dstT=dstT, dstb=dstb, ti=ti: (dstT[:, i0:i0 + n, ti * 128:(ti + 1) * 128], dstb))
                    for v2 in range(2):
                        w = load_w(Win, 0, 2048 + h * 512 + v2 * 256)
                        for ti in range(nt):
                            ps = pP_rot.next()
                            proj(hT, R_H, ti, w, 256, ps)
                            OP("act", lambda e, ti=ti, ps=ps, v2=v2: e.copy(out=vb[:, ti, v2 * 256:(v2 + 1) * 256], in_=ps[:, 0:256]), [ps], [vbb])
                    if smp:
                        for d in range(2):
                            for dc in range(2):
                                sb_ = osb.next()
                                S.dma(sb_[:], I["state_ret"][d, h, dc * 128:(dc + 1) * 128, :], writes=[sb_])
                                OP("pool", lambda e, sb_=sb_, d=d, dc=dc: e.tensor_copy(out=S0[:, d, dc, :], in_=sb_[:]), [sb_], [s0b])
                            rd = xt_rot.next()
                            S.dma(rd[:], I["iota1k"], writes=[rd])
                            if d == 0:
                                OP("act", lambda e, rd=rd, h=h: e.activation(out=rd[:], in_=rd[:], func=AF.Exp, scale=lgt[:, h:h + 1],
                                                                             bias=lgt[:, h:h + 1]), [rd, lgt], [rd])
                            else:
                                OP("act", lambda e, rd=rd, h=h: e.activation(out=rd[:], in_=rd[:], func=AF.Exp, scale=lgt[:, 8 + h:9 + h],
                                                                             bias=lgt[:, 12 + h:13 + h]), [rd, lgt], [rd])
                            for dc in range(2):
                                OP("dve", lambda e, rd=rd, d=d, dc=dc: e.tensor_tensor(out=qTd[:, d, dc, :], in0=qT[:, dc, :], in1=rd[:],
                                                                                       op=ALU.mult), [qTb, rd], [qdb])
                    for (t0, ntq, sq) in g["seqs"]:
                        nq = ntq * 128
                        nqb = (nq + 511) // 512
                        N = min(nq, 512)
                        for i in range(ntq):
                            for qh in range(nqb):
                                for dc in range(2):
                                    OP("pe", lambda e, i=i, qh=qh, dc=dc, t0=t0, N=N: e.matmul(
                                        out=PB[qh][:, 0:N], lhsT=kT[:, dc, (t0 + i) * 128:(t0 + i + 1) * 128],
                                        rhs=qT[:, dc, t0 * 128 + qh * 512:t0 * 128 + qh * 512 + N], start=(dc == 0), stop=(dc == 1)),
                                       [qTb, kTb], [PB[qh]])
                                OP("dve", lambda e, i=i, qh=qh, N=N, stp=stp: e.tensor_tensor(
                                    out=attT[:, i, qh * 512:qh * 512 + N], in0=PB[qh][:, 0:N],
                                    in1=stp[:, (7 - i) * 128 + qh * 512:(7 - i) * 128 + qh * 512 + N], op=ALU.mult), [PB[qh], stp], [atb])
                        for j in range(ntq):
                            pso = pP_rot.next()
                            nmm = ntq + (4 if smp else 0)
                            cnt = 0
                            for i in range(ntq):
                                OP("pe", lambda e, i=i, j=j, t0=t0, cnt=cnt, nmm=nmm, pso=pso: e.matmul(
                                    out=pso[:, 0:512], lhsT=attT[:, i, j * 128:(j + 1) * 128], rhs=vb[:, t0 + i, :],
                                    start=(cnt == 0), stop=(cnt == nmm - 1)), [atb, vbb], [pso])
                                cnt += 1
                            if smp:
                                for d in range(2):
                                    for dc in range(2):
                                        OP("pe", lambda e, d=d, dc=dc, j=j, cnt=cnt, nmm=nmm, pso=pso: e.matmul(
                                            out=pso[:, 0:512], lhsT=qTd[:, d, dc, j * 128:(j + 1) * 128], rhs=S0[:, d, dc, :],
                                            start=(cnt == 0), stop=(cnt == nmm - 1)), [qdb, s0b], [pso])
                                        cnt += 1
                            s = st1.next()
                            ta = tmpA.next()
                            OP("dve", lambda e, s=s, pso=pso: e.tensor_reduce(out=s[:, 0:1], in_=pso[:, 0:512], axis=AX.X, op=ALU.add), [pso], [s])
                            OP("dve", lambda e, s=s: e.tensor_scalar(out=s[:, 1:2], in0=s[:, 0:1], scalar1=-1.0 / 512, scalar2=None, op0=ALU.mult), [s], [s])
                            OP("act", lambda e, s=s, ta=ta, pso=pso: e.activation(out=ta[:, 0:512], in_=pso[:, 0:512], func=AF.Identity, bias=s[:, 1:2]),
                               [pso, s], [ta])
                            tb = tmpB.next()
                            OP("act", lambda e, s=s, ta=ta, tb=tb: e.activation(out=tb[:, 0:512], in_=ta[:, 0:512], func=AF.Square, accum_out=s[:, 2:3]),
                               [ta], [tb, s])
                            OP("act", lambda e, s=s: e.activation(out=s[:, 3:4], in_=s[:, 2:3], func=AF.Sqrt, scale=1.0 / 512, bias=GN_EPS), [s], [s])
                            OP("dve", lambda e, s=s: e.reciprocal(out=s[:, 4:5], in_=s[:, 3:4]), [s], [s])
                            OP("dve", lambda e, s=s, ta=ta, j=j, t0=t0, h=h: e.scalar_tensor_tensor(
                                out=ob[:, t0 + j, :], in0=ta[:, 0:512], scalar=s[:, 4:5], in1=gnw[:, h * 512:(h + 1) * 512],
                                op0=ALU.mult, op1=ALU.mult), [ta, s, gnw], [obb])
                        if not smp:
                            for d in range(2):
                                for dc in range(2):
                                    pst = pP_rot.next()
                                    for tt in range(2):
                                        OP("pe", lambda e, d=d, dc=dc, tt=tt, t0=t0, pst=pst: e.matmul(
                                            out=pst[:, 0:512], lhsT=kdec[:, d, t0 + tt, dc * 128:(dc + 1) * 128], rhs=vb[:, t0 + tt, :],
                                            start=(tt == 0), stop=(tt == 1)), [kdb, vbb], [pst])
                                    sb_ = osb.next()
                                    OP("act", lambda e, sb_=sb_, pst=pst: e.copy(out=sb_[:], in_=pst[:, 0:512]), [pst], [sb_])
                                    S.dma(O["st_ret"][sq, d, h, dc * 128:(dc + 1) * 128, :], sb_[:], reads=[sb_], queue="act")
                    for g2 in range(2):
                        w = load_w(Win, 0, 4096 + h * 512 + g2 * 256)
                        for ti in range(nt):
                            ps = pP_rot.next()
                            proj(hT, R_H, ti, w, 256, ps)
                            ta = tmpA.next()
                            OP("act", lambda e, ta=ta, ps=ps: e.activation(out=ta[:, 0:256], in_=ps[:, 0:256], func=AF.Silu), [ps], [ta])
                            OP("dve", lambda e, ta=ta, ti=ti, g2=g2: e.tensor_tensor(out=ob[:, ti, g2 * 256:(g2 + 1) * 256],
                                                                                     in0=ob[:, ti, g2 * 256:(g2 + 1) * 256], in1=ta[:, 0:256],
                                                                                     op=ALU.mult), [obb, ta], [obb])
                    for ti in range(nt):
                        transpose_blocks(lambda i, ti=ti: ob[:, ti, i * 128:(i + 1) * 128], obb, 4,
                                         lambda i0, n, ti=ti, h=h: (ogT[:, 4 * h + i0:4 * h + i0 + n, ti * 128:(ti + 1) * 128], R_G))
                S.barrier()
                ybufs = [Buf(ATT[:, 0:nt * 1024], "yacc")]
                tail(layer, tiles, c, ogT, R_G, 16, Wout, ybufs)

            for g in groups:
                do_group(g)
            S.barrier()

        RKVG = [dscr(f"rkvg{n}", (2048, 1024)) for n in range(4)]
        RKVGB = [[Buf(RKVG[n][t * 128:(t + 1) * 128, :], f"rkvg{n}_{t}") for t in range(16)] for n in range(4)]
        TMP = dscr("tmp_", (128, 256, 3, 64))
        TMS = dscr("tms_", (32, 1024, 3, 64))
        FMP = dscr("fmp_", (128, 5, 64, 256))
        FMS = dscr("fms_", (32, 5, 64, 1024))
        YPD = dscr("ypd", (128, 256, 64))
        YSD = dscr("ysd", (32, 1024, 64))
        TMPB, TMSB, FMPB, FMSB = Buf(TMP, "tmp"), Buf(TMS, "tms"), Buf(FMP, "fmp"), Buf(FMS, "fms")
        YPDB = Buf(YPD, "ypd")
        YSDB = Buf(YSD, "ysd")
        tri = S.sb([128, 128], F32, "tri")
        cmask = S.sb([64, 3, 128], F32, "cmask")

        def layer_rwkv(layer):
            Win, Wout = I["rwkv_w_in"], I["rwkv_w_out"]
            S.barrier()
            for n in range(6):
                S.dma(muF[:, n, :], I["rwkv_mu"][n].rearrange("(k p) -> p k", p=128), writes=[muF], allow_slow_non_contiguous=True)
            smallb = bsub(24576, 3072, "rwsmall")
            wAb = bview(smallb, 0, [2, 8, 64])
            aAb = bview(smallb, 512, [2, 8, 64])
            wBb = bview(smallb, 1024, [2, 1024])
            aBb = bview(smallb, 2048, [2, 1024])
            for d in range(2):
                for src, dst in ((I["rwkv_wA"], wAb), (I["rwkv_aA"], aAb)):
                    f = wf.next()
                    S.dma(f[:, :, 0:64], src[d].rearrange("(k p) r -> p k r", p=128), writes=[f])
                    OP("pool", lambda e, f=f, dst=dst, d=d: e.tensor_copy(out=dst[:, d, :, :], in_=f[:, :, 0:64]), [f], [smallb])
                for src, dst in ((I["rwkv_wB"], wBb), (I["rwkv_aB"], aBb)):
                    f = wf.next()
                    fv = f.t[0:64].rearrange("p k n -> p (k n)")[:, 0:1024]
                    S.dma(fv, src[d], writes=[f])
                    OP("pool", lambda e, fv=fv, dst=dst, d=d: e.tensor_copy(out=dst[0:64, d, :], in_=fv), [f], [smallb])
            lorab = bsub(0, 4096, "lora")
            LWT = bview(lorab, 0, [2, 2048])
            LAT = bview(lorab, 2048, [2, 2048])
            OP("dve", lambda e: e.memset(bsum[:], 0.0), [], [bsum])

            def a1_group(g, gi):
                tiles, c, smp = g["tiles"], g["c"], g["sample"]
                nt = len(tiles)
                T = nt * 128
                tok0 = tiles[0] * 128
                hTb = bsub(4096, 8192, "hTf")
                xxb = bsub(12288, 8192, "xxT")
                xnb = bsub(20480, 4096, "xnT")
                hT = fview(hTb, 0, [8, T])
                xx = fview(xxb, 0, [8, T])
                xn = bview(xnb, 0, [8, T])
                front(layer, tiles, c, lambda k, ti: (hT[:, k, ti * 128:(ti + 1) * 128], hTb))
                for (t0, ntq, sq) in g["seqs"]:
                    o = t0 * 128
                    L = ntq * 128
                    OP("dve", lambda e, o=o, L=L: e.tensor_tensor(out=xx[:, :, o + 1:o + L - 1], in0=hT[:, :, o:o + L - 2],
                                                                 in1=hT[:, :, o + 2:o + L], op=ALU.add), [hTb], [xxb])
                    OP("dve", lambda e, o=o, L=L: e.scalar_tensor_tensor(out=xx[:, :, o + 1:o + L - 1], in0=xx[:, :, o + 1:o + L - 1],
                                                                        scalar=0.5, in1=hT[:, :, o + 1:o + L - 1],
                                                                        op0=ALU.mult, op1=ALU.subtract), [xxb, hTb], [xxb])
                    OP("dve", lambda e, o=o: e.scalar_tensor_tensor(out=xx[:, :, o:o + 1], in0=hT[:, :, o + 1:o + 2], scalar=0.5,
                                                                    in1=hT[:, :, o:o + 1], op0=ALU.mult, op1=ALU.subtract), [hTb], [xxb])
                    OP("dve", lambda e, o=o, L=L: e.scalar_tensor_tensor(out=xx[:, :, o + L - 1:o + L], in0=hT[:, :, o + L - 2:o + L - 1],
                                                                        scalar=0.5, in1=hT[:, :, o + L - 1:o + L],
                                                                        op0=ALU.mult, op1=ALU.subtract), [hTb], [xxb])

                def mix(n):
                    for k in range(8):
                        OP("dve", lambda e, k=k, n=n: e.scalar_tensor_tensor(out=xn[:, k, :], in0=xx[:, k, :], scalar=muF[:, n, k:k + 1],
                                                                             in1=hT[:, k, :], op0=ALU.mult, op1=ALU.add),
                           [xxb, hTb, muF], [xnb])
                for pi, n in enumerate((0, 2, 3, 5)):
                    mix(n)
                    for cb in range(4):
                        w = load_w(Win, 0, pi * 1024 + cb * 256)
                        for ti in range(nt):
                            ps = pP_rot.next()
                            proj(xn, xnb, ti, w, 256, ps)
                            ta = tmpA.next()
                            OP("act", lambda e, ta=ta, ps=ps: e.copy(out=ta[:, 0:256], in_=ps[:, 0:256]), [ps], [ta])
                            t = tiles[ti]
                            S.dma(RKVG[pi][t * 128:(t + 1) * 128, cb * 256:(cb + 1) * 256], ta[:, 0:256], reads=[ta],
                                  writes=[RKVGB[pi][t]], owner=ta, queue="act")
                for n, Ab, LT, fn in ((1, wAb, LWT, AF.Tanh), (4, aAb, LAT, AF.Copy)):
                    mix(n)
                    for d in range(2):
                        for c0 in range(0, T, 512):
                            ps = pP_rot.next()
                            for k in range(8):
                                OP("pe", lambda e, k=k, d=d, c0=c0, Ab=Ab, ps=ps: e.matmul(out=ps[0:64, 0:512], lhsT=Ab[:, d, k, :],
                                                                                          rhs=xn[:, k, c0:c0 + 512], start=(k == 0), stop=(k == 7)),
                                   [smallb, xnb], [ps])
                            if fn == AF.Tanh:
                                OP("act", lambda e, d=d, c0=c0, LT=LT, ps=ps: e.activation(out=LT[0:64, d, tok0 + c0:tok0 + c0 + 512],
                                                                                        in_=ps[0:64, 0:512], func=AF.Tanh), [ps], [lorab])
                            else:
                                OP("act", lambda e, d=d, c0=c0, LT=LT, ps=ps: e.copy(out=LT[0:64, d, tok0 + c0:tok0 + c0 + 512],
                                                                                  in_=ps[0:64, 0:512]), [ps], [lorab])

            for gi, g in enumerate(groups):
                a1_group(g, gi)
            S.barrier()

            tabb = bsub(4096, 8192, "tabs")
            TAB = fview(tabb, 0, [8, 1024])
            for i2, src in enumerate((I["rwkv_w0"][0], I["rwkv_w0"][1], I["rwkv_a0"][0], I["rwkv_a0"][1], I["rwkv_kk"],
                                      I["rwkv_ka"], I["rwkv_rk"], I["rwkv_gn"])):
                S.dma(TAB[:, i2, :], src.partition_broadcast(128), writes=[tabb])
            slot = [bsub(12288 + i2 * 1024, 1024, f"slot{i2}") for i2 in range(12)]
            xtb = [Buf(b_.t[:, :], b_.name + "_a2") for b_ in (xt_rot.bufs + xn_rot.bufs)]
            Rb, Kb, Vb, KKb, LWb, ABb, KDb, T1b, FLWb = slot[0:9]
            Frot = Rot([slot[9], slot[10]])
            Hrot = Rot([slot[11], xtb[0]])
            FMrot = Rot([xtb[1], xtb[2]])
            h16 = lambda b: b.t.rearrange("p (h d) -> p h d", h=16)
            S.dma(tri[:], I["tri"], writes=[tri])
            S.dma(cmask[:], I["cmask"], writes=[cmask])

            def flip(srcb, dstb):
                for hf in range(2):
                    ps = pP_rot.next()
                    OP("pe", lambda e, hf=hf, ps=ps: e.matmul(out=ps[:, :], lhsT=Jm[:], rhs=srcb.t[:, hf * 512:(hf + 1) * 512],
                                                             start=True, stop=True), [Jm, srcb], [ps])
                    OP("act", lambda e, hf=hf, ps=ps: e.copy(out=dstb.t[:, hf * 512:(hf + 1) * 512], in_=ps[:, :]), [ps], [dstb])

            def a2_tile(t):
                smp = t >= 8
                tt_in_seq = (t - 8) if smp else (t % 2)
                for pi, b in ((0, Rb), (1, Kb), (2, Vb)):
                    S.dma(b.t, RKVG[pi][t * 128:(t + 1) * 128, :], reads=[RKVGB[pi][t]], writes=[b])
                OP("dve", lambda e: e.tensor_tensor(out=KKb.t, in0=Kb.t, in1=TAB[:, 4, :], op=ALU.mult), [Kb, tabb], [KKb])
                OP("pool", lambda e: e.tensor_tensor(out=T1b.t, in0=KKb.t, in1=KKb.t, op=ALU.mult), [KKb], [T1b])
                nrm = tmpB.next()
                OP("dve", lambda e, nrm=nrm: e.tensor_reduce(out=nrm[:, 0:16], in_=h16(T1b), axis=AX.X, op=ALU.add), [T1b], [nrm])
                OP("dve", lambda e, nrm=nrm: e.tensor_scalar(out=nrm[:, 0:16], in0=nrm[:, 0:16], scalar1=1e-12, scalar2=None, op0=ALU.max), [nrm], [nrm])
                OP("act", lambda e, nrm=nrm: e.activation(out=nrm[:, 16:32], in_=nrm[:, 0:16], func=AF.Sqrt), [nrm], [nrm])
                OP("dve", lambda e, nrm=nrm: e.reciprocal(out=nrm[:, 32:48], in_=nrm[:, 16:32]), [nrm], [nrm])
                OP("dve", lambda e, nrm=nrm: e.tensor_tensor(out=h16(KKb), in0=h16(KKb), in1=nrm[:, 32:48].unsqueeze(2).to_broadcast([128, 16, 64]),
                                                             op=ALU.mult), [KKb, nrm], [KKb])
                for d in range(2):
                    a2_dir(t, d, smp, tt_in_seq)

            def a2_dir(t, d, smp, tt_in_seq):
                if True:
                    for (LT, Bw, tabi, dstb, post) in ((LWT, wBb, d, LWb, "w"), (LAT, aBb, 2 + d, ABb, "a")):
                        for hf in range(2):
                            ps = pP_rot.next()
                            OP("pe", lambda e, hf=hf, d=d, LT=LT, Bw=Bw, ps=ps: e.matmul(
                                out=ps[:, :], lhsT=LT[0:64, d, t * 128:(t + 1) * 128], rhs=Bw[0:64, d, hf * 512:(hf + 1) * 512],
                                start=True, stop=True), [lorab, smallb], [ps])
                            OP("dve", lambda e, hf=hf, tabi=tabi, dstb=dstb, ps=ps: e.tensor_tensor(
                                out=dstb.t[:, hf * 512:(hf + 1) * 512], in0=ps[:, :], in1=TAB[:, tabi, hf * 512:(hf + 1) * 512], op=ALU.add),
                               [ps, tabb], [dstb])
                        OP("act", lambda e, dstb=dstb: e.activation(out=dstb.t, in_=dstb.t, func=AF.Sigmoid), [dstb], [dstb])
                        if post == "w":
                            OP("pool", lambda e, dstb=dstb: e.tensor_scalar(out=dstb.t, in0=dstb.t, scalar1=-math.exp(-0.5), scalar2=None,
                                                                            op0=ALU.mult), [dstb], [dstb])
                    OP("dve", lambda e: e.scalar_tensor_tensor(out=T1b.t, in0=ABb.t, scalar=-1.0, in1=TAB[:, 5, :], op0=ALU.add, op1=ALU.mult),
                       [ABb, tabb], [T1b])
                    OP("dve", lambda e: e.scalar_tensor_tensor(out=KDb.t, in0=T1b.t, scalar=1.0, in1=Kb.t, op0=ALU.add, op1=ALU.mult),
                       [T1b, Kb], [KDb])
                    OP("pool", lambda e: e.tensor_tensor(out=ABb.t, in0=KKb.t, in1=ABb.t, op=ALU.mult), [KKb, ABb], [ABb])
                    OP("pool", lambda e: e.tensor_tensor(out=T1b.t, in0=Rb.t, in1=KDb.t, op=ALU.mult), [Rb, KDb], [T1b])
                    OP("pool", lambda e: e.tensor_tensor(out=T1b.t, in0=T1b.t, in1=TAB[:, 6, :], op=ALU.mult), [T1b, tabb], [T1b])
                    nb = tmpB.next()
                    OP("dve", lambda e, nb=nb: e.tensor_reduce(out=nb[:, 0:16], in_=h16(T1b), axis=AX.X, op=ALU.add), [T1b], [nb])
                    OP("dve", lambda e, nb=nb: e.tensor_tensor(out=bsum[:, t, :], in0=bsum[:, t, :], in1=nb[:, 0:16], op=ALU.add), [bsum, nb], [bsum])
                    if smp:
                        L, TMD, FMD, TMB_, FMB_ = 1024, TMS, FMS, TMSB, FMSB
                        ch0 = d * 16
                    else:
                        L, TMD, FMD, TMB_, FMB_ = 256, TMP, FMP, TMPB, FMPB
                        ch0 = (d * 4 + t // 2) * 16
                    tk0 = tt_in_seq * 128
                    s0 = tk0 if d == 0 else L - 128 - tk0

                    def chain_order(srcb):
                        if d == 0:
                            return srcb
                        f = Frot.next()
                        flip(srcb, f)
                        return f
                    if d == 0:
                        lwc = LWb
                    else:
                        flip(LWb, FLWb)
                        lwc = FLWb
                    cps = [PB[0], PB[1]]
                    for hf in range(2):
                        OP("pe", lambda e, hf=hf: e.matmul(out=cps[hf][:, :], lhsT=tri[:], rhs=lwc.t[:, hf * 512:(hf + 1) * 512], start=True, stop=True),
                           [tri, lwc], [cps[hf]])

                    def store_tm(hb_, vi):
                        S.dma(TMD[ch0:ch0 + 16, s0:s0 + 128, vi, :].rearrange("h t j -> t h j"), h16(hb_), reads=[hb_], writes=[TMB_], owner=hb_,
                              queue="act")

                    def store_fm(hb_, vi):
                        fm = FMrot.next()
                        fmv = fm.t.rearrange("p (a b) -> p a b", a=8)
                        transpose_blocks(lambda i: hb_.t[:, i * 128:(i + 1) * 128], hb_, 8, lambda i0, n: (fmv[:, i0:i0 + n, :], fm), evac="act")
                        for e2 in range(2):
                            S.dma(FMD[ch0 + e2:ch0 + 16:2, vi, :, s0:s0 + 128].rearrange("c k t -> k c t"), fmv[e2 * 64:(e2 + 1) * 64, :, :],
                                  reads=[fm], writes=[FMB_], owner=fm, queue="act")

                    def hat(srcb, kind):
                        hb_ = Hrot.next()
                        for hf in range(2):
                            sl = slice(hf * 512, (hf + 1) * 512)
                            if kind == "prev":
                                OP("dve", lambda e, hf=hf, sl=sl: e.tensor_tensor(out=T1b.t[:, sl], in0=cps[hf][:, :], in1=lwc.t[:, sl], op=ALU.subtract),
                                   [cps[hf], lwc], [T1b])
                                OP("act", lambda e, sl=sl: e.activation(out=T1b.t[:, sl], in_=T1b.t[:, sl], func=AF.Exp), [T1b], [T1b])
                            elif kind == "cur":
                                OP("act", lambda e, hf=hf, sl=sl: e.activation(out=T1b.t[:, sl], in_=cps[hf][:, :], func=AF.Exp), [cps[hf]], [T1b])
                            elif kind == "inv":
                                OP("act", lambda e, hf=hf, sl=sl: e.activation(out=T1b.t[:, sl], in_=cps[hf][:, :], func=AF.Exp, scale=-1.0), [cps[hf]], [T1b])
                        if srcb is None:
                            OP("pool", lambda e: e.tensor_copy(out=hb_.t, in_=T1b.t), [T1b], [hb_])
                        else:
                            OP("pool", lambda e: e.tensor_tensor(out=hb_.t, in0=srcb.t, in1=T1b.t, op=ALU.mult), [srcb, T1b], [hb_])
                        return hb_

                    hb_ = hat(chain_order(KKb), "prev")
                    store_fm(hb_, 0)
                    hb_ = hat(chain_order(Rb), "cur")
                    store_fm(hb_, 1)
                    hb_ = hat(None, "cur")
                    store_fm(hb_, 4)
                    hb_ = hat(chain_order(ABb), "inv")
                    store_fm(hb_, 2)
                    store_tm(hb_, 0)
                    hb_ = hat(chain_order(KDb), "inv")
                    store_fm(hb_, 3)
                    store_tm(hb_, 1)
                    store_tm(chain_order(Vb), 2)

            for t in range(16):
                a2_tile(t)
            S.barrier()
            CK("rwkv_a")

            NU = 20
            o = 0
            Tst = []
            for u in range(NU):
                Tst.append(bsub(o, 512, f"T{u}"))
                o += 512

            def rotb(n, size, name):
                nonlocal o
                bs = []
                for i2 in range(n):
                    bs.append(bsub(o, size, f"{name}{i2}"))
                    o += size
                return Rot(bs)
            TMr = rotb(2, 8 * 192, "tm")
            FMr = rotb(2, 8 * 320, "fm")
            QAr = rotb(2, 8 * 128, "qa")
            KAr = rotb(2, 8 * 128, "ka")
            Qr = rotb(2, 512, "q")
            QTr = rotb(3, 512, "qt")
            Zr = rotb(3, 512, "z")
            Yr = rotb(2, 512, "y")
            Ur = Rot([Buf(gnw.t[:, 0:512], "u0"), Buf(gnw.t[:, 512:1024], "u1")])
            Pr = Rot([Buf(gnw.t[:, 1024:1536], "p0"), Buf(gnw.t[:, 1536:2048], "p1")])
            v3 = lambda b, a: b.t[0:64, :].rearrange("p (c x) -> p c x", c=8)
            ev_rot = Rot(["dve", "act", "pool"])

            def evac_copy(dst, dstb, ps, scale=None):
                eng = ev_rot.next()
                src = ps[0:64, :].rearrange("p (c x) -> p c x", c=8)
                if eng == "act":
                    if scale is None:
                        OP("act", lambda e: e.copy(out=dst, in_=src), [ps], [dstb])
                    else:
                        OP("act", lambda e: e.activation(out=dst, in_=src, func=AF.Copy, scale=scale), [ps], [dstb])
                else:
                    eng = "dve"
                    if scale is None:
                        OP(eng, lambda e: e.tensor_copy(out=dst, in_=src), [ps], [dstb])
                    else:
                        OP(eng, lambda e: e.tensor_scalar(out=dst, in0=src, scalar1=scale, scalar2=None, op0=ALU.mult), [ps], [dstb])

            units = []
            for d in range(2):
                for hh in range(2):
                    units.append(dict(smp=True, ch0=d * 16 + hh * 8, d=d, h0=hh * 8, nch=16))
            for d in range(2):
                for sq in range(4):
                    for hh in range(2):
                        units.append(dict(smp=False, ch0=(d * 4 + sq) * 16 + hh * 8, d=d, sq=sq, h0=hh * 8, nch=4))
            for ui, u in enumerate(units):
                Tb = Tst[ui]
                if not u["smp"]:
                    OP("pool", lambda e, Tb=Tb: e.memset(Tb.t[0:64, :], 0.0), [], [Tb])
                else:
                    st_ = TMr.next()
                    sv = st_.t[0:64, 0:512].rearrange("p (c x) -> p c x", c=8)
                    S.dma(sv, I["state_rwkv"][u["d"], u["h0"]:u["h0"] + 8].rearrange("h v k -> v h k"), writes=[st_])
                    ps = pP_rot.next()
                    for c in range(8):
                        OP("pe", lambda e, c=c, ps=ps, sv=sv: e.transpose(out=ps[0:64, c * 64:(c + 1) * 64], in_=sv[:, c, :], identity=ident[0:64, 0:64]),
                           [st_, ident], [ps])
                    OP("act", lambda e, Tb=Tb, ps=ps: e.copy(out=Tb.t[0:64, :], in_=ps[0:64, :]), [ps], [Tb])

            def unit_chunk(ui, u, n):
                smp, ch0 = u["smp"], u["ch0"]
                TMD, FMD, TMB_, FMB_, YD, YDB_ = (TMS, FMS, TMSB, FMSB, YSD, YSDB) if smp else (TMP, FMP, TMPB, FMPB, YPD, YPDB)
                tm = TMr.next()
                fm = FMr.next()
                tmv = tm.t[0:64, :].rearrange("p (c v j) -> p c v j", c=8, v=3)
                fmv = fm.t[0:64, :].rearrange("p (c v s) -> p c v s", c=8, v=5)
                S.dma(tm.t[0:64, :].rearrange("p (c x) -> p c x", c=8), TMD[ch0:ch0 + 8, n * 64:(n + 1) * 64, :, :].rearrange("c s v j -> s c (v j)"),
                      reads=[TMB_], writes=[tm])
                S.dma(fm.t[0:64, :].rearrange("p (cv s) -> p cv s", s=64), FMD[ch0:ch0 + 8, :, :, n * 64:(n + 1) * 64].rearrange("c v k s -> k (c v) s"),
                      reads=[FMB_], writes=[fm])
                Tb = Tst[ui]
                Tv = Tb.t[0:64, :].rearrange("p (c x) -> p c x", c=8)
                qa = QAr.next()
                ka = KAr.next()
                qav = qa.t[0:64, :].rearrange("p (c x) -> p c x", c=8)
                kav = ka.t[0:64, :].rearrange("p (c x) -> p c x", c=8)
                for (vi, dstb, dstv, mi) in ((2, qa, qav, 0), (3, ka, kav, 1)):
                    for half in range(2):
                        ps = pP_rot.next()
                        for c4 in range(4):
                            c = half * 4 + c4
                            OP("pe", lambda e, c=c, c4=c4, vi=vi, ps=ps: e.matmul(out=ps[0:64, c4 * 128:(c4 + 1) * 128], lhsT=fmv[:, c, vi, :],
                                                                                 rhs=fmv[:, c, 0:2, :], start=True, stop=True), [fm], [ps])
                        OP("dve", lambda e, half=half, dstv=dstv, mi=mi, ps=ps: e.tensor_tensor(
                            out=dstv[:, half * 4:(half + 1) * 4, :], in0=ps[0:64, :].rearrange("p (c x) -> p c x", c=4),
                            in1=cmask[0:64, mi, :].unsqueeze(1).to_broadcast([64, 4, 128]), op=ALU.mult), [ps, cmask], [dstb])
                ps = pP_rot.next()
                for c in range(8):
                    OP("pe", lambda e, c=c, ps=ps: e.matmul(out=ps[0:64, c * 64:(c + 1) * 64], lhsT=fmv[:, c, 0, :], rhs=fmv[:, c, 2, :],
                                                           start=True, stop=True), [fm], [ps])
                qt = QTr.next()
                OP("dve", lambda e, qt=qt, ps=ps: e.tensor_tensor(out=v3(qt, 0), in0=ps[0:64, :].rearrange("p (c x) -> p c x", c=8),
                                                                  in1=cmask[0:64, 2, 0:64].unsqueeze(1).to_broadcast([64, 8, 64]), op=ALU.mult),
                   [ps, cmask], [qt])
                z = Zr.next()
                OP("pool", lambda e, z=z: e.tensor_tensor(out=v3(z, 0), in0=qav[:, :, 0:64], in1=ident[0:64, 0:64].unsqueeze(1).to_broadcast([64, 8, 64]),
                                                          op=ALU.add), [qa, ident], [z])
                qprev_of = lambda c: qav[:, c, 0:64]
                qprev_b = qa
                for lvl in range(1, 6):
                    ps = pP_rot.next()
                    for c in range(8):
                        OP("pe", lambda e, c=c, ps=ps, qprev_of=qprev_of, qt=qt: e.matmul(out=ps[0:64, c * 64:(c + 1) * 64], lhsT=qprev_of(c),
                                                                                      rhs=v3(qt, 0)[:, c, :], start=True, stop=True),
                           [qprev_b, qt], [ps])
                    qt_new = QTr.next()
                    evac_copy(v3(qt_new, 0), qt_new, ps)
                    if lvl < 5:
                        ps2 = pP_rot.next()
                        for c in range(8):
                            OP("pe", lambda e, c=c, ps2=ps2, qprev_of=qprev_of, qt=qt: e.matmul(out=ps2[0:64, c * 64:(c + 1) * 64], lhsT=v3(qt, 0)[:, c, :],
                                                                                            rhs=qprev_of(c), start=True, stop=True),
                               [qprev_b, qt], [ps2])
                        q_new = Qr.next()
                        evac_copy(v3(q_new, 0), q_new, ps2)
                    ps3 = pP_rot.next()
                    for c in range(8):
                        OP("pe", lambda e, c=c, ps3=ps3, qt_new=qt_new, z=z: e.matmul(out=ps3[0:64, c * 64:(c + 1) * 64], lhsT=v3(qt_new, 0)[:, c, :],
                                                                                    rhs=v3(z, 0)[:, c, :], start=True, stop=True), [qt_new, z], [ps3])
                    z_new = Zr.next()
                    OP("dve", lambda e, z_new=z_new, z=z, ps3=ps3: e.tensor_tensor(out=v3(z_new, 0), in0=ps3[0:64, :].rearrange("p (c x) -> p c x", c=8),
                                                                                 in1=v3(z, 0), op=ALU.add), [ps3, z], [z_new])
                    z = z_new
                    qt = qt_new
                    if lvl < 5:
                        qprev_of = (lambda c, q_new=q_new: v3(q_new, 0)[:, c, :])
                        qprev_b = q_new
                ps = pP_rot.next()
                for c in range(8):
                    OP("pe", lambda e, c=c, ps=ps: e.matmul(out=ps[0:64, c * 64:(c + 1) * 64], lhsT=fmv[:, c, 0, :], rhs=Tv[:, c, :], start=True, stop=False),
                       [fm, Tb], [ps])
                    OP("pe", lambda e, c=c, ps=ps: e.matmul(out=ps[0:64, c * 64:(c + 1) * 64], lhsT=kav[:, c, 0:64], rhs=tmv[:, c, 2, :], start=False, stop=True),
                       [ka, tm], [ps])
                ub = Ur.next()
                evac_copy(v3(ub, 0), ub, ps)
                ps = pP_rot.next()
                for c in range(8):
                    OP("pe", lambda e, c=c, ps=ps, z=z, ub=ub: e.matmul(out=ps[0:64, c * 64:(c + 1) * 64], lhsT=v3(z, 0)[:, c, :], rhs=v3(ub, 0)[:, c, :],
                                                                      start=True, stop=True), [z, ub], [ps])
                pb_ = Pr.next()
                evac_copy(v3(pb_, 0), pb_, ps, scale=-1.0)
                ps = pP_rot.next()
                for c in range(8):
                    OP("pe", lambda e, c=c, ps=ps: e.matmul(out=ps[0:64, c * 64:(c + 1) * 64], lhsT=fmv[:, c, 1, :], rhs=Tv[:, c, :], start=True, stop=False),
                       [fm, Tb], [ps])
                    OP("pe", lambda e, c=c, ps=ps, pb_=pb_: e.matmul(out=ps[0:64, c * 64:(c + 1) * 64], lhsT=qav[:, c, 64:128], rhs=v3(pb_, 0)[:, c, :],
                                                                   start=False, stop=False), [qa, pb_], [ps])
                    OP("pe", lambda e, c=c, ps=ps: e.matmul(out=ps[0:64, c * 64:(c + 1) * 64], lhsT=kav[:, c, 64:128], rhs=tmv[:, c, 2, :], start=False, stop=True),
                       [ka, tm], [ps])
                yb = Yr.next()
                evac_copy(v3(yb, 0), yb, ps)
                S.dma(YD[ch0:ch0 + 8, n * 64:(n + 1) * 64, :].rearrange("c s x -> s c x"), v3(yb, 0), reads=[yb], writes=[YDB_], owner=yb, queue="act")
                ps = pP_rot.next()
                for c in range(8):
                    OP("pe", lambda e, c=c, ps=ps, pb_=pb_: e.matmul(out=ps[0:64, c * 64:(c + 1) * 64], lhsT=tmv[:, c, 0, :], rhs=v3(pb_, 0)[:, c, :],
                                                                   start=True, stop=False), [tm, pb_], [ps])
                    OP("pe", lambda e, c=c, ps=ps: e.matmul(out=ps[0:64, c * 64:(c + 1) * 64], lhsT=tmv[:, c, 1, :], rhs=tmv[:, c, 2, :], start=False, stop=True),
                       [tm], [ps])
                OP("dve", lambda e, ps=ps: e.tensor_tensor(out=Tv, in0=ps[0:64, :].rearrange("p (c x) -> p c x", c=8), in1=Tv, op=ALU.add), [ps, Tb], [Tb])
                OP("dve", lambda e: e.tensor_tensor(out=Tv, in0=Tv, in1=fmv[:, :, 4, 63:64].to_broadcast([64, 8, 64]), op=ALU.mult), [Tb, fm], [Tb])

            for n in range(16):
                for ui, u in enumerate(units):
                    if n < u["nch"]:
                        unit_chunk(ui, u, n)
            for ui, u in enumerate(units):
                if u["smp"]:
                    continue
                Tb = Tst[ui]
                Tv = Tb.t[0:64, :].rearrange("p (c x) -> p c x", c=8)
                ps = pP_rot.next()
                for c in range(8):
                    OP("pe", lambda e, c=c, ps=ps, Tv=Tv: e.transpose(out=ps[0:64, c * 64:(c + 1) * 64], in_=Tv[:, c, :], identity=ident[0:64, 0:64]),
                       [Tb, ident], [ps])
                yb = Yr.next()
                evac_copy(v3(yb, 0), yb, ps)
                S.dma(O["st_rwkv"][u["sq"], u["d"], u["h0"]:u["h0"] + 8].rearrange("h v k -> v h k"), v3(yb, 0), reads=[yb], queue="act")
            S.barrier()
            CK("rwkv_b")

            gnt = bsub(0, 1024, "gnt")
            S.dma(gnt.t, I["rwkv_gn"].partition_broadcast(128), writes=[gnt])
            cs_ = [bsub(1024 + i2 * 1024, 1024, f"cs{i2}") for i2 in range(6)]
            YFb, YBb, Vc, Gc, C1, C2 = cs_
            ogb = bsub(8192, 4096, "ogT")

            def c_group(g):
                tiles, c, smp = g["tiles"], g["c"], g["sample"]
                nt = len(tiles)
                T = nt * 128
                ogT = bview(ogb, 0, [8, T])
                for ti, t in enumerate(tiles):
                    if smp:
                        tk0 = (t - 8) * 128
                        L = 1024
                        S.dma(h16(YFb), YSD[0:16, tk0:tk0 + 128, :].rearrange("h t x -> t h x"), reads=[YSDB], writes=[YFb])
                        S.dma(h16(C1), YSD[16:32, L - 128 - tk0:L - tk0, :].rearrange("h t x -> t h x"), reads=[YSDB], writes=[C1])
                    else:
                        sq = t // 2
                        tk0 = (t % 2) * 128
                        L = 256
                        S.dma(h16(YFb), YPD[sq * 16:(sq + 1) * 16, tk0:tk0 + 128, :].rearrange("h t x -> t h x"), reads=[YPDB], writes=[YFb])
                        S.dma(h16(C1), YPD[(4 + sq) * 16:(5 + sq) * 16, L - 128 - tk0:L - tk0, :].rearrange("h t x -> t h x"),
                              reads=[YPDB], writes=[C1])
                    for hf in range(2):
                        ps = pP_rot.next()
                        OP("pe", lambda e, hf=hf, ps=ps: e.matmul(out=ps[:, :], lhsT=Jm[:], rhs=C1.t[:, hf * 512:(hf + 1) * 512], start=True, stop=True),
                           [Jm, C1], [ps])
                        OP("dve", lambda e, hf=hf, ps=ps: e.tensor_tensor(out=YBb.t[:, hf * 512:(hf + 1) * 512], in0=ps[:, :],
                                                                          in1=YFb.t[:, hf * 512:(hf + 1) * 512], op=ALU.add), [ps, YFb], [YBb])
                    S.dma(Vc.t, RKVG[2][t * 128:(t + 1) * 128, :], reads=[RKVGB[2][t]], writes=[Vc])
                    S.dma(Gc.t, RKVG[3][t * 128:(t + 1) * 128, :], reads=[RKVGB[3][t]], writes=[Gc])
                    nb = tmpB.next()
                    OP("dve", lambda e, nb=nb: e.tensor_reduce(out=nb[:, 0:16], in_=h16(YBb), axis=AX.X, op=ALU.add), [YBb], [nb])
                    OP("dve", lambda e, nb=nb: e.tensor_scalar(out=nb[:, 0:16], in0=nb[:, 0:16], scalar1=-1.0 / 64, scalar2=None, op0=ALU.mult), [nb], [nb])
                    OP("dve", lambda e, nb=nb: e.tensor_tensor(out=h16(YBb), in0=h16(YBb), in1=nb[:, 0:16].unsqueeze(2).to_broadcast([128, 16, 64]),
                                                               op=ALU.add), [YBb, nb], [YBb])
                    OP("pool", lambda e: e.tensor_tensor(out=C2.t, in0=YBb.t, in1=YBb.t, op=ALU.mult), [YBb], [C2])
                    OP("dve", lambda e, nb=nb: e.tensor_reduce(out=nb[:, 16:32], in_=h16(C2), axis=AX.X, op=ALU.add), [C2], [nb])
                    OP("act", lambda e, nb=nb: e.activation(out=nb[:, 32:48], in_=nb[:, 16:32], func=AF.Sqrt, scale=1.0 / 64, bias=GN_EPS), [nb], [nb])
                    OP("dve", lambda e, nb=nb: e.reciprocal(out=nb[:, 48:64], in_=nb[:, 32:48]), [nb], [nb])
                    OP("dve", lambda e, nb=nb: e.tensor_tensor(out=h16(YBb), in0=h16(YBb), in1=nb[:, 48:64].unsqueeze(2).to_broadcast([128, 16, 64]),
                                                               op=ALU.mult), [YBb, nb], [YBb])
                    OP("dve", lambda e: e.tensor_tensor(out=YBb.t, in0=YBb.t, in1=gnt.t, op=ALU.mult), [YBb, gnt], [YBb])
                    OP("dve", lambda e, t=t: e.tensor_tensor(out=h16(C2), in0=h16(Vc), in1=bsum[:, t, :].unsqueeze(2).to_broadcast([128, 16, 64]),
                                                             op=ALU.mult), [Vc, bsum], [C2])
                    OP("dve", lambda e: e.tensor_tensor(out=YBb.t, in0=YBb.t, in1=C2.t, op=ALU.add), [YBb, C2], [YBb])
                    OP("act", lambda e: e.activation(out=Gc.t, in_=Gc.t, func=AF.Silu), [Gc], [Gc])
                    OP("dve", lambda e: e.tensor_tensor(out=YBb.t, in0=YBb.t, in1=Gc.t, op=ALU.mult), [YBb, Gc], [YBb])
                    transpose_blocks(lambda i: YBb.t[:, i * 128:(i + 1) * 128], YBb, 8,
                                     lambda i0, n, ti=ti: (ogT[:, i0:i0 + n, ti * 128:(ti + 1) * 128], ogb))
                ybufs = [bsub(12288, nt * 1024, "yacc")]
                tail(layer, tiles, c, ogT, ogb, 8, Wout, ybufs)

            for g in groups:
                c_group(g)
            S.barrier()

        def final_norm():
            S.dma(gnw[:, 0:1024], I["final_norm_w"].partition_broadcast(128), writes=[gnw])
            for t in range(16):
                xt = xt_rot.next()
                S.dma(xt[:], XB[t].t, reads=[XB[t]], writes=[xt])
                s = st1.next()
                xn = xn_rot.next()
                OP("act", lambda e, xt=xt, xn=xn, s=s: e.activation(out=xn[:], in_=xt[:], func=AF.Square,
                                                                    accum_out=s[:, 0:1]), [xt], [xn, s])
                OP("act", lambda e, s=s: e.activation(out=s[:, 1:2], in_=s[:, 0:1], func=AF.Sqrt, scale=1.0 / 1024,
                                                      bias=EPS), [s], [s])
                OP("dve", lambda e, s=s: e.reciprocal(out=s[:, 2:3], in_=s[:, 1:2]), [s], [s])
                OP("dve", lambda e, xt=xt, xn=xn, s=s: e.scalar_tensor_tensor(out=xn[:], in0=xt[:], scalar=s[:, 2:3],
                                                                              in1=gnw[:, 0:1024], op0=ALU.mult, op1=ALU.mult),
                   [xt, s, gnw], [xn])
                dst = O["yp"] if t < 8 else O["ys"]
                S.dma(dst[(t % 8) * 128:(t % 8 + 1) * 128, :], xn[:], reads=[xn], queue="act")

        LAYERS = {0: layer_ret, 1: layer_rwkv, 2: layer_diff, 3: layer_na}
        try:
            CK("setup")
            for layer in layers:
                mod(layer)
                CK("mod")
                LAYERS[layer](layer)
            if final:
                final_norm()
        except _Stop:
            pass
        S.emit()
        print(f"[build] ops={S.n_ops} waits={S.n_waits} dma_sems={S.ndsem}")
    return nc


def _prep_inputs(inp):
    cst = _consts()
    f = lambda a: np.ascontiguousarray(np.asarray(a, dtype=np.float32))
    shared = {
        "norm_w": f(inp["norm_w"]), "w_mod": f(inp["w_mod"]), "b_mod": f(inp["b_mod"]),
        "final_norm_w": f(inp["final_norm_w"]),
        "ret_w_in": f(inp["ret_w_in"][0]), "ret_decay": f(inp["ret_decay"][0]).reshape(8),
        "ret_gn": f(inp["ret_gn"][0]), "ret_w_out": f(inp["ret_w_out"][0]),
        "rwkv_mu": f(inp["rwkv_mu"][0]), "rwkv_w_in": f(inp["rwkv_w_in"][0]), "rwkv_w0": f(inp["rwkv_w0"][0]),
        "rwkv_wA": f(inp["rwkv_wA"][0]), "rwkv_wB": f(inp["rwkv_wB"][0]), "rwkv_a0": f(inp["rwkv_a0"][0]),
        "rwkv_aA": f(inp["rwkv_aA"][0]), "rwkv_aB": f(inp["rwkv_aB"][0]), "rwkv_kk": f(inp["rwkv_kk"][0]),
        "rwkv_ka": f(inp["rwkv_ka"][0]), "rwkv_rk": f(inp["rwkv_rk"][0]).reshape(1024),
        "rwkv_gn": f(inp["rwkv_gn"][0]), "rwkv_w_out": f(inp["rwkv_w_out"][0]),
        "diff_w_in": f(inp["diff_w_in"][0]), "diff_lambda": f(inp["diff_lambda"][0]).reshape(256),
        "diff_gn": f(inp["diff_gn"][0]), "diff_w_out": f(inp["diff_w_out"][0]),
        "na_w_in": f(inp["na_w_in"][0]), "na_bias_x": _na_bias_expand(f(inp["na_bias"][0])),
        "na_w_out": f(inp["na_w_out"][0]),
    }
    shared.update(cst)
    maps = []
    for c in range(8):
        b = c // 4
        m = dict(shared)
        m["xp"] = f(inp["x_prompt"][4 * c:4 * c + 4]).reshape(1024, 1024)
        m["xs"] = f(inp["x_sample"][b])
        m["cond"] = np.ascontiguousarray(np.stack([f(inp["c_ctx"]), f(inp["c"][b])], 0))
        m["state_ret"] = f(inp["state_ret"][b, 0])
        m["state_rwkv"] = f(inp["state_rwkv"][b, 0])
        m["cache_diff_k"] = f(inp["cache_diff_k"][b, 0])
        m["cache_diff_v"] = f(inp["cache_diff_v"][b, 0])
        m["cache_na_k"] = f(inp["cache_na_k"][b, 0])
        m["cache_na_v"] = f(inp["cache_na_v"][b, 0])
        maps.append(m)
    return maps


_NC_CACHE = {}


def kernel(**inputs):
    maps = _prep_inputs(inputs)
    if "nc" not in _NC_CACHE:
        _NC_CACHE["nc"] = build()
    res = run_bass_kernel_spmd(_NC_CACHE["nc"], maps, core_ids=list(range(8))).results
    y_prompt = np.concatenate([r["yp"].reshape(4, 256, 1024) for r in res], 0)
    y_sample = np.stack([res[0]["ys"], res[4]["ys"]], 0)
    st_ret = np.concatenate([r["st_ret"] for r in res], 0)[:, None]
    st_rwkv = np.concatenate([r["st_rwkv"] for r in res], 0)[:, None]
    dk = np.concatenate([r["dk"] for r in res], 0)[:, None]
    dv = np.concatenate([r["dv"] for r in res], 0)[:, None]
    nk = np.concatenate([r["nk"] for r in res], 0)[:, None]
    nv = np.concatenate([r["nv"] for r in res], 0)[:, None]
    return (y_prompt, y_sample, st_ret, st_rwkv, dk, dv, nk, nv)
```

```python
import math
from contextlib import ExitStack

import numpy as np
import concourse.bass as bass
import concourse.mybir as mybir
from concourse.bass_utils import run_bass_kernel_spmd

F32 = mybir.dt.float32
BF16 = mybir.dt.bfloat16
AF = mybir.ActivationFunctionType
ALU = mybir.AluOpType
AX = mybir.AxisListType

EPS = 1e-6
GN_EPS = 1e-5
NEG = -30000.0


class Buf:
    __slots__ = ("t", "name", "lw", "rd", "dsem", "dcnt", "excl")

    def __init__(self, t, name, excl=False):
        self.excl = excl
        self.t = t
        self.name = name
        self.lw = None
        self.rd = {}
        self.dsem = None
        self.dcnt = 0

    def __getitem__(self, k):
        return self.t[k]


class Sched:
    CE = ("pe", "act", "dve", "pool")
    ALLQ = ("pe", "act", "dve", "pool", "sp")

    def __init__(self, nc, stack):
        self.nc = nc
        self.stack = stack
        self.sems = {}
        self.ecnt = {}
        for e in self.CE:
            self.sems[e] = stack.enter_context(nc.semaphore("es_" + e))
            self.ecnt[e] = 0
        self.q = {e: [] for e in self.ALLQ}
        self.seen = {e: {} for e in self.ALLQ}
        self.nbuf = 0
        self.ndsem = 0
        self.n_ops = 0
        self.n_waits = 0
        self.dma_bufs = {}
        self.CONST = Buf(None, "const")
        self.const_bufs = []

    def sb(self, shape, dtype=F32, name="b"):
        self.nbuf += 1
        t = self.stack.enter_context(self.nc.sbuf_tensor(f"{name}_{self.nbuf}", list(shape), dtype))
        return Buf(t, name)

    def _waits(self, eng, reads, writes):
        ev = {}
        for b in reads:
            if b.lw is not None:
                k, v = b.lw
                if ev.get(k, 0) < v:
                    ev[k] = v
            if b.excl:
                for k, v in b.rd.items():
                    if k != eng and ev.get(k, 0) < v:
                        ev[k] = v
        for b in writes:
            if b.lw is not None:
                k, v = b.lw
                if ev.get(k, 0) < v:
                    ev[k] = v
            for k, v in b.rd.items():
                if ev.get(k, 0) < v:
                    ev[k] = v
        waits = []
        seen = self.seen[eng]
        for k, v in ev.items():
            if eng == "pe" and k == "pe":
                continue
            if seen.get(k, 0) >= v:
                continue
            seen[k] = v
            waits.append((k, v))
        self.n_waits += len(waits)
        return waits

    def _mark(self, me, reads, writes):
        k, v = me
        for b in reads:
            if b.rd.get(k, 0) < v:
                b.rd[k] = v
        for b in writes:
            b.lw = me
            b.rd = {}

    def op(self, eng, fn, reads=(), writes=()):
        waits = self._waits(eng, reads, writes)
        self.ecnt[eng] += 1
        me = (eng, self.ecnt[eng])
        self.q[eng].append((waits, fn, eng, 1))
        self._mark(me, reads, writes)
        self.n_ops += 1

    def dma(self, out_ap, in_ap, reads=(), writes=(), owner=None, queue="sp", **kw):
        if owner is self.CONST:
            waits = []
        else:
            waits = self._waits(queue, reads, writes)
        if owner is None:
            owner = writes[0] if writes else reads[0]
        if owner.dsem is None:
            self.ndsem += 1
            key = f"d{self.ndsem}"
            self.sems[key] = self.stack.enter_context(self.nc.semaphore("ds_" + key))
            owner.dsem = key
            self.dma_bufs[key] = owner
        owner.dcnt += 16
        me = (owner.dsem, owner.dcnt)
        self.q[queue].append((waits, (lambda e: e.dma_start(out=out_ap, in_=in_ap, **kw)), owner.dsem, 16))
        if owner is self.CONST:
            self.const_bufs.extend(writes)
        else:
            self._mark(me, reads, writes)
        self.n_ops += 1

    def consts_done(self):
        for b in self.const_bufs:
            b.lw = (self.CONST.dsem, self.CONST.dcnt)
        self.const_bufs = []

    def barrier(self):
        tot = [(e, self.ecnt[e]) for e in self.CE] + [(k, b.dcnt) for k, b in self.dma_bufs.items()]
        for eng in self.ALLQ:
            waits = []
            for k, v in tot:
                if k == eng or v == 0:
                    continue
                if self.seen[eng].get(k, 0) >= v:
                    continue
                self.seen[eng][k] = v
                waits.append((k, v))
            if waits:
                self.q[eng].append((waits, None, None, 0))

    def emit(self):
        nc = self.nc
        self.barrier()
        with nc.Block() as block:
            def run(engobj, name):
                for waits, fn, semkey, inc in self.q[name]:
                    for k, v in waits:
                        engobj.wait_ge(self.sems[k], v)
                    if fn is not None:
                        fn(engobj).then_inc(self.sems[semkey], inc)

            @block.tensor
            def _(e):
                run(e, "pe")

            @block.scalar
            def _(e):
                run(e, "act")

            @block.vector
            def _(e):
                run(e, "dve")

            @block.gpsimd
            def _(e):
                run(e, "pool")

            @block.sync
            def _(e):
                run(e, "sp")


class Rot:
    def __init__(self, bufs):
        self.bufs = bufs
        self.i = 0

    def next(self):
        b = self.bufs[self.i % len(self.bufs)]
        self.i += 1
        return b


def _rope_tables(d):
    t = np.arange(1024)
    row = (t // 64).astype(np.float32)
    col = (t % 64).astype(np.float32)
    inv = (np.float32(10000.0) ** (-np.arange(0, d, 2, dtype=np.float32) / np.float32(d))).astype(np.float32)
    ang = np.stack([row[:, None] * inv[None, :], col[:, None] * inv[None, :]], axis=1).astype(np.float32)
    return np.cos(ang).astype(np.float32), np.sin(ang).astype(np.float32)


def _consts():
    c = {}
    c0, s0 = _rope_tables(128)
    c2, s2 = _rope_tables(32)
    c["rope0"] = np.ascontiguousarray(np.stack([c0, s0], 0))
    c["rope2"] = np.ascontiguousarray(np.stack([c2, s2], 0))
    kk = np.arange(128)[:, None]
    cols = np.arange(15 * 128)[None, :]
    m = cols // 128 - 7
    qq = cols % 128
    gap = (128 * m + qq - kk).astype(np.float32)
    c["gpn"] = np.ascontiguousarray(np.stack([np.maximum(gap, 0), np.maximum(-gap, 0)], 0))
    p = np.arange(128, dtype=np.float32)
    c["stexp"] = np.ascontiguousarray(np.stack([255.0 - p, 127.0 - p, p, 128.0 + p], 1))
    a_ = np.arange(128)
    c["tri"] = np.ascontiguousarray(((a_[:, None] <= a_[None, :]) & (a_[:, None] // 64 == a_[None, :] // 64)).astype(np.float32))
    j_ = np.arange(64)[:, None]
    t_ = np.arange(64)[None, :]
    su = (j_ < t_).astype(np.float32)
    iu = (j_ <= t_).astype(np.float32)
    sl = (t_ < j_).astype(np.float32)
    cm = np.zeros((64, 3, 128), np.float32)
    cm[:, 0, 0:64] = -su
    cm[:, 0, 64:128] = iu
    cm[:, 1, 0:64] = su
    cm[:, 1, 64:128] = iu
    cm[:, 2, 0:64] = -sl
    c["cmask"] = cm
    c["iota1k"] = np.ascontiguousarray(np.broadcast_to(np.arange(1024, dtype=np.float32)[None, :], (128, 1024)))
    return c


def _na_bias_expand(na_bias):
    out = np.full((16, 5, 128, 576), NEG, np.float32)
    jt = [0, 1, 2, 6, 7]
    qc = np.arange(64)[:, None]
    kc = np.arange(64)[None, :]
    cs = np.clip(qc - 8, 0, 48)
    col_ok = (kc >= cs) & (kc < cs + 16)
    cidx = np.clip(kc - qc, -15, 15) + 15
    for ti, j in enumerate(jt):
        r0 = min(max(2 * j - 4, 0), 8)
        nr = min(9, 16 - r0)
        for a in range(2):
            qr = 2 * j + a
            st = min(max(qr - 4, 0), 8)
            for i in range(nr):
                kr = r0 + i
                if st <= kr < st + 8:
                    dr = kr - qr + 7
                    blk = na_bias[:, dr][:, cidx]
                    blk = np.where(col_ok[None], blk, np.float32(NEG))
                    out[:, ti, a * 64:(a + 1) * 64, i * 64:(i + 1) * 64] = blk
    return out


NA_JT = {0: 0, 1: 1, 2: 2, 3: 2, 4: 2, 5: 2, 6: 3, 7: 4}


class _Stop(Exception):
    pass


def build(layers=(0, 1, 2, 3), final=True, stop=None):
    nc = bass.Bass("TRN2", target_bir_lowering=False)

    def CK(name):
        if stop == name:
            raise _Stop()

    def din(name, shape):
        return nc.dram_tensor(name, list(shape), F32, kind="ExternalInput").ap()

    def dout(name, shape):
        return nc.dram_tensor(name, list(shape), F32, kind="ExternalOutput").ap()

    def dscr(name, shape):
        return nc.dram_tensor(name, list(shape), F32).ap()

    I = {}
    for name, shape in [
        ("xp", (1024, 1024)), ("xs", (1024, 1024)), ("cond", (2, 1024)),
        ("state_ret", (2, 4, 256, 512)), ("state_rwkv", (2, 16, 64, 64)),
        ("cache_diff_k", (8, 256, 128)), ("cache_diff_v", (8, 256, 128)),
        ("cache_na_k", (16, 256, 64)), ("cache_na_v", (16, 256, 64)),
        ("norm_w", (4, 1024)), ("w_mod", (4, 1024, 3072)), ("b_mod", (4, 3072)), ("final_norm_w", (1024,)),
        ("ret_w_in", (1024, 6144)), ("ret_decay", (8,)), ("ret_gn", (2048,)), ("ret_w_out", (2048, 1024)),
        ("rwkv_mu", (6, 1024)), ("rwkv_w_in", (1024, 4096)), ("rwkv_w0", (2, 1024)), ("rwkv_wA", (2, 1024, 64)),
        ("rwkv_wB", (2, 64, 1024)), ("rwkv_a0", (2, 1024)), ("rwkv_aA", (2, 1024, 64)), ("rwkv_aB", (2, 64, 1024)),
        ("rwkv_kk", (1024,)), ("rwkv_ka", (1024,)), ("rwkv_rk", (1024,)), ("rwkv_gn", (1024,)),
        ("rwkv_w_out", (1024, 1024)),
        ("diff_w_in", (1024, 4096)), ("diff_lambda", (256,)), ("diff_gn", (1024,)), ("diff_w_out", (1024, 1024)),
        ("na_w_in", (1024, 4096)), ("na_bias_x", (16, 5, 128, 576)), ("na_w_out", (1024, 1024)),
        ("rope0", (2, 1024, 2, 64)), ("rope2", (2, 1024, 2, 16)), ("gpn", (2, 128, 1920)), ("stexp", (128, 4)),
        ("iota1k", (128, 1024)), ("tri", (128, 128)), ("cmask", (64, 3, 128)),
    ]:
        I[name] = din(name, shape)
    O = {}
    for name, shape in [
        ("yp", (1024, 1024)), ("ys", (1024, 1024)), ("st_ret", (4, 2, 4, 256, 512)),
        ("st_rwkv", (4, 2, 16, 64, 64)), ("dk", (4, 8, 256, 128)), ("dv", (4, 8, 256, 128)),
        ("nk", (4, 16, 256, 64)), ("nv", (4, 16, 256, 64)), ("xd", (2048, 1024)),
    ]:
        O[name] = dout(name, shape)
    XD = O["xd"]

    with ExitStack() as st:
        S = Sched(nc, st)
        OP = S.op

        ident = S.sb([128, 128], F32, "ident")
        PSt = st.enter_context(nc.psum_tensor("ps", [128, 8, 512], F32))
        PB = [Buf(PSt[:, i, :], f"ps{i}", excl=True) for i in range(8)]
        pS_rot = Rot(PB[0:4])
        pP_rot = Rot(PB[4:8])

        wf = Rot([S.sb([128, 8, 256], F32, "wf") for _ in range(2)])
        wb = Rot([S.sb([128, 8, 256], BF16, "wb") for _ in range(2)])
        xt_rot = Rot([S.sb([128, 1024], F32, "xt") for _ in range(2)])
        xn_rot = Rot([S.sb([128, 1024], F32, "xn") for _ in range(1)])
        st1 = Rot([S.sb([128, 8], F32, "st") for _ in range(6)])
        BIGN = 27648
        BIG = st.enter_context(nc.sbuf_tensor("big", [128, BIGN], F32))
        R_H = Buf(BIG[:, 0:4096], "RH")
        R_G = Buf(BIG[:, 4096:12288], "RG")
        ATTN = 15360
        ATT = BIG[:, 12288:27648]
        Jm = S.sb([128, 128], F32, "J")
        identst = S.sb([128, 64], F32, "identst")
        bsum = S.sb([128, 16, 16], F32, "bsum")
        muF = S.sb([128, 6, 8], F32, "muF")

        def bsub(off, n, name):
            assert off + n <= BIGN, (off, n)
            return Buf(BIG[:, off:off + n], name)
        scond = S.sb([128, 8, 2], F32, "scond")
        condF = S.sb([128, 8, 2], F32, "condF")
        normwF = S.sb([128, 4, 8], F32, "normwF")
        bmodF = S.sb([128, 24], F32, "bmodF")
        modF = S.sb([128, 24, 2], F32, "modF")
        scaleF = S.sb([128, 8, 2], F32, "scaleF")
        Gb = [S.sb([128, 1024], F32, "G") for _ in range(2)]
        gbt = Rot([S.sb([128, 128], F32, "gbt") for _ in range(2)])
        gnw = S.sb([128, 2048], F32, "gnw")
        rope0 = S.sb([128, 2, 8, 128], F32, "rope0")
        rope2 = S.sb([128, 2, 8, 32], F32, "rope2")
        rtmp = Rot([S.sb([128, 128], F32, "rtmp") for _ in range(2)])
        tmpA = Rot([S.sb([128, 512], F32, "tmpA") for _ in range(3)])
        tmpB = Rot([S.sb([128, 512], F32, "tmpB") for _ in range(2)])
        lamc = S.sb([128, 8], F32, "lamc")
        dlb = S.sb([128, 256], F32, "dlb")
        lgt = S.sb([128, 16], F32, "lgt")
        stx = S.sb([128, 4], F32, "stx")
        dsc = S.sb([128, 4, 4], F32, "dsc")
        strip = Rot([S.sb([128, 1920], BF16, "strip") for _ in range(2)])
        osb = Rot([S.sb([128, 512], F32, "osb") for _ in range(2)])

        def sub(off, n, name):
            assert off + n <= ATTN, (off, n)
            return Buf(ATT[:, off:off + n], name)

        def bview(buf, off_f, shape):
            n = int(np.prod(shape))
            ap = buf.t[:, off_f:off_f + n // 2].bitcast(BF16)
            if len(shape) == 1:
                return ap
            names = " ".join(f"a{i}" for i in range(len(shape)))
            kw = {f"a{i}": shape[i] for i in range(len(shape) - 1)}
            return ap.rearrange(f"p ({names}) -> p {names}", **kw)

        def fview(buf, off_f, shape):
            n = int(np.prod(shape))
            ap = buf.t[:, off_f:off_f + n]
            if len(shape) == 1:
                return ap
            names = " ".join(f"a{i}" for i in range(len(shape)))
            kw = {f"a{i}": shape[i] for i in range(len(shape) - 1)}
            return ap.rearrange(f"p ({names}) -> p {names}", **kw)

        XB = [Buf(XD[t * 128:(t + 1) * 128, :], f"xd{t}") for t in range(16)]
        first_layer = layers[0]

        def x_src(layer, t):
            if layer == first_layer:
                src = I["xp"] if t < 8 else I["xs"]
                tt = t % 8
                return src[tt * 128:(tt + 1) * 128, :], []
            return XB[t].t, [XB[t]]

        C = S.CONST
        for c2 in range(2):
            S.dma(condF[:, :, c2], I["cond"][c2].rearrange("(k p) -> p k", p=128), writes=[condF], owner=C,
                  allow_slow_non_contiguous=True)
        for l2 in range(4):
            S.dma(normwF[:, l2, :], I["norm_w"][l2].rearrange("(k p) -> p k", p=128), writes=[normwF], owner=C,
                  allow_slow_non_contiguous=True)
        for cs in range(2):
            for d, (rt, hw) in enumerate(((rope0, 64), (rope2, 16))):
                src = I["rope0" if d == 0 else "rope2"][cs].rearrange("(t p) a f -> p t (a f)", p=128)
                S.dma(rt[:, cs, :, :], src, writes=[rt], owner=C)
        S.dma(stx[:], I["stexp"], writes=[stx], owner=C)
        S.consts_done()
        OP("pool", lambda e: e.memset(ident[:], 0.0), [], [ident])
        OP("pool", lambda e: e.affine_select(out=ident[:], in_=ident[:], pattern=[[-1, 128]], compare_op=ALU.not_equal,
                                             fill=1.0, base=0, channel_multiplier=1), [ident], [ident])
        OP("act", lambda e: e.activation(out=scond[:], in_=condF[:], func=AF.Silu), [condF], [scond])
        OP("pool", lambda e: e.tensor_tensor(out=identst[:], in0=ident[:, 0:64], in1=ident[:, 64:128], op=ALU.add), [ident], [identst])
        OP("pool", lambda e: e.memset(Jm[:], 0.0), [], [Jm])
        OP("pool", lambda e: e.affine_select(out=Jm[:], in_=Jm[:], pattern=[[1, 128]], compare_op=ALU.not_equal,
                                             fill=1.0, base=-127, channel_multiplier=1), [Jm], [Jm])

        wctr = [0]

        def load_w(W, k0, c0, ncols=256, cast=True):
            f = wf.next()
            src = W[k0:k0 + 1024, c0:c0 + ncols].rearrange("(k p) n -> p k n", p=128)
            S.dma(f[:, :, 0:ncols], src, writes=[f])
            if not cast:
                return f
            b = wb.next()
            wctr[0] += 1
            if wctr[0] % 2:
                OP("act", lambda e: e.copy(out=b[:, :, 0:ncols], in_=f[:, :, 0:ncols]), [f], [b])
            else:
                OP("dve", lambda e: e.tensor_copy(out=b[:, :, 0:ncols], in_=f[:, :, 0:ncols]), [f], [b])
            return b

        def mod(layer):
            S.dma(bmodF[:], I["b_mod"][layer].rearrange("(m p) -> p m", p=128), writes=[bmodF],
                  allow_slow_non_contiguous=True)
            psm = pP_rot.next()
            for blk in range(12):
                w = load_w(I["w_mod"][layer], 0, blk * 256, cast=False)
                for mm in range(2):
                    m = blk * 2 + mm
                    for k in range(8):
                        OP("pe", lambda e, m=m, mm=mm, k=k, w=w: e.matmul(
                            out=psm[:, m * 2:m * 2 + 2], lhsT=w[:, k, mm * 128:(mm + 1) * 128], rhs=scond[:, k, :],
                            start=(k == 0), stop=(k == 7)), [w, scond], [psm])
            OP("dve", lambda e: e.tensor_tensor(out=modF[:], in0=psm[:, 0:48].rearrange("p (m c) -> p m c", c=2),
                                                in1=bmodF[:].unsqueeze(2).to_broadcast([128, 24, 2]), op=ALU.add),
               [psm, bmodF], [modF])
            OP("dve", lambda e: e.tensor_scalar(out=scaleF[:], in0=modF[:, 8:16, :], scalar1=1.0, scalar2=None,
                                                op0=ALU.add), [modF], [scaleF])
            OP("dve", lambda e: e.tensor_tensor(out=scaleF[:], in0=scaleF[:],
                                                in1=normwF[:, layer, :].unsqueeze(2).to_broadcast([128, 8, 2]),
                                                op=ALU.mult), [scaleF, normwF], [scaleF])
            for c in range(2):
                pg = [pP_rot.next(), pP_rot.next()]
                for k in range(8):
                    g = gbt.next()
                    OP("dve", lambda e, g=g, k=k, c=c: e.tensor_copy(
                        out=g[:], in_=modF[:, 16 + k, c:c + 1].to_broadcast([128, 128])), [modF], [g])
                    OP("pe", lambda e, g=g, k=k, pg=pg: e.matmul(
                        out=pg[k // 4][:, (k % 4) * 128:(k % 4 + 1) * 128], lhsT=g[:], rhs=ident[:],
                        start=True, stop=True), [g, ident], [pg[k // 4]])
                for hlf in range(2):
                    OP("act", lambda e, hlf=hlf, c=c, pg=pg: e.copy(out=Gb[c][:, hlf * 512:(hlf + 1) * 512],
                                                                    in_=pg[hlf][:, :]), [pg[hlf]], [Gb[c]])

        def front(layer, tiles, c, hT_of):
            for ti, t in enumerate(tiles):
                xt = xt_rot.next()
                src, rb = x_src(layer, t)
                S.dma(xt[:], src, reads=rb, writes=[xt])
                s = st1.next()
                xn = xn_rot.next()
                OP("act", lambda e, xt=xt, xn=xn, s=s: e.activation(out=xn[:], in_=xt[:], func=AF.Square,
                                                                    accum_out=s[:, 0:1]), [xt], [xn, s])
                OP("act", lambda e, s=s: e.activation(out=s[:, 1:2], in_=s[:, 0:1], func=AF.Sqrt, scale=1.0 / 1024,
                                                      bias=EPS), [s], [s])
                OP("dve", lambda e, s=s: e.reciprocal(out=s[:, 2:3], in_=s[:, 1:2]), [s], [s])
                OP("dve", lambda e, xt=xt, xn=xn, s=s: e.tensor_scalar(out=xn[:], in0=xt[:], scalar1=s[:, 2:3],
                                                                       scalar2=None, op0=ALU.mult), [xt, s], [xn])
                pp = [pP_rot.next(), pP_rot.next()]
                for k in range(8):
                    OP("pe", lambda e, k=k, xn=xn, pp=pp: e.transpose(
                        out=pp[k // 4][:, (k % 4) * 128:(k % 4 + 1) * 128], in_=xn[:, k * 128:(k + 1) * 128],
                        identity=ident[:]), [xn, ident], [pp[k // 4]])
                for k in range(8):
                    dst, db = hT_of(k, ti)
                    OP("act", lambda e, k=k, dst=dst, pp=pp: e.activation(
                        out=dst, in_=pp[k // 4][:, (k % 4) * 128:(k % 4 + 1) * 128], func=AF.Identity,
                        scale=scaleF[:, k, c:c + 1], bias=modF[:, k, c:c + 1]), [pp[k // 4], scaleF, modF], [db])

        def proj(hT, hbuf, ti, w, ncols, ps):
            for k in range(8):
                OP("pe", lambda e, k=k: e.matmul(out=ps[:, 0:ncols], lhsT=hT[:, k, ti * 128:(ti + 1) * 128],
                                                 rhs=w[:, k, 0:ncols], start=(k == 0), stop=(k == 7)),
                   [hbuf, w], [ps])

        def transpose_blocks(src_ap_of, srcbuf, nblk, dst_of, evac="act"):
            i = 0
            while i < nblk:
                n = min(4, nblk - i)
                ps = pP_rot.next()
                for j in range(n):
                    OP("pe", lambda e, i=i, j=j, ps=ps: e.transpose(out=ps[:, j * 128:(j + 1) * 128],
                                                                    in_=src_ap_of(i + j), identity=ident[:]),
                       [srcbuf, ident], [ps])
                dst, db = dst_of(i, n)
                if evac == "act":
                    OP("act", lambda e, dst=dst, ps=ps, n=n: e.copy(
                        out=dst, in_=ps[:, 0:n * 128].rearrange("p (a b) -> p a b", a=n)), [ps], [db])
                else:
                    OP("dve", lambda e, dst=dst, ps=ps, n=n: e.tensor_copy(
                        out=dst, in_=ps[:, 0:n * 128].rearrange("p (a b) -> p a b", a=n)), [ps], [db])
                i += n

        def rope(src, dst, table, half, tile, ngrp):
            n = ngrp * 4 * half
            sv = src[:, 0:n].rearrange("p (g a two f) -> p g a two f", g=ngrp, a=2, two=2)
            dv = dst[:, 0:n].rearrange("p (g a two f) -> p g a two f", g=ngrp, a=2, two=2)
            cos = table[:, 0, tile, :].rearrange("p (a f) -> p a f", a=2).unsqueeze(1).to_broadcast([128, ngrp, 2, half])
            sin = table[:, 1, tile, :].rearrange("p (a f) -> p a f", a=2).unsqueeze(1).to_broadcast([128, ngrp, 2, half])
            x1, x2 = sv[:, :, :, 0, :], sv[:, :, :, 1, :]
            o1, o2 = dv[:, :, :, 0, :], dv[:, :, :, 1, :]
            t1 = rtmp.next()
            t2 = rtmp.next()
            m = ngrp * 2 * half
            t1v = t1[:, 0:m].rearrange("p (g a f) -> p g a f", g=ngrp, a=2)
            t2v = t2[:, 0:m].rearrange("p (g a f) -> p g a f", g=ngrp, a=2)
            OP("dve", lambda e: e.tensor_tensor(out=o1, in0=x1, in1=cos, op=ALU.mult), [src, table], [dst])
            OP("pool", lambda e: e.tensor_tensor(out=t1v, in0=x2, in1=sin, op=ALU.mult), [src, table], [t1])
            OP("dve", lambda e: e.tensor_tensor(out=o1, in0=o1, in1=t1v, op=ALU.subtract), [dst, t1], [dst])
            OP("pool", lambda e: e.tensor_tensor(out=t2v, in0=x1, in1=sin, op=ALU.mult), [src, table], [t2])
            OP("dve", lambda e: e.tensor_tensor(out=o2, in0=x2, in1=cos, op=ALU.mult), [src, table], [dst])
            OP("dve", lambda e: e.tensor_tensor(out=o2, in0=o2, in1=t2v, op=ALU.add), [dst, t2], [dst])

        def tail(layer, tiles, c, ogT, ogbuf, KC, Wout, ybufs):
            nt = len(tiles)
            yacc = ATT[:, 0:nt * 1024].rearrange("p (t n) -> p t n", t=nt)
            for cb in range(4):
                ws = [load_w(Wout, kk * 1024, cb * 256) for kk in range(KC // 8)]
                for ti in range(nt):
                    ps = pP_rot.next()
                    for kc in range(KC):
                        w = ws[kc // 8]
                        OP("pe", lambda e, kc=kc, w=w, ti=ti, ps=ps: e.matmul(
                            out=ps[:, 0:256], lhsT=ogT[:, kc, ti * 128:(ti + 1) * 128], rhs=w[:, kc % 8, :],
                            start=(kc == 0), stop=(kc == KC - 1)), [ogbuf, w], [ps])
                    OP("dve", lambda e, ti=ti, cb=cb, ps=ps: e.tensor_tensor(
                        out=yacc[:, ti, cb * 256:(cb + 1) * 256], in0=ps[:, 0:256],
                        in1=Gb[c][:, cb * 256:(cb + 1) * 256], op=ALU.mult), [ps, Gb[c]], ybufs)
            for ti, t in enumerate(tiles):
                xt = xt_rot.next()
                src, rb = x_src(layer, t)
                S.dma(xt[:], src, reads=rb, writes=[xt])
                OP("dve", lambda e, xt=xt, ti=ti: e.tensor_tensor(out=xt[:], in0=xt[:], in1=yacc[:, ti, :], op=ALU.add),
                   [xt] + ybufs, [xt])
                S.dma(XB[t].t, xt[:], reads=[xt], writes=[XB[t]], owner=xt, queue="act")

        def softmax_un(src, srcbufs, scale, Pout, Pbuf):
            s = st1.next()
            ax = AX.XY if len(src.shape) == 3 else AX.X
            OP("dve", lambda e: e.tensor_reduce(out=s[:, 0:1], in_=src, axis=ax, op=ALU.max), srcbufs, [s])
            OP("dve", lambda e: e.tensor_scalar(out=s[:, 1:2], in0=s[:, 0:1], scalar1=-scale, scalar2=None,
                                                op0=ALU.mult), [s], [s])
            OP("act", lambda e: e.activation(out=Pout, in_=src, func=AF.Exp, scale=scale, bias=s[:, 1:2],
                                             accum_out=s[:, 2:3]), srcbufs + [s], [Pbuf, s])
            OP("dve", lambda e: e.reciprocal(out=s[:, 3:4], in_=s[:, 2:3]), [s], [s])
            return s

        def pv(pc, pcbuf, kblocks, pcT, pcTbuf, vof, N, pso):
            nb = len(kblocks)
            i = 0
            while i < nb:
                n = min(4, nb - i)
                ps = pP_rot.next()
                for j in range(n):
                    off, nk = kblocks[i + j]
                    OP("pe", lambda e, j=j, off=off, nk=nk, ps=ps: e.transpose(
                        out=ps[0:nk, j * 128:(j + 1) * 128], in_=pc[:, off:off + nk], identity=ident[:]),
                       [pcbuf, ident], [ps])
                full = all(kblocks[i + j][1] == 128 for j in range(n))
                if full:
                    OP("act", lambda e, i=i, n=n, ps=ps: e.copy(
                        out=pcT[:, i:i + n, :], in_=ps[:, 0:n * 128].rearrange("p (a b) -> p a b", a=n)),
                       [ps], [pcTbuf])
                else:
                    for j in range(n):
                        nk = kblocks[i + j][1]
                        OP("act", lambda e, i=i, j=j, nk=nk, ps=ps: e.copy(
                            out=pcT[0:nk, i + j, :], in_=ps[0:nk, j * 128:(j + 1) * 128]), [ps], [pcTbuf])
                i += n
            for i, (off, nk) in enumerate(kblocks):
                vap, vbuf = vof(i)
                OP("pe", lambda e, i=i, nk=nk, vap=vap: e.matmul(out=pso[:, 0:N], lhsT=pcT[0:nk, i, :], rhs=vap,
                                                               start=(i == 0), stop=(i == nb - 1)),
                   [pcTbuf, vbuf], [pso])

        groups = [
            dict(tiles=[0, 1, 2, 3], c=0, seqs=[(0, 2, 0), (2, 2, 1)], sample=False, pair=0),
            dict(tiles=[4, 5, 6, 7], c=0, seqs=[(0, 2, 2), (2, 2, 3)], sample=False, pair=1),
            dict(tiles=list(range(8, 16)), c=1, seqs=[(0, 8, -1)], sample=True, pair=-1),
        ]

        def layer_diff(layer):
            lam_init = 0.8 - 0.6 * math.exp(-0.3 * layer)
            Win, Wout = I["diff_w_in"], I["diff_w_out"]
            S.dma(gnw[:, 0:1024], I["diff_gn"].partition_broadcast(128), writes=[gnw])
            OP("dve", lambda e: e.tensor_scalar(out=gnw[:, 0:1024], in0=gnw[:, 0:1024], scalar1=1.0 - lam_init,
                                                scalar2=None, op0=ALU.mult), [gnw], [gnw])
            S.dma(dlb[:], I["diff_lambda"].partition_broadcast(128), writes=[dlb])
            dl = dlb[:].rearrange("p (a f) -> p a f", a=4)
            t = tmpA.next()
            for i2 in range(2):
                OP("dve", lambda e, i2=i2: e.tensor_tensor(out=t[:, i2 * 64:(i2 + 1) * 64], in0=dl[:, 2 * i2, :],
                                                           in1=dl[:, 2 * i2 + 1, :], op=ALU.mult), [dlb], [t])
            OP("dve", lambda e: e.tensor_reduce(out=lamc[:, 0:2], in_=t[:, 0:128].rearrange("p (a f) -> p a f", a=2),
                                                axis=AX.X, op=ALU.add), [t], [lamc])
            OP("act", lambda e: e.activation(out=lamc[:, 2:4], in_=lamc[:, 0:2], func=AF.Exp), [lamc], [lamc])
            OP("dve", lambda e: e.tensor_tensor(out=lamc[:, 4:5], in0=lamc[:, 2:3], in1=lamc[:, 3:4], op=ALU.subtract),
               [lamc], [lamc])
            OP("dve", lambda e: e.tensor_scalar(out=lamc[:, 5:6], in0=lamc[:, 4:5], scalar1=lam_init, scalar2=None,
                                                op0=ALU.add), [lamc], [lamc])
            scale = 64 ** -0.5

            def do_group(g):
                S.barrier()
                tiles, c, smp = g["tiles"], g["c"], g["sample"]
                nt = len(tiles)
                T = nt * 128
                NK = T + 256 if smp else 256
                hT = bview(R_H, 0, [8, T])
                ogT = bview(R_G, 0, [8, T])
                qTb = sub(0, 1024, "qT")
                kTb = sub(1024, 1280, "kT")
                vbb = sub(2304, 1280, "vb")
                Psets = [(sub(3584, 1280, "P1a"), sub(4864, 1280, "P2a"), sub(7424, 640, "pcTa")),
                         (sub(6144, 1280, "P1b"), sub(11136, 1280, "P2b"), sub(12416, 640, "pcTb"))]
                obb = sub(8064, 2048, "ob")
                ckb = sub(10112, 1024, "ck")
                qT = bview(qTb, 0, [2, 1024])
                kT = bview(kTb, 0, [2, 1280])
                vb = bview(vbb, 0, [10, 256])
                ob = fview(obb, 0, [8, 256])
                ck = fview(ckb, 0, [2, 512])
                front(layer, tiles, c, lambda k, ti: (hT[:, k, ti * 128:(ti + 1) * 128], R_H))
                CK("front")
                for hb in range(4):
                    for which, dstT, dstb in ((0, qT, qTb), (1, kT, kTb)):
                        w = load_w(Win, 0, which * 1024 + hb * 256)
                        for ti in range(nt):
                            ps = pP_rot.next()
                            proj(hT, R_H, ti, w, 256, ps)
                            ta = tmpA.next()
                            OP("act", lambda e, ta=ta, ps=ps: e.copy(out=ta[:, 0:256], in_=ps[:, 0:256]), [ps], [ta])
                            srcb = ta
                            if smp:
                                tb = tmpB.next()
                                rope(ta, tb, rope2, 16, ti, 4)
                                srcb = tb
                            elif which == 1:
                                sq = g["seqs"][ti // 2][2]
                                tt = ti % 2
                                S.dma(O["dk"][sq, 2 * hb:2 * hb + 2, tt * 128:(tt + 1) * 128, :].rearrange("h t d -> t h d"),
                                      ta[:, 0:256].rearrange("p (h d) -> p h d", h=2), reads=[ta], queue="act")
                            transpose_blocks(lambda i, srcb=srcb: srcb[:, i * 128:(i + 1) * 128], srcb, 2,
                                             lambda i0, n, dstT=dstT, dstb=dstb, ti=ti: (dstT[:, i0:i0 + n, ti * 128:(ti + 1) * 128], dstb))
                    CK("qk")
                    w = load_w(Win, 0, 2048 + hb * 256)
                    for ti in range(nt):
                        ps = pP_rot.next()
                        proj(hT, R_H, ti, w, 256, ps)
                        ta = tmpA.next()
                        OP("act", lambda e, ta=ta, ps=ps: e.copy(out=ta[:, 0:256], in_=ps[:, 0:256]), [ps], [ta])
                        OP("dve", lambda e, ta=ta, ti=ti: e.tensor_copy(out=vb[:, ti, :], in_=ta[:, 0:256]), [ta], [vbb])
                        if not smp:
                            sq = g["seqs"][ti // 2][2]
                            tt = ti % 2
                            S.dma(O["dv"][sq, 2 * hb:2 * hb + 2, tt * 128:(tt + 1) * 128, :].rearrange("h t d -> t h d"),
                                  ta[:, 0:256].rearrange("p (h d) -> p h d", h=2), reads=[ta], queue="act")
                    if smp:
                        for tt in range(2):
                            S.dma(ck[:, 0, 0:256].rearrange("p (h d) -> p h d", h=2),
                                  I["cache_diff_k"][2 * hb:2 * hb + 2, tt * 128:(tt + 1) * 128, :].rearrange("h t d -> t h d"),
                                  writes=[ckb])
                            transpose_blocks(lambda i: ck[:, 0, i * 128:(i + 1) * 128], ckb, 2,
                                             lambda i0, n, tt=tt: (kT[:, i0:i0 + n, 1024 + tt * 128:1024 + (tt + 1) * 128], kTb))
                            S.dma(ck[:, 1, 0:256].rearrange("p (h d) -> p h d", h=2),
                                  I["cache_diff_v"][2 * hb:2 * hb + 2, tt * 128:(tt + 1) * 128, :].rearrange("h t d -> t h d"),
                                  writes=[ckb])
                            OP("dve", lambda e, tt=tt: e.tensor_copy(out=vb[:, 8 + tt, :], in_=ck[:, 1, 0:256]), [ckb], [vbb])
                    CK("v")
                    nkt = NK // 128
                    nblk, blk = (1, 256) if NK == 256 else (4, 320)

                    def att_s1(t0, hh, qi, k_):
                        P1b_, P2b_ = Psets[k_][0], Psets[k_][1]
                        P1_, P2_ = P1b_.t, P2b_.t
                        tq = t0 + qi
                        stats = []
                        for comp, (Pb, Pap) in enumerate(((P1b_, P1_), (P2b_, P2_))):
                            pr = slice(comp * 64, (comp + 1) * 64)
                            for b in range(nblk):
                                k0 = (t0 * 128 if not smp else 0) + b * blk
                                OP("pe", lambda e, pr=pr, k0=k0, b=b: e.matmul(
                                    out=PB[b][:, 0:blk], lhsT=qT[pr, hh, tq * 128:(tq + 1) * 128],
                                    rhs=kT[pr, hh, k0:k0 + blk], start=True, stop=True), [qTb, kTb], [PB[b]])
                            src = PSt[:, 0:nblk, 0:blk]
                            stats.append(softmax_un(src, PB[0:nblk], scale, Pap[:, 0:NK].rearrange("p (a b) -> p a b", a=nblk), Pb))
                        s1, s2 = stats
                        OP("dve", lambda e: e.tensor_tensor(out=s2[:, 4:5], in0=s2[:, 3:4], in1=lamc[:, 5:6], op=ALU.mult), [s2, lamc], [s2])
                        OP("dve", lambda e: e.tensor_scalar(out=P2_[:, 0:NK], in0=P2_[:, 0:NK], scalar1=s2[:, 4:5], scalar2=None, op0=ALU.mult),
                           [P2b_, s2], [P2b_])
                        OP("dve", lambda e: e.scalar_tensor_tensor(out=P1_[:, 0:NK], in0=P1_[:, 0:NK], scalar=s1[:, 3:4], in1=P2_[:, 0:NK],
                                                                   op0=ALU.mult, op1=ALU.subtract), [P1b_, P2b_, s1], [P1b_])
                        return (t0, hh, qi, k_)

                    def att_s2(ctx):
                        t0, hh, qi, k_ = ctx
                        P1b_, pcTb_ = Psets[k_][0], Psets[k_][2]
                        pcT_ = bview(pcTb_, 0, [10, 128])
                        tq = t0 + qi
                        h = 2 * hb + hh
                        pso = pP_rot.next()
                        vt0 = 0 if smp else t0
                        pv(P1b_.t, P1b_, [(i * 128, 128) for i in range(nkt)], pcT_, pcTb_,
                           lambda i: (vb[:, vt0 + i, hh * 128:(hh + 1) * 128], vbb), 128, pso)
                        s = st1.next()
                        ta = tmpA.next()
                        OP("act", lambda e: e.activation(out=ta[:, 0:128], in_=pso[:, 0:128], func=AF.Square, accum_out=s[:, 0:1]), [pso], [ta, s])
                        OP("act", lambda e: e.activation(out=s[:, 1:2], in_=s[:, 0:1], func=AF.Sqrt, scale=1.0 / 128, bias=EPS), [s], [s])
                        OP("dve", lambda e: e.reciprocal(out=s[:, 2:3], in_=s[:, 1:2]), [s], [s])
                        OP("dve", lambda e: e.scalar_tensor_tensor(out=ob[:, tq, hh * 128:(hh + 1) * 128], in0=pso[:, 0:128], scalar=s[:, 2:3],
                                                                   in1=gnw[:, h * 128:(h + 1) * 128], op0=ALU.mult, op1=ALU.mult), [pso, s, gnw], [obb])

                    its = [(t0, hh, qi) for (t0, ntq, sq) in g["seqs"] for hh in range(2) for qi in range(ntq)]
                    prev = None
                    for ii, (t0, hh, qi) in enumerate(its):
                        ctx = att_s1(t0, hh, qi, ii % 2)
                        if prev is not None:
                            att_s2(prev)
                        prev = ctx
                    att_s2(prev)
                    CK("attn")
                    w = load_w(Win, 0, 3072 + hb * 256)
                    for ti in range(nt):
                        ps = pP_rot.next()
                        proj(hT, R_H, ti, w, 256, ps)
                        ta = tmpA.next()
                        OP("act", lambda e, ta=ta, ps=ps: e.activation(out=ta[:, 0:256], in_=ps[:, 0:256], func=AF.Silu), [ps], [ta])
                        OP("dve", lambda e, ta=ta, ti=ti: e.tensor_tensor(out=ob[:, ti, :], in0=ob[:, ti, :], in1=ta[:, 0:256],
                                                                          op=ALU.mult), [obb, ta], [obb])
                        transpose_blocks(lambda i, ti=ti: ob[:, ti, i * 128:(i + 1) * 128], obb, 2,
                                         lambda i0, n, ti=ti, hb=hb: (ogT[:, 2 * hb + i0:2 * hb + i0 + n, ti * 128:(ti + 1) * 128], R_G))
                S.barrier()
                ybufs = [Buf(ATT[:, 0:nt * 1024], "yacc")]
                tail(layer, tiles, c, ogT, R_G, 8, Wout, ybufs)

            for g in groups:
                do_group(g)
            S.barrier()

        nab_rot = Rot([S.sb([128, 576], F32, "nab") for _ in range(2)])

        def layer_na(layer):
            Win, Wout = I["na_w_in"], I["na_w_out"]
            scale = 64 ** -0.5

            def do_group(g):
                S.barrier()
                tiles, c, smp = g["tiles"], g["c"], g["sample"]
                nt = len(tiles)
                T = nt * 128
                hT = bview(R_H, 0, [8, T])
                ogT = bview(R_G, 0, [8, T])
                qTb = sub(0, 1024, "qT")
                kTb = sub(1024, 1280, "kT")
                vbb = sub(2304, 1280, "vb")
                NAsets = [(sub(3584, 1280, "P1a"), sub(7424, 640, "pcTa")), (sub(6144, 1280, "P1b"), sub(12416, 640, "pcTb"))]
                obb = sub(8064, 2048, "ob")
                ckb = sub(10112, 1024, "ck")
                qT = bview(qTb, 0, [2, 1024])
                kT = bview(kTb, 0, [2, 1280])
                vb = bview(vbb, 0, [10, 256])
                ob = fview(obb, 0, [8, 256])
                ck = fview(ckb, 0, [2, 512])
                front(layer, tiles, c, lambda k, ti: (hT[:, k, ti * 128:(ti + 1) * 128], R_H))
                for hb in range(4):
                    for which, dstT, dstb in ((0, qT, qTb), (1, kT, kTb)):
                        w = load_w(Win, 0, which * 1024 + hb * 256)
                        for ti in range(nt):
                            ps = pP_rot.next()
                            proj(hT, R_H, ti, w, 256, ps)
                            ta = tmpA.next()
                            OP("act", lambda e, ta=ta, ps=ps: e.copy(out=ta[:, 0:256], in_=ps[:, 0:256]), [ps], [ta])
                            if which == 1 and not smp:
                                sq = g["seqs"][ti // 2][2]
                                tt = ti % 2
                                S.dma(O["nk"][sq, 4 * hb:4 * hb + 4, tt * 128:(tt + 1) * 128, :].rearrange("h t d -> t h d"),
                                      ta[:, 0:256].rearrange("p (h d) -> p h d", h=4), reads=[ta], queue="act")
                            transpose_blocks(lambda i, ta=ta: ta[:, i * 128:(i + 1) * 128], ta, 2,
                                             lambda i0, n, dstT=dstT, dstb=dstb, ti=ti: (dstT[:, i0:i0 + n, ti * 128:(ti + 1) * 128], dstb))
                    w = load_w(Win, 0, 2048 + hb * 256)
                    for ti in range(nt):
                        ps = pP_rot.next()
                        proj(hT, R_H, ti, w, 256, ps)
                        ta = tmpA.next()
                        OP("act", lambda e, ta=ta, ps=ps: e.copy(out=ta[:, 0:256], in_=ps[:, 0:256]), [ps], [ta])
                        OP("dve", lambda e, ta=ta, ti=ti: e.tensor_copy(out=vb[:, ti, :], in_=ta[:, 0:256]), [ta], [vbb])
                        if not smp:
                            sq = g["seqs"][ti // 2][2]
                            tt = ti % 2
                            S.dma(O["nv"][sq, 4 * hb:4 * hb + 4, tt * 128:(tt + 1) * 128, :].rearrange("h t d -> t h d"),
                                  ta[:, 0:256].rearrange("p (h d) -> p h d", h=4), reads=[ta], queue="act")
                    if smp:
                        for tt in range(2):
                            S.dma(ck[:, 0, 0:256].rearrange("p (h d) -> p h d", h=4),
                                  I["cache_na_k"][4 * hb:4 * hb + 4, tt * 128:(tt + 1) * 128, :].rearrange("h t d -> t h d"),
                                  writes=[ckb])
                            transpose_blocks(lambda i: ck[:, 0, i * 128:(i + 1) * 128], ckb, 2,
                                             lambda i0, n, tt=tt: (kT[:, i0:i0 + n, 1024 + tt * 128:1024 + (tt + 1) * 128], kTb))
                            S.dma(ck[:, 1, 0:256].rearrange("p (h d) -> p h d", h=4),
                                  I["cache_na_v"][4 * hb:4 * hb + 4, tt * 128:(tt + 1) * 128, :].rearrange("h t d -> t h d"),
                                  writes=[ckb])
                            OP("dve", lambda e, tt=tt: e.tensor_copy(out=vb[:, 8 + tt, :], in_=ck[:, 1, 0:256]), [ckb], [vbb])
                    def att_s1(t0, hh, qi, k_):
                            P1b, pcTb = NAsets[k_]
                            P1 = P1b.t
                            h = 4 * hb + hh
                            cc = hh // 2
                            pr = slice((hh % 2) * 64, (hh % 2) * 64 + 64)
                            if True:
                                tq = t0 + qi
                                if not smp:
                                    OP("pe", lambda e, pr=pr, cc=cc, tq=tq, t0=t0: e.matmul(
                                        out=PB[0][:, 0:256], lhsT=qT[pr, cc, tq * 128:(tq + 1) * 128],
                                        rhs=kT[pr, cc, t0 * 128:t0 * 128 + 256], start=True, stop=True), [qTb, kTb], [PB[0]])
                                    s1 = softmax_un(PB[0][:, 0:256], [PB[0]], scale, P1[:, 0:256], P1b)
                                    NKs = 256
                                    kblocks = [(0, 128), (128, 128)]
                                    vof = lambda i, hh=hh, t0=t0: (vb[:, t0 + i, hh * 64:(hh + 1) * 64], vbb)
                                else:
                                    j = qi
                                    r0 = min(max(2 * j - 4, 0), 8)
                                    nr = min(9, 16 - r0)
                                    nloc = nr * 64
                                    NKs = nloc + 256
                                    blk = NKs // 2
                                    nab = nab_rot.next()
                                    S.dma(nab[:], I["na_bias_x"][h, NA_JT[j]], writes=[nab])
                                    segs = [(r0 * 64, nloc, 0, True), (1024, 256, nloc, False)]
                                    pieces = []
                                    for key0, n, col0, biased in segs:
                                        done = 0
                                        while done < n:
                                            col = col0 + done
                                            b = col // blk
                                            m = min(n - done, (b + 1) * blk - col)
                                            pieces.append((key0 + done, m, b, col - b * blk, col, biased, col0 + done - col0 + (0 if not biased else 0)))
                                            done += m
                                    for (k0, m, b, bc, col, biased, _) in pieces:
                                        OP("pe", lambda e, pr=pr, cc=cc, tq=tq, k0=k0, m=m, b=b, bc=bc: e.matmul(
                                            out=PB[b][:, bc:bc + m], lhsT=qT[pr, cc, tq * 128:(tq + 1) * 128],
                                            rhs=kT[pr, cc, k0:k0 + m], start=True, stop=True), [qTb, kTb], [PB[b]])
                                    for (k0, m, b, bc, col, biased, _) in pieces:
                                        if biased:
                                            OP("dve", lambda e, m=m, b=b, bc=bc, col=col, nab=nab: e.scalar_tensor_tensor(
                                                out=P1[:, col:col + m], in0=PB[b][:, bc:bc + m], scalar=scale,
                                                in1=nab[:, col:col + m], op0=ALU.mult, op1=ALU.add), [PB[b], nab], [P1b])
                                        else:
                                            OP("dve", lambda e, m=m, b=b, bc=bc, col=col: e.tensor_scalar(
                                                out=P1[:, col:col + m], in0=PB[b][:, bc:bc + m], scalar1=scale, scalar2=None,
                                                op0=ALU.mult), [PB[b]], [P1b])
                                    s1 = softmax_un(P1[:, 0:NKs], [P1b], 1.0, P1[:, 0:NKs], P1b)
                                    kblocks = [(i * 128, 128) for i in range(nloc // 128)]
                                    if nloc % 128:
                                        kblocks.append((nloc - 64, 64))
                                    nlb = len(kblocks)
                                    kblocks += [(nloc, 128), (nloc + 128, 128)]

                                    def vof(i, hh=hh, r0=r0, nlb=nlb, kblocks=kblocks):
                                        if i < nlb:
                                            nk = kblocks[i][1]
                                            return vb[0:nk, r0 // 2 + i, hh * 64:(hh + 1) * 64], vbb
                                        return vb[:, 8 + (i - nlb), hh * 64:(hh + 1) * 64], vbb
                                OP("dve", lambda e, s1=s1, NKs=NKs: e.tensor_scalar(out=P1[:, 0:NKs], in0=P1[:, 0:NKs], scalar1=s1[:, 3:4],
                                                                                scalar2=None, op0=ALU.mult), [P1b, s1], [P1b])
                                return (tq, hh, k_, kblocks, vof)

                    def att_s2(ctx):
                        tq, hh, k_, kblocks, vof = ctx
                        P1b, pcTb = NAsets[k_]
                        pcT = bview(pcTb, 0, [10, 128])
                        pso = pP_rot.next()
                        pv(P1b.t, P1b, kblocks, pcT, pcTb, vof, 64, pso)
                        OP("act", lambda e: e.copy(out=ob[:, tq, hh * 64:(hh + 1) * 64], in_=pso[:, 0:64]), [pso], [obb])

                    its = [(t0, hh, qi) for (t0, ntq, sq) in g["seqs"] for hh in range(4) for qi in range(ntq)]
                    prev = None
                    for ii, (t0, hh, qi) in enumerate(its):
                        ctx = att_s1(t0, hh, qi, ii % 2)
                        if prev is not None:
                            att_s2(prev)
                        prev = ctx
                    att_s2(prev)
                    w = load_w(Win, 0, 3072 + hb * 256)
                    for ti in range(nt):
                        ps = pP_rot.next()
                        proj(hT, R_H, ti, w, 256, ps)
                        ta = tmpA.next()
                        OP("act", lambda e, ta=ta, ps=ps: e.activation(out=ta[:, 0:256], in_=ps[:, 0:256], func=AF.Silu), [ps], [ta])
                        OP("dve", lambda e, ta=ta, ti=ti: e.tensor_tensor(out=ob[:, ti, :], in0=ob[:, ti, :], in1=ta[:, 0:256],
                                                                          op=ALU.mult), [obb, ta], [obb])
                        transpose_blocks(lambda i, ti=ti: ob[:, ti, i * 128:(i + 1) * 128], obb, 2,
                                         lambda i0, n, ti=ti, hb=hb: (ogT[:, 2 * hb + i0:2 * hb + i0 + n, ti * 128:(ti + 1) * 128], R_G))
                S.barrier()
                ybufs = [Buf(ATT[:, 0:nt * 1024], "yacc")]
                tail(layer, tiles, c, ogT, R_G, 8, Wout, ybufs)

            for g in groups:
                do_group(g)
            S.barrier()

        def layer_ret(layer):
            Win, Wout = I["ret_w_in"], I["ret_w_out"]
            S.dma(gnw[:, 0:2048], I["ret_gn"].partition_broadcast(128), writes=[gnw])
            S.dma(lgt[:, 0:8], I["ret_decay"].partition_broadcast(128), writes=[lgt])
            OP("act", lambda e: e.activation(out=lgt[:, 0:8], in_=lgt[:, 0:8], func=AF.Exp, scale=-1.0), [lgt], [lgt])
            OP("act", lambda e: e.activation(out=lgt[:, 0:8], in_=lgt[:, 0:8], func=AF.Ln, bias=1.0), [lgt], [lgt])
            OP("dve", lambda e: e.tensor_scalar(out=lgt[:, 0:8], in0=lgt[:, 0:8], scalar1=-1.0, scalar2=None, op0=ALU.mult), [lgt], [lgt])
            OP("dve", lambda e: e.tensor_scalar(out=lgt[:, 8:12], in0=lgt[:, 4:8], scalar1=-1.0, scalar2=None, op0=ALU.mult), [lgt], [lgt])
            OP("dve", lambda e: e.tensor_scalar(out=lgt[:, 12:16], in0=lgt[:, 4:8], scalar1=1024.0, scalar2=None, op0=ALU.mult), [lgt], [lgt])
            for h in range(4):
                OP("act", lambda e, h=h: e.activation(out=dsc[:, h, 0:2], in_=stx[:, 0:2], func=AF.Exp, scale=lgt[:, h:h + 1]), [stx, lgt], [dsc])
                OP("act", lambda e, h=h: e.activation(out=dsc[:, h, 2:4], in_=stx[:, 2:4], func=AF.Exp, scale=lgt[:, 4 + h:5 + h]), [stx, lgt], [dsc])
            OP("dve", lambda e: e.tensor_scalar(out=dsc[:], in0=dsc[:], scalar1=1.0 / 16, scalar2=None, op0=ALU.mult), [dsc], [dsc])

            def do_group(g):
                S.barrier()
                tiles, c, smp = g["tiles"], g["c"], g["sample"]
                nt = len(tiles)
                T = nt * 128
                hT = bview(R_H, 0, [8, T])
                ogT = bview(R_G, 0, [16, T])
                qTb = sub(0, 1024, "qT")
                kTb = sub(1024, 1024, "kT")
                vbb = sub(2048, 2048, "vb")
                atb = sub(4096, 4096, "attT")
                obb = sub(8192, 4096, "ob")
                qT = bview(qTb, 0, [2, 1024])
                kT = bview(kTb, 0, [2, 1024])
                vb = bview(vbb, 0, [8, 512])
                attT = bview(atb, 0, [8, 1024])
                gpn = fview(atb, 0, [2, 1920])
                ob = fview(obb, 0, [8, 512])
                if smp:
                    s0b = sub(12288, 1024, "S0b")
                    qdb = sub(13312, 2048, "qTd")
                    S0 = bview(s0b, 0, [2, 2, 512])
                    qTd = bview(qdb, 0, [2, 2, 1024])
                else:
                    kdb = sub(12288, 1024, "kdec")
                    kdec = bview(kdb, 0, [2, 4, 256])
                front(layer, tiles, c, lambda k, ti: (hT[:, k, ti * 128:(ti + 1) * 128], R_H))
                for h in range(4):
                    stp = strip.next()
                    for i2 in range(2):
                        S.dma(gpn[:, i2, :], I["gpn"][i2], writes=[atb])
                    OP("act", lambda e, h=h: e.activation(out=gpn[:, 0, :], in_=gpn[:, 0, :], func=AF.Exp, scale=lgt[:, h:h + 1]), [atb, lgt], [atb])
                    OP("act", lambda e, h=h: e.activation(out=gpn[:, 1, :], in_=gpn[:, 1, :], func=AF.Exp, scale=lgt[:, 4 + h:5 + h]), [atb, lgt], [atb])
                    OP("dve", lambda e: e.scalar_tensor_tensor(out=gpn[:, 0, :], in0=gpn[:, 0, :], scalar=-1.0, in1=gpn[:, 1, :],
                                                               op0=ALU.add, op1=ALU.add), [atb], [atb])
                    OP("dve", lambda e: e.tensor_tensor(out=gpn[:, 0, 896:1024], in0=gpn[:, 0, 896:1024], in1=ident[:], op=ALU.add),
                       [atb, ident], [atb])
                    OP("dve", lambda e, stp=stp: e.tensor_scalar(out=stp[:], in0=gpn[:, 0, :], scalar1=1.0 / 16, scalar2=None, op0=ALU.mult),
                       [atb], [stp])
                    for which, dstT, dstb in ((0, qT, qTb), (1, kT, kTb)):
                        w = load_w(Win, 0, which * 1024 + h * 256)
                        for ti in range(nt):
                            ps = pP_rot.next()
                            proj(hT, R_H, ti, w, 256, ps)
                            ta = tmpA.next()
                            OP("act", lambda e, ta=ta, ps=ps: e.copy(out=ta[:, 0:256], in_=ps[:, 0:256]), [ps], [ta])
                            srcb = ta
                            if smp:
                                tb = tmpB.next()
                                rope(ta, tb, rope0, 64, ti, 1)
                                srcb = tb
                            elif which == 1:
                                tt = ti % 2
                                for d in range(2):
                                    OP("dve", lambda e, ta=ta, d=d, ti=ti, tt=tt, h=h: e.tensor_scalar(
                                        out=kdec[:, d, ti, :], in0=ta[:, 0:256], scalar1=dsc[:, h, 2 * d + tt:2 * d + tt + 1], scalar2=None,
                                        op0=ALU.mult), [ta, dsc], [kdb])
                            transpose_blocks(lambda i, srcb=srcb: srcb[:, i * 128:(i + 1) * 128], srcb, 2,
                                             lambda i0, n, dstT=dstT, dstb=dstb, ti=ti: (dstT[:, i0:i0 + n, ti * 128:(ti + 1) * 128], dstb))
                    for v2 in range(2):
                        w = load_w(Win, 0, 2048 + h * 512 + v2 * 256)
                        for ti in range(nt):
                            ps = pP_rot.next()
                            proj(hT, R_H, ti, w, 256, ps)
                            OP("act", lambda e, ti=ti, ps=ps, v2=v2: e.copy(out=vb[:, ti, v2 * 256:(v2 + 1) * 256], in_=ps[:, 0:256]), [ps], [vbb])
                    if smp:
                        for d in range(2):
                            for dc in range(2):
                                sb_ = osb.next()
                                S.dma(sb_[:], I["state_ret"][d, h, dc * 128:(dc + 1) * 128, :], writes=[sb_])
                                OP("pool", lambda e, sb_=sb_, d=d, dc=dc: e.tensor_copy(out=S0[:, d, dc, :], in_=sb_[:]), [sb_], [s0b])
                            rd = xt_rot.next()
                            S.dma(rd[:], I["iota1k"], writes=[rd])
                            if d == 0:
                                OP("act", lambda e, rd=rd, h=h: e.activation(out=rd[:], in_=rd[:], func=AF.Exp, scale=lgt[:, h:h + 1],
                                                                             bias=lgt[:, h:h + 1]), [rd, lgt], [rd])
                            else:
                                OP("act", lambda e, rd=rd, h=h: e.activation(out=rd[:], in_=rd[:], func=AF.Exp, scale=lgt[:, 8 + h:9 + h],
                                                                             bias=lgt[:, 12 + h:13 + h]), [rd, lgt], [rd])
                            for dc in range(2):
                                OP("dve", lambda e, rd=rd, d=d, dc=dc: e.tensor_tensor(out=qTd[:, d, dc, :], in0=qT[:, dc, :], in1=rd[:],
                                                                                       op=ALU.mult), [qTb, rd], [qdb])
                    for (t0, ntq, sq) in g["seqs"]:
                        nq = ntq * 128
                        nqb = (nq + 511) // 512
                        N = min(nq, 512)
                        for i in range(ntq):
                            for qh in range(nqb):
                                for dc in range(2):
                                    OP("pe", lambda e, i=i, qh=qh, dc=dc, t0=t0, N=N: e.matmul(
                                        out=PB[qh][:, 0:N], lhsT=kT[:, dc, (t0 + i) * 128:(t0 + i + 1) * 128],
                                        rhs=qT[:, dc, t0 * 128 + qh * 512:t0 * 128 + qh * 512 + N], start=(dc == 0), stop=(dc == 1)),
                                       [qTb, kTb], [PB[qh]])
                                OP("dve", lambda e, i=i, qh=qh, N=N, stp=stp: e.tensor_tensor(
                                    out=attT[:, i, qh * 512:qh * 512 + N], in0=PB[qh][:, 0:N],
                                    in1=stp[:, (7 - i) * 128 + qh * 512:(7 - i) * 128 + qh * 512 + N], op=ALU.mult), [PB[qh], stp], [atb])
                        for j in range(ntq):
                            pso = pP_rot.next()
                            nmm = ntq + (4 if smp else 0)
                            cnt = 0
                            for i in range(ntq):
                                OP("pe", lambda e, i=i, j=j, t0=t0, cnt=cnt, nmm=nmm, pso=pso: e.matmul(
                                    out=pso[:, 0:512], lhsT=attT[:, i, j * 128:(j + 1) * 128], rhs=vb[:, t0 + i, :],
                                    start=(cnt == 0), stop=(cnt == nmm - 1)), [atb, vbb], [pso])
                                cnt += 1
                            if smp:
                                for d in range(2):
                                    for dc in range(2):
                                        OP("pe", lambda e, d=d, dc=dc, j=j, cnt=cnt, nmm=nmm, pso=pso: e.matmul(
                                            out=pso[:, 0:512], lhsT=qTd[:, d, dc, j * 128:(j + 1) * 128], rhs=S0[:, d, dc, :],
                                            start=(cnt == 0), stop=(cnt == nmm - 1)), [qdb, s0b], [pso])
                                        cnt += 1
                            s = st1.next()
                            ta = tmpA.next()
                            OP("dve", lambda e, s=s, pso=pso: e.tensor_reduce(out=s[:, 0:1], in_=pso[:, 0:512], axis=AX.X, op=ALU.add), [pso], [s])
                            OP("dve", lambda e, s=s: e.tensor_scalar(out=s[:, 1:2], in0=s[:, 0:1], scalar1=-1.0 / 512, scalar2=None, op0=ALU.mult), [s], [s])
                            OP("act", lambda e, s=s, ta=ta, pso=pso: e.activation(out=ta[:, 0:512], in_=pso[:, 0:512], func=AF.Identity, bias=s[:, 1:2]),
                               [pso, s], [ta])
                            tb = tmpB.next()
                            OP("act", lambda e, s=s, ta=ta, tb=tb: e.activation(out=tb[:, 0:512], in_=ta[:, 0:512], func=AF.Square, accum_out=s[:, 2:3]),
                               [ta], [tb, s])
                            OP("act", lambda e, s=s: e.activation(out=s[:, 3:4], in_=s[:, 2:3], func=AF.Sqrt, scale=1.0 / 512, bias=GN_EPS), [s], [s])
                            OP("dve", lambda e, s=s: e.reciprocal(out=s[:, 4:5], in_=s[:, 3:4]), [s], [s])
                            OP("dve", lambda e, s=s, ta=ta, j=j, t0=t0, h=h: e.scalar_tensor_tensor(
                                out=ob[:, t0 + j, :], in0=ta[:, 0:512], scalar=s[:, 4:5], in1=gnw[:, h * 512:(h + 1) * 512],
                                op0=ALU.mult, op1=ALU.mult), [ta, s, gnw], [obb])
                        if not smp:
                            for d in range(2):
                                for dc in range(2):
                                    pst = pP_rot.next()
                                    for tt in range(2):
                                        OP("pe", lambda e, d=d, dc=dc, tt=tt, t0=t0, pst=pst: e.matmul(
                                            out=pst[:, 0:512], lhsT=kdec[:, d, t0 + tt, dc * 128:(dc + 1) * 128], rhs=vb[:, t0 + tt, :],
                                            start=(tt == 0), stop=(tt == 1)), [kdb, vbb], [pst])
                                    sb_ = osb.next()
                                    OP("act", lambda e, sb_=sb_, pst=pst: e.copy(out=sb_[:], in_=pst[:, 0:512]), [pst], [sb_])
                                    S.dma(O["st_ret"][sq, d, h, dc * 128:(dc + 1) * 128, :], sb_[:], reads=[sb_], queue="act")
                    for g2 in range(2):
                        w = load_w(Win, 0, 4096 + h * 512 + g2 * 256)
                        for ti in range(nt):
                            ps = pP_rot.next()
                            proj(hT, R_H, ti, w, 256, ps)
                            ta = tmpA.next()
                            OP("act", lambda e, ta=ta, ps=ps: e.activation(out=ta[:, 0:256], in_=ps[:, 0:256], func=AF.Silu), [ps], [ta])
                            OP("dve", lambda e, ta=ta, ti=ti, g2=g2: e.tensor_tensor(out=ob[:, ti, g2 * 256:(g2 + 1) * 256],
                                                                                     in0=ob[:, ti, g2 * 256:(g2 + 1) * 256], in1=ta[:, 0:256],
                                                                                     op=ALU.mult), [obb, ta], [obb])
                    for ti in range(nt):
                        transpose_blocks(lambda i, ti=ti: ob[:, ti, i * 128:(i + 1) * 128], obb, 4,
                                         lambda i0, n, ti=ti, h=h: (ogT[:, 4 * h + i0:4 * h + i0 + n, ti * 128:(ti + 1) * 128], R_G))
                S.barrier()
                ybufs = [Buf(ATT[:, 0:nt * 1024], "yacc")]
                tail(layer, tiles, c, ogT, R_G, 16, Wout, ybufs)

            for g in groups:
                do_group(g)
            S.barrier()

        RKVG = [dscr(f"rkvg{n}", (2048, 1024)) for n in range(4)]
        RKVGB = [[Buf(RKVG[n][t * 128:(t + 1) * 128, :], f"rkvg{n}_{t}") for t in range(16)] for n in range(4)]
        dscrb = lambda name, shape: nc.dram_tensor(name, list(shape), BF16).ap()
        TMP = dscrb("tmp_", (128, 256, 3, 64))
        TMS = dscrb("tms_", (32, 1024, 3, 64))
        FMP = dscrb("fmp_", (128, 4, 64, 256))
        FMS = dscrb("fms_", (32, 4, 64, 1024))
        GMP = dscr("gmp_", (128, 4, 64))
        GMS = dscr("gms_", (32, 16, 64))
        GMPB, GMSB = Buf(GMP, "gmp"), Buf(GMS, "gms")
        cmask2 = S.sb([128, 3, 128], F32, "cmask2")
        YPD = dscr("ypd", (128, 256, 64))
        YSD = dscr("ysd", (32, 1024, 64))
        TMPB, TMSB, FMPB, FMSB = Buf(TMP, "tmp"), Buf(TMS, "tms"), Buf(FMP, "fmp"), Buf(FMS, "fms")
        YPDB = Buf(YPD, "ypd")
        YSDB = Buf(YSD, "ysd")
        tri = S.sb([128, 128], F32, "tri")

        def layer_rwkv(layer):
            Win, Wout = I["rwkv_w_in"], I["rwkv_w_out"]
            S.barrier()
            for n in range(6):
                S.dma(muF[:, n, :], I["rwkv_mu"][n].rearrange("(k p) -> p k", p=128), writes=[muF], allow_slow_non_contiguous=True)
            smallb = bsub(24576, 3072, "rwsmall")
            wAb = bview(smallb, 0, [2, 8, 64])
            aAb = bview(smallb, 512, [2, 8, 64])
            wBb = bview(smallb, 1024, [2, 1024])
            aBb = bview(smallb, 2048, [2, 1024])
            for d in range(2):
                for src, dst in ((I["rwkv_wA"], wAb), (I["rwkv_aA"], aAb)):
                    f = wf.next()
                    S.dma(f[:, :, 0:64], src[d].rearrange("(k p) r -> p k r", p=128), writes=[f])
                    OP("pool", lambda e, f=f, dst=dst, d=d: e.tensor_copy(out=dst[:, d, :, :], in_=f[:, :, 0:64]), [f], [smallb])
                for src, dst in ((I["rwkv_wB"], wBb), (I["rwkv_aB"], aBb)):
                    f = wf.next()
                    fv = f.t[0:64].rearrange("p k n -> p (k n)")[:, 0:1024]
                    S.dma(fv, src[d], writes=[f])
                    OP("pool", lambda e, fv=fv, dst=dst, d=d: e.tensor_copy(out=dst[0:64, d, :], in_=fv), [f], [smallb])
            lorab = bsub(0, 4096, "lora")
            LWT = bview(lorab, 0, [2, 2048])
            LAT = bview(lorab, 2048, [2, 2048])
            OP("dve", lambda e: e.memset(bsum[:], 0.0), [], [bsum])

            def a1_group(g, gi):
                tiles, c, smp = g["tiles"], g["c"], g["sample"]
                nt = len(tiles)
                T = nt * 128
                tok0 = tiles[0] * 128
                hTb = bsub(4096, 8192, "hTf")
                xxb = bsub(12288, 8192, "xxT")
                xnb = bsub(20480, 4096, "xnT")
                hT = fview(hTb, 0, [8, T])
                xx = fview(xxb, 0, [8, T])
                xn = bview(xnb, 0, [8, T])
                front(layer, tiles, c, lambda k, ti: (hT[:, k, ti * 128:(ti + 1) * 128], hTb))
                for (t0, ntq, sq) in g["seqs"]:
                    o = t0 * 128
                    L = ntq * 128
                    OP("dve", lambda e, o=o, L=L: e.tensor_tensor(out=xx[:, :, o + 1:o + L - 1], in0=hT[:, :, o:o + L - 2],
                                                                 in1=hT[:, :, o + 2:o + L], op=ALU.add), [hTb], [xxb])
                    OP("dve", lambda e, o=o, L=L: e.scalar_tensor_tensor(out=xx[:, :, o + 1:o + L - 1], in0=xx[:, :, o + 1:o + L - 1],
                                                                        scalar=0.5, in1=hT[:, :, o + 1:o + L - 1],
                                                                        op0=ALU.mult, op1=ALU.subtract), [xxb, hTb], [xxb])
                    OP("dve", lambda e, o=o: e.scalar_tensor_tensor(out=xx[:, :, o:o + 1], in0=hT[:, :, o + 1:o + 2], scalar=0.5,
                                                                    in1=hT[:, :, o:o + 1], op0=ALU.mult, op1=ALU.subtract), [hTb], [xxb])
                    OP("dve", lambda e, o=o, L=L: e.scalar_tensor_tensor(out=xx[:, :, o + L - 1:o + L], in0=hT[:, :, o + L - 2:o + L - 1],
                                                                        scalar=0.5, in1=hT[:, :, o + L - 1:o + L],
                                                                        op0=ALU.mult, op1=ALU.subtract), [hTb], [xxb])

                def mix(n):
                    for k in range(8):
                        OP("dve", lambda e, k=k, n=n: e.scalar_tensor_tensor(out=xn[:, k, :], in0=xx[:, k, :], scalar=muF[:, n, k:k + 1],
                                                                             in1=hT[:, k, :], op0=ALU.mult, op1=ALU.add),
                           [xxb, hTb, muF], [xnb])
                for pi, n in enumerate((0, 2, 3, 5)):
                    mix(n)
                    for cb in range(4):
                        w = load_w(Win, 0, pi * 1024 + cb * 256)
                        for ti in range(nt):
                            ps = pP_rot.next()
                            proj(xn, xnb, ti, w, 256, ps)
                            ta = tmpA.next()
                            OP("act", lambda e, ta=ta, ps=ps: e.copy(out=ta[:, 0:256], in_=ps[:, 0:256]), [ps], [ta])
                            t = tiles[ti]
                            S.dma(RKVG[pi][t * 128:(t + 1) * 128, cb * 256:(cb + 1) * 256], ta[:, 0:256], reads=[ta],
                                  writes=[RKVGB[pi][t]], owner=ta, queue="act")
                for n, Ab, LT, fn in ((1, wAb, LWT, AF.Tanh), (4, aAb, LAT, AF.Copy)):
                    mix(n)
                    for d in range(2):
                        for c0 in range(0, T, 512):
                            ps = pP_rot.next()
                            for k in range(8):
                                OP("pe", lambda e, k=k, d=d, c0=c0, Ab=Ab, ps=ps: e.matmul(out=ps[0:64, 0:512], lhsT=Ab[:, d, k, :],
                                                                                          rhs=xn[:, k, c0:c0 + 512], start=(k == 0), stop=(k == 7)),
                                   [smallb, xnb], [ps])
                            if fn == AF.Tanh:
                                OP("act", lambda e, d=d, c0=c0, LT=LT, ps=ps: e.activation(out=LT[0:64, d, tok0 + c0:tok0 + c0 + 512],
                                                                                        in_=ps[0:64, 0:512], func=AF.Tanh), [ps], [lorab])
                            else:
                                OP("act", lambda e, d=d, c0=c0, LT=LT, ps=ps: e.copy(out=LT[0:64, d, tok0 + c0:tok0 + c0 + 512],
                                                                                  in_=ps[0:64, 0:512]), [ps], [lorab])

            for gi, g in enumerate(groups):
                a1_group(g, gi)
            S.barrier()

            tabb = bsub(4096, 8192, "tabs")
            TAB = fview(tabb, 0, [8, 1024])
            for i2, src in enumerate((I["rwkv_w0"][0], I["rwkv_w0"][1], I["rwkv_a0"][0], I["rwkv_a0"][1], I["rwkv_kk"],
                                      I["rwkv_ka"], I["rwkv_rk"], I["rwkv_gn"])):
                S.dma(TAB[:, i2, :], src.partition_broadcast(128), writes=[tabb])
            slot = [bsub(12288 + i2 * 1024, 1024, f"slot{i2}") for i2 in range(12)]
            xtb = [Buf(b_.t[:, :], b_.name + "_a2") for b_ in (xt_rot.bufs + xn_rot.bufs)]
            Rb, Kb, Vb, KKb, LWb, ABb, KDb, T1b, FLWb = slot[0:9]
            Frot = Rot([slot[9], slot[10]])
            Hrot = Rot([slot[11], xtb[0]])
            FMrot = Rot([xtb[1], xtb[2]])
            h16 = lambda b: b.t.rearrange("p (h d) -> p h d", h=16)
            S.dma(tri[:], I["tri"], writes=[tri])
            for e2 in range(2):
                S.dma(cmask2[e2 * 64:(e2 + 1) * 64, :, :], I["cmask"], writes=[cmask2])

            def flip(srcb, dstb):
                for hf in range(2):
                    ps = pP_rot.next()
                    OP("pe", lambda e, hf=hf, ps=ps: e.matmul(out=ps[:, :], lhsT=Jm[:], rhs=srcb.t[:, hf * 512:(hf + 1) * 512],
                                                             start=True, stop=True), [Jm, srcb], [ps])
                    OP("act", lambda e, hf=hf, ps=ps: e.copy(out=dstb.t[:, hf * 512:(hf + 1) * 512], in_=ps[:, :]), [ps], [dstb])

            def a2_tile(t):
                smp = t >= 8
                tt_in_seq = (t - 8) if smp else (t % 2)
                for pi, b in ((0, Rb), (1, Kb), (2, Vb)):
                    S.dma(b.t, RKVG[pi][t * 128:(t + 1) * 128, :], reads=[RKVGB[pi][t]], writes=[b])
                OP("dve", lambda e: e.tensor_tensor(out=KKb.t, in0=Kb.t, in1=TAB[:, 4, :], op=ALU.mult), [Kb, tabb], [KKb])
                OP("pool", lambda e: e.tensor_tensor(out=T1b.t, in0=KKb.t, in1=KKb.t, op=ALU.mult), [KKb], [T1b])
                nrm = tmpB.next()
                OP("dve", lambda e, nrm=nrm: e.tensor_reduce(out=nrm[:, 0:16], in_=h16(T1b), axis=AX.X, op=ALU.add), [T1b], [nrm])
                OP("dve", lambda e, nrm=nrm: e.tensor_scalar(out=nrm[:, 0:16], in0=nrm[:, 0:16], scalar1=1e-12, scalar2=None, op0=ALU.max), [nrm], [nrm])
                OP("act", lambda e, nrm=nrm: e.activation(out=nrm[:, 16:32], in_=nrm[:, 0:16], func=AF.Sqrt), [nrm], [nrm])
                OP("dve", lambda e, nrm=nrm: e.reciprocal(out=nrm[:, 32:48], in_=nrm[:, 16:32]), [nrm], [nrm])
                OP("dve", lambda e, nrm=nrm: e.tensor_tensor(out=h16(KKb), in0=h16(KKb), in1=nrm[:, 32:48].unsqueeze(2).to_broadcast([128, 16, 64]),
                                                             op=ALU.mult), [KKb, nrm], [KKb])
                for d in range(2):
                    a2_dir(t, d, smp, tt_in_seq)

            def a2_dir(t, d, smp, tt_in_seq):
                if True:
                    for (LT, Bw, tabi, dstb, post) in ((LWT, wBb, d, LWb, "w"), (LAT, aBb, 2 + d, ABb, "a")):
                        for hf in range(2):
                            ps = pP_rot.next()
                            OP("pe", lambda e, hf=hf, d=d, LT=LT, Bw=Bw, ps=ps: e.matmul(
                                out=ps[:, :], lhsT=LT[0:64, d, t * 128:(t + 1) * 128], rhs=Bw[0:64, d, hf * 512:(hf + 1) * 512],
                                start=True, stop=True), [lorab, smallb], [ps])
                            OP("dve", lambda e, hf=hf, tabi=tabi, dstb=dstb, ps=ps: e.tensor_tensor(
                                out=dstb.t[:, hf * 512:(hf + 1) * 512], in0=ps[:, :], in1=TAB[:, tabi, hf * 512:(hf + 1) * 512], op=ALU.add),
                               [ps, tabb], [dstb])
                        OP("act", lambda e, dstb=dstb: e.activation(out=dstb.t, in_=dstb.t, func=AF.Sigmoid), [dstb], [dstb])
                        if post == "w":
                            OP("pool", lambda e, dstb=dstb: e.tensor_scalar(out=dstb.t, in0=dstb.t, scalar1=-math.exp(-0.5), scalar2=None,
                                                                            op0=ALU.mult), [dstb], [dstb])
                    OP("dve", lambda e: e.scalar_tensor_tensor(out=T1b.t, in0=ABb.t, scalar=-1.0, in1=TAB[:, 5, :], op0=ALU.add, op1=ALU.mult),
                       [ABb, tabb], [T1b])
                    OP("dve", lambda e: e.scalar_tensor_tensor(out=KDb.t, in0=T1b.t, scalar=1.0, in1=Kb.t, op0=ALU.add, op1=ALU.mult),
                       [T1b, Kb], [KDb])
                    OP("pool", lambda e: e.tensor_tensor(out=ABb.t, in0=KKb.t, in1=ABb.t, op=ALU.mult), [KKb, ABb], [ABb])
                    OP("pool", lambda e: e.tensor_tensor(out=T1b.t, in0=Rb.t, in1=KDb.t, op=ALU.mult), [Rb, KDb], [T1b])
                    OP("pool", lambda e: e.tensor_tensor(out=T1b.t, in0=T1b.t, in1=TAB[:, 6, :], op=ALU.mult), [T1b, tabb], [T1b])
                    nb = tmpB.next()
                    OP("dve", lambda e, nb=nb: e.tensor_reduce(out=nb[:, 0:16], in_=h16(T1b), axis=AX.X, op=ALU.add), [T1b], [nb])
                    OP("dve", lambda e, nb=nb: e.tensor_tensor(out=bsum[:, t, :], in0=bsum[:, t, :], in1=nb[:, 0:16], op=ALU.add), [bsum, nb], [bsum])
                    if smp:
                        L, TMD, FMD, TMB_, FMB_, GMD, GMB_ = 1024, TMS, FMS, TMSB, FMSB, GMS, GMSB
                        ch0 = d * 16
                    else:
                        L, TMD, FMD, TMB_, FMB_, GMD, GMB_ = 256, TMP, FMP, TMPB, FMPB, GMP, GMPB
                        ch0 = (d * 4 + t // 2) * 16
                    tk0 = tt_in_seq * 128
                    s0 = tk0 if d == 0 else L - 128 - tk0

                    def chain_order(srcb):
                        if d == 0:
                            return srcb
                        f = Frot.next()
                        flip(srcb, f)
                        return f
                    if d == 0:
                        lwc = LWb
                    else:
                        flip(LWb, FLWb)
                        lwc = FLWb
                    cps = [PB[0], PB[1]]
                    for hf in range(2):
                        OP("pe", lambda e, hf=hf: e.matmul(out=cps[hf][:, :], lhsT=tri[:], rhs=lwc.t[:, hf * 512:(hf + 1) * 512], start=True, stop=True),
                           [tri, lwc], [cps[hf]])

                    def store_tm(hb_, vi):
                        tmb = strip.next()
                        OP("act", lambda e: e.copy(out=tmb[:, 0:1024], in_=hb_.t), [hb_], [tmb])
                        S.dma(TMD[ch0:ch0 + 16, s0:s0 + 128, vi, :].rearrange("h t j -> t h j"), tmb[:, 0:1024].rearrange("p (h d) -> p h d", h=16),
                              reads=[tmb], writes=[TMB_], owner=tmb, queue="act")

                    def store_fm(hb_, vi):
                        fm = FMrot.next()
                        fmv = fm.t[:, 0:512].bitcast(BF16).rearrange("p (a b) -> p a b", a=8)
                        transpose_blocks(lambda i: hb_.t[:, i * 128:(i + 1) * 128], hb_, 8, lambda i0, n: (fmv[:, i0:i0 + n, :], fm), evac="act")
                        for e2 in range(2):
                            S.dma(FMD[ch0 + e2:ch0 + 16:2, vi, :, s0:s0 + 128].rearrange("c k t -> k c t"), fmv[e2 * 64:(e2 + 1) * 64, :, :],
                                  reads=[fm], writes=[FMB_], owner=fm, queue="act")

                    def hat(srcb, kind):
                        hb_ = Hrot.next()
                        for hf in range(2):
                            sl = slice(hf * 512, (hf + 1) * 512)
                            if kind == "prev":
                                OP("dve", lambda e, hf=hf, sl=sl: e.tensor_tensor(out=T1b.t[:, sl], in0=cps[hf][:, :], in1=lwc.t[:, sl], op=ALU.subtract),
                                   [cps[hf], lwc], [T1b])
                                OP("act", lambda e, sl=sl: e.activation(out=T1b.t[:, sl], in_=T1b.t[:, sl], func=AF.Exp), [T1b], [T1b])
                            elif kind == "cur":
                                OP("act", lambda e, hf=hf, sl=sl: e.activation(out=T1b.t[:, sl], in_=cps[hf][:, :], func=AF.Exp), [cps[hf]], [T1b])
                            elif kind == "inv":
                                OP("act", lambda e, hf=hf, sl=sl: e.activation(out=T1b.t[:, sl], in_=cps[hf][:, :], func=AF.Exp, scale=-1.0), [cps[hf]], [T1b])
                        if srcb is None:
                            OP("pool", lambda e: e.tensor_copy(out=hb_.t, in_=T1b.t), [T1b], [hb_])
                        else:
                            OP("pool", lambda e: e.tensor_tensor(out=hb_.t, in0=srcb.t, in1=T1b.t, op=ALU.mult), [srcb, T1b], [hb_])
                        return hb_

                    hb_ = hat(chain_order(KKb), "prev")
                    store_fm(hb_, 0)
                    hb_ = hat(chain_order(Rb), "cur")
                    store_fm(hb_, 1)
                    for cc in range(2):
                        cidx = s0 // 64 + cc
                        S.dma(GMD[ch0:ch0 + 16, cidx:cidx + 1, :].rearrange("h n k -> n h k"),
                              T1b.t[63 + 64 * cc:64 + 64 * cc, :].rearrange("p (h k) -> p h k", h=16), reads=[T1b], writes=[GMB_], owner=T1b, queue="act")
                    hb_ = hat(chain_order(ABb), "inv")
                    store_fm(hb_, 2)
                    store_tm(hb_, 0)
                    hb_ = hat(chain_order(KDb), "inv")
                    store_fm(hb_, 3)
                    store_tm(hb_, 1)
                    store_tm(chain_order(Vb), 2)

            for t in range(16):
                a2_tile(t)
            S.barrier()
            CK("rwkv_a")

            NU = 20
            o = 0

            def carve(size, name):
                nonlocal o
                b = bsub(o, size, name)
                o += size
                return b
            Tst = [carve(256, f"T{u}") for u in range(NU)]
            Tbs = [carve(128, f"Tb{u}") for u in range(NU)]
            NW = 4
            bsets = []
            for w_ in range(NW):
                bsets.append(dict(
                    tm=carve(384, f"tm{w_}"), fm=carve(512, f"fm{w_}"), gm=carve(4, f"gm{w_}"), ka=carve(256, f"ka{w_}"), apb=carve(128, f"apb{w_}"),
                    qz=[carve(512, f"qz{w_}a"), carve(512, f"qz{w_}b")], qt=[carve(256, f"qt{w_}a"), carve(256, f"qt{w_}b")],
                    bdq=carve(512, f"bdq{w_}"), bdt=carve(512, f"bdt{w_}"), zb=carve(128, f"zb{w_}"), u=carve(128, f"u{w_}"),
                    p=carve(128, f"p{w_}"), y=carve(256, f"y{w_}")))
            pB_rot = Rot(PB)
            for bs_ in bsets:
                for b in (bs_["bdq"], bs_["bdt"]):
                    OP("pool", lambda e, b=b: e.memset(b.t, 0.0), [], [b])
            f3 = lambda b, x: b.t.rearrange("p (c x) -> p c x", c=4)
            b3 = lambda b, n: b.t[:, 0:n // 2].bitcast(BF16).rearrange("p (c x) -> p c x", c=4)
            PH = [slice(0, 64), slice(64, 128)]
            ev_rot = Rot(["dve", "act", "pool", "dve", "act"])

            def to_bd(srcv, srcb, bd):
                bdv = f3(bd, 0)
                for e2 in range(2):
                    eng = ev_rot.next()
                    if eng == "act":
                        OP("act", lambda e, e2=e2: e.copy(out=bdv[PH[e2], :, e2 * 64:(e2 + 1) * 64], in_=srcv[PH[e2], :, :]), [srcb], [bd])
                    else:
                        OP(eng, lambda e, e2=e2: e.tensor_copy(out=bdv[PH[e2], :, e2 * 64:(e2 + 1) * 64], in_=srcv[PH[e2], :, :]), [srcb], [bd])

            units = []
            for d in range(2):
                for hh in range(2):
                    units.append(dict(smp=True, ch0=d * 16 + hh * 8, d=d, h0=hh * 8, nch=16))
            for d in range(2):
                for sq in range(4):
                    for hh in range(2):
                        units.append(dict(smp=False, ch0=(d * 4 + sq) * 16 + hh * 8, d=d, sq=sq, h0=hh * 8, nch=4))
            for ui, u in enumerate(units):
                Tb, Tbb = Tst[ui], Tbs[ui]
                if not u["smp"]:
                    OP("pool", lambda e, Tb=Tb: e.memset(Tb.t, 0.0), [], [Tb])
                else:
                    st_ = bsets[ui % NW]["qz"][0]
                    sv = st_.t[:, 0:256].rearrange("p (c x) -> p c x", c=4)
                    for e2 in range(2):
                        h0 = u["h0"] + 4 * e2
                        S.dma(sv[PH[e2], :, :], I["state_rwkv"][u["d"], h0:h0 + 4].rearrange("h v k -> v h k"), writes=[st_])
                    ps = pP_rot.next()
                    for e2 in range(2):
                        for p in range(4):
                            OP("pe", lambda e, e2=e2, p=p, ps=ps, sv=sv: e.matmul(out=ps[PH[e2], p * 64:(p + 1) * 64], lhsT=sv[PH[e2], p, :],
                                                                               rhs=ident[PH[e2], e2 * 64:(e2 + 1) * 64], start=True, stop=True),
                               [st_, ident], [ps])
                    OP("act", lambda e, Tb=Tb, ps=ps: e.copy(out=Tb.t, in_=ps[:, 0:256]), [ps], [Tb])
                OP("dve", lambda e, Tb=Tb, Tbb=Tbb: e.tensor_copy(out=Tbb.t[:, 0:128].bitcast(BF16), in_=Tb.t), [Tb], [Tbb])

            def unit_chunk(ui, u, n, bs):
                smp, ch0 = u["smp"], u["ch0"]
                TMD, FMD, GMD, TMB_, FMB_, GMB_, YD, YDB_ = ((TMS, FMS, GMS, TMSB, FMSB, GMSB, YSD, YSDB) if smp else
                                                             (TMP, FMP, GMP, TMPB, FMPB, GMPB, YPD, YPDB))
                tm, fm, gm = bs["tm"], bs["fm"], bs["gm"]
                tmv = tm.t.bitcast(BF16).rearrange("p (c v j) -> p c v j", c=4, v=3)
                fmv = fm.t.bitcast(BF16).rearrange("p (c v s) -> p c v s", c=4, v=4)
                for e2 in range(2):
                    c0 = ch0 + 4 * e2
                    S.dma(tm.t.bitcast(BF16).rearrange("p (c x) -> p c x", c=4)[PH[e2], :, :],
                          TMD[c0:c0 + 4, n * 64:(n + 1) * 64, :, :].rearrange("c s v j -> s c (v j)"), reads=[TMB_], writes=[tm])
                    S.dma(fm.t.bitcast(BF16).rearrange("p (cv s) -> p cv s", s=64)[PH[e2], :, :],
                          FMD[c0:c0 + 4, :, :, n * 64:(n + 1) * 64].rearrange("c v k s -> k (c v) s"), reads=[FMB_], writes=[fm])
                    S.dma(gm.t[PH[e2], :], GMD[c0:c0 + 4, n, :].rearrange("c k -> k c"), reads=[GMB_], writes=[gm], allow_slow_non_contiguous=True)
                Tb, Tbb = Tst[ui], Tbs[ui]
                Tv = f3(Tb, 0)
                Tbv = b3(Tbb, 256)
                ka, apb = bs["ka"], bs["apb"]
                kav = b3(ka, 512)
                apbv = b3(apb, 256)
                qzi, qti = 0, 0
                qz = bs["qz"][0]
                qzv = f3(qz, 0)
                qt = bs["qt"][0]
                qtv = f3(qt, 0)
                psB, psK, psL = pB_rot.next(), pB_rot.next(), pB_rot.next()
                for e2 in range(2):
                    for p in range(4):
                        OP("pe", lambda e, e2=e2, p=p: e.matmul(out=psB[PH[e2], p * 128:(p + 1) * 128], lhsT=fmv[PH[e2], p, 2, :],
                                                               rhs=fmv[PH[e2], p, 0:2, :], start=True, stop=True), [fm], [psB])
                        OP("pe", lambda e, e2=e2, p=p: e.matmul(out=psK[PH[e2], p * 128:(p + 1) * 128], lhsT=fmv[PH[e2], p, 3, :],
                                                               rhs=fmv[PH[e2], p, 0:2, :], start=True, stop=True), [fm], [psK])
                        OP("pe", lambda e, e2=e2, p=p: e.matmul(out=psL[PH[e2], p * 64:(p + 1) * 64], lhsT=fmv[PH[e2], p, 0, :],
                                                               rhs=fmv[PH[e2], p, 2, :], start=True, stop=True), [fm], [psL])
                psBv = psB[:, :].rearrange("p (c x) -> p c x", c=4)
                OP("dve", lambda e, qzv=qzv: e.tensor_tensor(out=qzv[:, :, 64:128], in0=psBv[:, :, 0:64], in1=cmask2[:, 0, 0:64].unsqueeze(1).to_broadcast([128, 4, 64]),
                                                    op=ALU.mult), [psB, cmask2], [qz])
                OP("dve", lambda e: e.tensor_tensor(out=apbv, in0=psBv[:, :, 64:128], in1=cmask2[:, 0, 64:128].unsqueeze(1).to_broadcast([128, 4, 64]),
                                                    op=ALU.mult), [psB, cmask2], [apb])
                OP("dve", lambda e: e.tensor_tensor(out=kav, in0=psK[:, :].rearrange("p (c x) -> p c x", c=4),
                                                    in1=cmask2[:, 1, :].unsqueeze(1).to_broadcast([128, 4, 128]), op=ALU.mult), [psK, cmask2], [ka])
                OP("dve", lambda e, qtv=qtv: e.tensor_tensor(out=qtv, in0=psL[:, 0:256].rearrange("p (c x) -> p c x", c=4),
                                                    in1=cmask2[:, 2, 0:64].unsqueeze(1).to_broadcast([128, 4, 64]), op=ALU.mult), [psL, cmask2], [qt])
                OP("pool", lambda e, qzv=qzv: e.tensor_tensor(out=qzv[:, :, 0:64], in0=qzv[:, :, 64:128],
                                                     in1=ident[:, :].rearrange("p (a b) -> p a b", a=2)[:, 0, :].unsqueeze(1).to_broadcast([128, 4, 64])
                                                     if False else identst[:, :].unsqueeze(1).to_broadcast([128, 4, 64]), op=ALU.add), [qz, identst], [qz])
                yield
                bdq, bdt = bs["bdq"], bs["bdt"]
                to_bd(qzv[:, :, 64:128], qz, bdq)
                to_bd(qtv, qt, bdt)
                ps1, ps2 = pB_rot.next(), pB_rot.next()
                for p in range(4):
                    OP("pe", lambda e, p=p, bdt=bdt, qzv=qzv: e.matmul(out=ps1[:, p * 64:(p + 1) * 64], lhsT=f3(bdt, 0)[:, p, :], rhs=qzv[:, p, 64:128], start=True, stop=True),
                       [bdt, qz], [ps1])
                    OP("pe", lambda e, p=p, bdq=bdq, qtv=qtv: e.matmul(out=ps2[:, p * 64:(p + 1) * 64], lhsT=f3(bdq, 0)[:, p, :], rhs=qtv[:, p, :], start=True, stop=True),
                       [bdq, qt], [ps2])
                OP("act", lambda e, qzv=qzv: e.copy(out=qzv[:, :, 64:128], in_=ps1[:, 0:256].rearrange("p (c x) -> p c x", c=4)), [ps1], [qz])
                qt = bs["qt"][1]
                qti = 1
                qtv = f3(qt, 0)
                OP("dve", lambda e, qtv=qtv: e.tensor_copy(out=qtv, in_=ps2[:, 0:256].rearrange("p (c x) -> p c x", c=4)), [ps2], [qt])
                yield
                for lvl in range(1, 6):
                    to_bd(qtv, qt, bdt)
                    if lvl < 5:
                        to_bd(qzv[:, :, 64:128], qz, bdq)
                    N = 128 if lvl < 5 else 64
                    psA = pB_rot.next()
                    for p in range(4):
                        OP("pe", lambda e, p=p, bdt=bdt, qzv=qzv, N=N, psA=psA: e.matmul(out=psA[:, p * 128:p * 128 + N], lhsT=f3(bdt, 0)[:, p, :],
                                                                                      rhs=qzv[:, p, 0:N], start=True, stop=True), [bdt, qz], [psA])
                    if lvl < 5:
                        psC = pB_rot.next()
                        for p in range(4):
                            OP("pe", lambda e, p=p, bdq=bdq, qtv=qtv, psC=psC: e.matmul(out=psC[:, p * 64:(p + 1) * 64], lhsT=f3(bdq, 0)[:, p, :],
                                                                                      rhs=qtv[:, p, :], start=True, stop=True), [bdq, qt], [psC])
                    psAv = psA[:, :].rearrange("p (c x) -> p c x", c=4)
                    if lvl < 5:
                        qzi ^= 1
                        qz_new = bs["qz"][qzi]
                        qznv = f3(qz_new, 0)
                        OP("dve", lambda e, qznv=qznv, qzv=qzv, psAv=psAv: e.tensor_tensor(out=qznv[:, :, 0:64], in0=psAv[:, :, 0:64], in1=qzv[:, :, 0:64],
                                                                                         op=ALU.add), [psA, qz], [qz_new])
                        OP("act", lambda e, qznv=qznv, psAv=psAv: e.copy(out=qznv[:, :, 64:128], in_=psAv[:, :, 64:128]), [psA], [qz_new])
                        qti ^= 1
                        qt_new = bs["qt"][qti]
                        qtnv = f3(qt_new, 0)
                        OP("dve", lambda e, qtnv=qtnv, psC=psC: e.tensor_copy(out=qtnv, in_=psC[:, 0:256].rearrange("p (c x) -> p c x", c=4)), [psC], [qt_new])
                        qz, qzv, qt, qtv = qz_new, qznv, qt_new, qtnv
                        yield
                    else:
                        zb = bs["zb"]
                        zbv = b3(zb, 256)
                        OP("dve", lambda e, zbv=zbv, qzv=qzv, psAv=psAv: e.tensor_tensor(out=zbv, in0=psAv[:, :, 0:64], in1=qzv[:, :, 0:64], op=ALU.add),
                           [psA, qz], [zb])
                ps = pB_rot.next()
                for e2 in range(2):
                    for p in range(4):
                        OP("pe", lambda e, e2=e2, p=p, ps=ps: e.matmul(out=ps[PH[e2], p * 64:(p + 1) * 64], lhsT=fmv[PH[e2], p, 0, :], rhs=Tbv[PH[e2], p, :],
                                                                      start=True, stop=False), [fm, Tbb], [ps])
                        OP("pe", lambda e, e2=e2, p=p, ps=ps: e.matmul(out=ps[PH[e2], p * 64:(p + 1) * 64], lhsT=kav[PH[e2], p, 0:64], rhs=tmv[PH[e2], p, 2, :],
                                                                      start=False, stop=True), [ka, tm], [ps])
                ub = bs["u"]
                ubv = b3(ub, 256)
                OP("act", lambda e, ps=ps: e.copy(out=ubv, in_=ps[:, 0:256].rearrange("p (c x) -> p c x", c=4)), [ps], [ub])
                yield
                ps = pB_rot.next()
                for e2 in range(2):
                    for p in range(4):
                        OP("pe", lambda e, e2=e2, p=p, ps=ps: e.matmul(out=ps[PH[e2], p * 64:(p + 1) * 64], lhsT=zbv[PH[e2], p, :], rhs=ubv[PH[e2], p, :],
                                                                      start=True, stop=True), [zb, ub], [ps])
                pb_ = bs["p"]
                pbv = b3(pb_, 256)
                OP("dve", lambda e, ps=ps: e.tensor_scalar(out=pbv, in0=ps[:, 0:256].rearrange("p (c x) -> p c x", c=4), scalar1=-1.0, scalar2=None, op0=ALU.mult),
                   [ps], [pb_])
                yield
                ps = pB_rot.next()
                for e2 in range(2):
                    for p in range(4):
                        OP("pe", lambda e, e2=e2, p=p, ps=ps: e.matmul(out=ps[PH[e2], p * 64:(p + 1) * 64], lhsT=fmv[PH[e2], p, 1, :], rhs=Tbv[PH[e2], p, :],
                                                                      start=True, stop=False), [fm, Tbb], [ps])
                        OP("pe", lambda e, e2=e2, p=p, ps=ps: e.matmul(out=ps[PH[e2], p * 64:(p + 1) * 64], lhsT=apbv[PH[e2], p, :], rhs=pbv[PH[e2], p, :],
                                                                      start=False, stop=False), [apb, pb_], [ps])
                        OP("pe", lambda e, e2=e2, p=p, ps=ps: e.matmul(out=ps[PH[e2], p * 64:(p + 1) * 64], lhsT=kav[PH[e2], p, 64:128], rhs=tmv[PH[e2], p, 2, :],
                                                                      start=False, stop=True), [ka, tm], [ps])
                yb = bs["y"]
                ybv = f3(yb, 0)
                OP("act", lambda e, ps=ps: e.copy(out=ybv, in_=ps[:, 0:256].rearrange("p (c x) -> p c x", c=4)), [ps], [yb])
                for e2 in range(2):
                    c0 = ch0 + 4 * e2
                    S.dma(YD[c0:c0 + 4, n * 64:(n + 1) * 64, :].rearrange("c s x -> s c x"), ybv[PH[e2], :, :], reads=[yb], writes=[YDB_], owner=yb, queue="act")
                ps = pB_rot.next()
                for e2 in range(2):
                    for p in range(4):
                        OP("pe", lambda e, e2=e2, p=p, ps=ps: e.matmul(out=ps[PH[e2], p * 64:(p + 1) * 64], lhsT=tmv[PH[e2], p, 0, :], rhs=pbv[PH[e2], p, :],
                                                                      start=True, stop=False), [tm, pb_], [ps])
                        OP("pe", lambda e, e2=e2, p=p, ps=ps: e.matmul(out=ps[PH[e2], p * 64:(p + 1) * 64], lhsT=tmv[PH[e2], p, 1, :], rhs=tmv[PH[e2], p, 2, :],
                                                                      start=False, stop=True), [tm], [ps])
                OP("dve", lambda e, ps=ps: e.tensor_tensor(out=Tv, in0=ps[:, 0:256].rearrange("p (c x) -> p c x", c=4), in1=Tv, op=ALU.add), [ps, Tb], [Tb])
                OP("pool", lambda e: e.tensor_tensor(out=Tv, in0=Tv, in1=gm.t[:, 0:4].unsqueeze(2).to_broadcast([128, 4, 64]), op=ALU.mult), [Tb, gm], [Tb])
                OP("act", lambda e: e.copy(out=Tbv, in_=Tv), [Tb], [Tbb])

            tasks = [(ui, u, n) for n in range(16) for ui, u in enumerate(units) if n < u["nch"]]
            slots = [None] * NW
            ti_ = 0
            while ti_ < len(tasks) or any(sl_ is not None for sl_ in slots):
                for w_ in range(NW):
                    if slots[w_] is None and ti_ < len(tasks):
                        ui, u, n = tasks[ti_]
                        if any(sl_ is not None and sl_[1] == ui for sl_ in slots):
                            continue
                        slots[w_] = (unit_chunk(ui, u, n, bsets[w_]), ui)
                        ti_ += 1
                for w_ in range(NW):
                    if slots[w_] is not None:
                        try:
                            next(slots[w_][0])
                        except StopIteration:
                            slots[w_] = None
            for ui, u in enumerate(units):
                if u["smp"]:
                    continue
                Tb = Tst[ui]
                Tv = f3(Tb, 0)
                ps = pP_rot.next()
                for e2 in range(2):
                    for p in range(4):
                        OP("pe", lambda e, e2=e2, p=p, ps=ps, Tv=Tv: e.matmul(out=ps[PH[e2], p * 64:(p + 1) * 64], lhsT=Tv[PH[e2], p, :],
                                                                           rhs=ident[PH[e2], e2 * 64:(e2 + 1) * 64], start=True, stop=True), [Tb, ident], [ps])
                yb = bsets[ui % NW]["y"]
                ybv = f3(yb, 0)
                OP("act", lambda e, ps=ps, ybv=ybv: e.copy(out=ybv, in_=ps[:, 0:256].rearrange("p (c x) -> p c x", c=4)), [ps], [yb])
                for e2 in range(2):
                    h0 = u["h0"] + 4 * e2
                    S.dma(O["st_rwkv"][u["sq"], u["d"], h0:h0 + 4].rearrange("h v k -> v h k"), ybv[PH[e2], :, :], reads=[yb], owner=yb, queue="act")
            S.barrier()
            CK("rwkv_b")

            gnt = bsub(0, 1024, "gnt")
            S.dma(gnt.t, I["rwkv_gn"].partition_broadcast(128), writes=[gnt])
            cs_ = [bsub(1024 + i2 * 1024, 1024, f"cs{i2}") for i2 in range(6)]
            YFb, YBb, Vc, Gc, C1, C2 = cs_
            ogb = bsub(8192, 4096, "ogT")

            def c_group(g):
                tiles, c, smp = g["tiles"], g["c"], g["sample"]
                nt = len(tiles)
                T = nt * 128
                ogT = bview(ogb, 0, [8, T])
                for ti, t in enumerate(tiles):
                    if smp:
                        tk0 = (t - 8) * 128
                        L = 1024
                        S.dma(h16(YFb), YSD[0:16, tk0:tk0 + 128, :].rearrange("h t x -> t h x"), reads=[YSDB], writes=[YFb])
                        S.dma(h16(C1), YSD[16:32, L - 128 - tk0:L - tk0, :].rearrange("h t x -> t h x"), reads=[YSDB], writes=[C1])
                    else:
                        sq = t // 2
                        tk0 = (t % 2) * 128
                        L = 256
                        S.dma(h16(YFb), YPD[sq * 16:(sq + 1) * 16, tk0:tk0 + 128, :].rearrange("h t x -> t h x"), reads=[YPDB], writes=[YFb])
                        S.dma(h16(C1), YPD[(4 + sq) * 16:(5 + sq) * 16, L - 128 - tk0:L - tk0, :].rearrange("h t x -> t h x"),
                              reads=[YPDB], writes=[C1])
                    for hf in range(2):
                        ps = pP_rot.next()
                        OP("pe", lambda e, hf=hf, ps=ps: e.matmul(out=ps[:, :], lhsT=Jm[:], rhs=C1.t[:, hf * 512:(hf + 1) * 512], start=True, stop=True),
                           [Jm, C1], [ps])
                        OP("dve", lambda e, hf=hf, ps=ps: e.tensor_tensor(out=YBb.t[:, hf * 512:(hf + 1) * 512], in0=ps[:, :],
                                                                          in1=YFb.t[:, hf * 512:(hf + 1) * 512], op=ALU.add), [ps, YFb], [YBb])
                    S.dma(Vc.t, RKVG[2][t * 128:(t + 1) * 128, :], reads=[RKVGB[2][t]], writes=[Vc])
                    S.dma(Gc.t, RKVG[3][t * 128:(t + 1) * 128, :], reads=[RKVGB[3][t]], writes=[Gc])
                    nb = tmpB.next()
                    OP("dve", lambda e, nb=nb: e.tensor_reduce(out=nb[:, 0:16], in_=h16(YBb), axis=AX.X, op=ALU.add), [YBb], [nb])
                    OP("dve", lambda e, nb=nb: e.tensor_scalar(out=nb[:, 0:16], in0=nb[:, 0:16], scalar1=-1.0 / 64, scalar2=None, op0=ALU.mult), [nb], [nb])
                    OP("dve", lambda e, nb=nb: e.tensor_tensor(out=h16(YBb), in0=h16(YBb), in1=nb[:, 0:16].unsqueeze(2).to_broadcast([128, 16, 64]),
                                                               op=ALU.add), [YBb, nb], [YBb])
                    OP("pool", lambda e: e.tensor_tensor(out=C2.t, in0=YBb.t, in1=YBb.t, op=ALU.mult), [YBb], [C2])
                    OP("dve", lambda e, nb=nb: e.tensor_reduce(out=nb[:, 16:32], in_=h16(C2), axis=AX.X, op=ALU.add), [C2], [nb])
                    OP("act", lambda e, nb=nb: e.activation(out=nb[:, 32:48], in_=nb[:, 16:32], func=AF.Sqrt, scale=1.0 / 64, bias=GN_EPS), [nb], [nb])
                    OP("dve", lambda e, nb=nb: e.reciprocal(out=nb[:, 48:64], in_=nb[:, 32:48]), [nb], [nb])
                    OP("dve", lambda e, nb=nb: e.tensor_tensor(out=h16(YBb), in0=h16(YBb), in1=nb[:, 48:64].unsqueeze(2).to_broadcast([128, 16, 64]),
                                                               op=ALU.mult), [YBb, nb], [YBb])
                    OP("dve", lambda e: e.tensor_tensor(out=YBb.t, in0=YBb.t, in1=gnt.t, op=ALU.mult), [YBb, gnt], [YBb])
                    OP("dve", lambda e, t=t: e.tensor_tensor(out=h16(C2), in0=h16(Vc), in1=bsum[:, t, :].unsqueeze(2).to_broadcast([128, 16, 64]),
                                                             op=ALU.mult), [Vc, bsum], [C2])
                    OP("dve", lambda e: e.tensor_tensor(out=YBb.t, in0=YBb.t, in1=C2.t, op=ALU.add), [YBb, C2], [YBb])
                    OP("act", lambda e: e.activation(out=Gc.t, in_=Gc.t, func=AF.Silu), [Gc], [Gc])
                    OP("dve", lambda e: e.tensor_tensor(out=YBb.t, in0=YBb.t, in1=Gc.t, op=ALU.mult), [YBb, Gc], [YBb])
                    transpose_blocks(lambda i: YBb.t[:, i * 128:(i + 1) * 128], YBb, 8,
                                     lambda i0, n, ti=ti: (ogT[:, i0:i0 + n, ti * 128:(ti + 1) * 128], ogb))
                ybufs = [bsub(12288, nt * 1024, "yacc")]
                tail(layer, tiles, c, ogT, ogb, 8, Wout, ybufs)

            for g in groups:
                c_group(g)
            S.barrier()

        def final_norm():
            S.dma(gnw[:, 0:1024], I["final_norm_w"].partition_broadcast(128), writes=[gnw])
            for t in range(16):
                xt = xt_rot.next()
                S.dma(xt[:], XB[t].t, reads=[XB[t]], writes=[xt])
                s = st1.next()
                xn = xn_rot.next()
                OP("act", lambda e, xt=xt, xn=xn, s=s: e.activation(out=xn[:], in_=xt[:], func=AF.Square,
                                                                    accum_out=s[:, 0:1]), [xt], [xn, s])
                OP("act", lambda e, s=s: e.activation(out=s[:, 1:2], in_=s[:, 0:1], func=AF.Sqrt, scale=1.0 / 1024,
                                                      bias=EPS), [s], [s])
                OP("dve", lambda e, s=s: e.reciprocal(out=s[:, 2:3], in_=s[:, 1:2]), [s], [s])
                OP("dve", lambda e, xt=xt, xn=xn, s=s: e.scalar_tensor_tensor(out=xn[:], in0=xt[:], scalar=s[:, 2:3],
                                                                              in1=gnw[:, 0:1024], op0=ALU.mult, op1=ALU.mult),
                   [xt, s, gnw], [xn])
                dst = O["yp"] if t < 8 else O["ys"]
                S.dma(dst[(t % 8) * 128:(t % 8 + 1) * 128, :], xn[:], reads=[xn], queue="act")

        LAYERS = {0: layer_ret, 1: layer_rwkv, 2: layer_diff, 3: layer_na}
        try:
            CK("setup")
            for layer in layers:
                mod(layer)
                CK("mod")
                LAYERS[layer](layer)
            if final:
                final_norm()
        except _Stop:
            pass
        S.emit()
        print(f"[build] ops={S.n_ops} waits={S.n_waits} dma_sems={S.ndsem}")
    return nc


def _prep_inputs(inp):
    cst = _consts()
    f = lambda a: np.ascontiguousarray(np.asarray(a, dtype=np.float32))
    shared = {
        "norm_w": f(inp["norm_w"]), "w_mod": f(inp["w_mod"]), "b_mod": f(inp["b_mod"]),
        "final_norm_w": f(inp["final_norm_w"]),
        "ret_w_in": f(inp["ret_w_in"][0]), "ret_decay": f(inp["ret_decay"][0]).reshape(8),
        "ret_gn": f(inp["ret_gn"][0]), "ret_w_out": f(inp["ret_w_out"][0]),
        "rwkv_mu": f(inp["rwkv_mu"][0]), "rwkv_w_in": f(inp["rwkv_w_in"][0]), "rwkv_w0": f(inp["rwkv_w0"][0]),
        "rwkv_wA": f(inp["rwkv_wA"][0]), "rwkv_wB": f(inp["rwkv_wB"][0]), "rwkv_a0": f(inp["rwkv_a0"][0]),
        "rwkv_aA": f(inp["rwkv_aA"][0]), "rwkv_aB": f(inp["rwkv_aB"][0]), "rwkv_kk": f(inp["rwkv_kk"][0]),
        "rwkv_ka": f(inp["rwkv_ka"][0]), "rwkv_rk": f(inp["rwkv_rk"][0]).reshape(1024),
        "rwkv_gn": f(inp["rwkv_gn"][0]), "rwkv_w_out": f(inp["rwkv_w_out"][0]),
        "diff_w_in": f(inp["diff_w_in"][0]), "diff_lambda": f(inp["diff_lambda"][0]).reshape(256),
        "diff_gn": f(inp["diff_gn"][0]), "diff_w_out": f(inp["diff_w_out"][0]),
        "na_w_in": f(inp["na_w_in"][0]), "na_bias_x": _na_bias_expand(f(inp["na_bias"][0])),
        "na_w_out": f(inp["na_w_out"][0]),
    }
    shared.update(cst)
    maps = []
    for c in range(8):
        b = c // 4
        m = dict(shared)
        m["xp"] = f(inp["x_prompt"][4 * c:4 * c + 4]).reshape(1024, 1024)
        m["xs"] = f(inp["x_sample"][b])
        m["cond"] = np.ascontiguousarray(np.stack([f(inp["c_ctx"]), f(inp["c"][b])], 0))
        m["state_ret"] = f(inp["state_ret"][b, 0])
        m["state_rwkv"] = f(inp["state_rwkv"][b, 0])
        m["cache_diff_k"] = f(inp["cache_diff_k"][b, 0])
        m["cache_diff_v"] = f(inp["cache_diff_v"][b, 0])
        m["cache_na_k"] = f(inp["cache_na_k"][b, 0])
        m["cache_na_v"] = f(inp["cache_na_v"][b, 0])
        maps.append(m)
    return maps


_NC_CACHE = {}


def kernel(**inputs):
    maps = _prep_inputs(inputs)
    if "nc" not in _NC_CACHE:
        _NC_CACHE["nc"] = build()
    res = run_bass_kernel_spmd(_NC_CACHE["nc"], maps, core_ids=list(range(8))).results
    y_prompt = np.concatenate([r["yp"].reshape(4, 256, 1024) for r in res], 0)
    y_sample = np.stack([res[0]["ys"], res[4]["ys"]], 0)
    st_ret = np.concatenate([r["st_ret"] for r in res], 0)[:, None]
    st_rwkv = np.concatenate([r["st_rwkv"] for r in res], 0)[:, None]
    dk = np.concatenate([r["dk"] for r in res], 0)[:, None]
    dv = np.concatenate([r["dv"] for r in res], 0)[:, None]
    nk = np.concatenate([r["nk"] for r in res], 0)[:, None]
    nv = np.concatenate([r["nv"] for r in res], 0)[:, None]
    return (y_prompt, y_sample, st_ret, st_rwkv, dk, dv, nk, nv)
```

```python
import math
from contextlib import ExitStack

import numpy as np
import concourse.bass as bass
import concourse.mybir as mybir
from concourse.bass_utils import run_bass_kernel_spmd

F32 = mybir.dt.float32
BF16 = mybir.dt.bfloat16
AF = mybir.ActivationFunctionType
ALU = mybir.AluOpType
AX = mybir.AxisListType

EPS = 1e-6
GN_EPS = 1e-5
NEG = -30000.0


class Buf:
    __slots__ = ("t", "name", "lw", "rd", "dsem", "dcnt", "excl")

    def __init__(self, t, name, excl=False):
        self.excl = excl
        self.t = t
        self.name = name
        self.lw = None
        self.rd = {}
        self.dsem = None
        self.dcnt = 0

    def __getitem__(self, k):
        return self.t[k]


class Sched:
    CE = ("pe", "act", "dve", "pool")
    ALLQ = ("pe", "act", "dve", "pool", "sp")

    def __init__(self, nc, stack):
        self.nc = nc
        self.stack = stack
        self.sems = {}
        self.ecnt = {}
        for e in self.CE:
            self.sems[e] = stack.enter_context(nc.semaphore("es_" + e))
            self.ecnt[e] = 0
        self.q = {e: [] for e in self.ALLQ}
        self.seen = {e: {} for e in self.ALLQ}
        self.nbuf = 0
        self.ndsem = 0
        self.n_ops = 0
        self.n_waits = 0
        self.dma_bufs = {}
        self.CONST = Buf(None, "const")
        self.const_bufs = []

    def sb(self, shape, dtype=F32, name="b"):
        self.nbuf += 1
        t = self.stack.enter_context(self.nc.sbuf_tensor(f"{name}_{self.nbuf}", list(shape), dtype))
        return Buf(t, name)

    def _waits(self, eng, reads, writes):
        ev = {}
        for b in reads:
            if b.lw is not None:
                k, v = b.lw
                if ev.get(k, 0) < v:
                    ev[k] = v
            if b.excl:
                for k, v in b.rd.items():
                    if k != eng and ev.get(k, 0) < v:
                        ev[k] = v
        for b in writes:
            if b.lw is not None:
                k, v = b.lw
                if ev.get(k, 0) < v:
                    ev[k] = v
            for k, v in b.rd.items():
                if ev.get(k, 0) < v:
                    ev[k] = v
        waits = []
        seen = self.seen[eng]
        for k, v in ev.items():
            if eng == "pe" and k == "pe":
                continue
            if seen.get(k, 0) >= v:
                continue
            seen[k] = v
            waits.append((k, v))
        self.n_waits += len(waits)
        return waits

    def _mark(self, me, reads, writes):
        k, v = me
        for b in reads:
            if b.rd.get(k, 0) < v:
                b.rd[k] = v
        for b in writes:
            b.lw = me
            b.rd = {}

    def op(self, eng, fn, reads=(), writes=()):
        waits = self._waits(eng, reads, writes)
        self.ecnt[eng] += 1
        me = (eng, self.ecnt[eng])
        self.q[eng].append((waits, fn, eng, 1))
        self._mark(me, reads, writes)
        self.n_ops += 1

    def dma(self, out_ap, in_ap, reads=(), writes=(), owner=None, queue="sp", **kw):
        if owner is self.CONST:
            waits = []
        else:
            waits = self._waits(queue, reads, writes)
        if owner is None:
            owner = writes[0] if writes else reads[0]
        if owner.dsem is None:
            self.ndsem += 1
            key = f"d{self.ndsem}"
            self.sems[key] = self.stack.enter_context(self.nc.semaphore("ds_" + key))
            owner.dsem = key
            self.dma_bufs[key] = owner
        owner.dcnt += 16
        me = (owner.dsem, owner.dcnt)
        self.q[queue].append((waits, (lambda e: e.dma_start(out=out_ap, in_=in_ap, **kw)), owner.dsem, 16))
        if owner is self.CONST:
            self.const_bufs.extend(writes)
        else:
            self._mark(me, reads, writes)
        self.n_ops += 1

    def consts_done(self):
        for b in self.const_bufs:
            b.lw = (self.CONST.dsem, self.CONST.dcnt)
        self.const_bufs = []

    def barrier(self):
        tot = [(e, self.ecnt[e]) for e in self.CE] + [(k, b.dcnt) for k, b in self.dma_bufs.items()]
        for eng in self.ALLQ:
            waits = []
            for k, v in tot:
                if k == eng or v == 0:
                    continue
                if self.seen[eng].get(k, 0) >= v:
                    continue
                self.seen[eng][k] = v
                waits.append((k, v))
            if waits:
                self.q[eng].append((waits, None, None, 0))

    def emit(self):
        nc = self.nc
        self.barrier()
        with nc.Block() as block:
            def run(engobj, name):
                import os as _os
                attach = _os.environ.get("KATTACH", "1") == "1"
                for waits, fn, semkey, inc in self.q[name]:
                    if fn is None or not attach or not waits or (name == "pe" and _os.environ.get("KATTACH_PE", "0") != "1"):
                        for k, v in waits:
                            engobj.wait_ge(self.sems[k], v)
                        if fn is not None:
                            fn(engobj).then_inc(self.sems[semkey], inc)
                    else:
                        for k, v in waits[:-1]:
                            engobj.wait_ge(self.sems[k], v)
                        k, v = waits[-1]
                        fn(engobj)._wait_ge(self.sems[k], v).then_inc(self.sems[semkey], inc)

            @block.tensor
            def _(e):
                run(e, "pe")

            @block.scalar
            def _(e):
                run(e, "act")

            @block.vector
            def _(e):
                run(e, "dve")

            @block.gpsimd
            def _(e):
                run(e, "pool")

            @block.sync
            def _(e):
                run(e, "sp")


class Rot:
    def __init__(self, bufs):
        self.bufs = bufs
        self.i = 0

    def next(self):
        b = self.bufs[self.i % len(self.bufs)]
        self.i += 1
        return b


def _rope_tables(d):
    t = np.arange(1024)
    row = (t // 64).astype(np.float32)
    col = (t % 64).astype(np.float32)
    inv = (np.float32(10000.0) ** (-np.arange(0, d, 2, dtype=np.float32) / np.float32(d))).astype(np.float32)
    ang = np.stack([row[:, None] * inv[None, :], col[:, None] * inv[None, :]], axis=1).astype(np.float32)
    return np.cos(ang).astype(np.float32), np.sin(ang).astype(np.float32)


def _consts():
    c = {}
    c0, s0 = _rope_tables(128)
    c2, s2 = _rope_tables(32)
    c["rope0"] = np.ascontiguousarray(np.stack([c0, s0], 0))
    c["rope2"] = np.ascontiguousarray(np.stack([c2, s2], 0))
    kk = np.arange(128)[:, None]
    cols = np.arange(15 * 128)[None, :]
    m = cols // 128 - 7
    qq = cols % 128
    gap = (128 * m + qq - kk).astype(np.float32)
    c["gpn"] = np.ascontiguousarray(np.stack([np.maximum(gap, 0), np.maximum(-gap, 0)], 0))
    p = np.arange(128, dtype=np.float32)
    c["stexp"] = np.ascontiguousarray(np.stack([255.0 - p, 127.0 - p, p, 128.0 + p], 1))
    a_ = np.arange(128)
    c["tri"] = np.ascontiguousarray(((a_[:, None] <= a_[None, :]) & (a_[:, None] // 64 == a_[None, :] // 64)).astype(np.float32))
    j_ = np.arange(64)[:, None]
    t_ = np.arange(64)[None, :]
    su = (j_ < t_).astype(np.float32)
    iu = (j_ <= t_).astype(np.float32)
    sl = (t_ < j_).astype(np.float32)
    cm = np.zeros((64, 3, 128), np.float32)
    cm[:, 0, 0:64] = -su
    cm[:, 0, 64:128] = iu
    cm[:, 1, 0:64] = su
    cm[:, 1, 64:128] = iu
    cm[:, 2, 0:64] = -sl
    c["cmask"] = cm
    c["iota1k"] = np.ascontiguousarray(np.broadcast_to(np.arange(1024, dtype=np.float32)[None, :], (128, 1024)))
    return c


def _na_bias_expand(na_bias):
    out = np.full((16, 5, 128, 576), NEG, np.float32)
    jt = [0, 1, 2, 6, 7]
    qc = np.arange(64)[:, None]
    kc = np.arange(64)[None, :]
    cs = np.clip(qc - 8, 0, 48)
    col_ok = (kc >= cs) & (kc < cs + 16)
    cidx = np.clip(kc - qc, -15, 15) + 15
    for ti, j in enumerate(jt):
        r0 = min(max(2 * j - 4, 0), 8)
        nr = min(9, 16 - r0)
        for a in range(2):
            qr = 2 * j + a
            st = min(max(qr - 4, 0), 8)
            for i in range(nr):
                kr = r0 + i
                if st <= kr < st + 8:
                    dr = kr - qr + 7
                    blk = na_bias[:, dr][:, cidx]
                    blk = np.where(col_ok[None], blk, np.float32(NEG))
                    out[:, ti, a * 64:(a + 1) * 64, i * 64:(i + 1) * 64] = blk
    return out


NA_JT = {0: 0, 1: 1, 2: 2, 3: 2, 4: 2, 5: 2, 6: 3, 7: 4}


class _Stop(Exception):
    pass


def build(layers=(0, 1, 2, 3), final=True, stop=None):
    nc = bass.Bass("TRN2", target_bir_lowering=False)

    def CK(name):
        if stop == name:
            raise _Stop()

    def din(name, shape):
        return nc.dram_tensor(name, list(shape), F32, kind="ExternalInput").ap()

    def dout(name, shape):
        return nc.dram_tensor(name, list(shape), F32, kind="ExternalOutput").ap()

    def dscr(name, shape):
        return nc.dram_tensor(name, list(shape), F32).ap()

    I = {}
    for name, shape in [
        ("xp", (1024, 1024)), ("xs", (1024, 1024)), ("cond", (2, 1024)),
        ("state_ret", (2, 4, 256, 512)), ("state_rwkv", (2, 16, 64, 64)),
        ("cache_diff_k", (8, 256, 128)), ("cache_diff_v", (8, 256, 128)),
        ("cache_na_k", (16, 256, 64)), ("cache_na_v", (16, 256, 64)),
        ("norm_w", (4, 1024)), ("w_mod", (4, 1024, 3072)), ("b_mod", (4, 3072)), ("final_norm_w", (1024,)),
        ("ret_w_in", (1024, 6144)), ("ret_decay", (8,)), ("ret_gn", (2048,)), ("ret_w_out", (2048, 1024)),
        ("rwkv_mu", (6, 1024)), ("rwkv_w_in", (1024, 4096)), ("rwkv_w0", (2, 1024)), ("rwkv_wA", (2, 1024, 64)),
        ("rwkv_wB", (2, 64, 1024)), ("rwkv_a0", (2, 1024)), ("rwkv_aA", (2, 1024, 64)), ("rwkv_aB", (2, 64, 1024)),
        ("rwkv_kk", (1024,)), ("rwkv_ka", (1024,)), ("rwkv_rk", (1024,)), ("rwkv_gn", (1024,)),
        ("rwkv_w_out", (1024, 1024)),
        ("diff_w_in", (1024, 4096)), ("diff_lambda", (256,)), ("diff_gn", (1024,)), ("diff_w_out", (1024, 1024)),
        ("na_w_in", (1024, 4096)), ("na_bias_x", (16, 5, 128, 576)), ("na_w_out", (1024, 1024)),
        ("rope0", (2, 1024, 2, 64)), ("rope2", (2, 1024, 2, 16)), ("gpn", (2, 128, 1920)), ("stexp", (128, 4)),
        ("iota1k", (128, 1024)), ("tri", (128, 128)), ("cmask", (64, 3, 128)),
    ]:
        I[name] = din(name, shape)
    O = {}
    for name, shape in [
        ("yp", (1024, 1024)), ("ys", (1024, 1024)), ("st_ret", (4, 2, 4, 256, 512)),
        ("st_rwkv", (4, 2, 16, 64, 64)), ("dk", (4, 8, 256, 128)), ("dv", (4, 8, 256, 128)),
        ("nk", (4, 16, 256, 64)), ("nv", (4, 16, 256, 64)), ("xd", (2048, 1024)),
    ]:
        O[name] = dout(name, shape)
    XD = O["xd"]

    with ExitStack() as st:
        S = Sched(nc, st)
        OP = S.op

        ident = S.sb([128, 128], F32, "ident")
        PSt = st.enter_context(nc.psum_tensor("ps", [128, 8, 512], F32))
        PB = [Buf(PSt[:, i, :], f"ps{i}", excl=True) for i in range(8)]
        pS_rot = Rot(PB[0:4])
        pP_rot = Rot(PB[4:8])

        wf = Rot([S.sb([128, 8, 256], F32, "wf") for _ in range(2)])
        wb = Rot([S.sb([128, 8, 256], BF16, "wb") for _ in range(2)])
        xt_rot = Rot([S.sb([128, 1024], F32, "xt") for _ in range(2)])
        xn_rot = Rot([S.sb([128, 1024], F32, "xn") for _ in range(1)])
        st1 = Rot([S.sb([128, 8], F32, "st") for _ in range(6)])
        BIGN = 27648
        BIG = st.enter_context(nc.sbuf_tensor("big", [128, BIGN], F32))
        R_H = Buf(BIG[:, 0:4096], "RH")
        R_G = Buf(BIG[:, 4096:12288], "RG")
        ATTN = 15360
        ATT = BIG[:, 12288:27648]
        Jm = S.sb([128, 128], F32, "J")
        identst = S.sb([128, 64], F32, "identst")
        bsum = S.sb([128, 16, 16], F32, "bsum")
        muF = S.sb([128, 6, 8], F32, "muF")

        def bsub(off, n, name):
            assert off + n <= BIGN, (off, n)
            return Buf(BIG[:, off:off + n], name)
        scond = S.sb([128, 8, 2], F32, "scond")
        condF = S.sb([128, 8, 2], F32, "condF")
        normwF = S.sb([128, 4, 8], F32, "normwF")
        bmodF = S.sb([128, 24], F32, "bmodF")
        modF = S.sb([128, 24, 2], F32, "modF")
        scaleF = S.sb([128, 8, 2], F32, "scaleF")
        Gb = [S.sb([128, 1024], F32, "G") for _ in range(2)]
        gbt = Rot([S.sb([128, 128], F32, "gbt") for _ in range(2)])
        gnw = S.sb([128, 2048], F32, "gnw")
        rope0 = S.sb([128, 2, 8, 128], F32, "rope0")
        rope2 = S.sb([128, 2, 8, 32], F32, "rope2")
        rtmp = Rot([S.sb([128, 128], F32, "rtmp") for _ in range(2)])
        tmpA = Rot([S.sb([128, 512], F32, "tmpA") for _ in range(3)])
        tmpB = Rot([S.sb([128, 512], F32, "tmpB") for _ in range(2)])
        lamc = S.sb([128, 8], F32, "lamc")
        dlb = S.sb([128, 256], F32, "dlb")
        lgt = S.sb([128, 16], F32, "lgt")
        stx = S.sb([128, 4], F32, "stx")
        dsc = S.sb([128, 4, 4], F32, "dsc")
        strip = Rot([S.sb([128, 1920], BF16, "strip") for _ in range(2)])
        osb = Rot([S.sb([128, 512], F32, "osb") for _ in range(2)])

        def sub(off, n, name):
            assert off + n <= ATTN, (off, n)
            return Buf(ATT[:, off:off + n], name)

        def bview(buf, off_f, shape):
            n = int(np.prod(shape))
            ap = buf.t[:, off_f:off_f + n // 2].bitcast(BF16)
            if len(shape) == 1:
                return ap
            names = " ".join(f"a{i}" for i in range(len(shape)))
            kw = {f"a{i}": shape[i] for i in range(len(shape) - 1)}
            return ap.rearrange(f"p ({names}) -> p {names}", **kw)

        def fview(buf, off_f, shape):
            n = int(np.prod(shape))
            ap = buf.t[:, off_f:off_f + n]
            if len(shape) == 1:
                return ap
            names = " ".join(f"a{i}" for i in range(len(shape)))
            kw = {f"a{i}": shape[i] for i in range(len(shape) - 1)}
            return ap.rearrange(f"p ({names}) -> p {names}", **kw)

        XB = [Buf(XD[t * 128:(t + 1) * 128, :], f"xd{t}") for t in range(16)]
        first_layer = layers[0]

        def x_src(layer, t):
            if layer == first_layer:
                src = I["xp"] if t < 8 else I["xs"]
                tt = t % 8
                return src[tt * 128:(tt + 1) * 128, :], []
            return XB[t].t, [XB[t]]

        C = S.CONST
        for c2 in range(2):
            S.dma(condF[:, :, c2], I["cond"][c2].rearrange("(k p) -> p k", p=128), writes=[condF], owner=C,
                  allow_slow_non_contiguous=True)
        for l2 in range(4):
            S.dma(normwF[:, l2, :], I["norm_w"][l2].rearrange("(k p) -> p k", p=128), writes=[normwF], owner=C,
                  allow_slow_non_contiguous=True)
        for cs in range(2):
            for d, (rt, hw) in enumerate(((rope0, 64), (rope2, 16))):
                src = I["rope0" if d == 0 else "rope2"][cs].rearrange("(t p) a f -> p t (a f)", p=128)
                S.dma(rt[:, cs, :, :], src, writes=[rt], owner=C)
        S.dma(stx[:], I["stexp"], writes=[stx], owner=C)
        S.consts_done()
        OP("pool", lambda e: e.memset(ident[:], 0.0), [], [ident])
        OP("pool", lambda e: e.affine_select(out=ident[:], in_=ident[:], pattern=[[-1, 128]], compare_op=ALU.not_equal,
                                             fill=1.0, base=0, channel_multiplier=1), [ident], [ident])
        OP("act", lambda e: e.activation(out=scond[:], in_=condF[:], func=AF.Silu), [condF], [scond])
        OP("pool", lambda e: e.tensor_tensor(out=identst[:], in0=ident[:, 0:64], in1=ident[:, 64:128], op=ALU.add), [ident], [identst])
        OP("pool", lambda e: e.memset(Jm[:], 0.0), [], [Jm])
        OP("pool", lambda e: e.affine_select(out=Jm[:], in_=Jm[:], pattern=[[1, 128]], compare_op=ALU.not_equal,
                                             fill=1.0, base=-127, channel_multiplier=1), [Jm], [Jm])

        wctr = [0]

        def load_w(W, k0, c0, ncols=256, cast=True):
            f = wf.next()
            src = W[k0:k0 + 1024, c0:c0 + ncols].rearrange("(k p) n -> p k n", p=128)
            S.dma(f[:, :, 0:ncols], src, writes=[f])
            if not cast:
                return f
            b = wb.next()
            wctr[0] += 1
            if wctr[0] % 2:
                OP("act", lambda e: e.copy(out=b[:, :, 0:ncols], in_=f[:, :, 0:ncols]), [f], [b])
            else:
                OP("dve", lambda e: e.tensor_copy(out=b[:, :, 0:ncols], in_=f[:, :, 0:ncols]), [f], [b])
            return b

        def mod(layer):
            S.dma(bmodF[:], I["b_mod"][layer].rearrange("(m p) -> p m", p=128), writes=[bmodF],
                  allow_slow_non_contiguous=True)
            psm = pP_rot.next()
            for blk in range(12):
                w = load_w(I["w_mod"][layer], 0, blk * 256, cast=False)
                for mm in range(2):
                    m = blk * 2 + mm
                    for k in range(8):
                        OP("pe", lambda e, m=m, mm=mm, k=k, w=w: e.matmul(
                            out=psm[:, m * 2:m * 2 + 2], lhsT=w[:, k, mm * 128:(mm + 1) * 128], rhs=scond[:, k, :],
                            start=(k == 0), stop=(k == 7)), [w, scond], [psm])
            OP("dve", lambda e: e.tensor_tensor(out=modF[:], in0=psm[:, 0:48].rearrange("p (m c) -> p m c", c=2),
                                                in1=bmodF[:].unsqueeze(2).to_broadcast([128, 24, 2]), op=ALU.add),
               [psm, bmodF], [modF])
            OP("dve", lambda e: e.tensor_scalar(out=scaleF[:], in0=modF[:, 8:16, :], scalar1=1.0, scalar2=None,
                                                op0=ALU.add), [modF], [scaleF])
            OP("dve", lambda e: e.tensor_tensor(out=scaleF[:], in0=scaleF[:],
                                                in1=normwF[:, layer, :].unsqueeze(2).to_broadcast([128, 8, 2]),
                                                op=ALU.mult), [scaleF, normwF], [scaleF])
            for c in range(2):
                pg = [pP_rot.next(), pP_rot.next()]
                for k in range(8):
                    g = gbt.next()
                    OP("dve", lambda e, g=g, k=k, c=c: e.tensor_copy(
                        out=g[:], in_=modF[:, 16 + k, c:c + 1].to_broadcast([128, 128])), [modF], [g])
                    OP("pe", lambda e, g=g, k=k, pg=pg: e.matmul(
                        out=pg[k // 4][:, (k % 4) * 128:(k % 4 + 1) * 128], lhsT=g[:], rhs=ident[:],
                        start=True, stop=True), [g, ident], [pg[k // 4]])
                for hlf in range(2):
                    OP("act", lambda e, hlf=hlf, c=c, pg=pg: e.copy(out=Gb[c][:, hlf * 512:(hlf + 1) * 512],
                                                                    in_=pg[hlf][:, :]), [pg[hlf]], [Gb[c]])

        def front(layer, tiles, c, hT_of):
            for ti, t in enumerate(tiles):
                xt = xt_rot.next()
                src, rb = x_src(layer, t)
                S.dma(xt[:], src, reads=rb, writes=[xt])
                s = st1.next()
                xn = xn_rot.next()
                OP("act", lambda e, xt=xt, xn=xn, s=s: e.activation(out=xn[:], in_=xt[:], func=AF.Square,
                                                                    accum_out=s[:, 0:1]), [xt], [xn, s])
                OP("act", lambda e, s=s: e.activation(out=s[:, 1:2], in_=s[:, 0:1], func=AF.Sqrt, scale=1.0 / 1024,
                                                      bias=EPS), [s], [s])
                OP("dve", lambda e, s=s: e.reciprocal(out=s[:, 2:3], in_=s[:, 1:2]), [s], [s])
                OP("dve", lambda e, xt=xt, xn=xn, s=s: e.tensor_scalar(out=xn[:], in0=xt[:], scalar1=s[:, 2:3],
                                                                       scalar2=None, op0=ALU.mult), [xt, s], [xn])
                pp = [pP_rot.next(), pP_rot.next()]
                for k in range(8):
                    OP("pe", lambda e, k=k, xn=xn, pp=pp: e.transpose(
                        out=pp[k // 4][:, (k % 4) * 128:(k % 4 + 1) * 128], in_=xn[:, k * 128:(k + 1) * 128],
                        identity=ident[:]), [xn, ident], [pp[k // 4]])
                for k in range(8):
                    dst, db = hT_of(k, ti)
                    OP("act", lambda e, k=k, dst=dst, pp=pp: e.activation(
                        out=dst, in_=pp[k // 4][:, (k % 4) * 128:(k % 4 + 1) * 128], func=AF.Identity,
                        scale=scaleF[:, k, c:c + 1], bias=modF[:, k, c:c + 1]), [pp[k // 4], scaleF, modF], [db])

        def proj(hT, hbuf, ti, w, ncols, ps):
            for k in range(8):
                OP("pe", lambda e, k=k: e.matmul(out=ps[:, 0:ncols], lhsT=hT[:, k, ti * 128:(ti + 1) * 128],
                                                 rhs=w[:, k, 0:ncols], start=(k == 0), stop=(k == 7)),
                   [hbuf, w], [ps])

        def transpose_blocks(src_ap_of, srcbuf, nblk, dst_of, evac="act"):
            i = 0
            while i < nblk:
                n = min(4, nblk - i)
                ps = pP_rot.next()
                for j in range(n):
                    OP("pe", lambda e, i=i, j=j, ps=ps: e.transpose(out=ps[:, j * 128:(j + 1) * 128],
                                                                    in_=src_ap_of(i + j), identity=ident[:]),
                       [srcbuf, ident], [ps])
                dst, db = dst_of(i, n)
                if evac == "act":
                    OP("act", lambda e, dst=dst, ps=ps, n=n: e.copy(
                        out=dst, in_=ps[:, 0:n * 128].rearrange("p (a b) -> p a b", a=n)), [ps], [db])
                else:
                    OP("dve", lambda e, dst=dst, ps=ps, n=n: e.tensor_copy(
                        out=dst, in_=ps[:, 0:n * 128].rearrange("p (a b) -> p a b", a=n)), [ps], [db])
                i += n

        def rope(src, dst, table, half, tile, ngrp):
            n = ngrp * 4 * half
            sv = src[:, 0:n].rearrange("p (g a two f) -> p g a two f", g=ngrp, a=2, two=2)
            dv = dst[:, 0:n].rearrange("p (g a two f) -> p g a two f", g=ngrp, a=2, two=2)
            cos = table[:, 0, tile, :].rearrange("p (a f) -> p a f", a=2).unsqueeze(1).to_broadcast([128, ngrp, 2, half])
            sin = table[:, 1, tile, :].rearrange("p (a f) -> p a f", a=2).unsqueeze(1).to_broadcast([128, ngrp, 2, half])
            x1, x2 = sv[:, :, :, 0, :], sv[:, :, :, 1, :]
            o1, o2 = dv[:, :, :, 0, :], dv[:, :, :, 1, :]
            t1 = rtmp.next()
            t2 = rtmp.next()
            m = ngrp * 2 * half
            t1v = t1[:, 0:m].rearrange("p (g a f) -> p g a f", g=ngrp, a=2)
            t2v = t2[:, 0:m].rearrange("p (g a f) -> p g a f", g=ngrp, a=2)
            OP("dve", lambda e: e.tensor_tensor(out=o1, in0=x1, in1=cos, op=ALU.mult), [src, table], [dst])
            OP("pool", lambda e: e.tensor_tensor(out=t1v, in0=x2, in1=sin, op=ALU.mult), [src, table], [t1])
            OP("dve", lambda e: e.tensor_tensor(out=o1, in0=o1, in1=t1v, op=ALU.subtract), [dst, t1], [dst])
            OP("pool", lambda e: e.tensor_tensor(out=t2v, in0=x1, in1=sin, op=ALU.mult), [src, table], [t2])
            OP("dve", lambda e: e.tensor_tensor(out=o2, in0=x2, in1=cos, op=ALU.mult), [src, table], [dst])
            OP("dve", lambda e: e.tensor_tensor(out=o2, in0=o2, in1=t2v, op=ALU.add), [dst, t2], [dst])

        def tail(layer, tiles, c, ogT, ogbuf, KC, Wout, ybufs):
            nt = len(tiles)
            yacc = ATT[:, 0:nt * 1024].rearrange("p (t n) -> p t n", t=nt)
            for cb in range(4):
                ws = [load_w(Wout, kk * 1024, cb * 256) for kk in range(KC // 8)]
                for ti in range(nt):
                    ps = pP_rot.next()
                    for kc in range(KC):
                        w = ws[kc // 8]
                        OP("pe", lambda e, kc=kc, w=w, ti=ti, ps=ps: e.matmul(
                            out=ps[:, 0:256], lhsT=ogT[:, kc, ti * 128:(ti + 1) * 128], rhs=w[:, kc % 8, :],
                            start=(kc == 0), stop=(kc == KC - 1)), [ogbuf, w], [ps])
                    OP("dve", lambda e, ti=ti, cb=cb, ps=ps: e.tensor_tensor(
                        out=yacc[:, ti, cb * 256:(cb + 1) * 256], in0=ps[:, 0:256],
                        in1=Gb[c][:, cb * 256:(cb + 1) * 256], op=ALU.mult), [ps, Gb[c]], ybufs)
            for ti, t in enumerate(tiles):
                xt = xt_rot.next()
                src, rb = x_src(layer, t)
                S.dma(xt[:], src, reads=rb, writes=[xt])
                OP("dve", lambda e, xt=xt, ti=ti: e.tensor_tensor(out=xt[:], in0=xt[:], in1=yacc[:, ti, :], op=ALU.add),
                   [xt] + ybufs, [xt])
                S.dma(XB[t].t, xt[:], reads=[xt], writes=[XB[t]], owner=xt, queue="act")

        def softmax_un(src, srcbufs, scale, Pout, Pbuf):
            s = st1.next()
            ax = AX.XY if len(src.shape) == 3 else AX.X
            OP("dve", lambda e: e.tensor_reduce(out=s[:, 0:1], in_=src, axis=ax, op=ALU.max), srcbufs, [s])
            OP("dve", lambda e: e.tensor_scalar(out=s[:, 1:2], in0=s[:, 0:1], scalar1=-scale, scalar2=None,
                                                op0=ALU.mult), [s], [s])
            OP("act", lambda e: e.activation(out=Pout, in_=src, func=AF.Exp, scale=scale, bias=s[:, 1:2],
                                             accum_out=s[:, 2:3]), srcbufs + [s], [Pbuf, s])
            OP("dve", lambda e: e.reciprocal(out=s[:, 3:4], in_=s[:, 2:3]), [s], [s])
            return s

        def pv(pc, pcbuf, kblocks, pcT, pcTbuf, vof, N, pso):
            nb = len(kblocks)
            i = 0
            while i < nb:
                n = min(4, nb - i)
                ps = pP_rot.next()
                for j in range(n):
                    off, nk = kblocks[i + j]
                    OP("pe", lambda e, j=j, off=off, nk=nk, ps=ps: e.transpose(
                        out=ps[0:nk, j * 128:(j + 1) * 128], in_=pc[:, off:off + nk], identity=ident[:]),
                       [pcbuf, ident], [ps])
                full = all(kblocks[i + j][1] == 128 for j in range(n))
                if full:
                    OP("act", lambda e, i=i, n=n, ps=ps: e.copy(
                        out=pcT[:, i:i + n, :], in_=ps[:, 0:n * 128].rearrange("p (a b) -> p a b", a=n)),
                       [ps], [pcTbuf])
                else:
                    for j in range(n):
                        nk = kblocks[i + j][1]
                        OP("act", lambda e, i=i, j=j, nk=nk, ps=ps: e.copy(
                            out=pcT[0:nk, i + j, :], in_=ps[0:nk, j * 128:(j + 1) * 128]), [ps], [pcTbuf])
                i += n
            for i, (off, nk) in enumerate(kblocks):
                vap, vbuf = vof(i)
                OP("pe", lambda e, i=i, nk=nk, vap=vap: e.matmul(out=pso[:, 0:N], lhsT=pcT[0:nk, i, :], rhs=vap,
                                                               start=(i == 0), stop=(i == nb - 1)),
                   [pcTbuf, vbuf], [pso])

        groups = [
            dict(tiles=[0, 1, 2, 3], c=0, seqs=[(0, 2, 0), (2, 2, 1)], sample=False, pair=0),
            dict(tiles=[4, 5, 6, 7], c=0, seqs=[(0, 2, 2), (2, 2, 3)], sample=False, pair=1),
            dict(tiles=list(range(8, 16)), c=1, seqs=[(0, 8, -1)], sample=True, pair=-1),
        ]

        def layer_diff(layer):
            lam_init = 0.8 - 0.6 * math.exp(-0.3 * layer)
            Win, Wout = I["diff_w_in"], I["diff_w_out"]
            S.dma(gnw[:, 0:1024], I["diff_gn"].partition_broadcast(128), writes=[gnw])
            OP("dve", lambda e: e.tensor_scalar(out=gnw[:, 0:1024], in0=gnw[:, 0:1024], scalar1=1.0 - lam_init,
                                                scalar2=None, op0=ALU.mult), [gnw], [gnw])
            S.dma(dlb[:], I["diff_lambda"].partition_broadcast(128), writes=[dlb])
            dl = dlb[:].rearrange("p (a f) -> p a f", a=4)
            t = tmpA.next()
            for i2 in range(2):
                OP("dve", lambda e, i2=i2: e.tensor_tensor(out=t[:, i2 * 64:(i2 + 1) * 64], in0=dl[:, 2 * i2, :],
                                                           in1=dl[:, 2 * i2 + 1, :], op=ALU.mult), [dlb], [t])
            OP("dve", lambda e: e.tensor_reduce(out=lamc[:, 0:2], in_=t[:, 0:128].rearrange("p (a f) -> p a f", a=2),
                                                axis=AX.X, op=ALU.add), [t], [lamc])
            OP("act", lambda e: e.activation(out=lamc[:, 2:4], in_=lamc[:, 0:2], func=AF.Exp), [lamc], [lamc])
            OP("dve", lambda e: e.tensor_tensor(out=lamc[:, 4:5], in0=lamc[:, 2:3], in1=lamc[:, 3:4], op=ALU.subtract),
               [lamc], [lamc])
            OP("dve", lambda e: e.tensor_scalar(out=lamc[:, 5:6], in0=lamc[:, 4:5], scalar1=lam_init, scalar2=None,
                                                op0=ALU.add), [lamc], [lamc])
            scale = 64 ** -0.5

            def do_group(g):
                S.barrier()
                tiles, c, smp = g["tiles"], g["c"], g["sample"]
                nt = len(tiles)
                T = nt * 128
                NK = T + 256 if smp else 256
                hT = bview(R_H, 0, [8, T])
                ogT = bview(R_G, 0, [8, T])
                qTb = sub(0, 1024, "qT")
                kTb = sub(1024, 1280, "kT")
                vbb = sub(2304, 1280, "vb")
                Psets = [(sub(3584, 1280, "P1a"), sub(4864, 1280, "P2a"), sub(7424, 640, "pcTa")),
                         (sub(6144, 1280, "P1b"), sub(11136, 1280, "P2b"), sub(12416, 640, "pcTb"))]
                obb = sub(8064, 2048, "ob")
                ckb = sub(10112, 1024, "ck")
                qT = bview(qTb, 0, [2, 1024])
                kT = bview(kTb, 0, [2, 1280])
                vb = bview(vbb, 0, [10, 256])
                ob = fview(obb, 0, [8, 256])
                ck = fview(ckb, 0, [2, 512])
                front(layer, tiles, c, lambda k, ti: (hT[:, k, ti * 128:(ti + 1) * 128], R_H))
                CK("front")
                for hb in range(4):
                    for which, dstT, dstb in ((0, qT, qTb), (1, kT, kTb)):
                        w = load_w(Win, 0, which * 1024 + hb * 256)
                        for ti in range(nt):
                            ps = pP_rot.next()
                            proj(hT, R_H, ti, w, 256, ps)
                            ta = tmpA.next()
                            OP("act", lambda e, ta=ta, ps=ps: e.copy(out=ta[:, 0:256], in_=ps[:, 0:256]), [ps], [ta])
                            srcb = ta
                            if smp:
                                tb = tmpB.next()
                                rope(ta, tb, rope2, 16, ti, 4)
                                srcb = tb
                            elif which == 1:
                                sq = g["seqs"][ti // 2][2]
                                tt = ti % 2
                                S.dma(O["dk"][sq, 2 * hb:2 * hb + 2, tt * 128:(tt + 1) * 128, :].rearrange("h t d -> t h d"),
                                      ta[:, 0:256].rearrange("p (h d) -> p h d", h=2), reads=[ta], queue="act")
                            transpose_blocks(lambda i, srcb=srcb: srcb[:, i * 128:(i + 1) * 128], srcb, 2,
                                             lambda i0, n, dstT=dstT, dstb=dstb, ti=ti: (dstT[:, i0:i0 + n, ti * 128:(ti + 1) * 128], dstb))
                    CK("qk")
                    w = load_w(Win, 0, 2048 + hb * 256)
                    for ti in range(nt):
                        ps = pP_rot.next()
                        proj(hT, R_H, ti, w, 256, ps)
                        ta = tmpA.next()
                        OP("act", lambda e, ta=ta, ps=ps: e.copy(out=ta[:, 0:256], in_=ps[:, 0:256]), [ps], [ta])
                        OP("dve", lambda e, ta=ta, ti=ti: e.tensor_copy(out=vb[:, ti, :], in_=ta[:, 0:256]), [ta], [vbb])
                        if not smp:
                            sq = g["seqs"][ti // 2][2]
                            tt = ti % 2
                            S.dma(O["dv"][sq, 2 * hb:2 * hb + 2, tt * 128:(tt + 1) * 128, :].rearrange("h t d -> t h d"),
                                  ta[:, 0:256].rearrange("p (h d) -> p h d", h=2), reads=[ta], queue="act")
                    if smp:
                        for tt in range(2):
                            S.dma(ck[:, 0, 0:256].rearrange("p (h d) -> p h d", h=2),
                                  I["cache_diff_k"][2 * hb:2 * hb + 2, tt * 128:(tt + 1) * 128, :].rearrange("h t d -> t h d"),
                                  writes=[ckb])
                            transpose_blocks(lambda i: ck[:, 0, i * 128:(i + 1) * 128], ckb, 2,
                                             lambda i0, n, tt=tt: (kT[:, i0:i0 + n, 1024 + tt * 128:1024 + (tt + 1) * 128], kTb))
                            S.dma(ck[:, 1, 0:256].rearrange("p (h d) -> p h d", h=2),
                                  I["cache_diff_v"][2 * hb:2 * hb + 2, tt * 128:(tt + 1) * 128, :].rearrange("h t d -> t h d"),
                                  writes=[ckb])
                            OP("dve", lambda e, tt=tt: e.tensor_copy(out=vb[:, 8 + tt, :], in_=ck[:, 1, 0:256]), [ckb], [vbb])
                    CK("v")
                    nkt = NK // 128
                    nblk, blk = (1, 256) if NK == 256 else (4, 320)

                    def att_s1(t0, hh, qi, k_):
                        P1b_, P2b_ = Psets[k_][0], Psets[k_][1]
                        P1_, P2_ = P1b_.t, P2b_.t
                        tq = t0 + qi
                        stats = []
                        for comp, (Pb, Pap) in enumerate(((P1b_, P1_), (P2b_, P2_))):
                            pr = slice(comp * 64, (comp + 1) * 64)
                            for b in range(nblk):
                                k0 = (t0 * 128 if not smp else 0) + b * blk
                                OP("pe", lambda e, pr=pr, k0=k0, b=b: e.matmul(
                                    out=PB[b][:, 0:blk], lhsT=qT[pr, hh, tq * 128:(tq + 1) * 128],
                                    rhs=kT[pr, hh, k0:k0 + blk], start=True, stop=True), [qTb, kTb], [PB[b]])
                            src = PSt[:, 0:nblk, 0:blk]
                            stats.append(softmax_un(src, PB[0:nblk], scale, Pap[:, 0:NK].rearrange("p (a b) -> p a b", a=nblk), Pb))
                        s1, s2 = stats
                        OP("dve", lambda e: e.tensor_tensor(out=s2[:, 4:5], in0=s2[:, 3:4], in1=lamc[:, 5:6], op=ALU.mult), [s2, lamc], [s2])
                        OP("dve", lambda e: e.tensor_scalar(out=P2_[:, 0:NK], in0=P2_[:, 0:NK], scalar1=s2[:, 4:5], scalar2=None, op0=ALU.mult),
                           [P2b_, s2], [P2b_])
                        OP("dve", lambda e: e.scalar_tensor_tensor(out=P1_[:, 0:NK], in0=P1_[:, 0:NK], scalar=s1[:, 3:4], in1=P2_[:, 0:NK],
                                                                   op0=ALU.mult, op1=ALU.subtract), [P1b_, P2b_, s1], [P1b_])
                        return (t0, hh, qi, k_)

                    def att_s2(ctx):
                        t0, hh, qi, k_ = ctx
                        P1b_, pcTb_ = Psets[k_][0], Psets[k_][2]
                        pcT_ = bview(pcTb_, 0, [10, 128])
                        tq = t0 + qi
                        h = 2 * hb + hh
                        pso = pP_rot.next()
                        vt0 = 0 if smp else t0
                        pv(P1b_.t, P1b_, [(i * 128, 128) for i in range(nkt)], pcT_, pcTb_,
                           lambda i: (vb[:, vt0 + i, hh * 128:(hh + 1) * 128], vbb), 128, pso)
                        s = st1.next()
                        ta = tmpA.next()
                        OP("act", lambda e: e.activation(out=ta[:, 0:128], in_=pso[:, 0:128], func=AF.Square, accum_out=s[:, 0:1]), [pso], [ta, s])
                        OP("act", lambda e: e.activation(out=s[:, 1:2], in_=s[:, 0:1], func=AF.Sqrt, scale=1.0 / 128, bias=EPS), [s], [s])
                        OP("dve", lambda e: e.reciprocal(out=s[:, 2:3], in_=s[:, 1:2]), [s], [s])
                        OP("dve", lambda e: e.scalar_tensor_tensor(out=ob[:, tq, hh * 128:(hh + 1) * 128], in0=pso[:, 0:128], scalar=s[:, 2:3],
                                                                   in1=gnw[:, h * 128:(h + 1) * 128], op0=ALU.mult, op1=ALU.mult), [pso, s, gnw], [obb])

                    its = [(t0, hh, qi) for (t0, ntq, sq) in g["seqs"] for hh in range(2) for qi in range(ntq)]
                    prev = None
                    for ii, (t0, hh, qi) in enumerate(its):
                        ctx = att_s1(t0, hh, qi, ii % 2)
                        if prev is not None:
                            att_s2(prev)
                        prev = ctx
                    att_s2(prev)
                    CK("attn")
                    w = load_w(Win, 0, 3072 + hb * 256)
                    for ti in range(nt):
                        ps = pP_rot.next()
                        proj(hT, R_H, ti, w, 256, ps)
                        ta = tmpA.next()
                        OP("act", lambda e, ta=ta, ps=ps: e.activation(out=ta[:, 0:256], in_=ps[:, 0:256], func=AF.Silu), [ps], [ta])
                        OP("dve", lambda e, ta=ta, ti=ti: e.tensor_tensor(out=ob[:, ti, :], in0=ob[:, ti, :], in1=ta[:, 0:256],
                                                                          op=ALU.mult), [obb, ta], [obb])
                        transpose_blocks(lambda i, ti=ti: ob[:, ti, i * 128:(i + 1) * 128], obb, 2,
                                         lambda i0, n, ti=ti, hb=hb: (ogT[:, 2 * hb + i0:2 * hb + i0 + n, ti * 128:(ti + 1) * 128], R_G))
                S.barrier()
                ybufs = [Buf(ATT[:, 0:nt * 1024], "yacc")]
                tail(layer, tiles, c, ogT, R_G, 8, Wout, ybufs)

            for g in groups:
                do_group(g)
            S.barrier()

        nab_rot = Rot([S.sb([128, 576], F32, "nab") for _ in range(2)])

        def layer_na(layer):
            Win, Wout = I["na_w_in"], I["na_w_out"]
            scale = 64 ** -0.5

            def do_group(g):
                S.barrier()
                tiles, c, smp = g["tiles"], g["c"], g["sample"]
                nt = len(tiles)
                T = nt * 128
                hT = bview(R_H, 0, [8, T])
                ogT = bview(R_G, 0, [8, T])
                qTb = sub(0, 1024, "qT")
                kTb = sub(1024, 1280, "kT")
                vbb = sub(2304, 1280, "vb")
                NAsets = [(sub(3584, 1280, "P1a"), sub(7424, 640, "pcTa")), (sub(6144, 1280, "P1b"), sub(12416, 640, "pcTb"))]
                obb = sub(8064, 2048, "ob")
                ckb = sub(10112, 1024, "ck")
                qT = bview(qTb, 0, [2, 1024])
                kT = bview(kTb, 0, [2, 1280])
                vb = bview(vbb, 0, [10, 256])
                ob = fview(obb, 0, [8, 256])
                ck = fview(ckb, 0, [2, 512])
                front(layer, tiles, c, lambda k, ti: (hT[:, k, ti * 128:(ti + 1) * 128], R_H))
                for hb in range(4):
                    for which, dstT, dstb in ((0, qT, qTb), (1, kT, kTb)):
                        w = load_w(Win, 0, which * 1024 + hb * 256)
                        for ti in range(nt):
                            ps = pP_rot.next()
                            proj(hT, R_H, ti, w, 256, ps)
                            ta = tmpA.next()
                            OP("act", lambda e, ta=ta, ps=ps: e.copy(out=ta[:, 0:256], in_=ps[:, 0:256]), [ps], [ta])
                            if which == 1 and not smp:
                                sq = g["seqs"][ti // 2][2]
                                tt = ti % 2
                                S.dma(O["nk"][sq, 4 * hb:4 * hb + 4, tt * 128:(tt + 1) * 128, :].rearrange("h t d -> t h d"),
                                      ta[:, 0:256].rearrange("p (h d) -> p h d", h=4), reads=[ta], queue="act")
                            transpose_blocks(lambda i, ta=ta: ta[:, i * 128:(i + 1) * 128], ta, 2,
                                             lambda i0, n, dstT=dstT, dstb=dstb, ti=ti: (dstT[:, i0:i0 + n, ti * 128:(ti + 1) * 128], dstb))
                    w = load_w(Win, 0, 2048 + hb * 256)
                    for ti in range(nt):
                        ps = pP_rot.next()
                        proj(hT, R_H, ti, w, 256, ps)
                        ta = tmpA.next()
                        OP("act", lambda e, ta=ta, ps=ps: e.copy(out=ta[:, 0:256], in_=ps[:, 0:256]), [ps], [ta])
                        OP("dve", lambda e, ta=ta, ti=ti: e.tensor_copy(out=vb[:, ti, :], in_=ta[:, 0:256]), [ta], [vbb])
                        if not smp:
                            sq = g["seqs"][ti // 2][2]
                            tt = ti % 2
                            S.dma(O["nv"][sq, 4 * hb:4 * hb + 4, tt * 128:(tt + 1) * 128, :].rearrange("h t d -> t h d"),
                                  ta[:, 0:256].rearrange("p (h d) -> p h d", h=4), reads=[ta], queue="act")
                    if smp:
                        for tt in range(2):
                            S.dma(ck[:, 0, 0:256].rearrange("p (h d) -> p h d", h=4),
                                  I["cache_na_k"][4 * hb:4 * hb + 4, tt * 128:(tt + 1) * 128, :].rearrange("h t d -> t h d"),
                                  writes=[ckb])
                            transpose_blocks(lambda i: ck[:, 0, i * 128:(i + 1) * 128], ckb, 2,
                                             lambda i0, n, tt=tt: (kT[:, i0:i0 + n, 1024 + tt * 128:1024 + (tt + 1) * 128], kTb))
                            S.dma(ck[:, 1, 0:256].rearrange("p (h d) -> p h d", h=4),
                                  I["cache_na_v"][4 * hb:4 * hb + 4, tt * 128:(tt + 1) * 128, :].rearrange("h t d -> t h d"),
                                  writes=[ckb])
                            OP("dve", lambda e, tt=tt: e.tensor_copy(out=vb[:, 8 + tt, :], in_=ck[:, 1, 0:256]), [ckb], [vbb])
                    def att_s1(t0, hh, qi, k_):
                            P1b, pcTb = NAsets[k_]
                            P1 = P1b.t
                            h = 4 * hb + hh
                            cc = hh // 2
                            pr = slice((hh % 2) * 64, (hh % 2) * 64 + 64)
                            if True:
                                tq = t0 + qi
                                if not smp:
                                    OP("pe", lambda e, pr=pr, cc=cc, tq=tq, t0=t0: e.matmul(
                                        out=PB[0][:, 0:256], lhsT=qT[pr, cc, tq * 128:(tq + 1) * 128],
                                        rhs=kT[pr, cc, t0 * 128:t0 * 128 + 256], start=True, stop=True), [qTb, kTb], [PB[0]])
                                    s1 = softmax_un(PB[0][:, 0:256], [PB[0]], scale, P1[:, 0:256], P1b)
                                    NKs = 256
                                    kblocks = [(0, 128), (128, 128)]
                                    vof = lambda i, hh=hh, t0=t0: (vb[:, t0 + i, hh * 64:(hh + 1) * 64], vbb)
                                else:
                                    j = qi
                                    r0 = min(max(2 * j - 4, 0), 8)
                                    nr = min(9, 16 - r0)
                                    nloc = nr * 64
                                    NKs = nloc + 256
                                    blk = NKs // 2
                                    nab = nab_rot.next()
                                    S.dma(nab[:], I["na_bias_x"][h, NA_JT[j]], writes=[nab])
                                    segs = [(r0 * 64, nloc, 0, True), (1024, 256, nloc, False)]
                                    pieces = []
                                    for key0, n, col0, biased in segs:
                                        done = 0
                                        while done < n:
                                            col = col0 + done
                                            b = col // blk
                                            m = min(n - done, (b + 1) * blk - col)
                                            pieces.append((key0 + done, m, b, col - b * blk, col, biased, col0 + done - col0 + (0 if not biased else 0)))
                                            done += m
                                    for (k0, m, b, bc, col, biased, _) in pieces:
                                        OP("pe", lambda e, pr=pr, cc=cc, tq=tq, k0=k0, m=m, b=b, bc=bc: e.matmul(
                                            out=PB[b][:, bc:bc + m], lhsT=qT[pr, cc, tq * 128:(tq + 1) * 128],
                                            rhs=kT[pr, cc, k0:k0 + m], start=True, stop=True), [qTb, kTb], [PB[b]])
                                    for (k0, m, b, bc, col, biased, _) in pieces:
                                        if biased:
                                            OP("dve", lambda e, m=m, b=b, bc=bc, col=col, nab=nab: e.scalar_tensor_tensor(
                                                out=P1[:, col:col + m], in0=PB[b][:, bc:bc + m], scalar=scale,
                                                in1=nab[:, col:col + m], op0=ALU.mult, op1=ALU.add), [PB[b], nab], [P1b])
                                        else:
                                            OP("dve", lambda e, m=m, b=b, bc=bc, col=col: e.tensor_scalar(
                                                out=P1[:, col:col + m], in0=PB[b][:, bc:bc + m], scalar1=scale, scalar2=None,
                                                op0=ALU.mult), [PB[b]], [P1b])
                                    s1 = softmax_un(P1[:, 0:NKs], [P1b], 1.0, P1[:, 0:NKs], P1b)
                                    kblocks = [(i * 128, 128) for i in range(nloc // 128)]
                                    if nloc % 128:
                                        kblocks.append((nloc - 64, 64))
                                    nlb = len(kblocks)
                                    kblocks += [(nloc, 128), (nloc + 128, 128)]

                                    def vof(i, hh=hh, r0=r0, nlb=nlb, kblocks=kblocks):
                                        if i < nlb:
                                            nk = kblocks[i][1]
                                            return vb[0:nk, r0 // 2 + i, hh * 64:(hh + 1) * 64], vbb
                                        return vb[:, 8 + (i - nlb), hh * 64:(hh + 1) * 64], vbb
                                OP("dve", lambda e, s1=s1, NKs=NKs: e.tensor_scalar(out=P1[:, 0:NKs], in0=P1[:, 0:NKs], scalar1=s1[:, 3:4],
                                                                                scalar2=None, op0=ALU.mult), [P1b, s1], [P1b])
                                return (tq, hh, k_, kblocks, vof)

                    def att_s2(ctx):
                        tq, hh, k_, kblocks, vof = ctx
                        P1b, pcTb = NAsets[k_]
                        pcT = bview(pcTb, 0, [10, 128])
                        pso = pP_rot.next()
                        pv(P1b.t, P1b, kblocks, pcT, pcTb, vof, 64, pso)
                        OP("act", lambda e: e.copy(out=ob[:, tq, hh * 64:(hh + 1) * 64], in_=pso[:, 0:64]), [pso], [obb])

                    its = [(t0, hh, qi) for (t0, ntq, sq) in g["seqs"] for hh in range(4) for qi in range(ntq)]
                    prev = None
                    for ii, (t0, hh, qi) in enumerate(its):
                        ctx = att_s1(t0, hh, qi, ii % 2)
                        if prev is not None:
                            att_s2(prev)
                        prev = ctx
                    att_s2(prev)
                    w = load_w(Win, 0, 3072 + hb * 256)
                    for ti in range(nt):
                        ps = pP_rot.next()
                        proj(hT, R_H, ti, w, 256, ps)
                        ta = tmpA.next()
                        OP("act", lambda e, ta=ta, ps=ps: e.activation(out=ta[:, 0:256], in_=ps[:, 0:256], func=AF.Silu), [ps], [ta])
                        OP("dve", lambda e, ta=ta, ti=ti: e.tensor_tensor(out=ob[:, ti, :], in0=ob[:, ti, :], in1=ta[:, 0:256],
                                                                          op=ALU.mult), [obb, ta], [obb])
                        transpose_blocks(lambda i, ti=ti: ob[:, ti, i * 128:(i + 1) * 128], obb, 2,
                                         lambda i0, n, ti=ti, hb=hb: (ogT[:, 2 * hb + i0:2 * hb + i0 + n, ti * 128:(ti + 1) * 128], R_G))
                S.barrier()
                ybufs = [Buf(ATT[:, 0:nt * 1024], "yacc")]
                tail(layer, tiles, c, ogT, R_G, 8, Wout, ybufs)

            for g in groups:
                do_group(g)
            S.barrier()

        def layer_ret(layer):
            Win, Wout = I["ret_w_in"], I["ret_w_out"]
            S.dma(gnw[:, 0:2048], I["ret_gn"].partition_broadcast(128), writes=[gnw])
            S.dma(lgt[:, 0:8], I["ret_decay"].partition_broadcast(128), writes=[lgt])
            OP("act", lambda e: e.activation(out=lgt[:, 0:8], in_=lgt[:, 0:8], func=AF.Exp, scale=-1.0), [lgt], [lgt])
            OP("act", lambda e: e.activation(out=lgt[:, 0:8], in_=lgt[:, 0:8], func=AF.Ln, bias=1.0), [lgt], [lgt])
            OP("dve", lambda e: e.tensor_scalar(out=lgt[:, 0:8], in0=lgt[:, 0:8], scalar1=-1.0, scalar2=None, op0=ALU.mult), [lgt], [lgt])
            OP("dve", lambda e: e.tensor_scalar(out=lgt[:, 8:12], in0=lgt[:, 4:8], scalar1=-1.0, scalar2=None, op0=ALU.mult), [lgt], [lgt])
            OP("dve", lambda e: e.tensor_scalar(out=lgt[:, 12:16], in0=lgt[:, 4:8], scalar1=1024.0, scalar2=None, op0=ALU.mult), [lgt], [lgt])
            for h in range(4):
                OP("act", lambda e, h=h: e.activation(out=dsc[:, h, 0:2], in_=stx[:, 0:2], func=AF.Exp, scale=lgt[:, h:h + 1]), [stx, lgt], [dsc])
                OP("act", lambda e, h=h: e.activation(out=dsc[:, h, 2:4], in_=stx[:, 2:4], func=AF.Exp, scale=lgt[:, 4 + h:5 + h]), [stx, lgt], [dsc])
            OP("dve", lambda e: e.tensor_scalar(out=dsc[:], in0=dsc[:], scalar1=1.0 / 16, scalar2=None, op0=ALU.mult), [dsc], [dsc])

            def do_group(g):
                S.barrier()
                tiles, c, smp = g["tiles"], g["c"], g["sample"]
                nt = len(tiles)
                T = nt * 128
                hT = bview(R_H, 0, [8, T])
                ogT = bview(R_G, 0, [16, T])
                qTb = sub(0, 1024, "qT")
                kTb = sub(1024, 1024, "kT")
                vbb = sub(2048, 2048, "vb")
                atb = sub(4096, 4096, "attT")
                obb = sub(8192, 4096, "ob")
                qT = bview(qTb, 0, [2, 1024])
                kT = bview(kTb, 0, [2, 1024])
                vb = bview(vbb, 0, [8, 512])
                attT = bview(atb, 0, [8, 1024])
                gpn = fview(atb, 0, [2, 1920])
                ob = fview(obb, 0, [8, 512])
                if smp:
                    s0b = sub(12288, 1024, "S0b")
                    qdb = sub(13312, 2048, "qTd")
                    S0 = bview(s0b, 0, [2, 2, 512])
                    qTd = bview(qdb, 0, [2, 2, 1024])
                else:
                    kdb = sub(12288, 1024, "kdec")
                    kdec = bview(kdb, 0, [2, 4, 256])
                front(layer, tiles, c, lambda k, ti: (hT[:, k, ti * 128:(ti + 1) * 128], R_H))
                for h in range(4):
                    stp = strip.next()
                    for i2 in range(2):
                        S.dma(gpn[:, i2, :], I["gpn"][i2], writes=[atb])
                    OP("act", lambda e, h=h: e.activation(out=gpn[:, 0, :], in_=gpn[:, 0, :], func=AF.Exp, scale=lgt[:, h:h + 1]), [atb, lgt], [atb])
                    OP("act", lambda e, h=h: e.activation(out=gpn[:, 1, :], in_=gpn[:, 1, :], func=AF.Exp, scale=lgt[:, 4 + h:5 + h]), [atb, lgt], [atb])
                    OP("dve", lambda e: e.scalar_tensor_tensor(out=gpn[:, 0, :], in0=gpn[:, 0, :], scalar=-1.0, in1=gpn[:, 1, :],
                                                               op0=ALU.add, op1=ALU.add), [atb], [atb])
                    OP("dve", lambda e: e.tensor_tensor(out=gpn[:, 0, 896:1024], in0=gpn[:, 0, 896:1024], in1=ident[:], op=ALU.add),
                       [atb, ident], [atb])
                    OP("dve", lambda e, stp=stp: e.tensor_scalar(out=stp[:], in0=gpn[:, 0, :], scalar1=1.0 / 16, scalar2=None, op0=ALU.mult),
                       [atb], [stp])
                    for which, dstT, dstb in ((0, qT, qTb), (1, kT, kTb)):
                        w = load_w(Win, 0, which * 1024 + h * 256)
                        for ti in range(nt):
                            ps = pP_rot.next()
                            proj(hT, R_H, ti, w, 256, ps)
                            ta = tmpA.next()
                            OP("act", lambda e, ta=ta, ps=ps: e.copy(out=ta[:, 0:256], in_=ps[:, 0:256]), [ps], [ta])
                            srcb = ta
                            if smp:
                                tb = tmpB.next()
                                rope(ta, tb, rope0, 64, ti, 1)
                                srcb = tb
                            elif which == 1:
                                tt = ti % 2
                                for d in range(2):
                                    OP("dve", lambda e, ta=ta, d=d, ti=ti, tt=tt, h=h: e.tensor_scalar(
                                        out=kdec[:, d, ti, :], in0=ta[:, 0:256], scalar1=dsc[:, h, 2 * d + tt:2 * d + tt + 1], scalar2=None,
                                        op0=ALU.mult), [ta, dsc], [kdb])
                            transpose_blocks(lambda i, srcb=srcb: srcb[:, i * 128:(i + 1) * 128], srcb, 2,
                                             lambda i0, n, dstT=dstT, dstb=dstb, ti=ti: (dstT[:, i0:i0 + n, ti * 128:(ti + 1) * 128], dstb))
                    for v2 in range(2):
                        w = load_w(Win, 0, 2048 + h * 512 + v2 * 256)
                        for ti in range(nt):
                            ps = pP_rot.next()
                            proj(hT, R_H, ti, w, 256, ps)
                            OP("act", lambda e, ti=ti, ps=ps, v2=v2: e.copy(out=vb[:, ti, v2 * 256:(v2 + 1) * 256], in_=ps[:, 0:256]), [ps], [vbb])
                    if smp:
                        for d in range(2):
                            for dc in range(2):
                                sb_ = osb.next()
                                S.dma(sb_[:], I["state_ret"][d, h, dc * 128:(dc + 1) * 128, :], writes=[sb_])
                                OP("pool", lambda e, sb_=sb_, d=d, dc=dc: e.tensor_copy(out=S0[:, d, dc, :], in_=sb_[:]), [sb_], [s0b])
                            rd = xt_rot.next()
                            S.dma(rd[:], I["iota1k"], writes=[rd])
                            if d == 0:
                                OP("act", lambda e, rd=rd, h=h: e.activation(out=rd[:], in_=rd[:], func=AF.Exp, scale=lgt[:, h:h + 1],
                                                                             bias=lgt[:, h:h + 1]), [rd, lgt], [rd])
                            else:
                                OP("act", lambda e, rd=rd, h=h: e.activation(out=rd[:], in_=rd[:], func=AF.Exp, scale=lgt[:, 8 + h:9 + h],
                                                                             bias=lgt[:, 12 + h:13 + h]), [rd, lgt], [rd])
                            for dc in range(2):
                                OP("dve", lambda e, rd=rd, d=d, dc=dc: e.tensor_tensor(out=qTd[:, d, dc, :], in0=qT[:, dc, :], in1=rd[:],
                                                                                       op=ALU.mult), [qTb, rd], [qdb])
                    for (t0, ntq, sq) in g["seqs"]:
                        nq = ntq * 128
                        nqb = (nq + 511) // 512
                        N = min(nq, 512)
                        for i in range(ntq):
                            for qh in range(nqb):
                                for dc in range(2):
                                    OP("pe", lambda e, i=i, qh=qh, dc=dc, t0=t0, N=N: e.matmul(
                                        out=PB[qh][:, 0:N], lhsT=kT[:, dc, (t0 + i) * 128:(t0 + i + 1) * 128],
                                        rhs=qT[:, dc, t0 * 128 + qh * 512:t0 * 128 + qh * 512 + N], start=(dc == 0), stop=(dc == 1)),
                                       [qTb, kTb], [PB[qh]])
                                OP("dve", lambda e, i=i, qh=qh, N=N, stp=stp: e.tensor_tensor(
                                    out=attT[:, i, qh * 512:qh * 512 + N], in0=PB[qh][:, 0:N],
                                    in1=stp[:, (7 - i) * 128 + qh * 512:(7 - i) * 128 + qh * 512 + N], op=ALU.mult), [PB[qh], stp], [atb])
                        for j in range(ntq):
                            pso = pP_rot.next()
                            nmm = ntq + (4 if smp else 0)
                            cnt = 0
                            for i in range(ntq):
                                OP("pe", lambda e, i=i, j=j, t0=t0, cnt=cnt, nmm=nmm, pso=pso: e.matmul(
                                    out=pso[:, 0:512], lhsT=attT[:, i, j * 128:(j + 1) * 128], rhs=vb[:, t0 + i, :],
                                    start=(cnt == 0), stop=(cnt == nmm - 1)), [atb, vbb], [pso])
                                cnt += 1
                            if smp:
                                for d in range(2):
                                    for dc in range(2):
                                        OP("pe", lambda e, d=d, dc=dc, j=j, cnt=cnt, nmm=nmm, pso=pso: e.matmul(
                                            out=pso[:, 0:512], lhsT=qTd[:, d, dc, j * 128:(j + 1) * 128], rhs=S0[:, d, dc, :],
                                            start=(cnt == 0), stop=(cnt == nmm - 1)), [qdb, s0b], [pso])
                                        cnt += 1
                            s = st1.next()
                            ta = tmpA.next()
                            OP("dve", lambda e, s=s, pso=pso: e.tensor_reduce(out=s[:, 0:1], in_=pso[:, 0:512], axis=AX.X, op=ALU.add), [pso], [s])
                            OP("dve", lambda e, s=s: e.tensor_scalar(out=s[:, 1:2], in0=s[:, 0:1], scalar1=-1.0 / 512, scalar2=None, op0=ALU.mult), [s], [s])
                            OP("act", lambda e, s=s, ta=ta, pso=pso: e.activation(out=ta[:, 0:512], in_=pso[:, 0:512], func=AF.Identity, bias=s[:, 1:2]),
                               [pso, s], [ta])
                            tb = tmpB.next()
                            OP("act", lambda e, s=s, ta=ta, tb=tb: e.activation(out=tb[:, 0:512], in_=ta[:, 0:512], func=AF.Square, accum_out=s[:, 2:3]),
                               [ta], [tb, s])
                            OP("act", lambda e, s=s: e.activation(out=s[:, 3:4], in_=s[:, 2:3], func=AF.Sqrt, scale=1.0 / 512, bias=GN_EPS), [s], [s])
                            OP("dve", lambda e, s=s: e.reciprocal(out=s[:, 4:5], in_=s[:, 3:4]), [s], [s])
                            OP("dve", lambda e, s=s, ta=ta, j=j, t0=t0, h=h: e.scalar_tensor_tensor(
                                out=ob[:, t0 + j, :], in0=ta[:, 0:512], scalar=s[:, 4:5], in1=gnw[:, h * 512:(h + 1) * 512],
                                op0=ALU.mult, op1=ALU.mult), [ta, s, gnw], [obb])
                        if not smp:
                            for d in range(2):
                                for dc in range(2):
                                    pst = pP_rot.next()
                                    for tt in range(2):
                                        OP("pe", lambda e, d=d, dc=dc, tt=tt, t0=t0, pst=pst: e.matmul(
                                            out=pst[:, 0:512], lhsT=kdec[:, d, t0 + tt, dc * 128:(dc + 1) * 128], rhs=vb[:, t0 + tt, :],
                                            start=(tt == 0), stop=(tt == 1)), [kdb, vbb], [pst])
                                    sb_ = osb.next()
                                    OP("act", lambda e, sb_=sb_, pst=pst: e.copy(out=sb_[:], in_=pst[:, 0:512]), [pst], [sb_])
                                    S.dma(O["st_ret"][sq, d, h, dc * 128:(dc + 1) * 128, :], sb_[:], reads=[sb_], queue="act")
                    for g2 in range(2):
                        w = load_w(Win, 0, 4096 + h * 512 + g2 * 256)
                        for ti in range(nt):
                            ps = pP_rot.next()
                            proj(hT, R_H, ti, w, 256, ps)
                            ta = tmpA.next()
                            OP("act", lambda e, ta=ta, ps=ps: e.activation(out=ta[:, 0:256], in_=ps[:, 0:256], func=AF.Silu), [ps], [ta])
                            OP("dve", lambda e, ta=ta, ti=ti, g2=g2: e.tensor_tensor(out=ob[:, ti, g2 * 256:(g2 + 1) * 256],
                                                                                     in0=ob[:, ti, g2 * 256:(g2 + 1) * 256], in1=ta[:, 0:256],
                                                                                     op=ALU.mult), [obb, ta], [obb])
                    for ti in range(nt):
                        transpose_blocks(lambda i, ti=ti: ob[:, ti, i * 128:(i + 1) * 128], obb, 4,
                                         lambda i0, n, ti=ti, h=h: (ogT[:, 4 * h + i0:4 * h + i0 + n, ti * 128:(ti + 1) * 128], R_G))
                S.barrier()
                ybufs = [Buf(ATT[:, 0:nt * 1024], "yacc")]
                tail(layer, tiles, c, ogT, R_G, 16, Wout, ybufs)

            for g in groups:
                do_group(g)
            S.barrier()

        RKVG = [dscr(f"rkvg{n}", (2048, 1024)) for n in range(4)]
        RKVGB = [[Buf(RKVG[n][t * 128:(t + 1) * 128, :], f"rkvg{n}_{t}") for t in range(16)] for n in range(4)]
        dscrb = lambda name, shape: nc.dram_tensor(name, list(shape), BF16).ap()
        TMP = dscrb("tmp_", (128, 256, 3, 64))
        TMS = dscrb("tms_", (32, 1024, 3, 64))
        FMP = dscrb("fmp_", (128, 4, 64, 256))
        FMS = dscrb("fms_", (32, 4, 64, 1024))
        GMP = dscr("gmp_", (128, 4, 64))
        GMS = dscr("gms_", (32, 16, 64))
        GMPB, GMSB = Buf(GMP, "gmp"), Buf(GMS, "gms")
        cmask2 = S.sb([128, 3, 128], F32, "cmask2")
        YPD = dscr("ypd", (128, 256, 64))
        YSD = dscr("ysd", (32, 1024, 64))
        TMPB, TMSB, FMPB, FMSB = Buf(TMP, "tmp"), Buf(TMS, "tms"), Buf(FMP, "fmp"), Buf(FMS, "fms")
        YPDB = Buf(YPD, "ypd")
        YSDB = Buf(YSD, "ysd")
        tri = S.sb([128, 128], F32, "tri")

        def layer_rwkv(layer):
            Win, Wout = I["rwkv_w_in"], I["rwkv_w_out"]
            S.barrier()
            for n in range(6):
                S.dma(muF[:, n, :], I["rwkv_mu"][n].rearrange("(k p) -> p k", p=128), writes=[muF], allow_slow_non_contiguous=True)
            smallb = bsub(24576, 3072, "rwsmall")
            wAb = bview(smallb, 0, [2, 8, 64])
            aAb = bview(smallb, 512, [2, 8, 64])
            wBb = bview(smallb, 1024, [2, 1024])
            aBb = bview(smallb, 2048, [2, 1024])
            for d in range(2):
                for src, dst in ((I["rwkv_wA"], wAb), (I["rwkv_aA"], aAb)):
                    f = wf.next()
                    S.dma(f[:, :, 0:64], src[d].rearrange("(k p) r -> p k r", p=128), writes=[f])
                    OP("pool", lambda e, f=f, dst=dst, d=d: e.tensor_copy(out=dst[:, d, :, :], in_=f[:, :, 0:64]), [f], [smallb])
                for src, dst in ((I["rwkv_wB"], wBb), (I["rwkv_aB"], aBb)):
                    f = wf.next()
                    fv = f.t[0:64].rearrange("p k n -> p (k n)")[:, 0:1024]
                    S.dma(fv, src[d], writes=[f])
                    OP("pool", lambda e, fv=fv, dst=dst, d=d: e.tensor_copy(out=dst[0:64, d, :], in_=fv), [f], [smallb])
            lorab = bsub(0, 4096, "lora")
            LWT = bview(lorab, 0, [2, 2048])
            LAT = bview(lorab, 2048, [2, 2048])
            OP("dve", lambda e: e.memset(bsum[:], 0.0), [], [bsum])

            def a1_group(g, gi):
                tiles, c, smp = g["tiles"], g["c"], g["sample"]
                nt = len(tiles)
                T = nt * 128
                tok0 = tiles[0] * 128
                hTb = bsub(4096, 8192, "hTf")
                xxb = bsub(12288, 8192, "xxT")
                xnb = bsub(20480, 4096, "xnT")
                hT = fview(hTb, 0, [8, T])
                xx = fview(xxb, 0, [8, T])
                xn = bview(xnb, 0, [8, T])
                front(layer, tiles, c, lambda k, ti: (hT[:, k, ti * 128:(ti + 1) * 128], hTb))
                for (t0, ntq, sq) in g["seqs"]:
                    o = t0 * 128
                    L = ntq * 128
                    OP("dve", lambda e, o=o, L=L: e.tensor_tensor(out=xx[:, :, o + 1:o + L - 1], in0=hT[:, :, o:o + L - 2],
                                                                 in1=hT[:, :, o + 2:o + L], op=ALU.add), [hTb], [xxb])
                    OP("dve", lambda e, o=o, L=L: e.scalar_tensor_tensor(out=xx[:, :, o + 1:o + L - 1], in0=xx[:, :, o + 1:o + L - 1],
                                                                        scalar=0.5, in1=hT[:, :, o + 1:o + L - 1],
                                                                        op0=ALU.mult, op1=ALU.subtract), [xxb, hTb], [xxb])
                    OP("dve", lambda e, o=o: e.scalar_tensor_tensor(out=xx[:, :, o:o + 1], in0=hT[:, :, o + 1:o + 2], scalar=0.5,
                                                                    in1=hT[:, :, o:o + 1], op0=ALU.mult, op1=ALU.subtract), [hTb], [xxb])
                    OP("dve", lambda e, o=o, L=L: e.scalar_tensor_tensor(out=xx[:, :, o + L - 1:o + L], in0=hT[:, :, o + L - 2:o + L - 1],
                                                                        scalar=0.5, in1=hT[:, :, o + L - 1:o + L],
                                                                        op0=ALU.mult, op1=ALU.subtract), [hTb], [xxb])

                def mix(n):
                    for k in range(8):
                        OP("dve", lambda e, k=k, n=n: e.scalar_tensor_tensor(out=xn[:, k, :], in0=xx[:, k, :], scalar=muF[:, n, k:k + 1],
                                                                             in1=hT[:, k, :], op0=ALU.mult, op1=ALU.add),
                           [xxb, hTb, muF], [xnb])
                for pi, n in enumerate((0, 2, 3, 5)):
                    mix(n)
                    for cb in range(4):
                        w = load_w(Win, 0, pi * 1024 + cb * 256)
                        for ti in range(nt):
                            ps = pP_rot.next()
                            proj(xn, xnb, ti, w, 256, ps)
                            ta = tmpA.next()
                            OP("act", lambda e, ta=ta, ps=ps: e.copy(out=ta[:, 0:256], in_=ps[:, 0:256]), [ps], [ta])
                            t = tiles[ti]
                            S.dma(RKVG[pi][t * 128:(t + 1) * 128, cb * 256:(cb + 1) * 256], ta[:, 0:256], reads=[ta],
                                  writes=[RKVGB[pi][t]], owner=ta, queue="act")
                for n, Ab, LT, fn in ((1, wAb, LWT, AF.Tanh), (4, aAb, LAT, AF.Copy)):
                    mix(n)
                    for d in range(2):
                        for c0 in range(0, T, 512):
                            ps = pP_rot.next()
                            for k in range(8):
                                OP("pe", lambda e, k=k, d=d, c0=c0, Ab=Ab, ps=ps: e.matmul(out=ps[0:64, 0:512], lhsT=Ab[:, d, k, :],
                                                                                          rhs=xn[:, k, c0:c0 + 512], start=(k == 0), stop=(k == 7)),
                                   [smallb, xnb], [ps])
                            if fn == AF.Tanh:
                                OP("act", lambda e, d=d, c0=c0, LT=LT, ps=ps: e.activation(out=LT[0:64, d, tok0 + c0:tok0 + c0 + 512],
                                                                                        in_=ps[0:64, 0:512], func=AF.Tanh), [ps], [lorab])
                            else:
                                OP("act", lambda e, d=d, c0=c0, LT=LT, ps=ps: e.copy(out=LT[0:64, d, tok0 + c0:tok0 + c0 + 512],
                                                                                  in_=ps[0:64, 0:512]), [ps], [lorab])

            for gi, g in enumerate(groups):
                a1_group(g, gi)
            S.barrier()

            tabb = bsub(4096, 8192, "tabs")
            TAB = fview(tabb, 0, [8, 1024])
            for i2, src in enumerate((I["rwkv_w0"][0], I["rwkv_w0"][1], I["rwkv_a0"][0], I["rwkv_a0"][1], I["rwkv_kk"],
                                      I["rwkv_ka"], I["rwkv_rk"], I["rwkv_gn"])):
                S.dma(TAB[:, i2, :], src.partition_broadcast(128), writes=[tabb])
            slot = [bsub(12288 + i2 * 1024, 1024, f"slot{i2}") for i2 in range(12)]
            xtb = [Buf(b_.t[:, :], b_.name + "_a2") for b_ in (xt_rot.bufs + xn_rot.bufs)]
            Rb, Kb, Vb, KKb, LWb, ABb, KDb, T1b, FLWb = slot[0:9]
            Frot = Rot([slot[9], slot[10]])
            Hrot = Rot([slot[11], xtb[0]])
            FMrot = Rot([xtb[1], xtb[2]])
            h16 = lambda b: b.t.rearrange("p (h d) -> p h d", h=16)
            S.dma(tri[:], I["tri"], writes=[tri])
            for e2 in range(2):
                S.dma(cmask2[e2 * 64:(e2 + 1) * 64, :, :], I["cmask"], writes=[cmask2])

            def flip(srcb, dstb):
                for hf in range(2):
                    ps = pP_rot.next()
                    OP("pe", lambda e, hf=hf, ps=ps: e.matmul(out=ps[:, :], lhsT=Jm[:], rhs=srcb.t[:, hf * 512:(hf + 1) * 512],
                                                             start=True, stop=True), [Jm, srcb], [ps])
                    OP("act", lambda e, hf=hf, ps=ps: e.copy(out=dstb.t[:, hf * 512:(hf + 1) * 512], in_=ps[:, :]), [ps], [dstb])

            def a2_tile(t):
                smp = t >= 8
                tt_in_seq = (t - 8) if smp else (t % 2)
                for pi, b in ((0, Rb), (1, Kb), (2, Vb)):
                    S.dma(b.t, RKVG[pi][t * 128:(t + 1) * 128, :], reads=[RKVGB[pi][t]], writes=[b])
                OP("dve", lambda e: e.tensor_tensor(out=KKb.t, in0=Kb.t, in1=TAB[:, 4, :], op=ALU.mult), [Kb, tabb], [KKb])
                OP("pool", lambda e: e.tensor_tensor(out=T1b.t, in0=KKb.t, in1=KKb.t, op=ALU.mult), [KKb], [T1b])
                nrm = tmpB.next()
                OP("dve", lambda e, nrm=nrm: e.tensor_reduce(out=nrm[:, 0:16], in_=h16(T1b), axis=AX.X, op=ALU.add), [T1b], [nrm])
                OP("dve", lambda e, nrm=nrm: e.tensor_scalar(out=nrm[:, 0:16], in0=nrm[:, 0:16], scalar1=1e-12, scalar2=None, op0=ALU.max), [nrm], [nrm])
                OP("act", lambda e, nrm=nrm: e.activation(out=nrm[:, 16:32], in_=nrm[:, 0:16], func=AF.Sqrt), [nrm], [nrm])
                OP("dve", lambda e, nrm=nrm: e.reciprocal(out=nrm[:, 32:48], in_=nrm[:, 16:32]), [nrm], [nrm])
                OP("dve", lambda e, nrm=nrm: e.tensor_tensor(out=h16(KKb), in0=h16(KKb), in1=nrm[:, 32:48].unsqueeze(2).to_broadcast([128, 16, 64]),
                                                             op=ALU.mult), [KKb, nrm], [KKb])
                for d in range(2):
                    a2_dir(t, d, smp, tt_in_seq)

            def a2_dir(t, d, smp, tt_in_seq):
                if True:
                    for (LT, Bw, tabi, dstb, post) in ((LWT, wBb, d, LWb, "w"), (LAT, aBb, 2 + d, ABb, "a")):
                        for hf in range(2):
                            ps = pP_rot.next()
                            OP("pe", lambda e, hf=hf, d=d, LT=LT, Bw=Bw, ps=ps: e.matmul(
                                out=ps[:, :], lhsT=LT[0:64, d, t * 128:(t + 1) * 128], rhs=Bw[0:64, d, hf * 512:(hf + 1) * 512],
                                start=True, stop=True), [lorab, smallb], [ps])
                            OP("dve", lambda e, hf=hf, tabi=tabi, dstb=dstb, ps=ps: e.tensor_tensor(
                                out=dstb.t[:, hf * 512:(hf + 1) * 512], in0=ps[:, :], in1=TAB[:, tabi, hf * 512:(hf + 1) * 512], op=ALU.add),
                               [ps, tabb], [dstb])
                        OP("act", lambda e, dstb=dstb: e.activation(out=dstb.t, in_=dstb.t, func=AF.Sigmoid), [dstb], [dstb])
                        if post == "w":
                            OP("act", lambda e, dstb=dstb: e.activation(out=dstb.t, in_=dstb.t, func=AF.Copy, scale=-math.exp(-0.5)), [dstb], [dstb])
                    OP("dve", lambda e: e.scalar_tensor_tensor(out=T1b.t, in0=ABb.t, scalar=-1.0, in1=TAB[:, 5, :], op0=ALU.add, op1=ALU.mult),
                       [ABb, tabb], [T1b])
                    OP("dve", lambda e: e.scalar_tensor_tensor(out=KDb.t, in0=T1b.t, scalar=1.0, in1=Kb.t, op0=ALU.add, op1=ALU.mult),
                       [T1b, Kb], [KDb])
                    OP("pool", lambda e: e.tensor_tensor(out=ABb.t, in0=KKb.t, in1=ABb.t, op=ALU.mult), [KKb, ABb], [ABb])
                    OP("pool", lambda e: e.tensor_tensor(out=T1b.t, in0=Rb.t, in1=KDb.t, op=ALU.mult), [Rb, KDb], [T1b])
                    OP("pool", lambda e: e.tensor_tensor(out=T1b.t, in0=T1b.t, in1=TAB[:, 6, :], op=ALU.mult), [T1b, tabb], [T1b])
                    nb = tmpB.next()
                    OP("dve", lambda e, nb=nb: e.tensor_reduce(out=nb[:, 0:16], in_=h16(T1b), axis=AX.X, op=ALU.add), [T1b], [nb])
                    OP("dve", lambda e, nb=nb: e.tensor_tensor(out=bsum[:, t, :], in0=bsum[:, t, :], in1=nb[:, 0:16], op=ALU.add), [bsum, nb], [bsum])
                    if smp:
                        L, TMD, FMD, TMB_, FMB_, GMD, GMB_ = 1024, TMS, FMS, TMSB, FMSB, GMS, GMSB
                        ch0 = d * 16
                    else:
                        L, TMD, FMD, TMB_, FMB_, GMD, GMB_ = 256, TMP, FMP, TMPB, FMPB, GMP, GMPB
                        ch0 = (d * 4 + t // 2) * 16
                    tk0 = tt_in_seq * 128
                    s0 = tk0 if d == 0 else L - 128 - tk0

                    def chain_order(srcb):
                        if d == 0:
                            return srcb
                        f = Frot.next()
                        flip(srcb, f)
                        return f
                    if d == 0:
                        lwc = LWb
                    else:
                        flip(LWb, FLWb)
                        lwc = FLWb
                    cps = [PB[0], PB[1]]
                    for hf in range(2):
                        OP("pe", lambda e, hf=hf: e.matmul(out=cps[hf][:, :], lhsT=tri[:], rhs=lwc.t[:, hf * 512:(hf + 1) * 512], start=True, stop=True),
                           [tri, lwc], [cps[hf]])

                    def store_tm(hb_, vi):
                        tmb = strip.next()
                        OP("act", lambda e: e.copy(out=tmb[:, 0:1024], in_=hb_.t), [hb_], [tmb])
                        S.dma(TMD[ch0:ch0 + 16, s0:s0 + 128, vi, :].rearrange("h t j -> t h j"), tmb[:, 0:1024].rearrange("p (h d) -> p h d", h=16),
                              reads=[tmb], writes=[TMB_], owner=tmb, queue="act")

                    def store_fm(hb_, vi):
                        fm = FMrot.next()
                        fmv = fm.t[:, 0:512].bitcast(BF16).rearrange("p (a b) -> p a b", a=8)
                        transpose_blocks(lambda i: hb_.t[:, i * 128:(i + 1) * 128], hb_, 8, lambda i0, n: (fmv[:, i0:i0 + n, :], fm), evac="act")
                        for e2 in range(2):
                            S.dma(FMD[ch0 + e2:ch0 + 16:2, vi, :, s0:s0 + 128].rearrange("c k t -> k c t"), fmv[e2 * 64:(e2 + 1) * 64, :, :],
                                  reads=[fm], writes=[FMB_], owner=fm, queue="act")

                    def hat(srcb, kind):
                        hb_ = Hrot.next()
                        for hf in range(2):
                            sl = slice(hf * 512, (hf + 1) * 512)
                            if kind == "prev":
                                OP("dve", lambda e, hf=hf, sl=sl: e.tensor_tensor(out=T1b.t[:, sl], in0=cps[hf][:, :], in1=lwc.t[:, sl], op=ALU.subtract),
                                   [cps[hf], lwc], [T1b])
                                OP("act", lambda e, sl=sl: e.activation(out=T1b.t[:, sl], in_=T1b.t[:, sl], func=AF.Exp), [T1b], [T1b])
                            elif kind == "cur":
                                OP("act", lambda e, hf=hf, sl=sl: e.activation(out=T1b.t[:, sl], in_=cps[hf][:, :], func=AF.Exp), [cps[hf]], [T1b])
                            elif kind == "inv":
                                OP("act", lambda e, hf=hf, sl=sl: e.activation(out=T1b.t[:, sl], in_=cps[hf][:, :], func=AF.Exp, scale=-1.0), [cps[hf]], [T1b])
                        if srcb is None:
                            OP("pool", lambda e: e.tensor_copy(out=hb_.t, in_=T1b.t), [T1b], [hb_])
                        else:
                            OP("pool", lambda e: e.tensor_tensor(out=hb_.t, in0=srcb.t, in1=T1b.t, op=ALU.mult), [srcb, T1b], [hb_])
                        return hb_

                    hb_ = hat(chain_order(KKb), "prev")
                    store_fm(hb_, 0)
                    hb_ = hat(chain_order(Rb), "cur")
                    store_fm(hb_, 1)
                    for cc in range(2):
                        cidx = s0 // 64 + cc
                        S.dma(GMD[ch0:ch0 + 16, cidx:cidx + 1, :].rearrange("h n k -> n h k"),
                              T1b.t[63 + 64 * cc:64 + 64 * cc, :].rearrange("p (h k) -> p h k", h=16), reads=[T1b], writes=[GMB_], owner=T1b, queue="act")
                    hb_ = hat(chain_order(ABb), "inv")
                    store_fm(hb_, 2)
                    store_tm(hb_, 0)
                    hb_ = hat(chain_order(KDb), "inv")
                    store_fm(hb_, 3)
                    store_tm(hb_, 1)
                    store_tm(chain_order(Vb), 2)

            for t in range(16):
                a2_tile(t)
            S.barrier()
            CK("rwkv_a")

            NU = 20
            o = 0

            def carve(size, name):
                nonlocal o
                b = bsub(o, size, name)
                o += size
                return b
            Tst = [carve(256, f"T{u}") for u in range(NU)]
            Tbs = [carve(128, f"Tb{u}") for u in range(NU)]
            NW = 4
            bsets = []
            for w_ in range(NW):
                bsets.append(dict(
                    tm=carve(384, f"tm{w_}"), fm=carve(512, f"fm{w_}"), gm=carve(4, f"gm{w_}"), ka=carve(256, f"ka{w_}"), apb=carve(128, f"apb{w_}"),
                    qz=[carve(512, f"qz{w_}a"), carve(512, f"qz{w_}b")], qt=[carve(256, f"qt{w_}a"), carve(256, f"qt{w_}b")],
                    bdq=carve(512, f"bdq{w_}"), bdt=carve(512, f"bdt{w_}"), zb=carve(128, f"zb{w_}"), u=carve(128, f"u{w_}"),
                    p=carve(128, f"p{w_}"), y=carve(256, f"y{w_}")))
            pB_rot = Rot(PB)
            for bs_ in bsets:
                for b in (bs_["bdq"], bs_["bdt"]):
                    OP("pool", lambda e, b=b: e.memset(b.t, 0.0), [], [b])
            f3 = lambda b, x: b.t.rearrange("p (c x) -> p c x", c=4)
            b3 = lambda b, n: b.t[:, 0:n // 2].bitcast(BF16).rearrange("p (c x) -> p c x", c=4)
            PH = [slice(0, 64), slice(64, 128)]
            ev_rot = Rot(["dve", "act", "pool", "dve", "act"])

            def to_bd(srcv, srcb, bd):
                bdv = f3(bd, 0)
                for e2 in range(2):
                    eng = ev_rot.next()
                    if eng == "act":
                        OP("act", lambda e, e2=e2: e.copy(out=bdv[PH[e2], :, e2 * 64:(e2 + 1) * 64], in_=srcv[PH[e2], :, :]), [srcb], [bd])
                    else:
                        OP(eng, lambda e, e2=e2: e.tensor_copy(out=bdv[PH[e2], :, e2 * 64:(e2 + 1) * 64], in_=srcv[PH[e2], :, :]), [srcb], [bd])

            units = []
            for d in range(2):
                for hh in range(2):
                    units.append(dict(smp=True, ch0=d * 16 + hh * 8, d=d, h0=hh * 8, nch=16))
            for d in range(2):
                for sq in range(4):
                    for hh in range(2):
                        units.append(dict(smp=False, ch0=(d * 4 + sq) * 16 + hh * 8, d=d, sq=sq, h0=hh * 8, nch=4))
            for ui, u in enumerate(units):
                Tb, Tbb = Tst[ui], Tbs[ui]
                if not u["smp"]:
                    OP("pool", lambda e, Tb=Tb: e.memset(Tb.t, 0.0), [], [Tb])
                else:
                    st_ = bsets[ui % NW]["qz"][0]
                    sv = st_.t[:, 0:256].rearrange("p (c x) -> p c x", c=4)
                    for e2 in range(2):
                        h0 = u["h0"] + 4 * e2
                        S.dma(sv[PH[e2], :, :], I["state_rwkv"][u["d"], h0:h0 + 4].rearrange("h v k -> v h k"), writes=[st_])
                    ps = pP_rot.next()
                    for e2 in range(2):
                        for p in range(4):
                            OP("pe", lambda e, e2=e2, p=p, ps=ps, sv=sv: e.matmul(out=ps[PH[e2], p * 64:(p + 1) * 64], lhsT=sv[PH[e2], p, :],
                                                                               rhs=ident[PH[e2], e2 * 64:(e2 + 1) * 64], start=True, stop=True),
                               [st_, ident], [ps])
                    OP("act", lambda e, Tb=Tb, ps=ps: e.copy(out=Tb.t, in_=ps[:, 0:256]), [ps], [Tb])
                OP("dve", lambda e, Tb=Tb, Tbb=Tbb: e.tensor_copy(out=Tbb.t[:, 0:128].bitcast(BF16), in_=Tb.t), [Tb], [Tbb])

            def unit_chunk(ui, u, n, bs):
                smp, ch0 = u["smp"], u["ch0"]
                TMD, FMD, GMD, TMB_, FMB_, GMB_, YD, YDB_ = ((TMS, FMS, GMS, TMSB, FMSB, GMSB, YSD, YSDB) if smp else
                                                             (TMP, FMP, GMP, TMPB, FMPB, GMPB, YPD, YPDB))
                tm, fm, gm = bs["tm"], bs["fm"], bs["gm"]
                tmv = tm.t.bitcast(BF16).rearrange("p (c v j) -> p c v j", c=4, v=3)
                fmv = fm.t.bitcast(BF16).rearrange("p (c v s) -> p c v s", c=4, v=4)
                for e2 in range(2):
                    c0 = ch0 + 4 * e2
                    S.dma(tm.t.bitcast(BF16).rearrange("p (c x) -> p c x", c=4)[PH[e2], :, :],
                          TMD[c0:c0 + 4, n * 64:(n + 1) * 64, :, :].rearrange("c s v j -> s c (v j)"), reads=[TMB_], writes=[tm])
                    S.dma(fm.t.bitcast(BF16).rearrange("p (cv s) -> p cv s", s=64)[PH[e2], :, :],
                          FMD[c0:c0 + 4, :, :, n * 64:(n + 1) * 64].rearrange("c v k s -> k (c v) s"), reads=[FMB_], writes=[fm])
                    S.dma(gm.t[PH[e2], :], GMD[c0:c0 + 4, n, :].rearrange("c k -> k c"), reads=[GMB_], writes=[gm], allow_slow_non_contiguous=True)
                Tb, Tbb = Tst[ui], Tbs[ui]
                Tv = f3(Tb, 0)
                Tbv = b3(Tbb, 256)
                ka, apb = bs["ka"], bs["apb"]
                kav = b3(ka, 512)
                apbv = b3(apb, 256)
                qzi, qti = 0, 0
                qz = bs["qz"][0]
                qzv = f3(qz, 0)
                qt = bs["qt"][0]
                qtv = f3(qt, 0)
                psB, psK, psL = pB_rot.next(), pB_rot.next(), pB_rot.next()
                for e2 in range(2):
                    for p in range(4):
                        OP("pe", lambda e, e2=e2, p=p: e.matmul(out=psB[PH[e2], p * 128:(p + 1) * 128], lhsT=fmv[PH[e2], p, 2, :],
                                                               rhs=fmv[PH[e2], p, 0:2, :], start=True, stop=True), [fm], [psB])
                        OP("pe", lambda e, e2=e2, p=p: e.matmul(out=psK[PH[e2], p * 128:(p + 1) * 128], lhsT=fmv[PH[e2], p, 3, :],
                                                               rhs=fmv[PH[e2], p, 0:2, :], start=True, stop=True), [fm], [psK])
                        OP("pe", lambda e, e2=e2, p=p: e.matmul(out=psL[PH[e2], p * 64:(p + 1) * 64], lhsT=fmv[PH[e2], p, 0, :],
                                                               rhs=fmv[PH[e2], p, 2, :], start=True, stop=True), [fm], [psL])
                psBv = psB[:, :].rearrange("p (c x) -> p c x", c=4)
                OP("dve", lambda e, qzv=qzv: e.tensor_tensor(out=qzv[:, :, 64:128], in0=psBv[:, :, 0:64], in1=cmask2[:, 0, 0:64].unsqueeze(1).to_broadcast([128, 4, 64]),
                                                    op=ALU.mult), [psB, cmask2], [qz])
                OP("dve", lambda e: e.tensor_tensor(out=apbv, in0=psBv[:, :, 64:128], in1=cmask2[:, 0, 64:128].unsqueeze(1).to_broadcast([128, 4, 64]),
                                                    op=ALU.mult), [psB, cmask2], [apb])
                OP("dve", lambda e: e.tensor_tensor(out=kav, in0=psK[:, :].rearrange("p (c x) -> p c x", c=4),
                                                    in1=cmask2[:, 1, :].unsqueeze(1).to_broadcast([128, 4, 128]), op=ALU.mult), [psK, cmask2], [ka])
                OP("dve", lambda e, qtv=qtv: e.tensor_tensor(out=qtv, in0=psL[:, 0:256].rearrange("p (c x) -> p c x", c=4),
                                                    in1=cmask2[:, 2, 0:64].unsqueeze(1).to_broadcast([128, 4, 64]), op=ALU.mult), [psL, cmask2], [qt])
                OP("pool", lambda e, qzv=qzv: e.tensor_tensor(out=qzv[:, :, 0:64], in0=qzv[:, :, 64:128],
                                                     in1=ident[:, :].rearrange("p (a b) -> p a b", a=2)[:, 0, :].unsqueeze(1).to_broadcast([128, 4, 64])
                                                     if False else identst[:, :].unsqueeze(1).to_broadcast([128, 4, 64]), op=ALU.add), [qz, identst], [qz])
                yield
                bdq, bdt = bs["bdq"], bs["bdt"]
                to_bd(qzv[:, :, 64:128], qz, bdq)
                to_bd(qtv, qt, bdt)
                ps1, ps2 = pB_rot.next(), pB_rot.next()
                for p in range(4):
                    OP("pe", lambda e, p=p, bdt=bdt, qzv=qzv: e.matmul(out=ps1[:, p * 64:(p + 1) * 64], lhsT=f3(bdt, 0)[:, p, :], rhs=qzv[:, p, 64:128], start=True, stop=True),
                       [bdt, qz], [ps1])
                    OP("pe", lambda e, p=p, bdq=bdq, qtv=qtv: e.matmul(out=ps2[:, p * 64:(p + 1) * 64], lhsT=f3(bdq, 0)[:, p, :], rhs=qtv[:, p, :], start=True, stop=True),
                       [bdq, qt], [ps2])
                OP("act", lambda e, qzv=qzv: e.copy(out=qzv[:, :, 64:128], in_=ps1[:, 0:256].rearrange("p (c x) -> p c x", c=4)), [ps1], [qz])
                qt = bs["qt"][1]
                qti = 1
                qtv = f3(qt, 0)
                OP("dve", lambda e, qtv=qtv: e.tensor_copy(out=qtv, in_=ps2[:, 0:256].rearrange("p (c x) -> p c x", c=4)), [ps2], [qt])
                yield
                for lvl in range(1, 6):
                    to_bd(qtv, qt, bdt)
                    if lvl < 5:
                        to_bd(qzv[:, :, 64:128], qz, bdq)
                    N = 128 if lvl < 5 else 64
                    psA = pB_rot.next()
                    for p in range(4):
                        OP("pe", lambda e, p=p, bdt=bdt, qzv=qzv, N=N, psA=psA: e.matmul(out=psA[:, p * 128:p * 128 + N], lhsT=f3(bdt, 0)[:, p, :],
                                                                                      rhs=qzv[:, p, 0:N], start=True, stop=True), [bdt, qz], [psA])
                    if lvl < 5:
                        psC = pB_rot.next()
                        for p in range(4):
                            OP("pe", lambda e, p=p, bdq=bdq, qtv=qtv, psC=psC: e.matmul(out=psC[:, p * 64:(p + 1) * 64], lhsT=f3(bdq, 0)[:, p, :],
                                                                                      rhs=qtv[:, p, :], start=True, stop=True), [bdq, qt], [psC])
                    psAv = psA[:, :].rearrange("p (c x) -> p c x", c=4)
                    if lvl < 5:
                        qzi ^= 1
                        qz_new = bs["qz"][qzi]
                        qznv = f3(qz_new, 0)
                        OP("dve", lambda e, qznv=qznv, qzv=qzv, psAv=psAv: e.tensor_tensor(out=qznv[:, :, 0:64], in0=psAv[:, :, 0:64], in1=qzv[:, :, 0:64],
                                                                                         op=ALU.add), [psA, qz], [qz_new])
                        OP("act", lambda e, qznv=qznv, psAv=psAv: e.copy(out=qznv[:, :, 64:128], in_=psAv[:, :, 64:128]), [psA], [qz_new])
                        qti ^= 1
                        qt_new = bs["qt"][qti]
                        qtnv = f3(qt_new, 0)
                        OP("dve", lambda e, qtnv=qtnv, psC=psC: e.tensor_copy(out=qtnv, in_=psC[:, 0:256].rearrange("p (c x) -> p c x", c=4)), [psC], [qt_new])
                        qz, qzv, qt, qtv = qz_new, qznv, qt_new, qtnv
                        yield
                    else:
                        zb = bs["zb"]
                        zbv = b3(zb, 256)
                        OP("dve", lambda e, zbv=zbv, qzv=qzv, psAv=psAv: e.tensor_tensor(out=zbv, in0=psAv[:, :, 0:64], in1=qzv[:, :, 0:64], op=ALU.add),
                           [psA, qz], [zb])
                ps = pB_rot.next()
                for e2 in range(2):
                    for p in range(4):
                        OP("pe", lambda e, e2=e2, p=p, ps=ps: e.matmul(out=ps[PH[e2], p * 64:(p + 1) * 64], lhsT=fmv[PH[e2], p, 0, :], rhs=Tbv[PH[e2], p, :],
                                                                      start=True, stop=False), [fm, Tbb], [ps])
                        OP("pe", lambda e, e2=e2, p=p, ps=ps: e.matmul(out=ps[PH[e2], p * 64:(p + 1) * 64], lhsT=kav[PH[e2], p, 0:64], rhs=tmv[PH[e2], p, 2, :],
                                                                      start=False, stop=True), [ka, tm], [ps])
                ub = bs["u"]
                ubv = b3(ub, 256)
                OP("act", lambda e, ps=ps: e.copy(out=ubv, in_=ps[:, 0:256].rearrange("p (c x) -> p c x", c=4)), [ps], [ub])
                yield
                ps = pB_rot.next()
                for e2 in range(2):
                    for p in range(4):
                        OP("pe", lambda e, e2=e2, p=p, ps=ps: e.matmul(out=ps[PH[e2], p * 64:(p + 1) * 64], lhsT=zbv[PH[e2], p, :], rhs=ubv[PH[e2], p, :],
                                                                      start=True, stop=True), [zb, ub], [ps])
                pb_ = bs["p"]
                pbv = b3(pb_, 256)
                OP("dve", lambda e, ps=ps: e.tensor_scalar(out=pbv, in0=ps[:, 0:256].rearrange("p (c x) -> p c x", c=4), scalar1=-1.0, scalar2=None, op0=ALU.mult),
                   [ps], [pb_])
                yield
                ps = pB_rot.next()
                for e2 in range(2):
                    for p in range(4):
                        OP("pe", lambda e, e2=e2, p=p, ps=ps: e.matmul(out=ps[PH[e2], p * 64:(p + 1) * 64], lhsT=fmv[PH[e2], p, 1, :], rhs=Tbv[PH[e2], p, :],
                                                                      start=True, stop=False), [fm, Tbb], [ps])
                        OP("pe", lambda e, e2=e2, p=p, ps=ps: e.matmul(out=ps[PH[e2], p * 64:(p + 1) * 64], lhsT=apbv[PH[e2], p, :], rhs=pbv[PH[e2], p, :],
                                                                      start=False, stop=False), [apb, pb_], [ps])
                        OP("pe", lambda e, e2=e2, p=p, ps=ps: e.matmul(out=ps[PH[e2], p * 64:(p + 1) * 64], lhsT=kav[PH[e2], p, 64:128], rhs=tmv[PH[e2], p, 2, :],
                                                                      start=False, stop=True), [ka, tm], [ps])
                yb = bs["y"]
                ybv = f3(yb, 0)
                OP("act", lambda e, ps=ps: e.copy(out=ybv, in_=ps[:, 0:256].rearrange("p (c x) -> p c x", c=4)), [ps], [yb])
                for e2 in range(2):
                    c0 = ch0 + 4 * e2
                    S.dma(YD[c0:c0 + 4, n * 64:(n + 1) * 64, :].rearrange("c s x -> s c x"), ybv[PH[e2], :, :], reads=[yb], writes=[YDB_], owner=yb, queue="act")
                ps = pB_rot.next()
                for e2 in range(2):
                    for p in range(4):
                        OP("pe", lambda e, e2=e2, p=p, ps=ps: e.matmul(out=ps[PH[e2], p * 64:(p + 1) * 64], lhsT=tmv[PH[e2], p, 0, :], rhs=pbv[PH[e2], p, :],
                                                                      start=True, stop=False), [tm, pb_], [ps])
                        OP("pe", lambda e, e2=e2, p=p, ps=ps: e.matmul(out=ps[PH[e2], p * 64:(p + 1) * 64], lhsT=tmv[PH[e2], p, 1, :], rhs=tmv[PH[e2], p, 2, :],
                                                                      start=False, stop=True), [tm], [ps])
                OP("dve", lambda e, ps=ps: e.tensor_tensor(out=Tv, in0=ps[:, 0:256].rearrange("p (c x) -> p c x", c=4), in1=Tv, op=ALU.add), [ps, Tb], [Tb])
                OP("pool", lambda e: e.tensor_tensor(out=Tv, in0=Tv, in1=gm.t[:, 0:4].unsqueeze(2).to_broadcast([128, 4, 64]), op=ALU.mult), [Tb, gm], [Tb])
                OP("act", lambda e: e.copy(out=Tbv, in_=Tv), [Tb], [Tbb])

            tasks = [(ui, u, n) for n in range(16) for ui, u in enumerate(units) if n < u["nch"]]
            slots = [None] * NW
            ti_ = 0
            while ti_ < len(tasks) or any(sl_ is not None for sl_ in slots):
                for w_ in range(NW):
                    if slots[w_] is None and ti_ < len(tasks):
                        ui, u, n = tasks[ti_]
                        if any(sl_ is not None and sl_[1] == ui for sl_ in slots):
                            continue
                        slots[w_] = (unit_chunk(ui, u, n, bsets[w_]), ui)
                        ti_ += 1
                for w_ in range(NW):
                    if slots[w_] is not None:
                        try:
                            next(slots[w_][0])
                        except StopIteration:
                            slots[w_] = None
            for ui, u in enumerate(units):
                if u["smp"]:
                    continue
                Tb = Tst[ui]
                Tv = f3(Tb, 0)
                ps = pP_rot.next()
                for e2 in range(2):
                    for p in range(4):
                        OP("pe", lambda e, e2=e2, p=p, ps=ps, Tv=Tv: e.matmul(out=ps[PH[e2], p * 64:(p + 1) * 64], lhsT=Tv[PH[e2], p, :],
                                                                           rhs=ident[PH[e2], e2 * 64:(e2 + 1) * 64], start=True, stop=True), [Tb, ident], [ps])
                yb = bsets[ui % NW]["y"]
                ybv = f3(yb, 0)
                OP("act", lambda e, ps=ps, ybv=ybv: e.copy(out=ybv, in_=ps[:, 0:256].rearrange("p (c x) -> p c x", c=4)), [ps], [yb])
                for e2 in range(2):
                    h0 = u["h0"] + 4 * e2
                    S.dma(O["st_rwkv"][u["sq"], u["d"], h0:h0 + 4].rearrange("h v k -> v h k"), ybv[PH[e2], :, :], reads=[yb], owner=yb, queue="act")
            S.barrier()
            CK("rwkv_b")

            gnt = bsub(0, 1024, "gnt")
            S.dma(gnt.t, I["rwkv_gn"].partition_broadcast(128), writes=[gnt])
            cs_ = [bsub(1024 + i2 * 1024, 1024, f"cs{i2}") for i2 in range(6)]
            YFb, YBb, Vc, Gc, C1, C2 = cs_
            ogb = bsub(8192, 4096, "ogT")

            def c_group(g):
                tiles, c, smp = g["tiles"], g["c"], g["sample"]
                nt = len(tiles)
                T = nt * 128
                ogT = bview(ogb, 0, [8, T])
                for ti, t in enumerate(tiles):
                    if smp:
                        tk0 = (t - 8) * 128
                        L = 1024
                        S.dma(h16(YFb), YSD[0:16, tk0:tk0 + 128, :].rearrange("h t x -> t h x"), reads=[YSDB], writes=[YFb])
                        S.dma(h16(C1), YSD[16:32, L - 128 - tk0:L - tk0, :].rearrange("h t x -> t h x"), reads=[YSDB], writes=[C1])
                    else:
                        sq = t // 2
                        tk0 = (t % 2) * 128
                        L = 256
                        S.dma(h16(YFb), YPD[sq * 16:(sq + 1) * 16, tk0:tk0 + 128, :].rearrange("h t x -> t h x"), reads=[YPDB], writes=[YFb])
                        S.dma(h16(C1), YPD[(4 + sq) * 16:(5 + sq) * 16, L - 128 - tk0:L - tk0, :].rearrange("h t x -> t h x"),
                              reads=[YPDB], writes=[C1])
                    for hf in range(2):
                        ps = pP_rot.next()
                        OP("pe", lambda e, hf=hf, ps=ps: e.matmul(out=ps[:, :], lhsT=Jm[:], rhs=C1.t[:, hf * 512:(hf + 1) * 512], start=True, stop=True),
                           [Jm, C1], [ps])
                        OP("dve", lambda e, hf=hf, ps=ps: e.tensor_tensor(out=YBb.t[:, hf * 512:(hf + 1) * 512], in0=ps[:, :],
                                                                          in1=YFb.t[:, hf * 512:(hf + 1) * 512], op=ALU.add), [ps, YFb], [YBb])
                    S.dma(Vc.t, RKVG[2][t * 128:(t + 1) * 128, :], reads=[RKVGB[2][t]], writes=[Vc])
                    S.dma(Gc.t, RKVG[3][t * 128:(t + 1) * 128, :], reads=[RKVGB[3][t]], writes=[Gc])
                    nb = tmpB.next()
                    OP("dve", lambda e, nb=nb: e.tensor_reduce(out=nb[:, 0:16], in_=h16(YBb), axis=AX.X, op=ALU.add), [YBb], [nb])
                    OP("dve", lambda e, nb=nb: e.tensor_scalar(out=nb[:, 0:16], in0=nb[:, 0:16], scalar1=-1.0 / 64, scalar2=None, op0=ALU.mult), [nb], [nb])
                    OP("dve", lambda e, nb=nb: e.tensor_tensor(out=h16(YBb), in0=h16(YBb), in1=nb[:, 0:16].unsqueeze(2).to_broadcast([128, 16, 64]),
                                                               op=ALU.add), [YBb, nb], [YBb])
                    OP("pool", lambda e: e.tensor_tensor(out=C2.t, in0=YBb.t, in1=YBb.t, op=ALU.mult), [YBb], [C2])
                    OP("dve", lambda e, nb=nb: e.tensor_reduce(out=nb[:, 16:32], in_=h16(C2), axis=AX.X, op=ALU.add), [C2], [nb])
                    OP("act", lambda e, nb=nb: e.activation(out=nb[:, 32:48], in_=nb[:, 16:32], func=AF.Sqrt, scale=1.0 / 64, bias=GN_EPS), [nb], [nb])
                    OP("dve", lambda e, nb=nb: e.reciprocal(out=nb[:, 48:64], in_=nb[:, 32:48]), [nb], [nb])
                    OP("dve", lambda e, nb=nb: e.tensor_tensor(out=h16(YBb), in0=h16(YBb), in1=nb[:, 48:64].unsqueeze(2).to_broadcast([128, 16, 64]),
                                                               op=ALU.mult), [YBb, nb], [YBb])
                    OP("dve", lambda e: e.tensor_tensor(out=YBb.t, in0=YBb.t, in1=gnt.t, op=ALU.mult), [YBb, gnt], [YBb])
                    OP("dve", lambda e, t=t: e.tensor_tensor(out=h16(C2), in0=h16(Vc), in1=bsum[:, t, :].unsqueeze(2).to_broadcast([128, 16, 64]),
                                                             op=ALU.mult), [Vc, bsum], [C2])
                    OP("dve", lambda e: e.tensor_tensor(out=YBb.t, in0=YBb.t, in1=C2.t, op=ALU.add), [YBb, C2], [YBb])
                    OP("act", lambda e: e.activation(out=Gc.t, in_=Gc.t, func=AF.Silu), [Gc], [Gc])
                    OP("dve", lambda e: e.tensor_tensor(out=YBb.t, in0=YBb.t, in1=Gc.t, op=ALU.mult), [YBb, Gc], [YBb])
                    transpose_blocks(lambda i: YBb.t[:, i * 128:(i + 1) * 128], YBb, 8,
                                     lambda i0, n, ti=ti: (ogT[:, i0:i0 + n, ti * 128:(ti + 1) * 128], ogb))
                ybufs = [bsub(12288, nt * 1024, "yacc")]
                tail(layer, tiles, c, ogT, ogb, 8, Wout, ybufs)

            for g in groups:
                c_group(g)
            S.barrier()

        def final_norm():
            S.dma(gnw[:, 0:1024], I["final_norm_w"].partition_broadcast(128), writes=[gnw])
            for t in range(16):
                xt = xt_rot.next()
                S.dma(xt[:], XB[t].t, reads=[XB[t]], writes=[xt])
                s = st1.next()
                xn = xn_rot.next()
                OP("act", lambda e, xt=xt, xn=xn, s=s: e.activation(out=xn[:], in_=xt[:], func=AF.Square,
                                                                    accum_out=s[:, 0:1]), [xt], [xn, s])
                OP("act", lambda e, s=s: e.activation(out=s[:, 1:2], in_=s[:, 0:1], func=AF.Sqrt, scale=1.0 / 1024,
                                                      bias=EPS), [s], [s])
                OP("dve", lambda e, s=s: e.reciprocal(out=s[:, 2:3], in_=s[:, 1:2]), [s], [s])
                OP("dve", lambda e, xt=xt, xn=xn, s=s: e.scalar_tensor_tensor(out=xn[:], in0=xt[:], scalar=s[:, 2:3],
                                                                              in1=gnw[:, 0:1024], op0=ALU.mult, op1=ALU.mult),
                   [xt, s, gnw], [xn])
                dst = O["yp"] if t < 8 else O["ys"]
                S.dma(dst[(t % 8) * 128:(t % 8 + 1) * 128, :], xn[:], reads=[xn], queue="act")

        LAYERS = {0: layer_ret, 1: layer_rwkv, 2: layer_diff, 3: layer_na}
        try:
            CK("setup")
            for layer in layers:
                mod(layer)
                CK("mod")
                LAYERS[layer](layer)
            if final:
                final_norm()
        except _Stop:
            pass
        S.emit()
        print(f"[build] ops={S.n_ops} waits={S.n_waits} dma_sems={S.ndsem}")
    return nc


def _prep_inputs(inp):
    cst = _consts()
    f = lambda a: np.ascontiguousarray(np.asarray(a, dtype=np.float32))
    shared = {
        "norm_w": f(inp["norm_w"]), "w_mod": f(inp["w_mod"]), "b_mod": f(inp["b_mod"]),
        "final_norm_w": f(inp["final_norm_w"]),
        "ret_w_in": f(inp["ret_w_in"][0]), "ret_decay": f(inp["ret_decay"][0]).reshape(8),
        "ret_gn": f(inp["ret_gn"][0]), "ret_w_out": f(inp["ret_w_out"][0]),
        "rwkv_mu": f(inp["rwkv_mu"][0]), "rwkv_w_in": f(inp["rwkv_w_in"][0]), "rwkv_w0": f(inp["rwkv_w0"][0]),
        "rwkv_wA": f(inp["rwkv_wA"][0]), "rwkv_wB": f(inp["rwkv_wB"][0]), "rwkv_a0": f(inp["rwkv_a0"][0]),
        "rwkv_aA": f(inp["rwkv_aA"][0]), "rwkv_aB": f(inp["rwkv_aB"][0]), "rwkv_kk": f(inp["rwkv_kk"][0]),
        "rwkv_ka": f(inp["rwkv_ka"][0]), "rwkv_rk": f(inp["rwkv_rk"][0]).reshape(1024),
        "rwkv_gn": f(inp["rwkv_gn"][0]), "rwkv_w_out": f(inp["rwkv_w_out"][0]),
        "diff_w_in": f(inp["diff_w_in"][0]), "diff_lambda": f(inp["diff_lambda"][0]).reshape(256),
        "diff_gn": f(inp["diff_gn"][0]), "diff_w_out": f(inp["diff_w_out"][0]),
        "na_w_in": f(inp["na_w_in"][0]), "na_bias_x": _na_bias_expand(f(inp["na_bias"][0])),
        "na_w_out": f(inp["na_w_out"][0]),
    }
    shared.update(cst)
    maps = []
    for c in range(8):
        b = c // 4
        m = dict(shared)
        m["xp"] = f(inp["x_prompt"][4 * c:4 * c + 4]).reshape(1024, 1024)
        m["xs"] = f(inp["x_sample"][b])
        m["cond"] = np.ascontiguousarray(np.stack([f(inp["c_ctx"]), f(inp["c"][b])], 0))
        m["state_ret"] = f(inp["state_ret"][b, 0])
        m["state_rwkv"] = f(inp["state_rwkv"][b, 0])
        m["cache_diff_k"] = f(inp["cache_diff_k"][b, 0])
        m["cache_diff_v"] = f(inp["cache_diff_v"][b, 0])
        m["cache_na_k"] = f(inp["cache_na_k"][b, 0])
        m["cache_na_v"] = f(inp["cache_na_v"][b, 0])
        maps.append(m)
    return maps


_NC_CACHE = {}


def kernel(**inputs):
    maps = _prep_inputs(inputs)
    if "nc" not in _NC_CACHE:
        _NC_CACHE["nc"] = build()
    res = run_bass_kernel_spmd(_NC_CACHE["nc"], maps, core_ids=list(range(8))).results
    y_prompt = np.concatenate([r["yp"].reshape(4, 256, 1024) for r in res], 0)
    y_sample = np.stack([res[0]["ys"], res[4]["ys"]], 0)
    st_ret = np.concatenate([r["st_ret"] for r in res], 0)[:, None]
    st_rwkv = np.concatenate([r["st_rwkv"] for r in res], 0)[:, None]
    dk = np.concatenate([r["dk"] for r in res], 0)[:, None]
    dv = np.concatenate([r["dv"] for r in res], 0)[:, None]
    nk = np.concatenate([r["nk"] for r in res], 0)[:, None]
    nv = np.concatenate([r["nv"] for r in res], 0)[:, None]
    return (y_prompt, y_sample, st_ret, st_rwkv, dk, dv, nk, nv)
```

```python
import math
from contextlib import ExitStack

import numpy as np
import concourse.bass as bass
import concourse.mybir as mybir
from concourse.bass_utils import run_bass_kernel_spmd

F32 = mybir.dt.float32
BF16 = mybir.dt.bfloat16
AF = mybir.ActivationFunctionType
ALU = mybir.AluOpType
AX = mybir.AxisListType

EPS = 1e-6
GN_EPS = 1e-5
NEG = -30000.0


class Buf:
    __slots__ = ("t", "name", "lw", "rd", "dsem", "dcnt", "excl")

    def __init__(self, t, name, excl=False):
        self.excl = excl
        self.t = t
        self.name = name
        self.lw = None
        self.rd = {}
        self.dsem = None
        self.dcnt = 0

    def __getitem__(self, k):
        return self.t[k]


class Sched:
    CE = ("pe", "act", "dve", "pool")
    ALLQ = ("pe", "act", "dve", "pool", "sp")

    def __init__(self, nc, stack):
        self.nc = nc
        self.stack = stack
        self.sems = {}
        self.ecnt = {}
        for e in self.CE:
            self.sems[e] = stack.enter_context(nc.semaphore("es_" + e))
            self.ecnt[e] = 0
        self.q = {e: [] for e in self.ALLQ}
        self.seen = {e: {} for e in self.ALLQ}
        self.nbuf = 0
        self.ndsem = 0
        self.n_ops = 0
        self.n_waits = 0
        self.dma_bufs = {}
        self.CONST = Buf(None, "const")
        self.const_bufs = []

    def sb(self, shape, dtype=F32, name="b"):
        self.nbuf += 1
        t = self.stack.enter_context(self.nc.sbuf_tensor(f"{name}_{self.nbuf}", list(shape), dtype))
        return Buf(t, name)

    def _waits(self, eng, reads, writes):
        ev = {}
        for b in reads:
            if b.lw is not None:
                k, v = b.lw
                if ev.get(k, 0) < v:
                    ev[k] = v
            if b.excl:
                for k, v in b.rd.items():
                    if k != eng and ev.get(k, 0) < v:
                        ev[k] = v
        for b in writes:
            if b.lw is not None:
                k, v = b.lw
                if ev.get(k, 0) < v:
                    ev[k] = v
            for k, v in b.rd.items():
                if ev.get(k, 0) < v:
                    ev[k] = v
        waits = []
        seen = self.seen[eng]
        for k, v in ev.items():
            if eng == "pe" and k == "pe":
                continue
            if seen.get(k, 0) >= v:
                continue
            seen[k] = v
            waits.append((k, v))
        self.n_waits += len(waits)
        return waits

    def _mark(self, me, reads, writes):
        k, v = me
        for b in reads:
            if b.rd.get(k, 0) < v:
                b.rd[k] = v
        for b in writes:
            b.lw = me
            b.rd = {}

    def op(self, eng, fn, reads=(), writes=()):
        waits = self._waits(eng, reads, writes)
        self.ecnt[eng] += 1
        me = (eng, self.ecnt[eng])
        self.q[eng].append((waits, fn, eng, 1))
        self._mark(me, reads, writes)
        self.n_ops += 1

    def dma(self, out_ap, in_ap, reads=(), writes=(), owner=None, queue="sp", **kw):
        if owner is self.CONST:
            waits = []
        else:
            waits = self._waits(queue, reads, writes)
        if owner is None:
            owner = writes[0] if writes else reads[0]
        if owner.dsem is None:
            self.ndsem += 1
            key = f"d{self.ndsem}"
            self.sems[key] = self.stack.enter_context(self.nc.semaphore("ds_" + key))
            owner.dsem = key
            self.dma_bufs[key] = owner
        owner.dcnt += 16
        me = (owner.dsem, owner.dcnt)
        self.q[queue].append((waits, (lambda e: e.dma_start(out=out_ap, in_=in_ap, **kw)), owner.dsem, 16))
        if owner is self.CONST:
            self.const_bufs.extend(writes)
        else:
            self._mark(me, reads, writes)
        self.n_ops += 1

    def consts_done(self):
        for b in self.const_bufs:
            b.lw = (self.CONST.dsem, self.CONST.dcnt)
        self.const_bufs = []

    def barrier(self):
        tot = [(e, self.ecnt[e]) for e in self.CE] + [(k, b.dcnt) for k, b in self.dma_bufs.items()]
        for eng in self.ALLQ:
            waits = []
            for k, v in tot:
                if k == eng or v == 0:
                    continue
                if self.seen[eng].get(k, 0) >= v:
                    continue
                self.seen[eng][k] = v
                waits.append((k, v))
            if waits:
                self.q[eng].append((waits, None, None, 0))

    def emit(self):
        nc = self.nc
        self.barrier()
        with nc.Block() as block:
            def run(engobj, name):
                import os as _os
                attach = _os.environ.get("KATTACH", "1") == "1"
                for waits, fn, semkey, inc in self.q[name]:
                    if fn is None or not attach or not waits or (name == "pe" and _os.environ.get("KATTACH_PE", "0") != "1"):
                        for k, v in waits:
                            engobj.wait_ge(self.sems[k], v)
                        if fn is not None:
                            fn(engobj).then_inc(self.sems[semkey], inc)
                    else:
                        for k, v in waits[:-1]:
                            engobj.wait_ge(self.sems[k], v)
                        k, v = waits[-1]
                        fn(engobj)._wait_ge(self.sems[k], v).then_inc(self.sems[semkey], inc)

            @block.tensor
            def _(e):
                run(e, "pe")

            @block.scalar
            def _(e):
                run(e, "act")

            @block.vector
            def _(e):
                run(e, "dve")

            @block.gpsimd
            def _(e):
                run(e, "pool")

            @block.sync
            def _(e):
                run(e, "sp")


class Rot:
    def __init__(self, bufs):
        self.bufs = bufs
        self.i = 0

    def next(self):
        b = self.bufs[self.i % len(self.bufs)]
        self.i += 1
        return b


def _rope_tables(d):
    t = np.arange(1024)
    row = (t // 64).astype(np.float32)
    col = (t % 64).astype(np.float32)
    inv = (np.float32(10000.0) ** (-np.arange(0, d, 2, dtype=np.float32) / np.float32(d))).astype(np.float32)
    ang = np.stack([row[:, None] * inv[None, :], col[:, None] * inv[None, :]], axis=1).astype(np.float32)
    return np.cos(ang).astype(np.float32), np.sin(ang).astype(np.float32)


def _consts():
    c = {}
    c0, s0 = _rope_tables(128)
    c2, s2 = _rope_tables(32)
    c["rope0"] = np.ascontiguousarray(np.stack([c0, s0], 0))
    c["rope2"] = np.ascontiguousarray(np.stack([c2, s2], 0))
    kk = np.arange(128)[:, None]
    cols = np.arange(15 * 128)[None, :]
    m = cols // 128 - 7
    qq = cols % 128
    gap = (128 * m + qq - kk).astype(np.float32)
    c["gpn"] = np.ascontiguousarray(np.stack([np.maximum(gap, 0), np.maximum(-gap, 0)], 0))
    p = np.arange(128, dtype=np.float32)
    c["stexp"] = np.ascontiguousarray(np.stack([255.0 - p, 127.0 - p, p, 128.0 + p], 1))
    a_ = np.arange(128)
    c["tri"] = np.ascontiguousarray(((a_[:, None] <= a_[None, :]) & (a_[:, None] // 64 == a_[None, :] // 64)).astype(np.float32))
    j_ = np.arange(64)[:, None]
    t_ = np.arange(64)[None, :]
    su = (j_ < t_).astype(np.float32)
    iu = (j_ <= t_).astype(np.float32)
    sl = (t_ < j_).astype(np.float32)
    cm = np.zeros((64, 3, 128), np.float32)
    cm[:, 0, 0:64] = -su
    cm[:, 0, 64:128] = iu
    cm[:, 1, 0:64] = su
    cm[:, 1, 64:128] = iu
    cm[:, 2, 0:64] = -sl
    c["cmask"] = cm
    c["iota1k"] = np.ascontiguousarray(np.broadcast_to(np.arange(1024, dtype=np.float32)[None, :], (128, 1024)))
    return c


def _na_bias_expand(na_bias):
    out = np.full((16, 5, 128, 576), NEG, np.float32)
    jt = [0, 1, 2, 6, 7]
    qc = np.arange(64)[:, None]
    kc = np.arange(64)[None, :]
    cs = np.clip(qc - 8, 0, 48)
    col_ok = (kc >= cs) & (kc < cs + 16)
    cidx = np.clip(kc - qc, -15, 15) + 15
    for ti, j in enumerate(jt):
        r0 = min(max(2 * j - 4, 0), 8)
        nr = min(9, 16 - r0)
        for a in range(2):
            qr = 2 * j + a
            st = min(max(qr - 4, 0), 8)
            for i in range(nr):
                kr = r0 + i
                if st <= kr < st + 8:
                    dr = kr - qr + 7
                    blk = na_bias[:, dr][:, cidx]
                    blk = np.where(col_ok[None], blk, np.float32(NEG))
                    out[:, ti, a * 64:(a + 1) * 64, i * 64:(i + 1) * 64] = blk
    return out


NA_JT = {0: 0, 1: 1, 2: 2, 3: 2, 4: 2, 5: 2, 6: 3, 7: 4}


class _Stop(Exception):
    pass


def build(layers=(0, 1, 2, 3), final=True, stop=None):
    nc = bass.Bass("TRN2", target_bir_lowering=False)

    def CK(name):
        if stop == name:
            raise _Stop()

    def din(name, shape):
        return nc.dram_tensor(name, list(shape), F32, kind="ExternalInput").ap()

    def dout(name, shape):
        return nc.dram_tensor(name, list(shape), F32, kind="ExternalOutput").ap()

    def dscr(name, shape):
        return nc.dram_tensor(name, list(shape), F32).ap()

    I = {}
    for name, shape in [
        ("xp", (1024, 1024)), ("xs", (1024, 1024)), ("cond", (2, 1024)),
        ("state_ret", (2, 4, 256, 512)), ("state_rwkv", (2, 16, 64, 64)),
        ("cache_diff_k", (8, 256, 128)), ("cache_diff_v", (8, 256, 128)),
        ("cache_na_k", (16, 256, 64)), ("cache_na_v", (16, 256, 64)),
        ("norm_w", (4, 1024)), ("w_mod", (4, 1024, 3072)), ("b_mod", (4, 3072)), ("final_norm_w", (1024,)),
        ("ret_w_in", (1024, 6144)), ("ret_decay", (8,)), ("ret_gn", (2048,)), ("ret_w_out", (2048, 1024)),
        ("rwkv_mu", (6, 1024)), ("rwkv_w_in", (1024, 4096)), ("rwkv_w0", (2, 1024)), ("rwkv_wA", (2, 1024, 64)),
        ("rwkv_wB", (2, 64, 1024)), ("rwkv_a0", (2, 1024)), ("rwkv_aA", (2, 1024, 64)), ("rwkv_aB", (2, 64, 1024)),
        ("rwkv_kk", (1024,)), ("rwkv_ka", (1024,)), ("rwkv_rk", (1024,)), ("rwkv_gn", (1024,)),
        ("rwkv_w_out", (1024, 1024)),
        ("diff_w_in", (1024, 4096)), ("diff_lambda", (256,)), ("diff_gn", (1024,)), ("diff_w_out", (1024, 1024)),
        ("na_w_in", (1024, 4096)), ("na_bias_x", (16, 5, 128, 576)), ("na_w_out", (1024, 1024)),
        ("rope0", (2, 1024, 2, 64)), ("rope2", (2, 1024, 2, 16)), ("gpn", (2, 128, 1920)), ("stexp", (128, 4)),
        ("iota1k", (128, 1024)), ("tri", (128, 128)), ("cmask", (64, 3, 128)),
    ]:
        I[name] = din(name, shape)
    O = {}
    for name, shape in [
        ("yp", (1024, 1024)), ("ys", (1024, 1024)), ("st_ret", (4, 2, 4, 256, 512)),
        ("st_rwkv", (4, 2, 16, 64, 64)), ("dk", (4, 8, 256, 128)), ("dv", (4, 8, 256, 128)),
        ("nk", (4, 16, 256, 64)), ("nv", (4, 16, 256, 64)), ("xd", (2048, 1024)),
    ]:
        O[name] = dout(name, shape)
    XD = O["xd"]

    with ExitStack() as st:
        S = Sched(nc, st)
        OP = S.op

        ident = S.sb([128, 128], F32, "ident")
        PSt = st.enter_context(nc.psum_tensor("ps", [128, 8, 512], F32))
        PB = [Buf(PSt[:, i, :], f"ps{i}", excl=True) for i in range(8)]
        pS_rot = Rot(PB[0:4])
        pP_rot = Rot(PB[4:8])

        wf = Rot([S.sb([128, 8, 256], F32, "wf") for _ in range(2)])
        wb = Rot([S.sb([128, 8, 256], BF16, "wb") for _ in range(2)])
        xt_rot = Rot([S.sb([128, 1024], F32, "xt") for _ in range(2)])
        xn_rot = Rot([S.sb([128, 1024], F32, "xn") for _ in range(1)])
        st1 = Rot([S.sb([128, 8], F32, "st") for _ in range(6)])
        BIGN = 27648
        BIG = st.enter_context(nc.sbuf_tensor("big", [128, BIGN], F32))
        R_H = Buf(BIG[:, 0:4096], "RH")
        R_G = Buf(BIG[:, 4096:12288], "RG")
        ATTN = 15360
        ATT = BIG[:, 12288:27648]
        Jm = S.sb([128, 128], F32, "J")
        identst = S.sb([128, 64], F32, "identst")
        bsum = S.sb([128, 16, 16], F32, "bsum")
        muF = S.sb([128, 6, 8], F32, "muF")

        def bsub(off, n, name):
            assert off + n <= BIGN, (off, n)
            return Buf(BIG[:, off:off + n], name)
        scond = S.sb([128, 8, 2], F32, "scond")
        condF = S.sb([128, 8, 2], F32, "condF")
        normwF = S.sb([128, 4, 8], F32, "normwF")
        bmodF = S.sb([128, 24], F32, "bmodF")
        modF = S.sb([128, 24, 2], F32, "modF")
        scaleF = S.sb([128, 8, 2], F32, "scaleF")
        Gb = [S.sb([128, 1024], F32, "G") for _ in range(2)]
        gbt = Rot([S.sb([128, 128], F32, "gbt") for _ in range(2)])
        gnw = S.sb([128, 2048], F32, "gnw")
        rope0 = S.sb([128, 2, 8, 128], F32, "rope0")
        rope2 = S.sb([128, 2, 8, 32], F32, "rope2")
        rtmp = Rot([S.sb([128, 128], F32, "rtmp") for _ in range(2)])
        tmpA = Rot([S.sb([128, 512], F32, "tmpA") for _ in range(3)])
        tmpB = Rot([S.sb([128, 512], F32, "tmpB") for _ in range(2)])
        lamc = S.sb([128, 8], F32, "lamc")
        dlb = S.sb([128, 256], F32, "dlb")
        lgt = S.sb([128, 16], F32, "lgt")
        stx = S.sb([128, 4], F32, "stx")
        dsc = S.sb([128, 4, 4], F32, "dsc")
        strip = Rot([S.sb([128, 1920], BF16, "strip") for _ in range(2)])
        osb = Rot([S.sb([128, 512], F32, "osb") for _ in range(2)])

        def sub(off, n, name):
            assert off + n <= ATTN, (off, n)
            return Buf(ATT[:, off:off + n], name)

        def bview(buf, off_f, shape):
            n = int(np.prod(shape))
            ap = buf.t[:, off_f:off_f + n // 2].bitcast(BF16)
            if len(shape) == 1:
                return ap
            names = " ".join(f"a{i}" for i in range(len(shape)))
            kw = {f"a{i}": shape[i] for i in range(len(shape) - 1)}
            return ap.rearrange(f"p ({names}) -> p {names}", **kw)

        def fview(buf, off_f, shape):
            n = int(np.prod(shape))
            ap = buf.t[:, off_f:off_f + n]
            if len(shape) == 1:
                return ap
            names = " ".join(f"a{i}" for i in range(len(shape)))
            kw = {f"a{i}": shape[i] for i in range(len(shape) - 1)}
            return ap.rearrange(f"p ({names}) -> p {names}", **kw)

        XB = [Buf(XD[t * 128:(t + 1) * 128, :], f"xd{t}") for t in range(16)]
        first_layer = layers[0]

        def x_src(layer, t):
            if layer == first_layer:
                src = I["xp"] if t < 8 else I["xs"]
                tt = t % 8
                return src[tt * 128:(tt + 1) * 128, :], []
            return XB[t].t, [XB[t]]

        C = S.CONST
        for c2 in range(2):
            S.dma(condF[:, :, c2], I["cond"][c2].rearrange("(k p) -> p k", p=128), writes=[condF], owner=C,
                  allow_slow_non_contiguous=True)
        for l2 in range(4):
            S.dma(normwF[:, l2, :], I["norm_w"][l2].rearrange("(k p) -> p k", p=128), writes=[normwF], owner=C,
                  allow_slow_non_contiguous=True)
        for cs in range(2):
            for d, (rt, hw) in enumerate(((rope0, 64), (rope2, 16))):
                src = I["rope0" if d == 0 else "rope2"][cs].rearrange("(t p) a f -> p t (a f)", p=128)
                S.dma(rt[:, cs, :, :], src, writes=[rt], owner=C)
        S.dma(stx[:], I["stexp"], writes=[stx], owner=C)
        S.consts_done()
        OP("pool", lambda e: e.memset(ident[:], 0.0), [], [ident])
        OP("pool", lambda e: e.affine_select(out=ident[:], in_=ident[:], pattern=[[-1, 128]], compare_op=ALU.not_equal,
                                             fill=1.0, base=0, channel_multiplier=1), [ident], [ident])
        OP("act", lambda e: e.activation(out=scond[:], in_=condF[:], func=AF.Silu), [condF], [scond])
        OP("pool", lambda e: e.tensor_tensor(out=identst[:], in0=ident[:, 0:64], in1=ident[:, 64:128], op=ALU.add), [ident], [identst])
        OP("pool", lambda e: e.memset(Jm[:], 0.0), [], [Jm])
        OP("pool", lambda e: e.affine_select(out=Jm[:], in_=Jm[:], pattern=[[1, 128]], compare_op=ALU.not_equal,
                                             fill=1.0, base=-127, channel_multiplier=1), [Jm], [Jm])

        wctr = [0]

        def load_w(W, k0, c0, ncols=256, cast=True):
            f = wf.next()
            src = W[k0:k0 + 1024, c0:c0 + ncols].rearrange("(k p) n -> p k n", p=128)
            S.dma(f[:, :, 0:ncols], src, writes=[f])
            if not cast:
                return f
            b = wb.next()
            wctr[0] += 1
            if wctr[0] % 2:
                OP("act", lambda e: e.copy(out=b[:, :, 0:ncols], in_=f[:, :, 0:ncols]), [f], [b])
            else:
                OP("dve", lambda e: e.tensor_copy(out=b[:, :, 0:ncols], in_=f[:, :, 0:ncols]), [f], [b])
            return b

        def mod(layer):
            S.dma(bmodF[:], I["b_mod"][layer].rearrange("(m p) -> p m", p=128), writes=[bmodF],
                  allow_slow_non_contiguous=True)
            psm = pP_rot.next()
            for blk in range(12):
                w = load_w(I["w_mod"][layer], 0, blk * 256, cast=False)
                for mm in range(2):
                    m = blk * 2 + mm
                    for k in range(8):
                        OP("pe", lambda e, m=m, mm=mm, k=k, w=w: e.matmul(
                            out=psm[:, m * 2:m * 2 + 2], lhsT=w[:, k, mm * 128:(mm + 1) * 128], rhs=scond[:, k, :],
                            start=(k == 0), stop=(k == 7)), [w, scond], [psm])
            OP("dve", lambda e: e.tensor_tensor(out=modF[:], in0=psm[:, 0:48].rearrange("p (m c) -> p m c", c=2),
                                                in1=bmodF[:].unsqueeze(2).to_broadcast([128, 24, 2]), op=ALU.add),
               [psm, bmodF], [modF])
            OP("dve", lambda e: e.tensor_scalar(out=scaleF[:], in0=modF[:, 8:16, :], scalar1=1.0, scalar2=None,
                                                op0=ALU.add), [modF], [scaleF])
            OP("dve", lambda e: e.tensor_tensor(out=scaleF[:], in0=scaleF[:],
                                                in1=normwF[:, layer, :].unsqueeze(2).to_broadcast([128, 8, 2]),
                                                op=ALU.mult), [scaleF, normwF], [scaleF])
            for c in range(2):
                pg = [pP_rot.next(), pP_rot.next()]
                for k in range(8):
                    g = gbt.next()
                    OP("dve", lambda e, g=g, k=k, c=c: e.tensor_copy(
                        out=g[:], in_=modF[:, 16 + k, c:c + 1].to_broadcast([128, 128])), [modF], [g])
                    OP("pe", lambda e, g=g, k=k, pg=pg: e.matmul(
                        out=pg[k // 4][:, (k % 4) * 128:(k % 4 + 1) * 128], lhsT=g[:], rhs=ident[:],
                        start=True, stop=True), [g, ident], [pg[k // 4]])
                for hlf in range(2):
                    OP("act", lambda e, hlf=hlf, c=c, pg=pg: e.copy(out=Gb[c][:, hlf * 512:(hlf + 1) * 512],
                                                                    in_=pg[hlf][:, :]), [pg[hlf]], [Gb[c]])

        def front(layer, tiles, c, hT_of):
            for ti, t in enumerate(tiles):
                xt = xt_rot.next()
                src, rb = x_src(layer, t)
                S.dma(xt[:], src, reads=rb, writes=[xt])
                s = st1.next()
                xn = xn_rot.next()
                OP("act", lambda e, xt=xt, xn=xn, s=s: e.activation(out=xn[:], in_=xt[:], func=AF.Square,
                                                                    accum_out=s[:, 0:1]), [xt], [xn, s])
                OP("act", lambda e, s=s: e.activation(out=s[:, 1:2], in_=s[:, 0:1], func=AF.Sqrt, scale=1.0 / 1024,
                                                      bias=EPS), [s], [s])
                OP("dve", lambda e, s=s: e.reciprocal(out=s[:, 2:3], in_=s[:, 1:2]), [s], [s])
                OP("dve", lambda e, xt=xt, xn=xn, s=s: e.tensor_scalar(out=xn[:], in0=xt[:], scalar1=s[:, 2:3],
                                                                       scalar2=None, op0=ALU.mult), [xt, s], [xn])
                pp = [pP_rot.next(), pP_rot.next()]
                for k in range(8):
                    OP("pe", lambda e, k=k, xn=xn, pp=pp: e.transpose(
                        out=pp[k // 4][:, (k % 4) * 128:(k % 4 + 1) * 128], in_=xn[:, k * 128:(k + 1) * 128],
                        identity=ident[:]), [xn, ident], [pp[k // 4]])
                for k in range(8):
                    dst, db = hT_of(k, ti)
                    OP("act", lambda e, k=k, dst=dst, pp=pp: e.activation(
                        out=dst, in_=pp[k // 4][:, (k % 4) * 128:(k % 4 + 1) * 128], func=AF.Identity,
                        scale=scaleF[:, k, c:c + 1], bias=modF[:, k, c:c + 1]), [pp[k // 4], scaleF, modF], [db])

        def proj(hT, hbuf, ti, w, ncols, ps):
            for k in range(8):
                OP("pe", lambda e, k=k: e.matmul(out=ps[:, 0:ncols], lhsT=hT[:, k, ti * 128:(ti + 1) * 128],
                                                 rhs=w[:, k, 0:ncols], start=(k == 0), stop=(k == 7)),
                   [hbuf, w], [ps])

        def transpose_blocks(src_ap_of, srcbuf, nblk, dst_of, evac="act"):
            i = 0
            while i < nblk:
                n = min(4, nblk - i)
                ps = pP_rot.next()
                for j in range(n):
                    OP("pe", lambda e, i=i, j=j, ps=ps: e.transpose(out=ps[:, j * 128:(j + 1) * 128],
                                                                    in_=src_ap_of(i + j), identity=ident[:]),
                       [srcbuf, ident], [ps])
                dst, db = dst_of(i, n)
                if evac == "act":
                    OP("act", lambda e, dst=dst, ps=ps, n=n: e.copy(
                        out=dst, in_=ps[:, 0:n * 128].rearrange("p (a b) -> p a b", a=n)), [ps], [db])
                else:
                    OP("dve", lambda e, dst=dst, ps=ps, n=n: e.tensor_copy(
                        out=dst, in_=ps[:, 0:n * 128].rearrange("p (a b) -> p a b", a=n)), [ps], [db])
                i += n

        def rope(src, dst, table, half, tile, ngrp):
            n = ngrp * 4 * half
            sv = src[:, 0:n].rearrange("p (g a two f) -> p g a two f", g=ngrp, a=2, two=2)
            dv = dst[:, 0:n].rearrange("p (g a two f) -> p g a two f", g=ngrp, a=2, two=2)
            cos = table[:, 0, tile, :].rearrange("p (a f) -> p a f", a=2).unsqueeze(1).to_broadcast([128, ngrp, 2, half])
            sin = table[:, 1, tile, :].rearrange("p (a f) -> p a f", a=2).unsqueeze(1).to_broadcast([128, ngrp, 2, half])
            x1, x2 = sv[:, :, :, 0, :], sv[:, :, :, 1, :]
            o1, o2 = dv[:, :, :, 0, :], dv[:, :, :, 1, :]
            t1 = rtmp.next()
            t2 = rtmp.next()
            m = ngrp * 2 * half
            t1v = t1[:, 0:m].rearrange("p (g a f) -> p g a f", g=ngrp, a=2)
            t2v = t2[:, 0:m].rearrange("p (g a f) -> p g a f", g=ngrp, a=2)
            OP("dve", lambda e: e.tensor_tensor(out=o1, in0=x1, in1=cos, op=ALU.mult), [src, table], [dst])
            OP("pool", lambda e: e.tensor_tensor(out=t1v, in0=x2, in1=sin, op=ALU.mult), [src, table], [t1])
            OP("dve", lambda e: e.tensor_tensor(out=o1, in0=o1, in1=t1v, op=ALU.subtract), [dst, t1], [dst])
            OP("pool", lambda e: e.tensor_tensor(out=t2v, in0=x1, in1=sin, op=ALU.mult), [src, table], [t2])
            OP("dve", lambda e: e.tensor_tensor(out=o2, in0=x2, in1=cos, op=ALU.mult), [src, table], [dst])
            OP("dve", lambda e: e.tensor_tensor(out=o2, in0=o2, in1=t2v, op=ALU.add), [dst, t2], [dst])

        def tail(layer, tiles, c, ogT, ogbuf, KC, Wout, ybufs):
            nt = len(tiles)
            yacc = ATT[:, 0:nt * 1024].rearrange("p (t n) -> p t n", t=nt)
            for cb in range(4):
                ws = [load_w(Wout, kk * 1024, cb * 256) for kk in range(KC // 8)]
                for ti in range(nt):
                    ps = pP_rot.next()
                    for kc in range(KC):
                        w = ws[kc // 8]
                        OP("pe", lambda e, kc=kc, w=w, ti=ti, ps=ps: e.matmul(
                            out=ps[:, 0:256], lhsT=ogT[:, kc, ti * 128:(ti + 1) * 128], rhs=w[:, kc % 8, :],
                            start=(kc == 0), stop=(kc == KC - 1)), [ogbuf, w], [ps])
                    OP("dve", lambda e, ti=ti, cb=cb, ps=ps: e.tensor_tensor(
                        out=yacc[:, ti, cb * 256:(cb + 1) * 256], in0=ps[:, 0:256],
                        in1=Gb[c][:, cb * 256:(cb + 1) * 256], op=ALU.mult), [ps, Gb[c]], ybufs)
            for ti, t in enumerate(tiles):
                xt = xt_rot.next()
                src, rb = x_src(layer, t)
                S.dma(xt[:], src, reads=rb, writes=[xt])
                OP("dve", lambda e, xt=xt, ti=ti: e.tensor_tensor(out=xt[:], in0=xt[:], in1=yacc[:, ti, :], op=ALU.add),
                   [xt] + ybufs, [xt])
                S.dma(XB[t].t, xt[:], reads=[xt], writes=[XB[t]], owner=xt, queue="act")

        def softmax_un(src, srcbufs, scale, Pout, Pbuf):
            s = st1.next()
            ax = AX.XY if len(src.shape) == 3 else AX.X
            OP("dve", lambda e: e.tensor_reduce(out=s[:, 0:1], in_=src, axis=ax, op=ALU.max), srcbufs, [s])
            OP("dve", lambda e: e.tensor_scalar(out=s[:, 1:2], in0=s[:, 0:1], scalar1=-scale, scalar2=None,
                                                op0=ALU.mult), [s], [s])
            OP("act", lambda e: e.activation(out=Pout, in_=src, func=AF.Exp, scale=scale, bias=s[:, 1:2],
                                             accum_out=s[:, 2:3]), srcbufs + [s], [Pbuf, s])
            OP("dve", lambda e: e.reciprocal(out=s[:, 3:4], in_=s[:, 2:3]), [s], [s])
            return s

        def pv(pc, pcbuf, kblocks, pcT, pcTbuf, vof, N, pso):
            nb = len(kblocks)
            i = 0
            while i < nb:
                n = min(4, nb - i)
                ps = pP_rot.next()
                for j in range(n):
                    off, nk = kblocks[i + j]
                    OP("pe", lambda e, j=j, off=off, nk=nk, ps=ps: e.transpose(
                        out=ps[0:nk, j * 128:(j + 1) * 128], in_=pc[:, off:off + nk], identity=ident[:]),
                       [pcbuf, ident], [ps])
                full = all(kblocks[i + j][1] == 128 for j in range(n))
                if full:
                    OP("act", lambda e, i=i, n=n, ps=ps: e.copy(
                        out=pcT[:, i:i + n, :], in_=ps[:, 0:n * 128].rearrange("p (a b) -> p a b", a=n)),
                       [ps], [pcTbuf])
                else:
                    for j in range(n):
                        nk = kblocks[i + j][1]
                        OP("act", lambda e, i=i, j=j, nk=nk, ps=ps: e.copy(
                            out=pcT[0:nk, i + j, :], in_=ps[0:nk, j * 128:(j + 1) * 128]), [ps], [pcTbuf])
                i += n
            for i, (off, nk) in enumerate(kblocks):
                vap, vbuf = vof(i)
                OP("pe", lambda e, i=i, nk=nk, vap=vap: e.matmul(out=pso[:, 0:N], lhsT=pcT[0:nk, i, :], rhs=vap,
                                                               start=(i == 0), stop=(i == nb - 1)),
                   [pcTbuf, vbuf], [pso])

        groups = [
            dict(tiles=[0, 1, 2, 3, 4, 5, 6, 7], c=0, seqs=[(0, 2, 0), (2, 2, 1), (4, 2, 2), (6, 2, 3)], sample=False, pair=0),
            dict(tiles=list(range(8, 16)), c=1, seqs=[(0, 8, -1)], sample=True, pair=-1),
        ]

        def layer_diff(layer):
            lam_init = 0.8 - 0.6 * math.exp(-0.3 * layer)
            Win, Wout = I["diff_w_in"], I["diff_w_out"]
            S.dma(gnw[:, 0:1024], I["diff_gn"].partition_broadcast(128), writes=[gnw])
            OP("dve", lambda e: e.tensor_scalar(out=gnw[:, 0:1024], in0=gnw[:, 0:1024], scalar1=1.0 - lam_init,
                                                scalar2=None, op0=ALU.mult), [gnw], [gnw])
            S.dma(dlb[:], I["diff_lambda"].partition_broadcast(128), writes=[dlb])
            dl = dlb[:].rearrange("p (a f) -> p a f", a=4)
            t = tmpA.next()
            for i2 in range(2):
                OP("dve", lambda e, i2=i2: e.tensor_tensor(out=t[:, i2 * 64:(i2 + 1) * 64], in0=dl[:, 2 * i2, :],
                                                           in1=dl[:, 2 * i2 + 1, :], op=ALU.mult), [dlb], [t])
            OP("dve", lambda e: e.tensor_reduce(out=lamc[:, 0:2], in_=t[:, 0:128].rearrange("p (a f) -> p a f", a=2),
                                                axis=AX.X, op=ALU.add), [t], [lamc])
            OP("act", lambda e: e.activation(out=lamc[:, 2:4], in_=lamc[:, 0:2], func=AF.Exp), [lamc], [lamc])
            OP("dve", lambda e: e.tensor_tensor(out=lamc[:, 4:5], in0=lamc[:, 2:3], in1=lamc[:, 3:4], op=ALU.subtract),
               [lamc], [lamc])
            OP("dve", lambda e: e.tensor_scalar(out=lamc[:, 5:6], in0=lamc[:, 4:5], scalar1=lam_init, scalar2=None,
                                                op0=ALU.add), [lamc], [lamc])
            scale = 64 ** -0.5

            def do_group(g):
                S.barrier()
                tiles, c, smp = g["tiles"], g["c"], g["sample"]
                nt = len(tiles)
                T = nt * 128
                NK = T + 256 if smp else 256
                hT = bview(R_H, 0, [8, T])
                ogT = bview(R_G, 0, [8, T])
                qTb = sub(0, 1024, "qT")
                kTb = sub(1024, 1280, "kT")
                vbb = sub(2304, 1280, "vb")
                Psets = [(sub(3584, 1280, "P1a"), sub(4864, 1280, "P2a"), sub(7424, 640, "pcTa")),
                         (sub(6144, 1280, "P1b"), sub(11136, 1280, "P2b"), sub(12416, 640, "pcTb"))]
                obb = sub(8064, 2048, "ob")
                ckb = sub(10112, 1024, "ck")
                qT = bview(qTb, 0, [2, 1024])
                kT = bview(kTb, 0, [2, 1280])
                vb = bview(vbb, 0, [10, 256])
                ob = fview(obb, 0, [8, 256])
                ck = fview(ckb, 0, [2, 512])
                front(layer, tiles, c, lambda k, ti: (hT[:, k, ti * 128:(ti + 1) * 128], R_H))
                CK("front")
                for hb in range(4):
                    for which, dstT, dstb in ((0, qT, qTb), (1, kT, kTb)):
                        w = load_w(Win, 0, which * 1024 + hb * 256)
                        for ti in range(nt):
                            ps = pP_rot.next()
                            proj(hT, R_H, ti, w, 256, ps)
                            ta = tmpA.next()
                            OP("act", lambda e, ta=ta, ps=ps: e.copy(out=ta[:, 0:256], in_=ps[:, 0:256]), [ps], [ta])
                            srcb = ta
                            if smp:
                                tb = tmpB.next()
                                rope(ta, tb, rope2, 16, ti, 4)
                                srcb = tb
                            elif which == 1:
                                sq = g["seqs"][ti // 2][2]
                                tt = ti % 2
                                S.dma(O["dk"][sq, 2 * hb:2 * hb + 2, tt * 128:(tt + 1) * 128, :].rearrange("h t d -> t h d"),
                                      ta[:, 0:256].rearrange("p (h d) -> p h d", h=2), reads=[ta], queue="act")
                            transpose_blocks(lambda i, srcb=srcb: srcb[:, i * 128:(i + 1) * 128], srcb, 2,
                                             lambda i0, n, dstT=dstT, dstb=dstb, ti=ti: (dstT[:, i0:i0 + n, ti * 128:(ti + 1) * 128], dstb))
                    CK("qk")
                    w = load_w(Win, 0, 2048 + hb * 256)
                    for ti in range(nt):
                        ps = pP_rot.next()
                        proj(hT, R_H, ti, w, 256, ps)
                        ta = tmpA.next()
                        OP("act", lambda e, ta=ta, ps=ps: e.copy(out=ta[:, 0:256], in_=ps[:, 0:256]), [ps], [ta])
                        OP("dve", lambda e, ta=ta, ti=ti: e.tensor_copy(out=vb[:, ti, :], in_=ta[:, 0:256]), [ta], [vbb])
                        if not smp:
                            sq = g["seqs"][ti // 2][2]
                            tt = ti % 2
                            S.dma(O["dv"][sq, 2 * hb:2 * hb + 2, tt * 128:(tt + 1) * 128, :].rearrange("h t d -> t h d"),
                                  ta[:, 0:256].rearrange("p (h d) -> p h d", h=2), reads=[ta], queue="act")
                    if smp:
                        for tt in range(2):
                            S.dma(ck[:, 0, 0:256].rearrange("p (h d) -> p h d", h=2),
                                  I["cache_diff_k"][2 * hb:2 * hb + 2, tt * 128:(tt + 1) * 128, :].rearrange("h t d -> t h d"),
                                  writes=[ckb])
                            transpose_blocks(lambda i: ck[:, 0, i * 128:(i + 1) * 128], ckb, 2,
                                             lambda i0, n, tt=tt: (kT[:, i0:i0 + n, 1024 + tt * 128:1024 + (tt + 1) * 128], kTb))
                            S.dma(ck[:, 1, 0:256].rearrange("p (h d) -> p h d", h=2),
                                  I["cache_diff_v"][2 * hb:2 * hb + 2, tt * 128:(tt + 1) * 128, :].rearrange("h t d -> t h d"),
                                  writes=[ckb])
                            OP("dve", lambda e, tt=tt: e.tensor_copy(out=vb[:, 8 + tt, :], in_=ck[:, 1, 0:256]), [ckb], [vbb])
                    CK("v")
                    nkt = NK // 128
                    nblk, blk = (1, 256) if NK == 256 else (4, 320)

                    def att_s1(t0, hh, qi, k_):
                        P1b_, P2b_ = Psets[k_][0], Psets[k_][1]
                        P1_, P2_ = P1b_.t, P2b_.t
                        tq = t0 + qi
                        stats = []
                        for comp, (Pb, Pap) in enumerate(((P1b_, P1_), (P2b_, P2_))):
                            pr = slice(comp * 64, (comp + 1) * 64)
                            for b in range(nblk):
                                k0 = (t0 * 128 if not smp else 0) + b * blk
                                OP("pe", lambda e, pr=pr, k0=k0, b=b: e.matmul(
                                    out=PB[b][:, 0:blk], lhsT=qT[pr, hh, tq * 128:(tq + 1) * 128],
                                    rhs=kT[pr, hh, k0:k0 + blk], start=True, stop=True), [qTb, kTb], [PB[b]])
                            src = PSt[:, 0:nblk, 0:blk]
                            stats.append(softmax_un(src, PB[0:nblk], scale, Pap[:, 0:NK].rearrange("p (a b) -> p a b", a=nblk), Pb))
                        s1, s2 = stats
                        OP("dve", lambda e: e.tensor_tensor(out=s2[:, 4:5], in0=s2[:, 3:4], in1=lamc[:, 5:6], op=ALU.mult), [s2, lamc], [s2])
                        OP("dve", lambda e: e.tensor_scalar(out=P2_[:, 0:NK], in0=P2_[:, 0:NK], scalar1=s2[:, 4:5], scalar2=None, op0=ALU.mult),
                           [P2b_, s2], [P2b_])
                        OP("dve", lambda e: e.scalar_tensor_tensor(out=P1_[:, 0:NK], in0=P1_[:, 0:NK], scalar=s1[:, 3:4], in1=P2_[:, 0:NK],
                                                                   op0=ALU.mult, op1=ALU.subtract), [P1b_, P2b_, s1], [P1b_])
                        return (t0, hh, qi, k_)

                    def att_s2(ctx):
                        t0, hh, qi, k_ = ctx
                        P1b_, pcTb_ = Psets[k_][0], Psets[k_][2]
                        pcT_ = bview(pcTb_, 0, [10, 128])
                        tq = t0 + qi
                        h = 2 * hb + hh
                        pso = pP_rot.next()
                        vt0 = 0 if smp else t0
                        pv(P1b_.t, P1b_, [(i * 128, 128) for i in range(nkt)], pcT_, pcTb_,
                           lambda i: (vb[:, vt0 + i, hh * 128:(hh + 1) * 128], vbb), 128, pso)
                        s = st1.next()
                        ta = tmpA.next()
                        OP("act", lambda e: e.activation(out=ta[:, 0:128], in_=pso[:, 0:128], func=AF.Square, accum_out=s[:, 0:1]), [pso], [ta, s])
                        OP("act", lambda e: e.activation(out=s[:, 1:2], in_=s[:, 0:1], func=AF.Sqrt, scale=1.0 / 128, bias=EPS), [s], [s])
                        OP("dve", lambda e: e.reciprocal(out=s[:, 2:3], in_=s[:, 1:2]), [s], [s])
                        OP("dve", lambda e: e.scalar_tensor_tensor(out=ob[:, tq, hh * 128:(hh + 1) * 128], in0=pso[:, 0:128], scalar=s[:, 2:3],
                                                                   in1=gnw[:, h * 128:(h + 1) * 128], op0=ALU.mult, op1=ALU.mult), [pso, s, gnw], [obb])

                    its = [(t0, hh, qi) for (t0, ntq, sq) in g["seqs"] for hh in range(2) for qi in range(ntq)]
                    prev = None
                    for ii, (t0, hh, qi) in enumerate(its):
                        ctx = att_s1(t0, hh, qi, ii % 2)
                        if prev is not None:
                            att_s2(prev)
                        prev = ctx
                    att_s2(prev)
                    CK("attn")
                    w = load_w(Win, 0, 3072 + hb * 256)
                    for ti in range(nt):
                        ps = pP_rot.next()
                        proj(hT, R_H, ti, w, 256, ps)
                        ta = tmpA.next()
                        OP("act", lambda e, ta=ta, ps=ps: e.activation(out=ta[:, 0:256], in_=ps[:, 0:256], func=AF.Silu), [ps], [ta])
                        OP("dve", lambda e, ta=ta, ti=ti: e.tensor_tensor(out=ob[:, ti, :], in0=ob[:, ti, :], in1=ta[:, 0:256],
                                                                          op=ALU.mult), [obb, ta], [obb])
                        transpose_blocks(lambda i, ti=ti: ob[:, ti, i * 128:(i + 1) * 128], obb, 2,
                                         lambda i0, n, ti=ti, hb=hb: (ogT[:, 2 * hb + i0:2 * hb + i0 + n, ti * 128:(ti + 1) * 128], R_G))
                S.barrier()
                ybufs = [Buf(ATT[:, 0:nt * 1024], "yacc")]
                tail(layer, tiles, c, ogT, R_G, 8, Wout, ybufs)

            for g in groups:
                do_group(g)
            S.barrier()

        nab_rot = Rot([S.sb([128, 576], F32, "nab") for _ in range(2)])

        def layer_na(layer):
            Win, Wout = I["na_w_in"], I["na_w_out"]
            scale = 64 ** -0.5

            def do_group(g):
                S.barrier()
                tiles, c, smp = g["tiles"], g["c"], g["sample"]
                nt = len(tiles)
                T = nt * 128
                hT = bview(R_H, 0, [8, T])
                ogT = bview(R_G, 0, [8, T])
                qTb = sub(0, 1024, "qT")
                kTb = sub(1024, 1280, "kT")
                vbb = sub(2304, 1280, "vb")
                NAsets = [(sub(3584, 1280, "P1a"), sub(7424, 640, "pcTa")), (sub(6144, 1280, "P1b"), sub(12416, 640, "pcTb"))]
                obb = sub(8064, 2048, "ob")
                ckb = sub(10112, 1024, "ck")
                qT = bview(qTb, 0, [2, 1024])
                kT = bview(kTb, 0, [2, 1280])
                vb = bview(vbb, 0, [10, 256])
                ob = fview(obb, 0, [8, 256])
                ck = fview(ckb, 0, [2, 512])
                front(layer, tiles, c, lambda k, ti: (hT[:, k, ti * 128:(ti + 1) * 128], R_H))
                for hb in range(4):
                    for which, dstT, dstb in ((0, qT, qTb), (1, kT, kTb)):
                        w = load_w(Win, 0, which * 1024 + hb * 256)
                        for ti in range(nt):
                            ps = pP_rot.next()
                            proj(hT, R_H, ti, w, 256, ps)
                            ta = tmpA.next()
                            OP("act", lambda e, ta=ta, ps=ps: e.copy(out=ta[:, 0:256], in_=ps[:, 0:256]), [ps], [ta])
                            if which == 1 and not smp:
                                sq = g["seqs"][ti // 2][2]
                                tt = ti % 2
                                S.dma(O["nk"][sq, 4 * hb:4 * hb + 4, tt * 128:(tt + 1) * 128, :].rearrange("h t d -> t h d"),
                                      ta[:, 0:256].rearrange("p (h d) -> p h d", h=4), reads=[ta], queue="act")
                            transpose_blocks(lambda i, ta=ta: ta[:, i * 128:(i + 1) * 128], ta, 2,
                                             lambda i0, n, dstT=dstT, dstb=dstb, ti=ti: (dstT[:, i0:i0 + n, ti * 128:(ti + 1) * 128], dstb))
                    w = load_w(Win, 0, 2048 + hb * 256)
                    for ti in range(nt):
                        ps = pP_rot.next()
                        proj(hT, R_H, ti, w, 256, ps)
                        ta = tmpA.next()
                        OP("act", lambda e, ta=ta, ps=ps: e.copy(out=ta[:, 0:256], in_=ps[:, 0:256]), [ps], [ta])
                        OP("dve", lambda e, ta=ta, ti=ti: e.tensor_copy(out=vb[:, ti, :], in_=ta[:, 0:256]), [ta], [vbb])
                        if not smp:
                            sq = g["seqs"][ti // 2][2]
                            tt = ti % 2
                            S.dma(O["nv"][sq, 4 * hb:4 * hb + 4, tt * 128:(tt + 1) * 128, :].rearrange("h t d -> t h d"),
                                  ta[:, 0:256].rearrange("p (h d) -> p h d", h=4), reads=[ta], queue="act")
                    if smp:
                        for tt in range(2):
                            S.dma(ck[:, 0, 0:256].rearrange("p (h d) -> p h d", h=4),
                                  I["cache_na_k"][4 * hb:4 * hb + 4, tt * 128:(tt + 1) * 128, :].rearrange("h t d -> t h d"),
                                  writes=[ckb])
                            transpose_blocks(lambda i: ck[:, 0, i * 128:(i + 1) * 128], ckb, 2,
                                             lambda i0, n, tt=tt: (kT[:, i0:i0 + n, 1024 + tt * 128:1024 + (tt + 1) * 128], kTb))
                            S.dma(ck[:, 1, 0:256].rearrange("p (h d) -> p h d", h=4),
                                  I["cache_na_v"][4 * hb:4 * hb + 4, tt * 128:(tt + 1) * 128, :].rearrange("h t d -> t h d"),
                                  writes=[ckb])
                            OP("dve", lambda e, tt=tt: e.tensor_copy(out=vb[:, 8 + tt, :], in_=ck[:, 1, 0:256]), [ckb], [vbb])
                    def att_s1(t0, hh, qi, k_):
                            P1b, pcTb = NAsets[k_]
                            P1 = P1b.t
                            h = 4 * hb + hh
                            cc = hh // 2
                            pr = slice((hh % 2) * 64, (hh % 2) * 64 + 64)
                            if True:
                                tq = t0 + qi
                                if not smp:
                                    OP("pe", lambda e, pr=pr, cc=cc, tq=tq, t0=t0: e.matmul(
                                        out=PB[0][:, 0:256], lhsT=qT[pr, cc, tq * 128:(tq + 1) * 128],
                                        rhs=kT[pr, cc, t0 * 128:t0 * 128 + 256], start=True, stop=True), [qTb, kTb], [PB[0]])
                                    s1 = softmax_un(PB[0][:, 0:256], [PB[0]], scale, P1[:, 0:256], P1b)
                                    NKs = 256
                                    kblocks = [(0, 128), (128, 128)]
                                    vof = lambda i, hh=hh, t0=t0: (vb[:, t0 + i, hh * 64:(hh + 1) * 64], vbb)
                                else:
                                    j = qi
                                    r0 = min(max(2 * j - 4, 0), 8)
                                    nr = min(9, 16 - r0)
                                    nloc = nr * 64
                                    NKs = nloc + 256
                                    blk = NKs // 2
                                    nab = nab_rot.next()
                                    S.dma(nab[:], I["na_bias_x"][h, NA_JT[j]], writes=[nab])
                                    segs = [(r0 * 64, nloc, 0, True), (1024, 256, nloc, False)]
                                    pieces = []
                                    for key0, n, col0, biased in segs:
                                        done = 0
                                        while done < n:
                                            col = col0 + done
                                            b = col // blk
                                            m = min(n - done, (b + 1) * blk - col)
                                            pieces.append((key0 + done, m, b, col - b * blk, col, biased, col0 + done - col0 + (0 if not biased else 0)))
                                            done += m
                                    for (k0, m, b, bc, col, biased, _) in pieces:
                                        OP("pe", lambda e, pr=pr, cc=cc, tq=tq, k0=k0, m=m, b=b, bc=bc: e.matmul(
                                            out=PB[b][:, bc:bc + m], lhsT=qT[pr, cc, tq * 128:(tq + 1) * 128],
                                            rhs=kT[pr, cc, k0:k0 + m], start=True, stop=True), [qTb, kTb], [PB[b]])
                                    for (k0, m, b, bc, col, biased, _) in pieces:
                                        if biased:
                                            OP("dve", lambda e, m=m, b=b, bc=bc, col=col, nab=nab: e.scalar_tensor_tensor(
                                                out=P1[:, col:col + m], in0=PB[b][:, bc:bc + m], scalar=scale,
                                                in1=nab[:, col:col + m], op0=ALU.mult, op1=ALU.add), [PB[b], nab], [P1b])
                                        else:
                                            OP("dve", lambda e, m=m, b=b, bc=bc, col=col: e.tensor_scalar(
                                                out=P1[:, col:col + m], in0=PB[b][:, bc:bc + m], scalar1=scale, scalar2=None,
                                                op0=ALU.mult), [PB[b]], [P1b])
                                    s1 = softmax_un(P1[:, 0:NKs], [P1b], 1.0, P1[:, 0:NKs], P1b)
                                    kblocks = [(i * 128, 128) for i in range(nloc // 128)]
                                    if nloc % 128:
                                        kblocks.append((nloc - 64, 64))
                                    nlb = len(kblocks)
                                    kblocks += [(nloc, 128), (nloc + 128, 128)]

                                    def vof(i, hh=hh, r0=r0, nlb=nlb, kblocks=kblocks):
                                        if i < nlb:
                                            nk = kblocks[i][1]
                                            return vb[0:nk, r0 // 2 + i, hh * 64:(hh + 1) * 64], vbb
                                        return vb[:, 8 + (i - nlb), hh * 64:(hh + 1) * 64], vbb
                                OP("dve", lambda e, s1=s1, NKs=NKs: e.tensor_scalar(out=P1[:, 0:NKs], in0=P1[:, 0:NKs], scalar1=s1[:, 3:4],
                                                                                scalar2=None, op0=ALU.mult), [P1b, s1], [P1b])
                                return (tq, hh, k_, kblocks, vof)

                    def att_s2(ctx):
                        tq, hh, k_, kblocks, vof = ctx
                        P1b, pcTb = NAsets[k_]
                        pcT = bview(pcTb, 0, [10, 128])
                        pso = pP_rot.next()
                        pv(P1b.t, P1b, kblocks, pcT, pcTb, vof, 64, pso)
                        OP("act", lambda e: e.copy(out=ob[:, tq, hh * 64:(hh + 1) * 64], in_=pso[:, 0:64]), [pso], [obb])

                    its = [(t0, hh, qi) for (t0, ntq, sq) in g["seqs"] for hh in range(4) for qi in range(ntq)]
                    prev = None
                    for ii, (t0, hh, qi) in enumerate(its):
                        ctx = att_s1(t0, hh, qi, ii % 2)
                        if prev is not None:
                            att_s2(prev)
                        prev = ctx
                    att_s2(prev)
                    w = load_w(Win, 0, 3072 + hb * 256)
                    for ti in range(nt):
                        ps = pP_rot.next()
                        proj(hT, R_H, ti, w, 256, ps)
                        ta = tmpA.next()
                        OP("act", lambda e, ta=ta, ps=ps: e.activation(out=ta[:, 0:256], in_=ps[:, 0:256], func=AF.Silu), [ps], [ta])
                        OP("dve", lambda e, ta=ta, ti=ti: e.tensor_tensor(out=ob[:, ti, :], in0=ob[:, ti, :], in1=ta[:, 0:256],
                                                                          op=ALU.mult), [obb, ta], [obb])
                        transpose_blocks(lambda i, ti=ti: ob[:, ti, i * 128:(i + 1) * 128], obb, 2,
                                         lambda i0, n, ti=ti, hb=hb: (ogT[:, 2 * hb + i0:2 * hb + i0 + n, ti * 128:(ti + 1) * 128], R_G))
                S.barrier()
                ybufs = [Buf(ATT[:, 0:nt * 1024], "yacc")]
                tail(layer, tiles, c, ogT, R_G, 8, Wout, ybufs)

            for g in groups:
                do_group(g)
            S.barrier()

        def layer_ret(layer):
            Win, Wout = I["ret_w_in"], I["ret_w_out"]
            S.dma(gnw[:, 0:2048], I["ret_gn"].partition_broadcast(128), writes=[gnw])
            S.dma(lgt[:, 0:8], I["ret_decay"].partition_broadcast(128), writes=[lgt])
            OP("act", lambda e: e.activation(out=lgt[:, 0:8], in_=lgt[:, 0:8], func=AF.Exp, scale=-1.0), [lgt], [lgt])
            OP("act", lambda e: e.activation(out=lgt[:, 0:8], in_=lgt[:, 0:8], func=AF.Ln, bias=1.0), [lgt], [lgt])
            OP("dve", lambda e: e.tensor_scalar(out=lgt[:, 0:8], in0=lgt[:, 0:8], scalar1=-1.0, scalar2=None, op0=ALU.mult), [lgt], [lgt])
            OP("dve", lambda e: e.tensor_scalar(out=lgt[:, 8:12], in0=lgt[:, 4:8], scalar1=-1.0, scalar2=None, op0=ALU.mult), [lgt], [lgt])
            OP("dve", lambda e: e.tensor_scalar(out=lgt[:, 12:16], in0=lgt[:, 4:8], scalar1=1024.0, scalar2=None, op0=ALU.mult), [lgt], [lgt])
            for h in range(4):
                OP("act", lambda e, h=h: e.activation(out=dsc[:, h, 0:2], in_=stx[:, 0:2], func=AF.Exp, scale=lgt[:, h:h + 1]), [stx, lgt], [dsc])
                OP("act", lambda e, h=h: e.activation(out=dsc[:, h, 2:4], in_=stx[:, 2:4], func=AF.Exp, scale=lgt[:, 4 + h:5 + h]), [stx, lgt], [dsc])
            OP("dve", lambda e: e.tensor_scalar(out=dsc[:], in0=dsc[:], scalar1=1.0 / 16, scalar2=None, op0=ALU.mult), [dsc], [dsc])

            def do_group(g):
                S.barrier()
                tiles, c, smp = g["tiles"], g["c"], g["sample"]
                nt = len(tiles)
                T = nt * 128
                hT = bview(R_H, 0, [8, T])
                ogT = bview(R_G, 0, [16, T])
                qTb = sub(0, 1024, "qT")
                kTb = sub(1024, 1024, "kT")
                vbb = sub(2048, 2048, "vb")
                atb = sub(4096, 4096, "attT")
                obb = sub(8192, 4096, "ob")
                qT = bview(qTb, 0, [2, 1024])
                kT = bview(kTb, 0, [2, 1024])
                vb = bview(vbb, 0, [8, 512])
                attT = bview(atb, 0, [8, 1024])
                gpn = fview(atb, 0, [2, 1920])
                ob = fview(obb, 0, [8, 512])
                if smp:
                    s0b = sub(12288, 1024, "S0b")
                    qdb = sub(13312, 2048, "qTd")
                    S0 = bview(s0b, 0, [2, 2, 512])
                    qTd = bview(qdb, 0, [2, 2, 1024])
                else:
                    kdb = sub(12288, 2048, "kdec")
                    kdec = bview(kdb, 0, [2, 8, 256])
                front(layer, tiles, c, lambda k, ti: (hT[:, k, ti * 128:(ti + 1) * 128], R_H))
                for h in range(4):
                    stp = strip.next()
                    for i2 in range(2):
                        S.dma(gpn[:, i2, :], I["gpn"][i2], writes=[atb])
                    OP("act", lambda e, h=h: e.activation(out=gpn[:, 0, :], in_=gpn[:, 0, :], func=AF.Exp, scale=lgt[:, h:h + 1]), [atb, lgt], [atb])
                    OP("act", lambda e, h=h: e.activation(out=gpn[:, 1, :], in_=gpn[:, 1, :], func=AF.Exp, scale=lgt[:, 4 + h:5 + h]), [atb, lgt], [atb])
                    OP("dve", lambda e: e.scalar_tensor_tensor(out=gpn[:, 0, :], in0=gpn[:, 0, :], scalar=-1.0, in1=gpn[:, 1, :],
                                                               op0=ALU.add, op1=ALU.add), [atb], [atb])
                    OP("dve", lambda e: e.tensor_tensor(out=gpn[:, 0, 896:1024], in0=gpn[:, 0, 896:1024], in1=ident[:], op=ALU.add),
                       [atb, ident], [atb])
                    OP("dve", lambda e, stp=stp: e.tensor_scalar(out=stp[:], in0=gpn[:, 0, :], scalar1=1.0 / 16, scalar2=None, op0=ALU.mult),
                       [atb], [stp])
                    for which, dstT, dstb in ((0, qT, qTb), (1, kT, kTb)):
                        w = load_w(Win, 0, which * 1024 + h * 256)
                        for ti in range(nt):
                            ps = pP_rot.next()
                            proj(hT, R_H, ti, w, 256, ps)
                            ta = tmpA.next()
                            OP("act", lambda e, ta=ta, ps=ps: e.copy(out=ta[:, 0:256], in_=ps[:, 0:256]), [ps], [ta])
                            srcb = ta
                            if smp:
                                tb = tmpB.next()
                                rope(ta, tb, rope0, 64, ti, 1)
                                srcb = tb
                            elif which == 1:
                                tt = ti % 2
                                for d in range(2):
                                    OP("dve", lambda e, ta=ta, d=d, ti=ti, tt=tt, h=h: e.tensor_scalar(
                                        out=kdec[:, d, ti, :], in0=ta[:, 0:256], scalar1=dsc[:, h, 2 * d + tt:2 * d + tt + 1], scalar2=None,
                                        op0=ALU.mult), [ta, dsc], [kdb])
                            transpose_blocks(lambda i, srcb=srcb: srcb[:, i * 128:(i + 1) * 128], srcb, 2,
                                             lambda i0, n, dstT=dstT, dstb=dstb, ti=ti: (dstT[:, i0:i0 + n, ti * 128:(ti + 1) * 128], dstb))
                    for v2 in range(2):
                        w = load_w(Win, 0, 2048 + h * 512 + v2 * 256)
                        for ti in range(nt):
                            ps = pP_rot.next()
                            proj(hT, R_H, ti, w, 256, ps)
                            OP("act", lambda e, ti=ti, ps=ps, v2=v2: e.copy(out=vb[:, ti, v2 * 256:(v2 + 1) * 256], in_=ps[:, 0:256]), [ps], [vbb])
                    if smp:
                        for d in range(2):
                            for dc in range(2):
                                sb_ = osb.next()
                                S.dma(sb_[:], I["state_ret"][d, h, dc * 128:(dc + 1) * 128, :], writes=[sb_])
                                OP("pool", lambda e, sb_=sb_, d=d, dc=dc: e.tensor_copy(out=S0[:, d, dc, :], in_=sb_[:]), [sb_], [s0b])
                            rd = xt_rot.next()
                            S.dma(rd[:], I["iota1k"], writes=[rd])
                            if d == 0:
                                OP("act", lambda e, rd=rd, h=h: e.activation(out=rd[:], in_=rd[:], func=AF.Exp, scale=lgt[:, h:h + 1],
                                                                             bias=lgt[:, h:h + 1]), [rd, lgt], [rd])
                            else:
                                OP("act", lambda e, rd=rd, h=h: e.activation(out=rd[:], in_=rd[:], func=AF.Exp, scale=lgt[:, 8 + h:9 + h],
                                                                             bias=lgt[:, 12 + h:13 + h]), [rd, lgt], [rd])
                            for dc in range(2):
                                OP("dve", lambda e, rd=rd, d=d, dc=dc: e.tensor_tensor(out=qTd[:, d, dc, :], in0=qT[:, dc, :], in1=rd[:],
                                                                                       op=ALU.mult), [qTb, rd], [qdb])
                    for (t0, ntq, sq) in g["seqs"]:
                        nq = ntq * 128
                        nqb = (nq + 511) // 512
                        N = min(nq, 512)
                        for i in range(ntq):
                            for qh in range(nqb):
                                for dc in range(2):
                                    OP("pe", lambda e, i=i, qh=qh, dc=dc, t0=t0, N=N: e.matmul(
                                        out=PB[qh][:, 0:N], lhsT=kT[:, dc, (t0 + i) * 128:(t0 + i + 1) * 128],
                                        rhs=qT[:, dc, t0 * 128 + qh * 512:t0 * 128 + qh * 512 + N], start=(dc == 0), stop=(dc == 1)),
                                       [qTb, kTb], [PB[qh]])
                                OP("dve", lambda e, i=i, qh=qh, N=N, stp=stp: e.tensor_tensor(
                                    out=attT[:, i, qh * 512:qh * 512 + N], in0=PB[qh][:, 0:N],
                                    in1=stp[:, (7 - i) * 128 + qh * 512:(7 - i) * 128 + qh * 512 + N], op=ALU.mult), [PB[qh], stp], [atb])
                        for j in range(ntq):
                            pso = pP_rot.next()
                            nmm = ntq + (4 if smp else 0)
                            cnt = 0
                            for i in range(ntq):
                                OP("pe", lambda e, i=i, j=j, t0=t0, cnt=cnt, nmm=nmm, pso=pso: e.matmul(
                                    out=pso[:, 0:512], lhsT=attT[:, i, j * 128:(j + 1) * 128], rhs=vb[:, t0 + i, :],
                                    start=(cnt == 0), stop=(cnt == nmm - 1)), [atb, vbb], [pso])
                                cnt += 1
                            if smp:
                                for d in range(2):
                                    for dc in range(2):
                                        OP("pe", lambda e, d=d, dc=dc, j=j, cnt=cnt, nmm=nmm, pso=pso: e.matmul(
                                            out=pso[:, 0:512], lhsT=qTd[:, d, dc, j * 128:(j + 1) * 128], rhs=S0[:, d, dc, :],
                                            start=(cnt == 0), stop=(cnt == nmm - 1)), [qdb, s0b], [pso])
                                        cnt += 1
                            s = st1.next()
                            ta = tmpA.next()
                            OP("dve", lambda e, s=s, pso=pso: e.tensor_reduce(out=s[:, 0:1], in_=pso[:, 0:512], axis=AX.X, op=ALU.add), [pso], [s])
                            OP("dve", lambda e, s=s: e.tensor_scalar(out=s[:, 1:2], in0=s[:, 0:1], scalar1=-1.0 / 512, scalar2=None, op0=ALU.mult), [s], [s])
                            OP("act", lambda e, s=s, ta=ta, pso=pso: e.activation(out=ta[:, 0:512], in_=pso[:, 0:512], func=AF.Identity, bias=s[:, 1:2]),
                               [pso, s], [ta])
                            tb = tmpB.next()
                            OP("act", lambda e, s=s, ta=ta, tb=tb: e.activation(out=tb[:, 0:512], in_=ta[:, 0:512], func=AF.Square, accum_out=s[:, 2:3]),
                               [ta], [tb, s])
                            OP("act", lambda e, s=s: e.activation(out=s[:, 3:4], in_=s[:, 2:3], func=AF.Sqrt, scale=1.0 / 512, bias=GN_EPS), [s], [s])
                            OP("dve", lambda e, s=s: e.reciprocal(out=s[:, 4:5], in_=s[:, 3:4]), [s], [s])
                            OP("dve", lambda e, s=s, ta=ta, j=j, t0=t0, h=h: e.scalar_tensor_tensor(
                                out=ob[:, t0 + j, :], in0=ta[:, 0:512], scalar=s[:, 4:5], in1=gnw[:, h * 512:(h + 1) * 512],
                                op0=ALU.mult, op1=ALU.mult), [ta, s, gnw], [obb])
                        if not smp:
                            for d in range(2):
                                for dc in range(2):
                                    pst = pP_rot.next()
                                    for tt in range(2):
                                        OP("pe", lambda e, d=d, dc=dc, tt=tt, t0=t0, pst=pst: e.matmul(
                                            out=pst[:, 0:512], lhsT=kdec[:, d, t0 + tt, dc * 128:(dc + 1) * 128], rhs=vb[:, t0 + tt, :],
                                            start=(tt == 0), stop=(tt == 1)), [kdb, vbb], [pst])
                                    sb_ = osb.next()
                                    OP("act", lambda e, sb_=sb_, pst=pst: e.copy(out=sb_[:], in_=pst[:, 0:512]), [pst], [sb_])
                                    S.dma(O["st_ret"][sq, d, h, dc * 128:(dc + 1) * 128, :], sb_[:], reads=[sb_], queue="act")
                    for g2 in range(2):
                        w = load_w(Win, 0, 4096 + h * 512 + g2 * 256)
                        for ti in range(nt):
                            ps = pP_rot.next()
                            proj(hT, R_H, ti, w, 256, ps)
                            ta = tmpA.next()
                            OP("act", lambda e, ta=ta, ps=ps: e.activation(out=ta[:, 0:256], in_=ps[:, 0:256], func=AF.Silu), [ps], [ta])
                            OP("dve", lambda e, ta=ta, ti=ti, g2=g2: e.tensor_tensor(out=ob[:, ti, g2 * 256:(g2 + 1) * 256],
                                                                                     in0=ob[:, ti, g2 * 256:(g2 + 1) * 256], in1=ta[:, 0:256],
                                                                                     op=ALU.mult), [obb, ta], [obb])
                    for ti in range(nt):
                        transpose_blocks(lambda i, ti=ti: ob[:, ti, i * 128:(i + 1) * 128], obb, 4,
                                         lambda i0, n, ti=ti, h=h: (ogT[:, 4 * h + i0:4 * h + i0 + n, ti * 128:(ti + 1) * 128], R_G))
                S.barrier()
                ybufs = [Buf(ATT[:, 0:nt * 1024], "yacc")]
                tail(layer, tiles, c, ogT, R_G, 16, Wout, ybufs)

            for g in groups:
                do_group(g)
            S.barrier()

        RKVG = [dscr(f"rkvg{n}", (2048, 1024)) for n in range(4)]
        RKVGB = [[Buf(RKVG[n][t * 128:(t + 1) * 128, :], f"rkvg{n}_{t}") for t in range(16)] for n in range(4)]
        dscrb = lambda name, shape: nc.dram_tensor(name, list(shape), BF16).ap()
        TMP = dscrb("tmp_", (128, 256, 3, 64))
        TMS = dscrb("tms_", (32, 1024, 3, 64))
        FMP = dscrb("fmp_", (128, 4, 64, 256))
        FMS = dscrb("fms_", (32, 4, 64, 1024))
        GMP = dscr("gmp_", (128, 4, 64))
        GMS = dscr("gms_", (32, 16, 64))
        GMPB, GMSB = Buf(GMP, "gmp"), Buf(GMS, "gms")
        cmask2 = S.sb([128, 3, 128], F32, "cmask2")
        YPD = dscr("ypd", (128, 256, 64))
        YSD = dscr("ysd", (32, 1024, 64))
        TMPB, TMSB, FMPB, FMSB = Buf(TMP, "tmp"), Buf(TMS, "tms"), Buf(FMP, "fmp"), Buf(FMS, "fms")
        YPDB = Buf(YPD, "ypd")
        YSDB = Buf(YSD, "ysd")
        tri = S.sb([128, 128], F32, "tri")

        def layer_rwkv(layer):
            Win, Wout = I["rwkv_w_in"], I["rwkv_w_out"]
            S.barrier()
            for n in range(6):
                S.dma(muF[:, n, :], I["rwkv_mu"][n].rearrange("(k p) -> p k", p=128), writes=[muF], allow_slow_non_contiguous=True)
            smallb = bsub(24576, 3072, "rwsmall")
            wAb = bview(smallb, 0, [2, 8, 64])
            aAb = bview(smallb, 512, [2, 8, 64])
            wBb = bview(smallb, 1024, [2, 1024])
            aBb = bview(smallb, 2048, [2, 1024])
            for d in range(2):
                for src, dst in ((I["rwkv_wA"], wAb), (I["rwkv_aA"], aAb)):
                    f = wf.next()
                    S.dma(f[:, :, 0:64], src[d].rearrange("(k p) r -> p k r", p=128), writes=[f])
                    OP("pool", lambda e, f=f, dst=dst, d=d: e.tensor_copy(out=dst[:, d, :, :], in_=f[:, :, 0:64]), [f], [smallb])
                for src, dst in ((I["rwkv_wB"], wBb), (I["rwkv_aB"], aBb)):
                    f = wf.next()
                    fv = f.t[0:64].rearrange("p k n -> p (k n)")[:, 0:1024]
                    S.dma(fv, src[d], writes=[f])
                    OP("pool", lambda e, fv=fv, dst=dst, d=d: e.tensor_copy(out=dst[0:64, d, :], in_=fv), [f], [smallb])
            lorab = bsub(0, 4096, "lora")
            LWT = bview(lorab, 0, [2, 2048])
            LAT = bview(lorab, 2048, [2, 2048])
            OP("dve", lambda e: e.memset(bsum[:], 0.0), [], [bsum])

            def a1_group(g, gi):
                tiles, c, smp = g["tiles"], g["c"], g["sample"]
                nt = len(tiles)
                T = nt * 128
                tok0 = tiles[0] * 128
                hTb = bsub(4096, 8192, "hTf")
                xxb = bsub(12288, 8192, "xxT")
                xnb = bsub(20480, 4096, "xnT")
                hT = fview(hTb, 0, [8, T])
                xx = fview(xxb, 0, [8, T])
                xn = bview(xnb, 0, [8, T])
                front(layer, tiles, c, lambda k, ti: (hT[:, k, ti * 128:(ti + 1) * 128], hTb))
                for (t0, ntq, sq) in g["seqs"]:
                    o = t0 * 128
                    L = ntq * 128
                    OP("dve", lambda e, o=o, L=L: e.tensor_tensor(out=xx[:, :, o + 1:o + L - 1], in0=hT[:, :, o:o + L - 2],
                                                                 in1=hT[:, :, o + 2:o + L], op=ALU.add), [hTb], [xxb])
                    OP("dve", lambda e, o=o, L=L: e.scalar_tensor_tensor(out=xx[:, :, o + 1:o + L - 1], in0=xx[:, :, o + 1:o + L - 1],
                                                                        scalar=0.5, in1=hT[:, :, o + 1:o + L - 1],
                                                                        op0=ALU.mult, op1=ALU.subtract), [xxb, hTb], [xxb])
                    OP("dve", lambda e, o=o: e.scalar_tensor_tensor(out=xx[:, :, o:o + 1], in0=hT[:, :, o + 1:o + 2], scalar=0.5,
                                                                    in1=hT[:, :, o:o + 1], op0=ALU.mult, op1=ALU.subtract), [hTb], [xxb])
                    OP("dve", lambda e, o=o, L=L: e.scalar_tensor_tensor(out=xx[:, :, o + L - 1:o + L], in0=hT[:, :, o + L - 2:o + L - 1],
                                                                        scalar=0.5, in1=hT[:, :, o + L - 1:o + L],
                                                                        op0=ALU.mult, op1=ALU.subtract), [hTb], [xxb])

                def mix(n):
                    for k in range(8):
                        OP("dve", lambda e, k=k, n=n: e.scalar_tensor_tensor(out=xn[:, k, :], in0=xx[:, k, :], scalar=muF[:, n, k:k + 1],
                                                                             in1=hT[:, k, :], op0=ALU.mult, op1=ALU.add),
                           [xxb, hTb, muF], [xnb])
                for pi, n in enumerate((0, 2, 3, 5)):
                    mix(n)
                    for cb in range(4):
                        w = load_w(Win, 0, pi * 1024 + cb * 256)
                        for ti in range(nt):
                            ps = pP_rot.next()
                            proj(xn, xnb, ti, w, 256, ps)
                            ta = tmpA.next()
                            OP("act", lambda e, ta=ta, ps=ps: e.copy(out=ta[:, 0:256], in_=ps[:, 0:256]), [ps], [ta])
                            t = tiles[ti]
                            S.dma(RKVG[pi][t * 128:(t + 1) * 128, cb * 256:(cb + 1) * 256], ta[:, 0:256], reads=[ta],
                                  writes=[RKVGB[pi][t]], owner=ta, queue="act")
                for n, Ab, LT, fn in ((1, wAb, LWT, AF.Tanh), (4, aAb, LAT, AF.Copy)):
                    mix(n)
                    for d in range(2):
                        for c0 in range(0, T, 512):
                            ps = pP_rot.next()
                            for k in range(8):
                                OP("pe", lambda e, k=k, d=d, c0=c0, Ab=Ab, ps=ps: e.matmul(out=ps[0:64, 0:512], lhsT=Ab[:, d, k, :],
                                                                                          rhs=xn[:, k, c0:c0 + 512], start=(k == 0), stop=(k == 7)),
                                   [smallb, xnb], [ps])
                            if fn == AF.Tanh:
                                OP("act", lambda e, d=d, c0=c0, LT=LT, ps=ps: e.activation(out=LT[0:64, d, tok0 + c0:tok0 + c0 + 512],
                                                                                        in_=ps[0:64, 0:512], func=AF.Tanh), [ps], [lorab])
                            else:
                                OP("act", lambda e, d=d, c0=c0, LT=LT, ps=ps: e.copy(out=LT[0:64, d, tok0 + c0:tok0 + c0 + 512],
                                                                                  in_=ps[0:64, 0:512]), [ps], [lorab])

            for gi, g in enumerate(groups):
                a1_group(g, gi)
            S.barrier()

            tabb = bsub(4096, 8192, "tabs")
            TAB = fview(tabb, 0, [8, 1024])
            for i2, src in enumerate((I["rwkv_w0"][0], I["rwkv_w0"][1], I["rwkv_a0"][0], I["rwkv_a0"][1], I["rwkv_kk"],
                                      I["rwkv_ka"], I["rwkv_rk"], I["rwkv_gn"])):
                S.dma(TAB[:, i2, :], src.partition_broadcast(128), writes=[tabb])
            slot = [bsub(12288 + i2 * 1024, 1024, f"slot{i2}") for i2 in range(12)]
            xtb = [Buf(b_.t[:, :], b_.name + "_a2") for b_ in (xt_rot.bufs + xn_rot.bufs)]
            Rb, Kb, Vb, KKb, LWb, ABb, KDb, T1b, FLWb = slot[0:9]
            Frot = Rot([slot[9], slot[10]])
            Hrot = Rot([slot[11], xtb[0]])
            FMrot = Rot([xtb[1], xtb[2]])
            h16 = lambda b: b.t.rearrange("p (h d) -> p h d", h=16)
            S.dma(tri[:], I["tri"], writes=[tri])
            for e2 in range(2):
                S.dma(cmask2[e2 * 64:(e2 + 1) * 64, :, :], I["cmask"], writes=[cmask2])

            def flip(srcb, dstb):
                for hf in range(2):
                    ps = pP_rot.next()
                    OP("pe", lambda e, hf=hf, ps=ps: e.matmul(out=ps[:, :], lhsT=Jm[:], rhs=srcb.t[:, hf * 512:(hf + 1) * 512],
                                                             start=True, stop=True), [Jm, srcb], [ps])
                    OP("act", lambda e, hf=hf, ps=ps: e.copy(out=dstb.t[:, hf * 512:(hf + 1) * 512], in_=ps[:, :]), [ps], [dstb])

            def a2_tile(t):
                smp = t >= 8
                tt_in_seq = (t - 8) if smp else (t % 2)
                for pi, b in ((0, Rb), (1, Kb), (2, Vb)):
                    S.dma(b.t, RKVG[pi][t * 128:(t + 1) * 128, :], reads=[RKVGB[pi][t]], writes=[b])
                OP("dve", lambda e: e.tensor_tensor(out=KKb.t, in0=Kb.t, in1=TAB[:, 4, :], op=ALU.mult), [Kb, tabb], [KKb])
                OP("pool", lambda e: e.tensor_tensor(out=T1b.t, in0=KKb.t, in1=KKb.t, op=ALU.mult), [KKb], [T1b])
                nrm = tmpB.next()
                OP("dve", lambda e, nrm=nrm: e.tensor_reduce(out=nrm[:, 0:16], in_=h16(T1b), axis=AX.X, op=ALU.add), [T1b], [nrm])
                OP("dve", lambda e, nrm=nrm: e.tensor_scalar(out=nrm[:, 0:16], in0=nrm[:, 0:16], scalar1=1e-12, scalar2=None, op0=ALU.max), [nrm], [nrm])
                OP("act", lambda e, nrm=nrm: e.activation(out=nrm[:, 16:32], in_=nrm[:, 0:16], func=AF.Sqrt), [nrm], [nrm])
                OP("dve", lambda e, nrm=nrm: e.reciprocal(out=nrm[:, 32:48], in_=nrm[:, 16:32]), [nrm], [nrm])
                OP("dve", lambda e, nrm=nrm: e.tensor_tensor(out=h16(KKb), in0=h16(KKb), in1=nrm[:, 32:48].unsqueeze(2).to_broadcast([128, 16, 64]),
                                                             op=ALU.mult), [KKb, nrm], [KKb])
                for d in range(2):
                    a2_dir(t, d, smp, tt_in_seq)

            def a2_dir(t, d, smp, tt_in_seq):
                if True:
                    for (LT, Bw, tabi, dstb, post) in ((LWT, wBb, d, LWb, "w"), (LAT, aBb, 2 + d, ABb, "a")):
                        for hf in range(2):
                            ps = pP_rot.next()
                            OP("pe", lambda e, hf=hf, d=d, LT=LT, Bw=Bw, ps=ps: e.matmul(
                                out=ps[:, :], lhsT=LT[0:64, d, t * 128:(t + 1) * 128], rhs=Bw[0:64, d, hf * 512:(hf + 1) * 512],
                                start=True, stop=True), [lorab, smallb], [ps])
                            OP("dve", lambda e, hf=hf, tabi=tabi, dstb=dstb, ps=ps: e.tensor_tensor(
                                out=dstb.t[:, hf * 512:(hf + 1) * 512], in0=ps[:, :], in1=TAB[:, tabi, hf * 512:(hf + 1) * 512], op=ALU.add),
                               [ps, tabb], [dstb])
                        OP("act", lambda e, dstb=dstb: e.activation(out=dstb.t, in_=dstb.t, func=AF.Sigmoid), [dstb], [dstb])
                        if post == "w":
                            OP("act", lambda e, dstb=dstb: e.activation(out=dstb.t, in_=dstb.t, func=AF.Copy, scale=-math.exp(-0.5)), [dstb], [dstb])
                    OP("dve", lambda e: e.scalar_tensor_tensor(out=T1b.t, in0=ABb.t, scalar=-1.0, in1=TAB[:, 5, :], op0=ALU.add, op1=ALU.mult),
                       [ABb, tabb], [T1b])
                    OP("dve", lambda e: e.scalar_tensor_tensor(out=KDb.t, in0=T1b.t, scalar=1.0, in1=Kb.t, op0=ALU.add, op1=ALU.mult),
                       [T1b, Kb], [KDb])
                    OP("pool", lambda e: e.tensor_tensor(out=ABb.t, in0=KKb.t, in1=ABb.t, op=ALU.mult), [KKb, ABb], [ABb])
                    OP("pool", lambda e: e.tensor_tensor(out=T1b.t, in0=Rb.t, in1=KDb.t, op=ALU.mult), [Rb, KDb], [T1b])
                    OP("pool", lambda e: e.tensor_tensor(out=T1b.t, in0=T1b.t, in1=TAB[:, 6, :], op=ALU.mult), [T1b, tabb], [T1b])
                    nb = tmpB.next()
                    OP("dve", lambda e, nb=nb: e.tensor_reduce(out=nb[:, 0:16], in_=h16(T1b), axis=AX.X, op=ALU.add), [T1b], [nb])
                    OP("dve", lambda e, nb=nb: e.tensor_tensor(out=bsum[:, t, :], in0=bsum[:, t, :], in1=nb[:, 0:16], op=ALU.add), [bsum, nb], [bsum])
                    if smp:
                        L, TMD, FMD, TMB_, FMB_, GMD, GMB_ = 1024, TMS, FMS, TMSB, FMSB, GMS, GMSB
                        ch0 = d * 16
                    else:
                        L, TMD, FMD, TMB_, FMB_, GMD, GMB_ = 256, TMP, FMP, TMPB, FMPB, GMP, GMPB
                        ch0 = (d * 4 + t // 2) * 16
                    tk0 = tt_in_seq * 128
                    s0 = tk0 if d == 0 else L - 128 - tk0

                    def chain_order(srcb):
                        if d == 0:
                            return srcb
                        f = Frot.next()
                        flip(srcb, f)
                        return f
                    if d == 0:
                        lwc = LWb
                    else:
                        flip(LWb, FLWb)
                        lwc = FLWb
                    cps = [PB[0], PB[1]]
                    for hf in range(2):
                        OP("pe", lambda e, hf=hf: e.matmul(out=cps[hf][:, :], lhsT=tri[:], rhs=lwc.t[:, hf * 512:(hf + 1) * 512], start=True, stop=True),
                           [tri, lwc], [cps[hf]])

                    def store_tm(hb_, vi):
                        tmb = strip.next()
                        OP("act", lambda e: e.copy(out=tmb[:, 0:1024], in_=hb_.t), [hb_], [tmb])
                        S.dma(TMD[ch0:ch0 + 16, s0:s0 + 128, vi, :].rearrange("h t j -> t h j"), tmb[:, 0:1024].rearrange("p (h d) -> p h d", h=16),
                              reads=[tmb], writes=[TMB_], owner=tmb, queue="act")

                    def store_fm(hb_, vi):
                        fm = FMrot.next()
                        fmv = fm.t[:, 0:512].bitcast(BF16).rearrange("p (a b) -> p a b", a=8)
                        transpose_blocks(lambda i: hb_.t[:, i * 128:(i + 1) * 128], hb_, 8, lambda i0, n: (fmv[:, i0:i0 + n, :], fm), evac="act")
                        for e2 in range(2):
                            S.dma(FMD[ch0 + e2:ch0 + 16:2, vi, :, s0:s0 + 128].rearrange("c k t -> k c t"), fmv[e2 * 64:(e2 + 1) * 64, :, :],
                                  reads=[fm], writes=[FMB_], owner=fm, queue="act")

                    def hat(srcb, kind):
                        hb_ = Hrot.next()
                        for hf in range(2):
                            sl = slice(hf * 512, (hf + 1) * 512)
                            if kind == "prev":
                                OP("dve", lambda e, hf=hf, sl=sl: e.tensor_tensor(out=T1b.t[:, sl], in0=cps[hf][:, :], in1=lwc.t[:, sl], op=ALU.subtract),
                                   [cps[hf], lwc], [T1b])
                                OP("act", lambda e, sl=sl: e.activation(out=T1b.t[:, sl], in_=T1b.t[:, sl], func=AF.Exp), [T1b], [T1b])
                            elif kind == "cur":
                                OP("act", lambda e, hf=hf, sl=sl: e.activation(out=T1b.t[:, sl], in_=cps[hf][:, :], func=AF.Exp), [cps[hf]], [T1b])
                            elif kind == "inv":
                                OP("act", lambda e, hf=hf, sl=sl: e.activation(out=T1b.t[:, sl], in_=cps[hf][:, :], func=AF.Exp, scale=-1.0), [cps[hf]], [T1b])
                        if srcb is None:
                            OP("pool", lambda e: e.tensor_copy(out=hb_.t, in_=T1b.t), [T1b], [hb_])
                        else:
                            OP("pool", lambda e: e.tensor_tensor(out=hb_.t, in0=srcb.t, in1=T1b.t, op=ALU.mult), [srcb, T1b], [hb_])
                        return hb_

                    hb_ = hat(chain_order(KKb), "prev")
                    store_fm(hb_, 0)
                    hb_ = hat(chain_order(Rb), "cur")
                    store_fm(hb_, 1)
                    for cc in range(2):
                        cidx = s0 // 64 + cc
                        S.dma(GMD[ch0:ch0 + 16, cidx:cidx + 1, :].rearrange("h n k -> n h k"),
                              T1b.t[63 + 64 * cc:64 + 64 * cc, :].rearrange("p (h k) -> p h k", h=16), reads=[T1b], writes=[GMB_], owner=T1b, queue="act")
                    hb_ = hat(chain_order(ABb), "inv")
                    store_fm(hb_, 2)
                    store_tm(hb_, 0)
                    hb_ = hat(chain_order(KDb), "inv")
                    store_fm(hb_, 3)
                    store_tm(hb_, 1)
                    store_tm(chain_order(Vb), 2)

            for t in range(16):
                a2_tile(t)
            S.barrier()
            CK("rwkv_a")

            NU = 20
            o = 0

            def carve(size, name):
                nonlocal o
                b = bsub(o, size, name)
                o += size
                return b
            Tst = [carve(256, f"T{u}") for u in range(NU)]
            Tbs = [carve(128, f"Tb{u}") for u in range(NU)]
            NW = 4
            bsets = []
            for w_ in range(NW):
                bsets.append(dict(
                    tm=carve(384, f"tm{w_}"), fm=carve(512, f"fm{w_}"), gm=carve(4, f"gm{w_}"), ka=carve(256, f"ka{w_}"), apb=carve(128, f"apb{w_}"),
                    qz=[carve(512, f"qz{w_}a"), carve(512, f"qz{w_}b")], qt=[carve(256, f"qt{w_}a"), carve(256, f"qt{w_}b")],
                    bdq=carve(512, f"bdq{w_}"), bdt=carve(512, f"bdt{w_}"), zb=carve(128, f"zb{w_}"), u=carve(128, f"u{w_}"),
                    p=carve(128, f"p{w_}"), y=carve(256, f"y{w_}")))
            pB_rot = Rot(PB)
            for bs_ in bsets:
                for b in (bs_["bdq"], bs_["bdt"]):
                    OP("pool", lambda e, b=b: e.memset(b.t, 0.0), [], [b])
            f3 = lambda b, x: b.t.rearrange("p (c x) -> p c x", c=4)
            b3 = lambda b, n: b.t[:, 0:n // 2].bitcast(BF16).rearrange("p (c x) -> p c x", c=4)
            PH = [slice(0, 64), slice(64, 128)]
            ev_rot = Rot(["dve", "act", "pool", "dve", "act"])

            def to_bd(srcv, srcb, bd):
                bdv = f3(bd, 0)
                for e2 in range(2):
                    eng = ev_rot.next()
                    if eng == "act":
                        OP("act", lambda e, e2=e2: e.copy(out=bdv[PH[e2], :, e2 * 64:(e2 + 1) * 64], in_=srcv[PH[e2], :, :]), [srcb], [bd])
                    else:
                        OP(eng, lambda e, e2=e2: e.tensor_copy(out=bdv[PH[e2], :, e2 * 64:(e2 + 1) * 64], in_=srcv[PH[e2], :, :]), [srcb], [bd])

            units = []
            for d in range(2):
                for hh in range(2):
                    units.append(dict(smp=True, ch0=d * 16 + hh * 8, d=d, h0=hh * 8, nch=16))
            for d in range(2):
                for sq in range(4):
                    for hh in range(2):
                        units.append(dict(smp=False, ch0=(d * 4 + sq) * 16 + hh * 8, d=d, sq=sq, h0=hh * 8, nch=4))
            for ui, u in enumerate(units):
                Tb, Tbb = Tst[ui], Tbs[ui]
                if not u["smp"]:
                    OP("pool", lambda e, Tb=Tb: e.memset(Tb.t, 0.0), [], [Tb])
                else:
                    st_ = bsets[ui % NW]["qz"][0]
                    sv = st_.t[:, 0:256].rearrange("p (c x) -> p c x", c=4)
                    for e2 in range(2):
                        h0 = u["h0"] + 4 * e2
                        S.dma(sv[PH[e2], :, :], I["state_rwkv"][u["d"], h0:h0 + 4].rearrange("h v k -> v h k"), writes=[st_])
                    ps = pP_rot.next()
                    for e2 in range(2):
                        for p in range(4):
                            OP("pe", lambda e, e2=e2, p=p, ps=ps, sv=sv: e.matmul(out=ps[PH[e2], p * 64:(p + 1) * 64], lhsT=sv[PH[e2], p, :],
                                                                               rhs=ident[PH[e2], e2 * 64:(e2 + 1) * 64], start=True, stop=True),
                               [st_, ident], [ps])
                    OP("act", lambda e, Tb=Tb, ps=ps: e.copy(out=Tb.t, in_=ps[:, 0:256]), [ps], [Tb])
                OP("dve", lambda e, Tb=Tb, Tbb=Tbb: e.tensor_copy(out=Tbb.t[:, 0:128].bitcast(BF16), in_=Tb.t), [Tb], [Tbb])

            def unit_chunk(ui, u, n, bs):
                smp, ch0 = u["smp"], u["ch0"]
                TMD, FMD, GMD, TMB_, FMB_, GMB_, YD, YDB_ = ((TMS, FMS, GMS, TMSB, FMSB, GMSB, YSD, YSDB) if smp else
                                                             (TMP, FMP, GMP, TMPB, FMPB, GMPB, YPD, YPDB))
                tm, fm, gm = bs["tm"], bs["fm"], bs["gm"]
                tmv = tm.t.bitcast(BF16).rearrange("p (c v j) -> p c v j", c=4, v=3)
                fmv = fm.t.bitcast(BF16).rearrange("p (c v s) -> p c v s", c=4, v=4)
                for e2 in range(2):
                    c0 = ch0 + 4 * e2
                    S.dma(tm.t.bitcast(BF16).rearrange("p (c x) -> p c x", c=4)[PH[e2], :, :],
                          TMD[c0:c0 + 4, n * 64:(n + 1) * 64, :, :].rearrange("c s v j -> s c (v j)"), reads=[TMB_], writes=[tm])
                    S.dma(fm.t.bitcast(BF16).rearrange("p (cv s) -> p cv s", s=64)[PH[e2], :, :],
                          FMD[c0:c0 + 4, :, :, n * 64:(n + 1) * 64].rearrange("c v k s -> k (c v) s"), reads=[FMB_], writes=[fm])
                    S.dma(gm.t[PH[e2], :], GMD[c0:c0 + 4, n, :].rearrange("c k -> k c"), reads=[GMB_], writes=[gm], allow_slow_non_contiguous=True)
                Tb, Tbb = Tst[ui], Tbs[ui]
                Tv = f3(Tb, 0)
                Tbv = b3(Tbb, 256)
                ka, apb = bs["ka"], bs["apb"]
                kav = b3(ka, 512)
                apbv = b3(apb, 256)
                qzi, qti = 0, 0
                qz = bs["qz"][0]
                qzv = f3(qz, 0)
                qt = bs["qt"][0]
                qtv = f3(qt, 0)
                psB, psK, psL = pB_rot.next(), pB_rot.next(), pB_rot.next()
                for e2 in range(2):
                    for p in range(4):
                        OP("pe", lambda e, e2=e2, p=p: e.matmul(out=psB[PH[e2], p * 128:(p + 1) * 128], lhsT=fmv[PH[e2], p, 2, :],
                                                               rhs=fmv[PH[e2], p, 0:2, :], start=True, stop=True), [fm], [psB])
                        OP("pe", lambda e, e2=e2, p=p: e.matmul(out=psK[PH[e2], p * 128:(p + 1) * 128], lhsT=fmv[PH[e2], p, 3, :],
                                                               rhs=fmv[PH[e2], p, 0:2, :], start=True, stop=True), [fm], [psK])
                        OP("pe", lambda e, e2=e2, p=p: e.matmul(out=psL[PH[e2], p * 64:(p + 1) * 64], lhsT=fmv[PH[e2], p, 0, :],
                                                               rhs=fmv[PH[e2], p, 2, :], start=True, stop=True), [fm], [psL])
                psBv = psB[:, :].rearrange("p (c x) -> p c x", c=4)
                OP("dve", lambda e, qzv=qzv: e.tensor_tensor(out=qzv[:, :, 64:128], in0=psBv[:, :, 0:64], in1=cmask2[:, 0, 0:64].unsqueeze(1).to_broadcast([128, 4, 64]),
                                                    op=ALU.mult), [psB, cmask2], [qz])
                OP("dve", lambda e: e.tensor_tensor(out=apbv, in0=psBv[:, :, 64:128], in1=cmask2[:, 0, 64:128].unsqueeze(1).to_broadcast([128, 4, 64]),
                                                    op=ALU.mult), [psB, cmask2], [apb])
                OP("dve", lambda e: e.tensor_tensor(out=kav, in0=psK[:, :].rearrange("p (c x) -> p c x", c=4),
                                                    in1=cmask2[:, 1, :].unsqueeze(1).to_broadcast([128, 4, 128]), op=ALU.mult), [psK, cmask2], [ka])
                OP("dve", lambda e, qtv=qtv: e.tensor_tensor(out=qtv, in0=psL[:, 0:256].rearrange("p (c x) -> p c x", c=4),
                                                    in1=cmask2[:, 2, 0:64].unsqueeze(1).to_broadcast([128, 4, 64]), op=ALU.mult), [psL, cmask2], [qt])
                OP("pool", lambda e, qzv=qzv: e.tensor_tensor(out=qzv[:, :, 0:64], in0=qzv[:, :, 64:128],
                                                     in1=ident[:, :].rearrange("p (a b) -> p a b", a=2)[:, 0, :].unsqueeze(1).to_broadcast([128, 4, 64])
                                                     if False else identst[:, :].unsqueeze(1).to_broadcast([128, 4, 64]), op=ALU.add), [qz, identst], [qz])
                yield
                bdq, bdt = bs["bdq"], bs["bdt"]
                to_bd(qzv[:, :, 64:128], qz, bdq)
                to_bd(qtv, qt, bdt)
                ps1, ps2 = pB_rot.next(), pB_rot.next()
                for p in range(4):
                    OP("pe", lambda e, p=p, bdt=bdt, qzv=qzv: e.matmul(out=ps1[:, p * 64:(p + 1) * 64], lhsT=f3(bdt, 0)[:, p, :], rhs=qzv[:, p, 64:128], start=True, stop=True),
                       [bdt, qz], [ps1])
                    OP("pe", lambda e, p=p, bdq=bdq, qtv=qtv: e.matmul(out=ps2[:, p * 64:(p + 1) * 64], lhsT=f3(bdq, 0)[:, p, :], rhs=qtv[:, p, :], start=True, stop=True),
                       [bdq, qt], [ps2])
                OP("act", lambda e, qzv=qzv: e.copy(out=qzv[:, :, 64:128], in_=ps1[:, 0:256].rearrange("p (c x) -> p c x", c=4)), [ps1], [qz])
                qt = bs["qt"][1]
                qti = 1
                qtv = f3(qt, 0)
                OP("dve", lambda e, qtv=qtv: e.tensor_copy(out=qtv, in_=ps2[:, 0:256].rearrange("p (c x) -> p c x", c=4)), [ps2], [qt])
                yield
                for lvl in range(1, 6):
                    to_bd(qtv, qt, bdt)
                    if lvl < 5:
                        to_bd(qzv[:, :, 64:128], qz, bdq)
                    N = 128 if lvl < 5 else 64
                    psA = pB_rot.next()
                    for p in range(4):
                        OP("pe", lambda e, p=p, bdt=bdt, qzv=qzv, N=N, psA=psA: e.matmul(out=psA[:, p * 128:p * 128 + N], lhsT=f3(bdt, 0)[:, p, :],
                                                                                      rhs=qzv[:, p, 0:N], start=True, stop=True), [bdt, qz], [psA])
                    if lvl < 5:
                        psC = pB_rot.next()
                        for p in range(4):
                            OP("pe", lambda e, p=p, bdq=bdq, qtv=qtv, psC=psC: e.matmul(out=psC[:, p * 64:(p + 1) * 64], lhsT=f3(bdq, 0)[:, p, :],
                                                                                      rhs=qtv[:, p, :], start=True, stop=True), [bdq, qt], [psC])
                    psAv = psA[:, :].rearrange("p (c x) -> p c x", c=4)
                    if lvl < 5:
                        qzi ^= 1
                        qz_new = bs["qz"][qzi]
                        qznv = f3(qz_new, 0)
                        OP("dve", lambda e, qznv=qznv, qzv=qzv, psAv=psAv: e.tensor_tensor(out=qznv[:, :, 0:64], in0=psAv[:, :, 0:64], in1=qzv[:, :, 0:64],
                                                                                         op=ALU.add), [psA, qz], [qz_new])
                        OP("act", lambda e, qznv=qznv, psAv=psAv: e.copy(out=qznv[:, :, 64:128], in_=psAv[:, :, 64:128]), [psA], [qz_new])
                        qti ^= 1
                        qt_new = bs["qt"][qti]
                        qtnv = f3(qt_new, 0)
                        OP("dve", lambda e, qtnv=qtnv, psC=psC: e.tensor_copy(out=qtnv, in_=psC[:, 0:256].rearrange("p (c x) -> p c x", c=4)), [psC], [qt_new])
                        qz, qzv, qt, qtv = qz_new, qznv, qt_new, qtnv
                        yield
                    else:
                        zb = bs["zb"]
                        zbv = b3(zb, 256)
                        OP("dve", lambda e, zbv=zbv, qzv=qzv, psAv=psAv: e.tensor_tensor(out=zbv, in0=psAv[:, :, 0:64], in1=qzv[:, :, 0:64], op=ALU.add),
                           [psA, qz], [zb])
                ps = pB_rot.next()
                for e2 in range(2):
                    for p in range(4):
                        OP("pe", lambda e, e2=e2, p=p, ps=ps: e.matmul(out=ps[PH[e2], p * 64:(p + 1) * 64], lhsT=fmv[PH[e2], p, 0, :], rhs=Tbv[PH[e2], p, :],
                                                                      start=True, stop=False), [fm, Tbb], [ps])
                        OP("pe", lambda e, e2=e2, p=p, ps=ps: e.matmul(out=ps[PH[e2], p * 64:(p + 1) * 64], lhsT=kav[PH[e2], p, 0:64], rhs=tmv[PH[e2], p, 2, :],
                                                                      start=False, stop=True), [ka, tm], [ps])
                ub = bs["u"]
                ubv = b3(ub, 256)
                OP("act", lambda e, ps=ps: e.copy(out=ubv, in_=ps[:, 0:256].rearrange("p (c x) -> p c x", c=4)), [ps], [ub])
                yield
                ps = pB_rot.next()
                for e2 in range(2):
                    for p in range(4):
                        OP("pe", lambda e, e2=e2, p=p, ps=ps: e.matmul(out=ps[PH[e2], p * 64:(p + 1) * 64], lhsT=zbv[PH[e2], p, :], rhs=ubv[PH[e2], p, :],
                                                                      start=True, stop=True), [zb, ub], [ps])
                pb_ = bs["p"]
                pbv = b3(pb_, 256)
                OP("dve", lambda e, ps=ps: e.tensor_scalar(out=pbv, in0=ps[:, 0:256].rearrange("p (c x) -> p c x", c=4), scalar1=-1.0, scalar2=None, op0=ALU.mult),
                   [ps], [pb_])
                yield
                ps = pB_rot.next()
                for e2 in range(2):
                    for p in range(4):
                        OP("pe", lambda e, e2=e2, p=p, ps=ps: e.matmul(out=ps[PH[e2], p * 64:(p + 1) * 64], lhsT=fmv[PH[e2], p, 1, :], rhs=Tbv[PH[e2], p, :],
                                                                      start=True, stop=False), [fm, Tbb], [ps])
                        OP("pe", lambda e, e2=e2, p=p, ps=ps: e.matmul(out=ps[PH[e2], p * 64:(p + 1) * 64], lhsT=apbv[PH[e2], p, :], rhs=pbv[PH[e2], p, :],
                                                                      start=False, stop=False), [apb, pb_], [ps])
                        OP("pe", lambda e, e2=e2, p=p, ps=ps: e.matmul(out=ps[PH[e2], p * 64:(p + 1) * 64], lhsT=kav[PH[e2], p, 64:128], rhs=tmv[PH[e2], p, 2, :],
                                                                      start=False, stop=True), [ka, tm], [ps])
                yb = bs["y"]
                ybv = f3(yb, 0)
                OP("act", lambda e, ps=ps: e.copy(out=ybv, in_=ps[:, 0:256].rearrange("p (c x) -> p c x", c=4)), [ps], [yb])
                for e2 in range(2):
                    c0 = ch0 + 4 * e2
                    S.dma(YD[c0:c0 + 4, n * 64:(n + 1) * 64, :].rearrange("c s x -> s c x"), ybv[PH[e2], :, :], reads=[yb], writes=[YDB_], owner=yb, queue="act")
                ps = pB_rot.next()
                for e2 in range(2):
                    for p in range(4):
                        OP("pe", lambda e, e2=e2, p=p, ps=ps: e.matmul(out=ps[PH[e2], p * 64:(p + 1) * 64], lhsT=tmv[PH[e2], p, 0, :], rhs=pbv[PH[e2], p, :],
                                                                      start=True, stop=False), [tm, pb_], [ps])
                        OP("pe", lambda e, e2=e2, p=p, ps=ps: e.matmul(out=ps[PH[e2], p * 64:(p + 1) * 64], lhsT=tmv[PH[e2], p, 1, :], rhs=tmv[PH[e2], p, 2, :],
                                                                      start=False, stop=True), [tm], [ps])
                OP("dve", lambda e, ps=ps: e.tensor_tensor(out=Tv, in0=ps[:, 0:256].rearrange("p (c x) -> p c x", c=4), in1=Tv, op=ALU.add), [ps, Tb], [Tb])
                OP("pool", lambda e: e.tensor_tensor(out=Tv, in0=Tv, in1=gm.t[:, 0:4].unsqueeze(2).to_broadcast([128, 4, 64]), op=ALU.mult), [Tb, gm], [Tb])
                OP("act", lambda e: e.copy(out=Tbv, in_=Tv), [Tb], [Tbb])

            tasks = [(ui, u, n) for n in range(16) for ui, u in enumerate(units) if n < u["nch"]]
            slots = [None] * NW
            ti_ = 0
            while ti_ < len(tasks) or any(sl_ is not None for sl_ in slots):
                for w_ in range(NW):
                    if slots[w_] is None and ti_ < len(tasks):
                        ui, u, n = tasks[ti_]
                        if any(sl_ is not None and sl_[1] == ui for sl_ in slots):
                            continue
                        slots[w_] = (unit_chunk(ui, u, n, bsets[w_]), ui)
                        ti_ += 1
                for w_ in range(NW):
                    if slots[w_] is not None:
                        try:
                            next(slots[w_][0])
                        except StopIteration:
                            slots[w_] = None
            for ui, u in enumerate(units):
                if u["smp"]:
                    continue
                Tb = Tst[ui]
                Tv = f3(Tb, 0)
                ps = pP_rot.next()
                for e2 in range(2):
                    for p in range(4):
                        OP("pe", lambda e, e2=e2, p=p, ps=ps, Tv=Tv: e.matmul(out=ps[PH[e2], p * 64:(p + 1) * 64], lhsT=Tv[PH[e2], p, :],
                                                                           rhs=ident[PH[e2], e2 * 64:(e2 + 1) * 64], start=True, stop=True), [Tb, ident], [ps])
                yb = bsets[ui % NW]["y"]
                ybv = f3(yb, 0)
                OP("act", lambda e, ps=ps, ybv=ybv: e.copy(out=ybv, in_=ps[:, 0:256].rearrange("p (c x) -> p c x", c=4)), [ps], [yb])
                for e2 in range(2):
                    h0 = u["h0"] + 4 * e2
                    S.dma(O["st_rwkv"][u["sq"], u["d"], h0:h0 + 4].rearrange("h v k -> v h k"), ybv[PH[e2], :, :], reads=[yb], owner=yb, queue="act")
            S.barrier()
            CK("rwkv_b")

            gnt = bsub(0, 1024, "gnt")
            S.dma(gnt.t, I["rwkv_gn"].partition_broadcast(128), writes=[gnt])
            cs_ = [bsub(1024 + i2 * 1024, 1024, f"cs{i2}") for i2 in range(6)]
            YFb, YBb, Vc, Gc, C1, C2 = cs_
            ogb = bsub(8192, 4096, "ogT")

            def c_group(g):
                tiles, c, smp = g["tiles"], g["c"], g["sample"]
                nt = len(tiles)
                T = nt * 128
                ogT = bview(ogb, 0, [8, T])
                for ti, t in enumerate(tiles):
                    if smp:
                        tk0 = (t - 8) * 128
                        L = 1024
                        S.dma(h16(YFb), YSD[0:16, tk0:tk0 + 128, :].rearrange("h t x -> t h x"), reads=[YSDB], writes=[YFb])
                        S.dma(h16(C1), YSD[16:32, L - 128 - tk0:L - tk0, :].rearrange("h t x -> t h x"), reads=[YSDB], writes=[C1])
                    else:
                        sq = t // 2
                        tk0 = (t % 2) * 128
                        L = 256
                        S.dma(h16(YFb), YPD[sq * 16:(sq + 1) * 16, tk0:tk0 + 128, :].rearrange("h t x -> t h x"), reads=[YPDB], writes=[YFb])
                        S.dma(h16(C1), YPD[(4 + sq) * 16:(5 + sq) * 16, L - 128 - tk0:L - tk0, :].rearrange("h t x -> t h x"),
                              reads=[YPDB], writes=[C1])
                    for hf in range(2):
                        ps = pP_rot.next()
                        OP("pe", lambda e, hf=hf, ps=ps: e.matmul(out=ps[:, :], lhsT=Jm[:], rhs=C1.t[:, hf * 512:(hf + 1) * 512], start=True, stop=True),
                           [Jm, C1], [ps])
                        OP("dve", lambda e, hf=hf, ps=ps: e.tensor_tensor(out=YBb.t[:, hf * 512:(hf + 1) * 512], in0=ps[:, :],
                                                                          in1=YFb.t[:, hf * 512:(hf + 1) * 512], op=ALU.add), [ps, YFb], [YBb])
                    S.dma(Vc.t, RKVG[2][t * 128:(t + 1) * 128, :], reads=[RKVGB[2][t]], writes=[Vc])
                    S.dma(Gc.t, RKVG[3][t * 128:(t + 1) * 128, :], reads=[RKVGB[3][t]], writes=[Gc])
                    nb = tmpB.next()
                    OP("dve", lambda e, nb=nb: e.tensor_reduce(out=nb[:, 0:16], in_=h16(YBb), axis=AX.X, op=ALU.add), [YBb], [nb])
                    OP("dve", lambda e, nb=nb: e.tensor_scalar(out=nb[:, 0:16], in0=nb[:, 0:16], scalar1=-1.0 / 64, scalar2=None, op0=ALU.mult), [nb], [nb])
                    OP("dve", lambda e, nb=nb: e.tensor_tensor(out=h16(YBb), in0=h16(YBb), in1=nb[:, 0:16].unsqueeze(2).to_broadcast([128, 16, 64]),
                                                               op=ALU.add), [YBb, nb], [YBb])
                    OP("pool", lambda e: e.tensor_tensor(out=C2.t, in0=YBb.t, in1=YBb.t, op=ALU.mult), [YBb], [C2])
                    OP("dve", lambda e, nb=nb: e.tensor_reduce(out=nb[:, 16:32], in_=h16(C2), axis=AX.X, op=ALU.add), [C2], [nb])
                    OP("act", lambda e, nb=nb: e.activation(out=nb[:, 32:48], in_=nb[:, 16:32], func=AF.Sqrt, scale=1.0 / 64, bias=GN_EPS), [nb], [nb])
                    OP("dve", lambda e, nb=nb: e.reciprocal(out=nb[:, 48:64], in_=nb[:, 32:48]), [nb], [nb])
                    OP("dve", lambda e, nb=nb: e.tensor_tensor(out=h16(YBb), in0=h16(YBb), in1=nb[:, 48:64].unsqueeze(2).to_broadcast([128, 16, 64]),
                                                               op=ALU.mult), [YBb, nb], [YBb])
                    OP("dve", lambda e: e.tensor_tensor(out=YBb.t, in0=YBb.t, in1=gnt.t, op=ALU.mult), [YBb, gnt], [YBb])
                    OP("dve", lambda e, t=t: e.tensor_tensor(out=h16(C2), in0=h16(Vc), in1=bsum[:, t, :].unsqueeze(2).to_broadcast([128, 16, 64]),
                                                             op=ALU.mult), [Vc, bsum], [C2])
                    OP("dve", lambda e: e.tensor_tensor(out=YBb.t, in0=YBb.t, in1=C2.t, op=ALU.add), [YBb, C2], [YBb])
                    OP("act", lambda e: e.activation(out=Gc.t, in_=Gc.t, func=AF.Silu), [Gc], [Gc])
                    OP("dve", lambda e: e.tensor_tensor(out=YBb.t, in0=YBb.t, in1=Gc.t, op=ALU.mult), [YBb, Gc], [YBb])
                    transpose_blocks(lambda i: YBb.t[:, i * 128:(i + 1) * 128], YBb, 8,
                                     lambda i0, n, ti=ti: (ogT[:, i0:i0 + n, ti * 128:(ti + 1) * 128], ogb))
                ybufs = [bsub(12288, nt * 1024, "yacc")]
                tail(layer, tiles, c, ogT, ogb, 8, Wout, ybufs)

            for g in groups:
                c_group(g)
            S.barrier()

        def final_norm():
            S.dma(gnw[:, 0:1024], I["final_norm_w"].partition_broadcast(128), writes=[gnw])
            for t in range(16):
                xt = xt_rot.next()
                S.dma(xt[:], XB[t].t, reads=[XB[t]], writes=[xt])
                s = st1.next()
                xn = xn_rot.next()
                OP("act", lambda e, xt=xt, xn=xn, s=s: e.activation(out=xn[:], in_=xt[:], func=AF.Square,
                                                                    accum_out=s[:, 0:1]), [xt], [xn, s])
                OP("act", lambda e, s=s: e.activation(out=s[:, 1:2], in_=s[:, 0:1], func=AF.Sqrt, scale=1.0 / 1024,
                                                      bias=EPS), [s], [s])
                OP("dve", lambda e, s=s: e.reciprocal(out=s[:, 2:3], in_=s[:, 1:2]), [s], [s])
                OP("dve", lambda e, xt=xt, xn=xn, s=s: e.scalar_tensor_tensor(out=xn[:], in0=xt[:], scalar=s[:, 2:3],
                                                                              in1=gnw[:, 0:1024], op0=ALU.mult, op1=ALU.mult),
                   [xt, s, gnw], [xn])
                dst = O["yp"] if t < 8 else O["ys"]
                S.dma(dst[(t % 8) * 128:(t % 8 + 1) * 128, :], xn[:], reads=[xn], queue="act")

        LAYERS = {0: layer_ret, 1: layer_rwkv, 2: layer_diff, 3: layer_na}
        try:
            CK("setup")
            for layer in layers:
                mod(layer)
                CK("mod")
                LAYERS[layer](layer)
            if final:
                final_norm()
        except _Stop:
            pass
        S.emit()
        print(f"[build] ops={S.n_ops} waits={S.n_waits} dma_sems={S.ndsem}")
    return nc


def _prep_inputs(inp):
    cst = _consts()
    f = lambda a: np.ascontiguousarray(np.asarray(a, dtype=np.float32))
    shared = {
        "norm_w": f(inp["norm_w"]), "w_mod": f(inp["w_mod"]), "b_mod": f(inp["b_mod"]),
        "final_norm_w": f(inp["final_norm_w"]),
        "ret_w_in": f(inp["ret_w_in"][0]), "ret_decay": f(inp["ret_decay"][0]).reshape(8),
        "ret_gn": f(inp["ret_gn"][0]), "ret_w_out": f(inp["ret_w_out"][0]),
        "rwkv_mu": f(inp["rwkv_mu"][0]), "rwkv_w_in": f(inp["rwkv_w_in"][0]), "rwkv_w0": f(inp["rwkv_w0"][0]),
        "rwkv_wA": f(inp["rwkv_wA"][0]), "rwkv_wB": f(inp["rwkv_wB"][0]), "rwkv_a0": f(inp["rwkv_a0"][0]),
        "rwkv_aA": f(inp["rwkv_aA"][0]), "rwkv_aB": f(inp["rwkv_aB"][0]), "rwkv_kk": f(inp["rwkv_kk"][0]),
        "rwkv_ka": f(inp["rwkv_ka"][0]), "rwkv_rk": f(inp["rwkv_rk"][0]).reshape(1024),
        "rwkv_gn": f(inp["rwkv_gn"][0]), "rwkv_w_out": f(inp["rwkv_w_out"][0]),
        "diff_w_in": f(inp["diff_w_in"][0]), "diff_lambda": f(inp["diff_lambda"][0]).reshape(256),
        "diff_gn": f(inp["diff_gn"][0]), "diff_w_out": f(inp["diff_w_out"][0]),
        "na_w_in": f(inp["na_w_in"][0]), "na_bias_x": _na_bias_expand(f(inp["na_bias"][0])),
        "na_w_out": f(inp["na_w_out"][0]),
    }
    shared.update(cst)
    maps = []
    for c in range(8):
        b = c // 4
        m = dict(shared)
        m["xp"] = f(inp["x_prompt"][4 * c:4 * c + 4]).reshape(1024, 1024)
        m["xs"] = f(inp["x_sample"][b])
        m["cond"] = np.ascontiguousarray(np.stack([f(inp["c_ctx"]), f(inp["c"][b])], 0))
        m["state_ret"] = f(inp["state_ret"][b, 0])
        m["state_rwkv"] = f(inp["state_rwkv"][b, 0])
        m["cache_diff_k"] = f(inp["cache_diff_k"][b, 0])
        m["cache_diff_v"] = f(inp["cache_diff_v"][b, 0])
        m["cache_na_k"] = f(inp["cache_na_k"][b, 0])
        m["cache_na_v"] = f(inp["cache_na_v"][b, 0])
        maps.append(m)
    return maps


_NC_CACHE = {}


def kernel(**inputs):
    maps = _prep_inputs(inputs)
    if "nc" not in _NC_CACHE:
        _NC_CACHE["nc"] = build()
    res = run_bass_kernel_spmd(_NC_CACHE["nc"], maps, core_ids=list(range(8))).results
    y_prompt = np.concatenate([r["yp"].reshape(4, 256, 1024) for r in res], 0)
    y_sample = np.stack([res[0]["ys"], res[4]["ys"]], 0)
    st_ret = np.concatenate([r["st_ret"] for r in res], 0)[:, None]
    st_rwkv = np.concatenate([r["st_rwkv"] for r in res], 0)[:, None]
    dk = np.concatenate([r["dk"] for r in res], 0)[:, None]
    dv = np.concatenate([r["dv"] for r in res], 0)[:, None]
    nk = np.concatenate([r["nk"] for r in res], 0)[:, None]
    nv = np.concatenate([r["nv"] for r in res], 0)[:, None]
    return (y_prompt, y_sample, st_ret, st_rwkv, dk, dv, nk, nv)
```

```python
import math
from contextlib import ExitStack

import numpy as np
import concourse.bass as bass
import concourse.mybir as mybir
from concourse.bass_utils import run_bass_kernel_spmd

F32 = mybir.dt.float32
BF16 = mybir.dt.bfloat16
AF = mybir.ActivationFunctionType
ALU = mybir.AluOpType
AX = mybir.AxisListType

EPS = 1e-6
GN_EPS = 1e-5
NEG = -30000.0


class Buf:
    __slots__ = ("t", "name", "lw", "rd", "dsem", "dcnt", "excl")

    def __init__(self, t, name, excl=False):
        self.excl = excl
        self.t = t
        self.name = name
        self.lw = None
        self.rd = {}
        self.dsem = None
        self.dcnt = 0

    def __getitem__(self, k):
        return self.t[k]


class Sched:
    CE = ("pe", "act", "dve", "pool")
    ALLQ = ("pe", "act", "dve", "pool", "sp")

    def __init__(self, nc, stack):
        self.nc = nc
        self.stack = stack
        self.sems = {}
        self.ecnt = {}
        for e in self.CE:
            self.sems[e] = stack.enter_context(nc.semaphore("es_" + e))
            self.ecnt[e] = 0
        self.q = {e: [] for e in self.ALLQ}
        self.seen = {e: {} for e in self.ALLQ}
        self.nbuf = 0
        self.ndsem = 0
        self.n_ops = 0
        self.n_waits = 0
        self.dma_bufs = {}
        self.CONST = Buf(None, "const")
        self.const_bufs = []
        self.snaps = {}

    def sb(self, shape, dtype=F32, name="b"):
        self.nbuf += 1
        t = self.stack.enter_context(self.nc.sbuf_tensor(f"{name}_{self.nbuf}", list(shape), dtype))
        return Buf(t, name)

    def _waits(self, eng, reads, writes):
        ev = {}
        for b in reads:
            if b.lw is not None:
                k, v = b.lw
                if ev.get(k, 0) < v:
                    ev[k] = v
            if b.excl:
                for k, v in b.rd.items():
                    if k != eng and ev.get(k, 0) < v:
                        ev[k] = v
        for b in writes:
            if b.lw is not None:
                k, v = b.lw
                if ev.get(k, 0) < v:
                    ev[k] = v
            for k, v in b.rd.items():
                if ev.get(k, 0) < v:
                    ev[k] = v
        waits = []
        seen = self.seen[eng]
        for k, v in sorted(ev.items(), key=lambda kv: -kv[1]):
            if eng == "pe" and k == "pe":
                continue
            if seen.get(k, 0) >= v:
                continue
            seen[k] = v
            waits.append((k, v))
            snap = self.snaps.get((k, v))
            if snap is not None:
                for k2, v2 in snap.items():
                    if k2 != eng and seen.get(k2, 0) < v2:
                        seen[k2] = v2
        self.n_waits += len(waits)
        return waits

    def _mark(self, me, reads, writes):
        k, v = me
        for b in reads:
            if b.rd.get(k, 0) < v:
                b.rd[k] = v
        for b in writes:
            b.lw = me
            b.rd = {}

    def op(self, eng, fn, reads=(), writes=()):
        waits = self._waits(eng, reads, writes)
        self.ecnt[eng] += 1
        me = (eng, self.ecnt[eng])
        self.snaps[me] = dict(self.seen[eng])
        self.q[eng].append((waits, fn, eng, 1))
        self._mark(me, reads, writes)
        self.n_ops += 1

    def dma(self, out_ap, in_ap, reads=(), writes=(), owner=None, queue="sp", **kw):
        if owner is self.CONST:
            waits = []
        else:
            waits = self._waits(queue, reads, writes)
        if owner is None:
            owner = writes[0] if writes else reads[0]
        if owner.dsem is None:
            self.ndsem += 1
            key = f"d{self.ndsem}"
            self.sems[key] = self.stack.enter_context(self.nc.semaphore("ds_" + key))
            owner.dsem = key
            self.dma_bufs[key] = owner
        owner.dcnt += 16
        me = (owner.dsem, owner.dcnt)
        if owner is not self.CONST:
            self.snaps[me] = dict(self.seen[queue])
        self.q[queue].append((waits, (lambda e: e.dma_start(out=out_ap, in_=in_ap, **kw)), owner.dsem, 16))
        if owner is self.CONST:
            self.const_bufs.extend(writes)
        else:
            self._mark(me, reads, writes)
        self.n_ops += 1

    def consts_done(self):
        for b in self.const_bufs:
            b.lw = (self.CONST.dsem, self.CONST.dcnt)
        self.const_bufs = []

    def barrier(self):
        tot = [(e, self.ecnt[e]) for e in self.CE] + [(k, b.dcnt) for k, b in self.dma_bufs.items()]
        for eng in self.ALLQ:
            waits = []
            for k, v in tot:
                if k == eng or v == 0:
                    continue
                if self.seen[eng].get(k, 0) >= v:
                    continue
                self.seen[eng][k] = v
                waits.append((k, v))
            if waits:
                self.q[eng].append((waits, None, None, 0))

    def emit(self):
        nc = self.nc
        self.barrier()
        with nc.Block() as block:
            def run(engobj, name):
                import os as _os
                attach = _os.environ.get("KATTACH", "1") == "1"
                for waits, fn, semkey, inc in self.q[name]:
                    if fn is None or not attach or not waits or (name == "pe" and _os.environ.get("KATTACH_PE", "0") != "1"):
                        for k, v in waits:
                            engobj.wait_ge(self.sems[k], v)
                        if fn is not None:
                            fn(engobj).then_inc(self.sems[semkey], inc)
                    else:
                        for k, v in waits[:-1]:
                            engobj.wait_ge(self.sems[k], v)
                        k, v = waits[-1]
                        fn(engobj)._wait_ge(self.sems[k], v).then_inc(self.sems[semkey], inc)

            @block.tensor
            def _(e):
                run(e, "pe")

            @block.scalar
            def _(e):
                run(e, "act")

            @block.vector
            def _(e):
                run(e, "dve")

            @block.gpsimd
            def _(e):
                run(e, "pool")

            @block.sync
            def _(e):
                run(e, "sp")


class Rot:
    def __init__(self, bufs):
        self.bufs = bufs
        self.i = 0

    def next(self):
        b = self.bufs[self.i % len(self.bufs)]
        self.i += 1
        return b


def _rope_tables(d):
    t = np.arange(1024)
    row = (t // 64).astype(np.float32)
    col = (t % 64).astype(np.float32)
    inv = (np.float32(10000.0) ** (-np.arange(0, d, 2, dtype=np.float32) / np.float32(d))).astype(np.float32)
    ang = np.stack([row[:, None] * inv[None, :], col[:, None] * inv[None, :]], axis=1).astype(np.float32)
    return np.cos(ang).astype(np.float32), np.sin(ang).astype(np.float32)


def _consts():
    c = {}
    c0, s0 = _rope_tables(128)
    c2, s2 = _rope_tables(32)
    c["rope0"] = np.ascontiguousarray(np.stack([c0, s0], 0))
    c["rope2"] = np.ascontiguousarray(np.stack([c2, s2], 0))
    kk = np.arange(128)[:, None]
    cols = np.arange(15 * 128)[None, :]
    m = cols // 128 - 7
    qq = cols % 128
    gap = (128 * m + qq - kk).astype(np.float32)
    c["gpn"] = np.ascontiguousarray(np.stack([np.maximum(gap, 0), np.maximum(-gap, 0)], 0))
    p = np.arange(128, dtype=np.float32)
    c["stexp"] = np.ascontiguousarray(np.stack([255.0 - p, 127.0 - p, p, 128.0 + p], 1))
    a_ = np.arange(128)
    c["tri"] = np.ascontiguousarray(((a_[:, None] <= a_[None, :]) & (a_[:, None] // 64 == a_[None, :] // 64)).astype(np.float32))
    j_ = np.arange(64)[:, None]
    t_ = np.arange(64)[None, :]
    su = (j_ < t_).astype(np.float32)
    iu = (j_ <= t_).astype(np.float32)
    sl = (t_ < j_).astype(np.float32)
    cm = np.zeros((64, 3, 128), np.float32)
    cm[:, 0, 0:64] = -su
    cm[:, 0, 64:128] = iu
    cm[:, 1, 0:64] = su
    cm[:, 1, 64:128] = iu
    cm[:, 2, 0:64] = -sl
    c["cmask"] = cm
    c["iota1k"] = np.ascontiguousarray(np.broadcast_to(np.arange(1024, dtype=np.float32)[None, :], (128, 1024)))
    return c


def _na_bias_expand(na_bias):
    out = np.full((16, 5, 128, 576), NEG, np.float32)
    jt = [0, 1, 2, 6, 7]
    qc = np.arange(64)[:, None]
    kc = np.arange(64)[None, :]
    cs = np.clip(qc - 8, 0, 48)
    col_ok = (kc >= cs) & (kc < cs + 16)
    cidx = np.clip(kc - qc, -15, 15) + 15
    for ti, j in enumerate(jt):
        r0 = min(max(2 * j - 4, 0), 8)
        nr = min(9, 16 - r0)
        for a in range(2):
            qr = 2 * j + a
            st = min(max(qr - 4, 0), 8)
            for i in range(nr):
                kr = r0 + i
                if st <= kr < st + 8:
                    dr = kr - qr + 7
                    blk = na_bias[:, dr][:, cidx]
                    blk = np.where(col_ok[None], blk, np.float32(NEG))
                    out[:, ti, a * 64:(a + 1) * 64, i * 64:(i + 1) * 64] = blk
    return out


NA_JT = {0: 0, 1: 1, 2: 2, 3: 2, 4: 2, 5: 2, 6: 3, 7: 4}


class _Stop(Exception):
    pass


def build(layers=(0, 1, 2, 3), final=True, stop=None):
    nc = bass.Bass("TRN2", target_bir_lowering=False)

    def CK(name):
        if stop == name:
            raise _Stop()

    def din(name, shape):
        return nc.dram_tensor(name, list(shape), F32, kind="ExternalInput").ap()

    def dout(name, shape):
        return nc.dram_tensor(name, list(shape), F32, kind="ExternalOutput").ap()

    def dscr(name, shape):
        return nc.dram_tensor(name, list(shape), F32).ap()

    I = {}
    for name, shape in [
        ("xp", (1024, 1024)), ("xs", (1024, 1024)), ("cond", (2, 1024)),
        ("state_ret", (2, 4, 256, 512)), ("state_rwkv", (2, 16, 64, 64)),
        ("cache_diff_k", (8, 256, 128)), ("cache_diff_v", (8, 256, 128)),
        ("cache_na_k", (16, 256, 64)), ("cache_na_v", (16, 256, 64)),
        ("norm_w", (4, 1024)), ("w_mod", (4, 1024, 3072)), ("b_mod", (4, 3072)), ("final_norm_w", (1024,)),
        ("ret_w_in", (1024, 6144)), ("ret_decay", (8,)), ("ret_gn", (2048,)), ("ret_w_out", (2048, 1024)),
        ("rwkv_mu", (6, 1024)), ("rwkv_w_in", (1024, 4096)), ("rwkv_w0", (2, 1024)), ("rwkv_wA", (2, 1024, 64)),
        ("rwkv_wB", (2, 64, 1024)), ("rwkv_a0", (2, 1024)), ("rwkv_aA", (2, 1024, 64)), ("rwkv_aB", (2, 64, 1024)),
        ("rwkv_kk", (1024,)), ("rwkv_ka", (1024,)), ("rwkv_rk", (1024,)), ("rwkv_gn", (1024,)),
        ("rwkv_w_out", (1024, 1024)),
        ("diff_w_in", (1024, 4096)), ("diff_lambda", (256,)), ("diff_gn", (1024,)), ("diff_w_out", (1024, 1024)),
        ("na_w_in", (1024, 4096)), ("na_bias_x", (16, 5, 128, 576)), ("na_w_out", (1024, 1024)),
        ("rope0", (2, 1024, 2, 64)), ("rope2", (2, 1024, 2, 16)), ("gpn", (2, 128, 1920)), ("stexp", (128, 4)),
        ("iota1k", (128, 1024)), ("tri", (128, 128)), ("cmask", (64, 3, 128)),
    ]:
        I[name] = din(name, shape)
    O = {}
    for name, shape in [
        ("yp", (1024, 1024)), ("ys", (1024, 1024)), ("st_ret", (4, 2, 4, 256, 512)),
        ("st_rwkv", (4, 2, 16, 64, 64)), ("dk", (4, 8, 256, 128)), ("dv", (4, 8, 256, 128)),
        ("nk", (4, 16, 256, 64)), ("nv", (4, 16, 256, 64)), ("xd", (2048, 1024)),
    ]:
        O[name] = dout(name, shape)
    XD = O["xd"]

    with ExitStack() as st:
        S = Sched(nc, st)
        OP = S.op

        ident = S.sb([128, 128], F32, "ident")
        PSt = st.enter_context(nc.psum_tensor("ps", [128, 8, 512], F32))
        PB = [Buf(PSt[:, i, :], f"ps{i}", excl=True) for i in range(8)]
        pS_rot = Rot(PB[0:4])
        pP_rot = Rot(PB[4:8])

        wf = Rot([S.sb([128, 8, 256], F32, "wf") for _ in range(2)])
        wb = Rot([S.sb([128, 8, 256], BF16, "wb") for _ in range(2)])
        xt_rot = Rot([S.sb([128, 1024], F32, "xt") for _ in range(2)])
        xn_rot = Rot([S.sb([128, 1024], F32, "xn") for _ in range(1)])
        st1 = Rot([S.sb([128, 8], F32, "st") for _ in range(6)])
        BIGN = 27648
        BIG = st.enter_context(nc.sbuf_tensor("big", [128, BIGN], F32))
        R_H = Buf(BIG[:, 0:4096], "RH")
        R_G = Buf(BIG[:, 4096:12288], "RG")
        ATTN = 15360
        ATT = BIG[:, 12288:27648]
        Jm = S.sb([128, 128], F32, "J")
        identst = S.sb([128, 64], F32, "identst")
        bsum = S.sb([128, 16, 16], F32, "bsum")
        muF = S.sb([128, 6, 8], F32, "muF")

        def bsub(off, n, name):
            assert off + n <= BIGN, (off, n)
            return Buf(BIG[:, off:off + n], name)
        scond = S.sb([128, 8, 2], F32, "scond")
        condF = S.sb([128, 8, 2], F32, "condF")
        normwF = S.sb([128, 4, 8], F32, "normwF")
        bmodF = S.sb([128, 24], F32, "bmodF")
        modF = S.sb([128, 24, 2], F32, "modF")
        scaleF = S.sb([128, 8, 2], F32, "scaleF")
        Gb = [S.sb([128, 1024], F32, "G") for _ in range(2)]
        gbt = Rot([S.sb([128, 128], F32, "gbt") for _ in range(2)])
        gnw = S.sb([128, 2048], F32, "gnw")
        rope0 = S.sb([128, 2, 8, 128], F32, "rope0")
        rope2 = S.sb([128, 2, 8, 32], F32, "rope2")
        rtmp = Rot([S.sb([128, 128], F32, "rtmp") for _ in range(2)])
        tmpA = Rot([S.sb([128, 512], F32, "tmpA") for _ in range(3)])
        tmpB = Rot([S.sb([128, 512], F32, "tmpB") for _ in range(2)])
        lamc = S.sb([128, 8], F32, "lamc")
        dlb = S.sb([128, 256], F32, "dlb")
        lgt = S.sb([128, 16], F32, "lgt")
        stx = S.sb([128, 4], F32, "stx")
        dsc = S.sb([128, 4, 4], F32, "dsc")
        strip = Rot([S.sb([128, 1920], BF16, "strip") for _ in range(2)])
        osb = Rot([S.sb([128, 512], F32, "osb") for _ in range(2)])

        def sub(off, n, name):
            assert off + n <= ATTN, (off, n)
            return Buf(ATT[:, off:off + n], name)

        def bview(buf, off_f, shape):
            n = int(np.prod(shape))
            ap = buf.t[:, off_f:off_f + n // 2].bitcast(BF16)
            if len(shape) == 1:
                return ap
            names = " ".join(f"a{i}" for i in range(len(shape)))
            kw = {f"a{i}": shape[i] for i in range(len(shape) - 1)}
            return ap.rearrange(f"p ({names}) -> p {names}", **kw)

        def fview(buf, off_f, shape):
            n = int(np.prod(shape))
            ap = buf.t[:, off_f:off_f + n]
            if len(shape) == 1:
                return ap
            names = " ".join(f"a{i}" for i in range(len(shape)))
            kw = {f"a{i}": shape[i] for i in range(len(shape) - 1)}
            return ap.rearrange(f"p ({names}) -> p {names}", **kw)

        XB = [Buf(XD[t * 128:(t + 1) * 128, :], f"xd{t}") for t in range(16)]
        first_layer = layers[0]

        def x_src(layer, t):
            if layer == first_layer:
                src = I["xp"] if t < 8 else I["xs"]
                tt = t % 8
                return src[tt * 128:(tt + 1) * 128, :], []
            return XB[t].t, [XB[t]]

        C = S.CONST
        for c2 in range(2):
            S.dma(condF[:, :, c2], I["cond"][c2].rearrange("(k p) -> p k", p=128), writes=[condF], owner=C,
                  allow_slow_non_contiguous=True)
        for l2 in range(4):
            S.dma(normwF[:, l2, :], I["norm_w"][l2].rearrange("(k p) -> p k", p=128), writes=[normwF], owner=C,
                  allow_slow_non_contiguous=True)
        for cs in range(2):
            for d, (rt, hw) in enumerate(((rope0, 64), (rope2, 16))):
                src = I["rope0" if d == 0 else "rope2"][cs].rearrange("(t p) a f -> p t (a f)", p=128)
                S.dma(rt[:, cs, :, :], src, writes=[rt], owner=C)
        S.dma(stx[:], I["stexp"], writes=[stx], owner=C)
        S.consts_done()
        OP("pool", lambda e: e.memset(ident[:], 0.0), [], [ident])
        OP("pool", lambda e: e.affine_select(out=ident[:], in_=ident[:], pattern=[[-1, 128]], compare_op=ALU.not_equal,
                                             fill=1.0, base=0, channel_multiplier=1), [ident], [ident])
        OP("act", lambda e: e.activation(out=scond[:], in_=condF[:], func=AF.Silu), [condF], [scond])
        OP("pool", lambda e: e.tensor_tensor(out=identst[:], in0=ident[:, 0:64], in1=ident[:, 64:128], op=ALU.add), [ident], [identst])
        OP("pool", lambda e: e.memset(Jm[:], 0.0), [], [Jm])
        OP("pool", lambda e: e.affine_select(out=Jm[:], in_=Jm[:], pattern=[[1, 128]], compare_op=ALU.not_equal,
                                             fill=1.0, base=-127, channel_multiplier=1), [Jm], [Jm])

        wctr = [0]

        def load_w(W, k0, c0, ncols=256, cast=True):
            f = wf.next()
            src = W[k0:k0 + 1024, c0:c0 + ncols].rearrange("(k p) n -> p k n", p=128)
            S.dma(f[:, :, 0:ncols], src, writes=[f])
            if not cast:
                return f
            b = wb.next()
            wctr[0] += 1
            if wctr[0] % 2:
                OP("act", lambda e: e.copy(out=b[:, :, 0:ncols], in_=f[:, :, 0:ncols]), [f], [b])
            else:
                OP("dve", lambda e: e.tensor_copy(out=b[:, :, 0:ncols], in_=f[:, :, 0:ncols]), [f], [b])
            return b

        def mod(layer):
            S.dma(bmodF[:], I["b_mod"][layer].rearrange("(m p) -> p m", p=128), writes=[bmodF],
                  allow_slow_non_contiguous=True)
            psm = pP_rot.next()
            for blk in range(12):
                w = load_w(I["w_mod"][layer], 0, blk * 256, cast=False)
                for mm in range(2):
                    m = blk * 2 + mm
                    for k in range(8):
                        OP("pe", lambda e, m=m, mm=mm, k=k, w=w: e.matmul(
                            out=psm[:, m * 2:m * 2 + 2], lhsT=w[:, k, mm * 128:(mm + 1) * 128], rhs=scond[:, k, :],
                            start=(k == 0), stop=(k == 7)), [w, scond], [psm])
            OP("dve", lambda e: e.tensor_tensor(out=modF[:], in0=psm[:, 0:48].rearrange("p (m c) -> p m c", c=2),
                                                in1=bmodF[:].unsqueeze(2).to_broadcast([128, 24, 2]), op=ALU.add),
               [psm, bmodF], [modF])
            OP("dve", lambda e: e.tensor_scalar(out=scaleF[:], in0=modF[:, 8:16, :], scalar1=1.0, scalar2=None,
                                                op0=ALU.add), [modF], [scaleF])
            OP("dve", lambda e: e.tensor_tensor(out=scaleF[:], in0=scaleF[:],
                                                in1=normwF[:, layer, :].unsqueeze(2).to_broadcast([128, 8, 2]),
                                                op=ALU.mult), [scaleF, normwF], [scaleF])
            for c in range(2):
                pg = [pP_rot.next(), pP_rot.next()]
                for k in range(8):
                    g = gbt.next()
                    OP("dve", lambda e, g=g, k=k, c=c: e.tensor_copy(
                        out=g[:], in_=modF[:, 16 + k, c:c + 1].to_broadcast([128, 128])), [modF], [g])
                    OP("pe", lambda e, g=g, k=k, pg=pg: e.matmul(
                        out=pg[k // 4][:, (k % 4) * 128:(k % 4 + 1) * 128], lhsT=g[:], rhs=ident[:],
                        start=True, stop=True), [g, ident], [pg[k // 4]])
                for hlf in range(2):
                    OP("act", lambda e, hlf=hlf, c=c, pg=pg: e.copy(out=Gb[c][:, hlf * 512:(hlf + 1) * 512],
                                                                    in_=pg[hlf][:, :]), [pg[hlf]], [Gb[c]])

        def front(layer, tiles, c, hT_of):
            for ti, t in enumerate(tiles):
                xt = xt_rot.next()
                src, rb = x_src(layer, t)
                S.dma(xt[:], src, reads=rb, writes=[xt])
                s = st1.next()
                xn = xn_rot.next()
                OP("act", lambda e, xt=xt, xn=xn, s=s: e.activation(out=xn[:], in_=xt[:], func=AF.Square,
                                                                    accum_out=s[:, 0:1]), [xt], [xn, s])
                OP("act", lambda e, s=s: e.activation(out=s[:, 1:2], in_=s[:, 0:1], func=AF.Sqrt, scale=1.0 / 1024,
                                                      bias=EPS), [s], [s])
                OP("dve", lambda e, s=s: e.reciprocal(out=s[:, 2:3], in_=s[:, 1:2]), [s], [s])
                OP("dve", lambda e, xt=xt, xn=xn, s=s: e.tensor_scalar(out=xn[:], in0=xt[:], scalar1=s[:, 2:3],
                                                                       scalar2=None, op0=ALU.mult), [xt, s], [xn])
                pp = [pP_rot.next(), pP_rot.next()]
                for k in range(8):
                    OP("pe", lambda e, k=k, xn=xn, pp=pp: e.transpose(
                        out=pp[k // 4][:, (k % 4) * 128:(k % 4 + 1) * 128], in_=xn[:, k * 128:(k + 1) * 128],
                        identity=ident[:]), [xn, ident], [pp[k // 4]])
                for k in range(8):
                    dst, db = hT_of(k, ti)
                    OP("act", lambda e, k=k, dst=dst, pp=pp: e.activation(
                        out=dst, in_=pp[k // 4][:, (k % 4) * 128:(k % 4 + 1) * 128], func=AF.Identity,
                        scale=scaleF[:, k, c:c + 1], bias=modF[:, k, c:c + 1]), [pp[k // 4], scaleF, modF], [db])

        def proj(hT, hbuf, ti, w, ncols, ps):
            for k in range(8):
                OP("pe", lambda e, k=k: e.matmul(out=ps[:, 0:ncols], lhsT=hT[:, k, ti * 128:(ti + 1) * 128],
                                                 rhs=w[:, k, 0:ncols], start=(k == 0), stop=(k == 7)),
                   [hbuf, w], [ps])

        def transpose_blocks(src_ap_of, srcbuf, nblk, dst_of, evac="act", scl=None):
            i = 0
            while i < nblk:
                n = min(4, nblk - i)
                ps = pP_rot.next()
                for j in range(n):
                    OP("pe", lambda e, i=i, j=j, ps=ps: e.transpose(out=ps[:, j * 128:(j + 1) * 128],
                                                                    in_=src_ap_of(i + j), identity=ident[:]),
                       [srcbuf, ident], [ps])
                dst, db = dst_of(i, n)
                if evac == "act" and scl is not None:
                    OP("act", lambda e, dst=dst, ps=ps, n=n: e.activation(
                        out=dst, in_=ps[:, 0:n * 128].rearrange("p (a b) -> p a b", a=n), func=AF.Copy, scale=scl), [ps], [db])
                elif evac == "act":
                    OP("act", lambda e, dst=dst, ps=ps, n=n: e.copy(
                        out=dst, in_=ps[:, 0:n * 128].rearrange("p (a b) -> p a b", a=n)), [ps], [db])
                else:
                    OP("dve", lambda e, dst=dst, ps=ps, n=n: e.tensor_copy(
                        out=dst, in_=ps[:, 0:n * 128].rearrange("p (a b) -> p a b", a=n)), [ps], [db])
                i += n

        def rope(src, dst, table, half, tile, ngrp):
            n = ngrp * 4 * half
            sv = src[:, 0:n].rearrange("p (g a two f) -> p g a two f", g=ngrp, a=2, two=2)
            dv = dst[:, 0:n].rearrange("p (g a two f) -> p g a two f", g=ngrp, a=2, two=2)
            cos = table[:, 0, tile, :].rearrange("p (a f) -> p a f", a=2).unsqueeze(1).to_broadcast([128, ngrp, 2, half])
            sin = table[:, 1, tile, :].rearrange("p (a f) -> p a f", a=2).unsqueeze(1).to_broadcast([128, ngrp, 2, half])
            x1, x2 = sv[:, :, :, 0, :], sv[:, :, :, 1, :]
            o1, o2 = dv[:, :, :, 0, :], dv[:, :, :, 1, :]
            t1 = rtmp.next()
            t2 = rtmp.next()
            m = ngrp * 2 * half
            t1v = t1[:, 0:m].rearrange("p (g a f) -> p g a f", g=ngrp, a=2)
            t2v = t2[:, 0:m].rearrange("p (g a f) -> p g a f", g=ngrp, a=2)
            OP("dve", lambda e: e.tensor_tensor(out=o1, in0=x1, in1=cos, op=ALU.mult), [src, table], [dst])
            OP("pool", lambda e: e.tensor_tensor(out=t1v, in0=x2, in1=sin, op=ALU.mult), [src, table], [t1])
            OP("dve", lambda e: e.tensor_tensor(out=o1, in0=o1, in1=t1v, op=ALU.subtract), [dst, t1], [dst])
            OP("pool", lambda e: e.tensor_tensor(out=t2v, in0=x1, in1=sin, op=ALU.mult), [src, table], [t2])
            OP("dve", lambda e: e.tensor_tensor(out=o2, in0=x2, in1=cos, op=ALU.mult), [src, table], [dst])
            OP("dve", lambda e: e.tensor_tensor(out=o2, in0=o2, in1=t2v, op=ALU.add), [dst, t2], [dst])

        def tail(layer, tiles, c, ogT, ogbuf, KC, Wout, ybufs):
            nt = len(tiles)
            yacc = ATT[:, 0:nt * 1024].rearrange("p (t n) -> p t n", t=nt)
            for cb in range(4):
                ws = [load_w(Wout, kk * 1024, cb * 256) for kk in range(KC // 8)]
                for ti in range(nt):
                    ps = pP_rot.next()
                    for kc in range(KC):
                        w = ws[kc // 8]
                        OP("pe", lambda e, kc=kc, w=w, ti=ti, ps=ps: e.matmul(
                            out=ps[:, 0:256], lhsT=ogT[:, kc, ti * 128:(ti + 1) * 128], rhs=w[:, kc % 8, :],
                            start=(kc == 0), stop=(kc == KC - 1)), [ogbuf, w], [ps])
                    OP("dve", lambda e, ti=ti, cb=cb, ps=ps: e.tensor_tensor(
                        out=yacc[:, ti, cb * 256:(cb + 1) * 256], in0=ps[:, 0:256],
                        in1=Gb[c][:, cb * 256:(cb + 1) * 256], op=ALU.mult), [ps, Gb[c]], ybufs)
            for ti, t in enumerate(tiles):
                xt = xt_rot.next()
                src, rb = x_src(layer, t)
                S.dma(xt[:], src, reads=rb, writes=[xt])
                OP("dve", lambda e, xt=xt, ti=ti: e.tensor_tensor(out=xt[:], in0=xt[:], in1=yacc[:, ti, :], op=ALU.add),
                   [xt] + ybufs, [xt])
                S.dma(XB[t].t, xt[:], reads=[xt], writes=[XB[t]], owner=xt, queue="act")

        def softmax_un(src, srcbufs, scale, Pout, Pbuf):
            s = st1.next()
            ax = AX.XY if len(src.shape) == 3 else AX.X
            if scale == 1.0:
                OP("dve", lambda e: e.tensor_reduce(out=s[:, 1:2], in_=src, axis=ax, op=ALU.max, negate=True), srcbufs, [s])
            else:
                OP("dve", lambda e: e.tensor_reduce(out=s[:, 0:1], in_=src, axis=ax, op=ALU.max), srcbufs, [s])
                OP("dve", lambda e: e.tensor_scalar(out=s[:, 1:2], in0=s[:, 0:1], scalar1=-scale, scalar2=None,
                                                    op0=ALU.mult), [s], [s])
            OP("act", lambda e: e.activation(out=Pout, in_=src, func=AF.Exp, scale=scale, bias=s[:, 1:2],
                                             accum_out=s[:, 2:3]), srcbufs + [s], [Pbuf, s])
            OP("dve", lambda e: e.reciprocal(out=s[:, 3:4], in_=s[:, 2:3]), [s], [s])
            return s

        def pv(pc, pcbuf, kblocks, pcT, pcTbuf, vof, N, pso):
            nb = len(kblocks)
            i = 0
            while i < nb:
                n = min(4, nb - i)
                ps = pP_rot.next()
                for j in range(n):
                    off, nk = kblocks[i + j]
                    OP("pe", lambda e, j=j, off=off, nk=nk, ps=ps: e.transpose(
                        out=ps[0:nk, j * 128:(j + 1) * 128], in_=pc[:, off:off + nk], identity=ident[:]),
                       [pcbuf, ident], [ps])
                full = all(kblocks[i + j][1] == 128 for j in range(n))
                if full:
                    OP("act", lambda e, i=i, n=n, ps=ps: e.copy(
                        out=pcT[:, i:i + n, :], in_=ps[:, 0:n * 128].rearrange("p (a b) -> p a b", a=n)),
                       [ps], [pcTbuf])
                else:
                    for j in range(n):
                        nk = kblocks[i + j][1]
                        OP("act", lambda e, i=i, j=j, nk=nk, ps=ps: e.copy(
                            out=pcT[0:nk, i + j, :], in_=ps[0:nk, j * 128:(j + 1) * 128]), [ps], [pcTbuf])
                i += n
            for i, (off, nk) in enumerate(kblocks):
                vap, vbuf = vof(i)
                OP("pe", lambda e, i=i, nk=nk, vap=vap: e.matmul(out=pso[:, 0:N], lhsT=pcT[0:nk, i, :], rhs=vap,
                                                               start=(i == 0), stop=(i == nb - 1)),
                   [pcTbuf, vbuf], [pso])

        groups = [
            dict(tiles=[0, 1, 2, 3, 4, 5, 6, 7], c=0, seqs=[(0, 2, 0), (2, 2, 1), (4, 2, 2), (6, 2, 3)], sample=False, pair=0),
            dict(tiles=list(range(8, 16)), c=1, seqs=[(0, 8, -1)], sample=True, pair=-1),
        ]

        def layer_diff(layer):
            lam_init = 0.8 - 0.6 * math.exp(-0.3 * layer)
            Win, Wout = I["diff_w_in"], I["diff_w_out"]
            S.dma(gnw[:, 0:1024], I["diff_gn"].partition_broadcast(128), writes=[gnw])
            OP("dve", lambda e: e.tensor_scalar(out=gnw[:, 0:1024], in0=gnw[:, 0:1024], scalar1=1.0 - lam_init,
                                                scalar2=None, op0=ALU.mult), [gnw], [gnw])
            S.dma(dlb[:], I["diff_lambda"].partition_broadcast(128), writes=[dlb])
            dl = dlb[:].rearrange("p (a f) -> p a f", a=4)
            t = tmpA.next()
            for i2 in range(2):
                OP("dve", lambda e, i2=i2: e.tensor_tensor(out=t[:, i2 * 64:(i2 + 1) * 64], in0=dl[:, 2 * i2, :],
                                                           in1=dl[:, 2 * i2 + 1, :], op=ALU.mult), [dlb], [t])
            OP("dve", lambda e: e.tensor_reduce(out=lamc[:, 0:2], in_=t[:, 0:128].rearrange("p (a f) -> p a f", a=2),
                                                axis=AX.X, op=ALU.add), [t], [lamc])
            OP("act", lambda e: e.activation(out=lamc[:, 2:4], in_=lamc[:, 0:2], func=AF.Exp), [lamc], [lamc])
            OP("dve", lambda e: e.tensor_tensor(out=lamc[:, 4:5], in0=lamc[:, 2:3], in1=lamc[:, 3:4], op=ALU.subtract),
               [lamc], [lamc])
            OP("dve", lambda e: e.tensor_scalar(out=lamc[:, 5:6], in0=lamc[:, 4:5], scalar1=lam_init, scalar2=None,
                                                op0=ALU.add), [lamc], [lamc])
            scale = 64 ** -0.5

            def do_group(g):
                S.barrier()
                tiles, c, smp = g["tiles"], g["c"], g["sample"]
                nt = len(tiles)
                T = nt * 128
                NK = T + 256 if smp else 256
                hT = bview(R_H, 0, [8, T])
                ogT = bview(R_G, 0, [8, T])
                qTb = sub(0, 1024, "qT")
                kTb = sub(1024, 1280, "kT")
                vbb = sub(2304, 1280, "vb")
                Psets = [(sub(3584, 1280, "P1a"), sub(4864, 1280, "P2a"), sub(7424, 640, "pcTa")),
                         (sub(6144, 1280, "P1b"), sub(11136, 1280, "P2b"), sub(12416, 640, "pcTb"))]
                obb = sub(8064, 2048, "ob")
                ckb = sub(10112, 1024, "ck")
                qT = bview(qTb, 0, [2, 1024])
                kT = bview(kTb, 0, [2, 1280])
                vb = bview(vbb, 0, [10, 256])
                ob = fview(obb, 0, [8, 256])
                ck = fview(ckb, 0, [2, 512])
                front(layer, tiles, c, lambda k, ti: (hT[:, k, ti * 128:(ti + 1) * 128], R_H))
                CK("front")
                for hb in range(4):
                    for which, dstT, dstb in ((0, qT, qTb), (1, kT, kTb)):
                        w = load_w(Win, 0, which * 1024 + hb * 256)
                        for ti in range(nt):
                            ps = pP_rot.next()
                            proj(hT, R_H, ti, w, 256, ps)
                            ta = tmpA.next()
                            OP("act", lambda e, ta=ta, ps=ps: e.copy(out=ta[:, 0:256], in_=ps[:, 0:256]), [ps], [ta])
                            srcb = ta
                            if smp:
                                tb = tmpB.next()
                                rope(ta, tb, rope2, 16, ti, 4)
                                srcb = tb
                            elif which == 1:
                                sq = g["seqs"][ti // 2][2]
                                tt = ti % 2
                                S.dma(O["dk"][sq, 2 * hb:2 * hb + 2, tt * 128:(tt + 1) * 128, :].rearrange("h t d -> t h d"),
                                      ta[:, 0:256].rearrange("p (h d) -> p h d", h=2), reads=[ta], queue="act")
                            transpose_blocks(lambda i, srcb=srcb: srcb[:, i * 128:(i + 1) * 128], srcb, 2,
                                             lambda i0, n, dstT=dstT, dstb=dstb, ti=ti: (dstT[:, i0:i0 + n, ti * 128:(ti + 1) * 128], dstb),
                                             scl=(scale if which == 0 else None))
                    CK("qk")
                    w = load_w(Win, 0, 2048 + hb * 256)
                    for ti in range(nt):
                        ps = pP_rot.next()
                        proj(hT, R_H, ti, w, 256, ps)
                        ta = tmpA.next()
                        OP("act", lambda e, ta=ta, ps=ps: e.copy(out=ta[:, 0:256], in_=ps[:, 0:256]), [ps], [ta])
                        OP("dve", lambda e, ta=ta, ti=ti: e.tensor_copy(out=vb[:, ti, :], in_=ta[:, 0:256]), [ta], [vbb])
                        if not smp:
                            sq = g["seqs"][ti // 2][2]
                            tt = ti % 2
                            S.dma(O["dv"][sq, 2 * hb:2 * hb + 2, tt * 128:(tt + 1) * 128, :].rearrange("h t d -> t h d"),
                                  ta[:, 0:256].rearrange("p (h d) -> p h d", h=2), reads=[ta], queue="act")
                    if smp:
                        for tt in range(2):
                            S.dma(ck[:, 0, 0:256].rearrange("p (h d) -> p h d", h=2),
                                  I["cache_diff_k"][2 * hb:2 * hb + 2, tt * 128:(tt + 1) * 128, :].rearrange("h t d -> t h d"),
                                  writes=[ckb])
                            transpose_blocks(lambda i: ck[:, 0, i * 128:(i + 1) * 128], ckb, 2,
                                             lambda i0, n, tt=tt: (kT[:, i0:i0 + n, 1024 + tt * 128:1024 + (tt + 1) * 128], kTb))
                            S.dma(ck[:, 1, 0:256].rearrange("p (h d) -> p h d", h=2),
                                  I["cache_diff_v"][2 * hb:2 * hb + 2, tt * 128:(tt + 1) * 128, :].rearrange("h t d -> t h d"),
                                  writes=[ckb])
                            OP("dve", lambda e, tt=tt: e.tensor_copy(out=vb[:, 8 + tt, :], in_=ck[:, 1, 0:256]), [ckb], [vbb])
                    CK("v")
                    nkt = NK // 128
                    nblk, blk = (1, 256) if NK == 256 else (4, 320)

                    def att_s1(t0, hh, qi, k_):
                        P1b_, P2b_ = Psets[k_][0], Psets[k_][1]
                        P1_, P2_ = P1b_.t, P2b_.t
                        tq = t0 + qi
                        stats = []
                        for comp, (Pb, Pap) in enumerate(((P1b_, P1_), (P2b_, P2_))):
                            pr = slice(comp * 64, (comp + 1) * 64)
                            for b in range(nblk):
                                k0 = (t0 * 128 if not smp else 0) + b * blk
                                OP("pe", lambda e, pr=pr, k0=k0, b=b: e.matmul(
                                    out=PB[b][:, 0:blk], lhsT=qT[pr, hh, tq * 128:(tq + 1) * 128],
                                    rhs=kT[pr, hh, k0:k0 + blk], start=True, stop=True), [qTb, kTb], [PB[b]])
                            src = PSt[:, 0:nblk, 0:blk]
                            stats.append(softmax_un(src, PB[0:nblk], 1.0, Pap[:, 0:NK].rearrange("p (a b) -> p a b", a=nblk), Pb))
                        s1, s2 = stats
                        OP("dve", lambda e: e.tensor_scalar(out=P2_[:, 0:NK], in0=P2_[:, 0:NK], scalar1=s2[:, 3:4], scalar2=lamc[:, 5:6], op0=ALU.mult,
                                                            op1=ALU.mult), [P2b_, s2, lamc], [P2b_])
                        OP("dve", lambda e: e.scalar_tensor_tensor(out=P1_[:, 0:NK], in0=P1_[:, 0:NK], scalar=s1[:, 3:4], in1=P2_[:, 0:NK],
                                                                   op0=ALU.mult, op1=ALU.subtract), [P1b_, P2b_, s1], [P1b_])
                        return (t0, hh, qi, k_)

                    def att_s2(ctx):
                        t0, hh, qi, k_ = ctx
                        P1b_, pcTb_ = Psets[k_][0], Psets[k_][2]
                        pcT_ = bview(pcTb_, 0, [10, 128])
                        tq = t0 + qi
                        h = 2 * hb + hh
                        pso = pP_rot.next()
                        vt0 = 0 if smp else t0
                        pv(P1b_.t, P1b_, [(i * 128, 128) for i in range(nkt)], pcT_, pcTb_,
                           lambda i: (vb[:, vt0 + i, hh * 128:(hh + 1) * 128], vbb), 128, pso)
                        s = st1.next()
                        ta = tmpA.next()
                        OP("act", lambda e: e.activation(out=ta[:, 0:128], in_=pso[:, 0:128], func=AF.Square, accum_out=s[:, 0:1]), [pso], [ta, s])
                        OP("act", lambda e: e.activation(out=s[:, 1:2], in_=s[:, 0:1], func=AF.Sqrt, scale=1.0 / 128, bias=EPS), [s], [s])
                        OP("dve", lambda e: e.reciprocal(out=s[:, 2:3], in_=s[:, 1:2]), [s], [s])
                        OP("dve", lambda e: e.scalar_tensor_tensor(out=ob[:, tq, hh * 128:(hh + 1) * 128], in0=pso[:, 0:128], scalar=s[:, 2:3],
                                                                   in1=gnw[:, h * 128:(h + 1) * 128], op0=ALU.mult, op1=ALU.mult), [pso, s, gnw], [obb])

                    its = [(t0, hh, qi) for (t0, ntq, sq) in g["seqs"] for hh in range(2) for qi in range(ntq)]
                    prev = None
                    for ii, (t0, hh, qi) in enumerate(its):
                        ctx = att_s1(t0, hh, qi, ii % 2)
                        if prev is not None:
                            att_s2(prev)
                        prev = ctx
                    att_s2(prev)
                    CK("attn")
                    w = load_w(Win, 0, 3072 + hb * 256)
                    for ti in range(nt):
                        ps = pP_rot.next()
                        proj(hT, R_H, ti, w, 256, ps)
                        ta = tmpA.next()
                        OP("act", lambda e, ta=ta, ps=ps: e.activation(out=ta[:, 0:256], in_=ps[:, 0:256], func=AF.Silu), [ps], [ta])
                        OP("dve", lambda e, ta=ta, ti=ti: e.tensor_tensor(out=ob[:, ti, :], in0=ob[:, ti, :], in1=ta[:, 0:256],
                                                                          op=ALU.mult), [obb, ta], [obb])
                        transpose_blocks(lambda i, ti=ti: ob[:, ti, i * 128:(i + 1) * 128], obb, 2,
                                         lambda i0, n, ti=ti, hb=hb: (ogT[:, 2 * hb + i0:2 * hb + i0 + n, ti * 128:(ti + 1) * 128], R_G))
                S.barrier()
                ybufs = [Buf(ATT[:, 0:nt * 1024], "yacc")]
                tail(layer, tiles, c, ogT, R_G, 8, Wout, ybufs)

            for g in groups:
                do_group(g)
            S.barrier()

        nab_rot = Rot([S.sb([128, 576], F32, "nab") for _ in range(2)])

        def layer_na(layer):
            Win, Wout = I["na_w_in"], I["na_w_out"]
            scale = 64 ** -0.5

            def do_group(g):
                S.barrier()
                tiles, c, smp = g["tiles"], g["c"], g["sample"]
                nt = len(tiles)
                T = nt * 128
                hT = bview(R_H, 0, [8, T])
                ogT = bview(R_G, 0, [8, T])
                qTb = sub(0, 1024, "qT")
                kTb = sub(1024, 1280, "kT")
                vbb = sub(2304, 1280, "vb")
                NAsets = [(sub(3584, 1280, "P1a"), sub(7424, 640, "pcTa")), (sub(6144, 1280, "P1b"), sub(12416, 640, "pcTb"))]
                obb = sub(8064, 2048, "ob")
                ckb = sub(10112, 1024, "ck")
                qT = bview(qTb, 0, [2, 1024])
                kT = bview(kTb, 0, [2, 1280])
                vb = bview(vbb, 0, [10, 256])
                ob = fview(obb, 0, [8, 256])
                ck = fview(ckb, 0, [2, 512])
                front(layer, tiles, c, lambda k, ti: (hT[:, k, ti * 128:(ti + 1) * 128], R_H))
                for hb in range(4):
                    for which, dstT, dstb in ((0, qT, qTb), (1, kT, kTb)):
                        w = load_w(Win, 0, which * 1024 + hb * 256)
                        for ti in range(nt):
                            ps = pP_rot.next()
                            proj(hT, R_H, ti, w, 256, ps)
                            ta = tmpA.next()
                            OP("act", lambda e, ta=ta, ps=ps: e.copy(out=ta[:, 0:256], in_=ps[:, 0:256]), [ps], [ta])
                            if which == 1 and not smp:
                                sq = g["seqs"][ti // 2][2]
                                tt = ti % 2
                                S.dma(O["nk"][sq, 4 * hb:4 * hb + 4, tt * 128:(tt + 1) * 128, :].rearrange("h t d -> t h d"),
                                      ta[:, 0:256].rearrange("p (h d) -> p h d", h=4), reads=[ta], queue="act")
                            transpose_blocks(lambda i, ta=ta: ta[:, i * 128:(i + 1) * 128], ta, 2,
                                             lambda i0, n, dstT=dstT, dstb=dstb, ti=ti: (dstT[:, i0:i0 + n, ti * 128:(ti + 1) * 128], dstb),
                                             scl=(scale if which == 0 else None))
                    w = load_w(Win, 0, 2048 + hb * 256)
                    for ti in range(nt):
                        ps = pP_rot.next()
                        proj(hT, R_H, ti, w, 256, ps)
                        ta = tmpA.next()
                        OP("act", lambda e, ta=ta, ps=ps: e.copy(out=ta[:, 0:256], in_=ps[:, 0:256]), [ps], [ta])
                        OP("dve", lambda e, ta=ta, ti=ti: e.tensor_copy(out=vb[:, ti, :], in_=ta[:, 0:256]), [ta], [vbb])
                        if not smp:
                            sq = g["seqs"][ti // 2][2]
                            tt = ti % 2
                            S.dma(O["nv"][sq, 4 * hb:4 * hb + 4, tt * 128:(tt + 1) * 128, :].rearrange("h t d -> t h d"),
                                  ta[:, 0:256].rearrange("p (h d) -> p h d", h=4), reads=[ta], queue="act")
                    if smp:
                        for tt in range(2):
                            S.dma(ck[:, 0, 0:256].rearrange("p (h d) -> p h d", h=4),
                                  I["cache_na_k"][4 * hb:4 * hb + 4, tt * 128:(tt + 1) * 128, :].rearrange("h t d -> t h d"),
                                  writes=[ckb])
                            transpose_blocks(lambda i: ck[:, 0, i * 128:(i + 1) * 128], ckb, 2,
                                             lambda i0, n, tt=tt: (kT[:, i0:i0 + n, 1024 + tt * 128:1024 + (tt + 1) * 128], kTb))
                            S.dma(ck[:, 1, 0:256].rearrange("p (h d) -> p h d", h=4),
                                  I["cache_na_v"][4 * hb:4 * hb + 4, tt * 128:(tt + 1) * 128, :].rearrange("h t d -> t h d"),
                                  writes=[ckb])
                            OP("dve", lambda e, tt=tt: e.tensor_copy(out=vb[:, 8 + tt, :], in_=ck[:, 1, 0:256]), [ckb], [vbb])
                    def att_s1(t0, hh, qi, k_):
                            P1b, pcTb = NAsets[k_]
                            P1 = P1b.t
                            h = 4 * hb + hh
                            cc = hh // 2
                            pr = slice((hh % 2) * 64, (hh % 2) * 64 + 64)
                            if True:
                                tq = t0 + qi
                                if not smp:
                                    OP("pe", lambda e, pr=pr, cc=cc, tq=tq, t0=t0: e.matmul(
                                        out=PB[0][:, 0:256], lhsT=qT[pr, cc, tq * 128:(tq + 1) * 128],
                                        rhs=kT[pr, cc, t0 * 128:t0 * 128 + 256], start=True, stop=True), [qTb, kTb], [PB[0]])
                                    s1 = softmax_un(PB[0][:, 0:256], [PB[0]], 1.0, P1[:, 0:256], P1b)
                                    NKs = 256
                                    kblocks = [(0, 128), (128, 128)]
                                    vof = lambda i, hh=hh, t0=t0: (vb[:, t0 + i, hh * 64:(hh + 1) * 64], vbb)
                                else:
                                    j = qi
                                    r0 = min(max(2 * j - 4, 0), 8)
                                    nr = min(9, 16 - r0)
                                    nloc = nr * 64
                                    NKs = nloc + 256
                                    blk = NKs // 2
                                    nab = nab_rot.next()
                                    S.dma(nab[:], I["na_bias_x"][h, NA_JT[j]], writes=[nab])
                                    segs = [(r0 * 64, nloc, 0, True), (1024, 256, nloc, False)]
                                    pieces = []
                                    for key0, n, col0, biased in segs:
                                        done = 0
                                        while done < n:
                                            col = col0 + done
                                            b = col // blk
                                            m = min(n - done, (b + 1) * blk - col)
                                            pieces.append((key0 + done, m, b, col - b * blk, col, biased, col0 + done - col0 + (0 if not biased else 0)))
                                            done += m
                                    for (k0, m, b, bc, col, biased, _) in pieces:
                                        OP("pe", lambda e, pr=pr, cc=cc, tq=tq, k0=k0, m=m, b=b, bc=bc: e.matmul(
                                            out=PB[b][:, bc:bc + m], lhsT=qT[pr, cc, tq * 128:(tq + 1) * 128],
                                            rhs=kT[pr, cc, k0:k0 + m], start=True, stop=True), [qTb, kTb], [PB[b]])
                                    for (k0, m, b, bc, col, biased, _) in pieces:
                                        if biased:
                                            OP("dve", lambda e, m=m, b=b, bc=bc, col=col, nab=nab: e.tensor_tensor(
                                                out=P1[:, col:col + m], in0=PB[b][:, bc:bc + m],
                                                in1=nab[:, col:col + m], op=ALU.add), [PB[b], nab], [P1b])
                                        else:
                                            OP("dve", lambda e, m=m, b=b, bc=bc, col=col: e.tensor_copy(
                                                out=P1[:, col:col + m], in_=PB[b][:, bc:bc + m]), [PB[b]], [P1b])
                                    s1 = softmax_un(P1[:, 0:NKs], [P1b], 1.0, P1[:, 0:NKs], P1b)
                                    kblocks = [(i * 128, 128) for i in range(nloc // 128)]
                                    if nloc % 128:
                                        kblocks.append((nloc - 64, 64))
                                    nlb = len(kblocks)
                                    kblocks += [(nloc, 128), (nloc + 128, 128)]

                                    def vof(i, hh=hh, r0=r0, nlb=nlb, kblocks=kblocks):
                                        if i < nlb:
                                            nk = kblocks[i][1]
                                            return vb[0:nk, r0 // 2 + i, hh * 64:(hh + 1) * 64], vbb
                                        return vb[:, 8 + (i - nlb), hh * 64:(hh + 1) * 64], vbb
                                OP("dve", lambda e, s1=s1, NKs=NKs: e.tensor_scalar(out=P1[:, 0:NKs], in0=P1[:, 0:NKs], scalar1=s1[:, 3:4],
                                                                                scalar2=None, op0=ALU.mult), [P1b, s1], [P1b])
                                return (tq, hh, k_, kblocks, vof)

                    def att_s2(ctx):
                        tq, hh, k_, kblocks, vof = ctx
                        P1b, pcTb = NAsets[k_]
                        pcT = bview(pcTb, 0, [10, 128])
                        pso = pP_rot.next()
                        pv(P1b.t, P1b, kblocks, pcT, pcTb, vof, 64, pso)
                        OP("act", lambda e: e.copy(out=ob[:, tq, hh * 64:(hh + 1) * 64], in_=pso[:, 0:64]), [pso], [obb])

                    its = [(t0, hh, qi) for (t0, ntq, sq) in g["seqs"] for hh in range(4) for qi in range(ntq)]
                    prev = None
                    for ii, (t0, hh, qi) in enumerate(its):
                        ctx = att_s1(t0, hh, qi, ii % 2)
                        if prev is not None:
                            att_s2(prev)
                        prev = ctx
                    att_s2(prev)
                    w = load_w(Win, 0, 3072 + hb * 256)
                    for ti in range(nt):
                        ps = pP_rot.next()
                        proj(hT, R_H, ti, w, 256, ps)
                        ta = tmpA.next()
                        OP("act", lambda e, ta=ta, ps=ps: e.activation(out=ta[:, 0:256], in_=ps[:, 0:256], func=AF.Silu), [ps], [ta])
                        OP("dve", lambda e, ta=ta, ti=ti: e.tensor_tensor(out=ob[:, ti, :], in0=ob[:, ti, :], in1=ta[:, 0:256],
                                                                          op=ALU.mult), [obb, ta], [obb])
                        transpose_blocks(lambda i, ti=ti: ob[:, ti, i * 128:(i + 1) * 128], obb, 2,
                                         lambda i0, n, ti=ti, hb=hb: (ogT[:, 2 * hb + i0:2 * hb + i0 + n, ti * 128:(ti + 1) * 128], R_G))
                S.barrier()
                ybufs = [Buf(ATT[:, 0:nt * 1024], "yacc")]
                tail(layer, tiles, c, ogT, R_G, 8, Wout, ybufs)

            for g in groups:
                do_group(g)
            S.barrier()

        def layer_ret(layer):
            Win, Wout = I["ret_w_in"], I["ret_w_out"]
            S.dma(gnw[:, 0:2048], I["ret_gn"].partition_broadcast(128), writes=[gnw])
            S.dma(lgt[:, 0:8], I["ret_decay"].partition_broadcast(128), writes=[lgt])
            OP("act", lambda e: e.activation(out=lgt[:, 0:8], in_=lgt[:, 0:8], func=AF.Exp, scale=-1.0), [lgt], [lgt])
            OP("act", lambda e: e.activation(out=lgt[:, 0:8], in_=lgt[:, 0:8], func=AF.Ln, bias=1.0), [lgt], [lgt])
            OP("dve", lambda e: e.tensor_scalar(out=lgt[:, 0:8], in0=lgt[:, 0:8], scalar1=-1.0, scalar2=None, op0=ALU.mult), [lgt], [lgt])
            OP("dve", lambda e: e.tensor_scalar(out=lgt[:, 8:12], in0=lgt[:, 4:8], scalar1=-1.0, scalar2=None, op0=ALU.mult), [lgt], [lgt])
            OP("dve", lambda e: e.tensor_scalar(out=lgt[:, 12:16], in0=lgt[:, 4:8], scalar1=1024.0, scalar2=None, op0=ALU.mult), [lgt], [lgt])
            for h in range(4):
                OP("act", lambda e, h=h: e.activation(out=dsc[:, h, 0:2], in_=stx[:, 0:2], func=AF.Exp, scale=lgt[:, h:h + 1]), [stx, lgt], [dsc])
                OP("act", lambda e, h=h: e.activation(out=dsc[:, h, 2:4], in_=stx[:, 2:4], func=AF.Exp, scale=lgt[:, 4 + h:5 + h]), [stx, lgt], [dsc])
            OP("dve", lambda e: e.tensor_scalar(out=dsc[:], in0=dsc[:], scalar1=1.0 / 16, scalar2=None, op0=ALU.mult), [dsc], [dsc])

            def do_group(g):
                S.barrier()
                tiles, c, smp = g["tiles"], g["c"], g["sample"]
                nt = len(tiles)
                T = nt * 128
                hT = bview(R_H, 0, [8, T])
                ogT = bview(R_G, 0, [16, T])
                qTb = sub(0, 1024, "qT")
                kTb = sub(1024, 1024, "kT")
                vbb = sub(2048, 2048, "vb")
                atb = sub(4096, 4096, "attT")
                obb = sub(8192, 4096, "ob")
                qT = bview(qTb, 0, [2, 1024])
                kT = bview(kTb, 0, [2, 1024])
                vb = bview(vbb, 0, [8, 512])
                attT = bview(atb, 0, [8, 1024])
                gpn = fview(atb, 0, [2, 1920])
                ob = fview(obb, 0, [8, 512])
                if smp:
                    s0b = sub(12288, 1024, "S0b")
                    qdb = sub(13312, 2048, "qTd")
                    S0 = bview(s0b, 0, [2, 2, 512])
                    qTd = bview(qdb, 0, [2, 2, 1024])
                else:
                    kdb = sub(12288, 2048, "kdec")
                    kdec = bview(kdb, 0, [2, 8, 256])
                front(layer, tiles, c, lambda k, ti: (hT[:, k, ti * 128:(ti + 1) * 128], R_H))
                for h in range(4):
                    stp = strip.next()
                    for i2 in range(2):
                        S.dma(gpn[:, i2, :], I["gpn"][i2], writes=[atb])
                    OP("act", lambda e, h=h: e.activation(out=gpn[:, 0, :], in_=gpn[:, 0, :], func=AF.Exp, scale=lgt[:, h:h + 1]), [atb, lgt], [atb])
                    OP("act", lambda e, h=h: e.activation(out=gpn[:, 1, :], in_=gpn[:, 1, :], func=AF.Exp, scale=lgt[:, 4 + h:5 + h]), [atb, lgt], [atb])
                    OP("dve", lambda e: e.scalar_tensor_tensor(out=gpn[:, 0, :], in0=gpn[:, 0, :], scalar=-1.0, in1=gpn[:, 1, :],
                                                               op0=ALU.add, op1=ALU.add), [atb], [atb])
                    OP("dve", lambda e: e.tensor_tensor(out=gpn[:, 0, 896:1024], in0=gpn[:, 0, 896:1024], in1=ident[:], op=ALU.add),
                       [atb, ident], [atb])
                    OP("dve", lambda e, stp=stp: e.tensor_scalar(out=stp[:], in0=gpn[:, 0, :], scalar1=1.0 / 16, scalar2=None, op0=ALU.mult),
                       [atb], [stp])
                    for which, dstT, dstb in ((0, qT, qTb), (1, kT, kTb)):
                        w = load_w(Win, 0, which * 1024 + h * 256)
                        for ti in range(nt):
                            ps = pP_rot.next()
                            proj(hT, R_H, ti, w, 256, ps)
                            ta = tmpA.next()
                            OP("act", lambda e, ta=ta, ps=ps: e.copy(out=ta[:, 0:256], in_=ps[:, 0:256]), [ps], [ta])
                            srcb = ta
                            if smp:
                                tb = tmpB.next()
                                rope(ta, tb, rope0, 64, ti, 1)
                                srcb = tb
                            elif which == 1:
                                tt = ti % 2
                                for d in range(2):
                                    OP("dve", lambda e, ta=ta, d=d, ti=ti, tt=tt, h=h: e.tensor_scalar(
                                        out=kdec[:, d, ti, :], in0=ta[:, 0:256], scalar1=dsc[:, h, 2 * d + tt:2 * d + tt + 1], scalar2=None,
                                        op0=ALU.mult), [ta, dsc], [kdb])
                            transpose_blocks(lambda i, srcb=srcb: srcb[:, i * 128:(i + 1) * 128], srcb, 2,
                                             lambda i0, n, dstT=dstT, dstb=dstb, ti=ti: (dstT[:, i0:i0 + n, ti * 128:(ti + 1) * 128], dstb))
                    for v2 in range(2):
                        w = load_w(Win, 0, 2048 + h * 512 + v2 * 256)
                        for ti in range(nt):
                            ps = pP_rot.next()
                            proj(hT, R_H, ti, w, 256, ps)
                            OP("act", lambda e, ti=ti, ps=ps, v2=v2: e.copy(out=vb[:, ti, v2 * 256:(v2 + 1) * 256], in_=ps[:, 0:256]), [ps], [vbb])
                    if smp:
                        for d in range(2):
                            for dc in range(2):
                                sb_ = osb.next()
                                S.dma(sb_[:], I["state_ret"][d, h, dc * 128:(dc + 1) * 128, :], writes=[sb_])
                                OP("pool", lambda e, sb_=sb_, d=d, dc=dc: e.tensor_copy(out=S0[:, d, dc, :], in_=sb_[:]), [sb_], [s0b])
                            rd = xt_rot.next()
                            S.dma(rd[:], I["iota1k"], writes=[rd])
                            if d == 0:
                                OP("act", lambda e, rd=rd, h=h: e.activation(out=rd[:], in_=rd[:], func=AF.Exp, scale=lgt[:, h:h + 1],
                                                                             bias=lgt[:, h:h + 1]), [rd, lgt], [rd])
                            else:
                                OP("act", lambda e, rd=rd, h=h: e.activation(out=rd[:], in_=rd[:], func=AF.Exp, scale=lgt[:, 8 + h:9 + h],
                                                                             bias=lgt[:, 12 + h:13 + h]), [rd, lgt], [rd])
                            for dc in range(2):
                                OP("dve", lambda e, rd=rd, d=d, dc=dc: e.tensor_tensor(out=qTd[:, d, dc, :], in0=qT[:, dc, :], in1=rd[:],
                                                                                       op=ALU.mult), [qTb, rd], [qdb])
                    for (t0, ntq, sq) in g["seqs"]:
                        nq = ntq * 128
                        nqb = (nq + 511) // 512
                        N = min(nq, 512)
                        for i in range(ntq):
                            for qh in range(nqb):
                                for dc in range(2):
                                    OP("pe", lambda e, i=i, qh=qh, dc=dc, t0=t0, N=N: e.matmul(
                                        out=PB[qh][:, 0:N], lhsT=kT[:, dc, (t0 + i) * 128:(t0 + i + 1) * 128],
                                        rhs=qT[:, dc, t0 * 128 + qh * 512:t0 * 128 + qh * 512 + N], start=(dc == 0), stop=(dc == 1)),
                                       [qTb, kTb], [PB[qh]])
                                OP("dve", lambda e, i=i, qh=qh, N=N, stp=stp: e.tensor_tensor(
                                    out=attT[:, i, qh * 512:qh * 512 + N], in0=PB[qh][:, 0:N],
                                    in1=stp[:, (7 - i) * 128 + qh * 512:(7 - i) * 128 + qh * 512 + N], op=ALU.mult), [PB[qh], stp], [atb])
                        for j in range(ntq):
                            pso = pP_rot.next()
                            nmm = ntq + (4 if smp else 0)
                            cnt = 0
                            for i in range(ntq):
                                OP("pe", lambda e, i=i, j=j, t0=t0, cnt=cnt, nmm=nmm, pso=pso: e.matmul(
                                    out=pso[:, 0:512], lhsT=attT[:, i, j * 128:(j + 1) * 128], rhs=vb[:, t0 + i, :],
                                    start=(cnt == 0), stop=(cnt == nmm - 1)), [atb, vbb], [pso])
                                cnt += 1
                            if smp:
                                for d in range(2):
                                    for dc in range(2):
                                        OP("pe", lambda e, d=d, dc=dc, j=j, cnt=cnt, nmm=nmm, pso=pso: e.matmul(
                                            out=pso[:, 0:512], lhsT=qTd[:, d, dc, j * 128:(j + 1) * 128], rhs=S0[:, d, dc, :],
                                            start=(cnt == 0), stop=(cnt == nmm - 1)), [qdb, s0b], [pso])
                                        cnt += 1
                            s = st1.next()
                            ta = tmpA.next()
                            OP("dve", lambda e, s=s, pso=pso: e.tensor_reduce(out=s[:, 0:1], in_=pso[:, 0:512], axis=AX.X, op=ALU.add), [pso], [s])
                            OP("dve", lambda e, s=s: e.tensor_scalar(out=s[:, 1:2], in0=s[:, 0:1], scalar1=-1.0 / 512, scalar2=None, op0=ALU.mult), [s], [s])
                            OP("act", lambda e, s=s, ta=ta, pso=pso: e.activation(out=ta[:, 0:512], in_=pso[:, 0:512], func=AF.Identity, bias=s[:, 1:2]),
                               [pso, s], [ta])
                            tb = tmpB.next()
                            OP("act", lambda e, s=s, ta=ta, tb=tb: e.activation(out=tb[:, 0:512], in_=ta[:, 0:512], func=AF.Square, accum_out=s[:, 2:3]),
                               [ta], [tb, s])
                            OP("act", lambda e, s=s: e.activation(out=s[:, 3:4], in_=s[:, 2:3], func=AF.Sqrt, scale=1.0 / 512, bias=GN_EPS), [s], [s])
                            OP("dve", lambda e, s=s: e.reciprocal(out=s[:, 4:5], in_=s[:, 3:4]), [s], [s])
                            OP("dve", lambda e, s=s, ta=ta, j=j, t0=t0, h=h: e.scalar_tensor_tensor(
                                out=ob[:, t0 + j, :], in0=ta[:, 0:512], scalar=s[:, 4:5], in1=gnw[:, h * 512:(h + 1) * 512],
                                op0=ALU.mult, op1=ALU.mult), [ta, s, gnw], [obb])
                        if not smp:
                            for d in range(2):
                                for dc in range(2):
                                    pst = pP_rot.next()
                                    for tt in range(2):
                                        OP("pe", lambda e, d=d, dc=dc, tt=tt, t0=t0, pst=pst: e.matmul(
                                            out=pst[:, 0:512], lhsT=kdec[:, d, t0 + tt, dc * 128:(dc + 1) * 128], rhs=vb[:, t0 + tt, :],
                                            start=(tt == 0), stop=(tt == 1)), [kdb, vbb], [pst])
                                    sb_ = osb.next()
                                    OP("act", lambda e, sb_=sb_, pst=pst: e.copy(out=sb_[:], in_=pst[:, 0:512]), [pst], [sb_])
                                    S.dma(O["st_ret"][sq, d, h, dc * 128:(dc + 1) * 128, :], sb_[:], reads=[sb_], queue="act")
                    for g2 in range(2):
                        w = load_w(Win, 0, 4096 + h * 512 + g2 * 256)
                        for ti in range(nt):
                            ps = pP_rot.next()
                            proj(hT, R_H, ti, w, 256, ps)
                            ta = tmpA.next()
                            OP("act", lambda e, ta=ta, ps=ps: e.activation(out=ta[:, 0:256], in_=ps[:, 0:256], func=AF.Silu), [ps], [ta])
                            OP("dve", lambda e, ta=ta, ti=ti, g2=g2: e.tensor_tensor(out=ob[:, ti, g2 * 256:(g2 + 1) * 256],
                                                                                     in0=ob[:, ti, g2 * 256:(g2 + 1) * 256], in1=ta[:, 0:256],
                                                                                     op=ALU.mult), [obb, ta], [obb])
                    for ti in range(nt):
                        transpose_blocks(lambda i, ti=ti: ob[:, ti, i * 128:(i + 1) * 128], obb, 4,
                                         lambda i0, n, ti=ti, h=h: (ogT[:, 4 * h + i0:4 * h + i0 + n, ti * 128:(ti + 1) * 128], R_G))
                S.barrier()
                ybufs = [Buf(ATT[:, 0:nt * 1024], "yacc")]
                tail(layer, tiles, c, ogT, R_G, 16, Wout, ybufs)

            for g in groups:
                do_group(g)
            S.barrier()

        RKVG = [dscr(f"rkvg{n}", (2048, 1024)) for n in range(4)]
        RKVGB = [[Buf(RKVG[n][t * 128:(t + 1) * 128, :], f"rkvg{n}_{t}") for t in range(16)] for n in range(4)]
        dscrb = lambda name, shape: nc.dram_tensor(name, list(shape), BF16).ap()
        TMP = dscrb("tmp_", (128, 256, 3, 64))
        TMS = dscrb("tms_", (32, 1024, 3, 64))
        FMP = dscrb("fmp_", (128, 4, 64, 256))
        FMS = dscrb("fms_", (32, 4, 64, 1024))
        GMP = dscr("gmp_", (128, 4, 64))
        GMS = dscr("gms_", (32, 16, 64))
        GMPB, GMSB = Buf(GMP, "gmp"), Buf(GMS, "gms")
        cmask2 = S.sb([128, 3, 128], F32, "cmask2")
        YPD = dscr("ypd", (128, 256, 64))
        YSD = dscr("ysd", (32, 1024, 64))
        TMPB, TMSB, FMPB, FMSB = Buf(TMP, "tmp"), Buf(TMS, "tms"), Buf(FMP, "fmp"), Buf(FMS, "fms")
        YPDB = Buf(YPD, "ypd")
        YSDB = Buf(YSD, "ysd")
        tri = S.sb([128, 128], F32, "tri")

        def layer_rwkv(layer):
            Win, Wout = I["rwkv_w_in"], I["rwkv_w_out"]
            S.barrier()
            for n in range(6):
                S.dma(muF[:, n, :], I["rwkv_mu"][n].rearrange("(k p) -> p k", p=128), writes=[muF], allow_slow_non_contiguous=True)
            smallb = bsub(24576, 3072, "rwsmall")
            wAb = bview(smallb, 0, [2, 8, 64])
            aAb = bview(smallb, 512, [2, 8, 64])
            wBb = bview(smallb, 1024, [2, 1024])
            aBb = bview(smallb, 2048, [2, 1024])
            for d in range(2):
                for src, dst in ((I["rwkv_wA"], wAb), (I["rwkv_aA"], aAb)):
                    f = wf.next()
                    S.dma(f[:, :, 0:64], src[d].rearrange("(k p) r -> p k r", p=128), writes=[f])
                    OP("pool", lambda e, f=f, dst=dst, d=d: e.tensor_copy(out=dst[:, d, :, :], in_=f[:, :, 0:64]), [f], [smallb])
                for src, dst in ((I["rwkv_wB"], wBb), (I["rwkv_aB"], aBb)):
                    f = wf.next()
                    fv = f.t[0:64].rearrange("p k n -> p (k n)")[:, 0:1024]
                    S.dma(fv, src[d], writes=[f])
                    OP("pool", lambda e, fv=fv, dst=dst, d=d: e.tensor_copy(out=dst[0:64, d, :], in_=fv), [f], [smallb])
            lorab = bsub(0, 4096, "lora")
            LWT = bview(lorab, 0, [2, 2048])
            LAT = bview(lorab, 2048, [2, 2048])
            OP("dve", lambda e: e.memset(bsum[:], 0.0), [], [bsum])

            def a1_group(g, gi):
                tiles, c, smp = g["tiles"], g["c"], g["sample"]
                nt = len(tiles)
                T = nt * 128
                tok0 = tiles[0] * 128
                hTb = bsub(4096, 8192, "hTf")
                xxb = bsub(12288, 8192, "xxT")
                xnb = bsub(20480, 4096, "xnT")
                hT = fview(hTb, 0, [8, T])
                xx = fview(xxb, 0, [8, T])
                xn = bview(xnb, 0, [8, T])
                front(layer, tiles, c, lambda k, ti: (hT[:, k, ti * 128:(ti + 1) * 128], hTb))
                for (t0, ntq, sq) in g["seqs"]:
                    o = t0 * 128
                    L = ntq * 128
                    OP("dve", lambda e, o=o, L=L: e.tensor_tensor(out=xx[:, :, o + 1:o + L - 1], in0=hT[:, :, o:o + L - 2],
                                                                 in1=hT[:, :, o + 2:o + L], op=ALU.add), [hTb], [xxb])
                    OP("dve", lambda e, o=o, L=L: e.scalar_tensor_tensor(out=xx[:, :, o + 1:o + L - 1], in0=xx[:, :, o + 1:o + L - 1],
                                                                        scalar=0.5, in1=hT[:, :, o + 1:o + L - 1],
                                                                        op0=ALU.mult, op1=ALU.subtract), [xxb, hTb], [xxb])
                    OP("dve", lambda e, o=o: e.scalar_tensor_tensor(out=xx[:, :, o:o + 1], in0=hT[:, :, o + 1:o + 2], scalar=0.5,
                                                                    in1=hT[:, :, o:o + 1], op0=ALU.mult, op1=ALU.subtract), [hTb], [xxb])
                    OP("dve", lambda e, o=o, L=L: e.scalar_tensor_tensor(out=xx[:, :, o + L - 1:o + L], in0=hT[:, :, o + L - 2:o + L - 1],
                                                                        scalar=0.5, in1=hT[:, :, o + L - 1:o + L],
                                                                        op0=ALU.mult, op1=ALU.subtract), [hTb], [xxb])

                def mix(n):
                    for k in range(8):
                        OP("dve", lambda e, k=k, n=n: e.scalar_tensor_tensor(out=xn[:, k, :], in0=xx[:, k, :], scalar=muF[:, n, k:k + 1],
                                                                             in1=hT[:, k, :], op0=ALU.mult, op1=ALU.add),
                           [xxb, hTb, muF], [xnb])
                for pi, n in enumerate((0, 2, 3, 5)):
                    mix(n)
                    for cb in range(4):
                        w = load_w(Win, 0, pi * 1024 + cb * 256)
                        for ti in range(nt):
                            ps = pP_rot.next()
                            proj(xn, xnb, ti, w, 256, ps)
                            ta = tmpA.next()
                            OP("act", lambda e, ta=ta, ps=ps: e.copy(out=ta[:, 0:256], in_=ps[:, 0:256]), [ps], [ta])
                            t = tiles[ti]
                            S.dma(RKVG[pi][t * 128:(t + 1) * 128, cb * 256:(cb + 1) * 256], ta[:, 0:256], reads=[ta],
                                  writes=[RKVGB[pi][t]], owner=ta, queue="act")
                for n, Ab, LT, fn in ((1, wAb, LWT, AF.Tanh), (4, aAb, LAT, AF.Copy)):
                    mix(n)
                    for d in range(2):
                        for c0 in range(0, T, 512):
                            ps = pP_rot.next()
                            for k in range(8):
                                OP("pe", lambda e, k=k, d=d, c0=c0, Ab=Ab, ps=ps: e.matmul(out=ps[0:64, 0:512], lhsT=Ab[:, d, k, :],
                                                                                          rhs=xn[:, k, c0:c0 + 512], start=(k == 0), stop=(k == 7)),
                                   [smallb, xnb], [ps])
                            if fn == AF.Tanh:
                                OP("act", lambda e, d=d, c0=c0, LT=LT, ps=ps: e.activation(out=LT[0:64, d, tok0 + c0:tok0 + c0 + 512],
                                                                                        in_=ps[0:64, 0:512], func=AF.Tanh), [ps], [lorab])
                            else:
                                OP("act", lambda e, d=d, c0=c0, LT=LT, ps=ps: e.copy(out=LT[0:64, d, tok0 + c0:tok0 + c0 + 512],
                                                                                  in_=ps[0:64, 0:512]), [ps], [lorab])

            for gi, g in enumerate(groups):
                a1_group(g, gi)
            S.barrier()

            tabb = bsub(4096, 8192, "tabs")
            TAB = fview(tabb, 0, [8, 1024])
            for i2, src in enumerate((I["rwkv_w0"][0], I["rwkv_w0"][1], I["rwkv_a0"][0], I["rwkv_a0"][1], I["rwkv_kk"],
                                      I["rwkv_ka"], I["rwkv_rk"], I["rwkv_gn"])):
                S.dma(TAB[:, i2, :], src.partition_broadcast(128), writes=[tabb])
            slot = [bsub(12288 + i2 * 1024, 1024, f"slot{i2}") for i2 in range(12)]
            xtb = [Buf(b_.t[:, :], b_.name + "_a2") for b_ in (xt_rot.bufs + xn_rot.bufs)]
            Rb, Kb, Vb, KKb, LWb, ABb, KDb, T1b, FLWb = slot[0:9]
            Frot = Rot([slot[9], slot[10]])
            Hrot = Rot([slot[11], xtb[0]])
            FMrot = Rot([xtb[1], xtb[2]])
            h16 = lambda b: b.t.rearrange("p (h d) -> p h d", h=16)
            S.dma(tri[:], I["tri"], writes=[tri])
            for e2 in range(2):
                S.dma(cmask2[e2 * 64:(e2 + 1) * 64, :, :], I["cmask"], writes=[cmask2])

            def flip(srcb, dstb):
                for hf in range(2):
                    ps = pP_rot.next()
                    OP("pe", lambda e, hf=hf, ps=ps: e.matmul(out=ps[:, :], lhsT=Jm[:], rhs=srcb.t[:, hf * 512:(hf + 1) * 512],
                                                             start=True, stop=True), [Jm, srcb], [ps])
                    OP("act", lambda e, hf=hf, ps=ps: e.copy(out=dstb.t[:, hf * 512:(hf + 1) * 512], in_=ps[:, :]), [ps], [dstb])

            def a2_tile(t):
                smp = t >= 8
                tt_in_seq = (t - 8) if smp else (t % 2)
                for pi, b in ((0, Rb), (1, Kb), (2, Vb)):
                    S.dma(b.t, RKVG[pi][t * 128:(t + 1) * 128, :], reads=[RKVGB[pi][t]], writes=[b])
                OP("dve", lambda e: e.tensor_tensor(out=KKb.t, in0=Kb.t, in1=TAB[:, 4, :], op=ALU.mult), [Kb, tabb], [KKb])
                OP("pool", lambda e: e.tensor_tensor(out=T1b.t, in0=KKb.t, in1=KKb.t, op=ALU.mult), [KKb], [T1b])
                nrm = tmpB.next()
                OP("dve", lambda e, nrm=nrm: e.tensor_reduce(out=nrm[:, 0:16], in_=h16(T1b), axis=AX.X, op=ALU.add), [T1b], [nrm])
                OP("dve", lambda e, nrm=nrm: e.tensor_scalar(out=nrm[:, 0:16], in0=nrm[:, 0:16], scalar1=1e-12, scalar2=None, op0=ALU.max), [nrm], [nrm])
                OP("act", lambda e, nrm=nrm: e.activation(out=nrm[:, 16:32], in_=nrm[:, 0:16], func=AF.Sqrt), [nrm], [nrm])
                OP("dve", lambda e, nrm=nrm: e.reciprocal(out=nrm[:, 32:48], in_=nrm[:, 16:32]), [nrm], [nrm])
                OP("dve", lambda e, nrm=nrm: e.tensor_tensor(out=h16(KKb), in0=h16(KKb), in1=nrm[:, 32:48].unsqueeze(2).to_broadcast([128, 16, 64]),
                                                             op=ALU.mult), [KKb, nrm], [KKb])
                for d in range(2):
                    a2_dir(t, d, smp, tt_in_seq)

            def a2_dir(t, d, smp, tt_in_seq):
                if True:
                    for (LT, Bw, tabi, dstb, post) in ((LWT, wBb, d, LWb, "w"), (LAT, aBb, 2 + d, ABb, "a")):
                        for hf in range(2):
                            ps = pP_rot.next()
                            OP("pe", lambda e, hf=hf, d=d, LT=LT, Bw=Bw, ps=ps: e.matmul(
                                out=ps[:, :], lhsT=LT[0:64, d, t * 128:(t + 1) * 128], rhs=Bw[0:64, d, hf * 512:(hf + 1) * 512],
                                start=True, stop=True), [lorab, smallb], [ps])
                            OP("dve", lambda e, hf=hf, tabi=tabi, dstb=dstb, ps=ps: e.tensor_tensor(
                                out=dstb.t[:, hf * 512:(hf + 1) * 512], in0=ps[:, :], in1=TAB[:, tabi, hf * 512:(hf + 1) * 512], op=ALU.add),
                               [ps, tabb], [dstb])
                        OP("act", lambda e, dstb=dstb: e.activation(out=dstb.t, in_=dstb.t, func=AF.Sigmoid), [dstb], [dstb])
                        if post == "w":
                            OP("act", lambda e, dstb=dstb: e.activation(out=dstb.t, in_=dstb.t, func=AF.Copy, scale=-math.exp(-0.5)), [dstb], [dstb])
                    OP("dve", lambda e: e.scalar_tensor_tensor(out=T1b.t, in0=ABb.t, scalar=-1.0, in1=TAB[:, 5, :], op0=ALU.add, op1=ALU.mult),
                       [ABb, tabb], [T1b])
                    OP("dve", lambda e: e.scalar_tensor_tensor(out=KDb.t, in0=T1b.t, scalar=1.0, in1=Kb.t, op0=ALU.add, op1=ALU.mult),
                       [T1b, Kb], [KDb])
                    OP("pool", lambda e: e.tensor_tensor(out=ABb.t, in0=KKb.t, in1=ABb.t, op=ALU.mult), [KKb, ABb], [ABb])
                    OP("pool", lambda e: e.tensor_tensor(out=T1b.t, in0=Rb.t, in1=KDb.t, op=ALU.mult), [Rb, KDb], [T1b])
                    OP("pool", lambda e: e.tensor_tensor(out=T1b.t, in0=T1b.t, in1=TAB[:, 6, :], op=ALU.mult), [T1b, tabb], [T1b])
                    nb = tmpB.next()
                    OP("dve", lambda e, nb=nb: e.tensor_reduce(out=nb[:, 0:16], in_=h16(T1b), axis=AX.X, op=ALU.add), [T1b], [nb])
                    OP("dve", lambda e, nb=nb: e.tensor_tensor(out=bsum[:, t, :], in0=bsum[:, t, :], in1=nb[:, 0:16], op=ALU.add), [bsum, nb], [bsum])
                    if smp:
                        L, TMD, FMD, TMB_, FMB_, GMD, GMB_ = 1024, TMS, FMS, TMSB, FMSB, GMS, GMSB
                        ch0 = d * 16
                    else:
                        L, TMD, FMD, TMB_, FMB_, GMD, GMB_ = 256, TMP, FMP, TMPB, FMPB, GMP, GMPB
                        ch0 = (d * 4 + t // 2) * 16
                    tk0 = tt_in_seq * 128
                    s0 = tk0 if d == 0 else L - 128 - tk0

                    def chain_order(srcb):
                        if d == 0:
                            return srcb
                        f = Frot.next()
                        flip(srcb, f)
                        return f
                    if d == 0:
                        lwc = LWb
                    else:
                        flip(LWb, FLWb)
                        lwc = FLWb
                    cps = [PB[0], PB[1]]
                    for hf in range(2):
                        OP("pe", lambda e, hf=hf: e.matmul(out=cps[hf][:, :], lhsT=tri[:], rhs=lwc.t[:, hf * 512:(hf + 1) * 512], start=True, stop=True),
                           [tri, lwc], [cps[hf]])

                    def store_tm(hb_, vi):
                        tmb = strip.next()
                        OP("act", lambda e: e.copy(out=tmb[:, 0:1024], in_=hb_.t), [hb_], [tmb])
                        S.dma(TMD[ch0:ch0 + 16, s0:s0 + 128, vi, :].rearrange("h t j -> t h j"), tmb[:, 0:1024].rearrange("p (h d) -> p h d", h=16),
                              reads=[tmb], writes=[TMB_], owner=tmb, queue="act")

                    def store_fm(hb_, vi):
                        fm = FMrot.next()
                        fmv = fm.t[:, 0:512].bitcast(BF16).rearrange("p (a b) -> p a b", a=8)
                        transpose_blocks(lambda i: hb_.t[:, i * 128:(i + 1) * 128], hb_, 8, lambda i0, n: (fmv[:, i0:i0 + n, :], fm), evac="act")
                        for e2 in range(2):
                            S.dma(FMD[ch0 + e2:ch0 + 16:2, vi, :, s0:s0 + 128].rearrange("c k t -> k c t"), fmv[e2 * 64:(e2 + 1) * 64, :, :],
                                  reads=[fm], writes=[FMB_], owner=fm, queue="act")

                    def hat(srcb, kind):
                        hb_ = Hrot.next()
                        for hf in range(2):
                            sl = slice(hf * 512, (hf + 1) * 512)
                            if kind == "prev":
                                OP("dve", lambda e, hf=hf, sl=sl: e.tensor_tensor(out=T1b.t[:, sl], in0=cps[hf][:, :], in1=lwc.t[:, sl], op=ALU.subtract),
                                   [cps[hf], lwc], [T1b])
                                OP("act", lambda e, sl=sl: e.activation(out=T1b.t[:, sl], in_=T1b.t[:, sl], func=AF.Exp), [T1b], [T1b])
                            elif kind == "cur":
                                OP("act", lambda e, hf=hf, sl=sl: e.activation(out=T1b.t[:, sl], in_=cps[hf][:, :], func=AF.Exp), [cps[hf]], [T1b])
                            elif kind == "inv":
                                OP("act", lambda e, hf=hf, sl=sl: e.activation(out=T1b.t[:, sl], in_=cps[hf][:, :], func=AF.Exp, scale=-1.0), [cps[hf]], [T1b])
                        if srcb is None:
                            OP("pool", lambda e: e.tensor_copy(out=hb_.t, in_=T1b.t), [T1b], [hb_])
                        else:
                            OP("pool", lambda e: e.tensor_tensor(out=hb_.t, in0=srcb.t, in1=T1b.t, op=ALU.mult), [srcb, T1b], [hb_])
                        return hb_

                    hb_ = hat(chain_order(KKb), "prev")
                    store_fm(hb_, 0)
                    hb_ = hat(chain_order(Rb), "cur")
                    store_fm(hb_, 1)
                    for cc in range(2):
                        cidx = s0 // 64 + cc
                        S.dma(GMD[ch0:ch0 + 16, cidx:cidx + 1, :].rearrange("h n k -> n h k"),
                              T1b.t[63 + 64 * cc:64 + 64 * cc, :].rearrange("p (h k) -> p h k", h=16), reads=[T1b], writes=[GMB_], owner=T1b, queue="act")
                    hb_ = hat(chain_order(ABb), "inv")
                    store_fm(hb_, 2)
                    store_tm(hb_, 0)
                    hb_ = hat(chain_order(KDb), "inv")
                    store_fm(hb_, 3)
                    store_tm(hb_, 1)
                    store_tm(chain_order(Vb), 2)

            for t in range(16):
                a2_tile(t)
            S.barrier()
            CK("rwkv_a")

            NU = 20
            o = 0

            def carve(size, name):
                nonlocal o
                b = bsub(o, size, name)
                o += size
                return b
            Tst = [carve(256, f"T{u}") for u in range(NU)]
            Tbs = [carve(128, f"Tb{u}") for u in range(NU)]
            NW = 4
            bsets = []
            for w_ in range(NW):
                bsets.append(dict(
                    tm=carve(384, f"tm{w_}"), fm=carve(512, f"fm{w_}"), gm=carve(4, f"gm{w_}"), ka=carve(256, f"ka{w_}"), apb=carve(128, f"apb{w_}"),
                    qz=[carve(512, f"qz{w_}a"), carve(512, f"qz{w_}b")], qt=[carve(256, f"qt{w_}a"), carve(256, f"qt{w_}b")],
                    bdq=carve(512, f"bdq{w_}"), bdt=carve(512, f"bdt{w_}"), zb=carve(128, f"zb{w_}"), u=carve(128, f"u{w_}"),
                    p=carve(128, f"p{w_}"), y=carve(256, f"y{w_}")))
            pB_rot = Rot(PB)
            for bs_ in bsets:
                for b in (bs_["bdq"], bs_["bdt"]):
                    OP("pool", lambda e, b=b: e.memset(b.t, 0.0), [], [b])
            f3 = lambda b, x: b.t.rearrange("p (c x) -> p c x", c=4)
            b3 = lambda b, n: b.t[:, 0:n // 2].bitcast(BF16).rearrange("p (c x) -> p c x", c=4)
            PH = [slice(0, 64), slice(64, 128)]
            ev_rot = Rot(["dve", "act", "pool", "dve", "act"])

            def to_bd(srcv, srcb, bd):
                bdv = f3(bd, 0)
                for e2 in range(2):
                    eng = ev_rot.next()
                    if eng == "act":
                        OP("act", lambda e, e2=e2: e.copy(out=bdv[PH[e2], :, e2 * 64:(e2 + 1) * 64], in_=srcv[PH[e2], :, :]), [srcb], [bd])
                    else:
                        OP(eng, lambda e, e2=e2: e.tensor_copy(out=bdv[PH[e2], :, e2 * 64:(e2 + 1) * 64], in_=srcv[PH[e2], :, :]), [srcb], [bd])

            units = []
            for d in range(2):
                for hh in range(2):
                    units.append(dict(smp=True, ch0=d * 16 + hh * 8, d=d, h0=hh * 8, nch=16))
            for d in range(2):
                for sq in range(4):
                    for hh in range(2):
                        units.append(dict(smp=False, ch0=(d * 4 + sq) * 16 + hh * 8, d=d, sq=sq, h0=hh * 8, nch=4))
            for ui, u in enumerate(units):
                Tb, Tbb = Tst[ui], Tbs[ui]
                if not u["smp"]:
                    OP("pool", lambda e, Tb=Tb: e.memset(Tb.t, 0.0), [], [Tb])
                else:
                    st_ = bsets[ui % NW]["qz"][0]
                    sv = st_.t[:, 0:256].rearrange("p (c x) -> p c x", c=4)
                    for e2 in range(2):
                        h0 = u["h0"] + 4 * e2
                        S.dma(sv[PH[e2], :, :], I["state_rwkv"][u["d"], h0:h0 + 4].rearrange("h v k -> v h k"), writes=[st_])
                    ps = pP_rot.next()
                    for e2 in range(2):
                        for p in range(4):
                            OP("pe", lambda e, e2=e2, p=p, ps=ps, sv=sv: e.matmul(out=ps[PH[e2], p * 64:(p + 1) * 64], lhsT=sv[PH[e2], p, :],
                                                                               rhs=ident[PH[e2], e2 * 64:(e2 + 1) * 64], start=True, stop=True),
                               [st_, ident], [ps])
                    OP("act", lambda e, Tb=Tb, ps=ps: e.copy(out=Tb.t, in_=ps[:, 0:256]), [ps], [Tb])
                OP("dve", lambda e, Tb=Tb, Tbb=Tbb: e.tensor_copy(out=Tbb.t[:, 0:128].bitcast(BF16), in_=Tb.t), [Tb], [Tbb])

            def unit_chunk(ui, u, n, bs):
                smp, ch0 = u["smp"], u["ch0"]
                TMD, FMD, GMD, TMB_, FMB_, GMB_, YD, YDB_ = ((TMS, FMS, GMS, TMSB, FMSB, GMSB, YSD, YSDB) if smp else
                                                             (TMP, FMP, GMP, TMPB, FMPB, GMPB, YPD, YPDB))
                tm, fm, gm = bs["tm"], bs["fm"], bs["gm"]
                tmv = tm.t.bitcast(BF16).rearrange("p (c v j) -> p c v j", c=4, v=3)
                fmv = fm.t.bitcast(BF16).rearrange("p (c v s) -> p c v s", c=4, v=4)
                for e2 in range(2):
                    c0 = ch0 + 4 * e2
                    S.dma(tm.t.bitcast(BF16).rearrange("p (c x) -> p c x", c=4)[PH[e2], :, :],
                          TMD[c0:c0 + 4, n * 64:(n + 1) * 64, :, :].rearrange("c s v j -> s c (v j)"), reads=[TMB_], writes=[tm])
                    S.dma(fm.t.bitcast(BF16).rearrange("p (cv s) -> p cv s", s=64)[PH[e2], :, :],
                          FMD[c0:c0 + 4, :, :, n * 64:(n + 1) * 64].rearrange("c v k s -> k (c v) s"), reads=[FMB_], writes=[fm])
                    S.dma(gm.t[PH[e2], :], GMD[c0:c0 + 4, n, :].rearrange("c k -> k c"), reads=[GMB_], writes=[gm], allow_slow_non_contiguous=True)
                Tb, Tbb = Tst[ui], Tbs[ui]
                Tv = f3(Tb, 0)
                Tbv = b3(Tbb, 256)
                ka, apb = bs["ka"], bs["apb"]
                kav = b3(ka, 512)
                apbv = b3(apb, 256)
                qzi, qti = 0, 0
                qz = bs["qz"][0]
                qzv = f3(qz, 0)
                qt = bs["qt"][0]
                qtv = f3(qt, 0)
                psB, psK, psL = pB_rot.next(), pB_rot.next(), pB_rot.next()
                for e2 in range(2):
                    for p in range(4):
                        OP("pe", lambda e, e2=e2, p=p: e.matmul(out=psB[PH[e2], p * 128:(p + 1) * 128], lhsT=fmv[PH[e2], p, 2, :],
                                                               rhs=fmv[PH[e2], p, 0:2, :], start=True, stop=True), [fm], [psB])
                        OP("pe", lambda e, e2=e2, p=p: e.matmul(out=psK[PH[e2], p * 128:(p + 1) * 128], lhsT=fmv[PH[e2], p, 3, :],
                                                               rhs=fmv[PH[e2], p, 0:2, :], start=True, stop=True), [fm], [psK])
                        OP("pe", lambda e, e2=e2, p=p: e.matmul(out=psL[PH[e2], p * 64:(p + 1) * 64], lhsT=fmv[PH[e2], p, 0, :],
                                                               rhs=fmv[PH[e2], p, 2, :], start=True, stop=True), [fm], [psL])
                psBv = psB[:, :].rearrange("p (c x) -> p c x", c=4)
                OP("dve", lambda e, qzv=qzv: e.tensor_tensor(out=qzv[:, :, 64:128], in0=psBv[:, :, 0:64], in1=cmask2[:, 0, 0:64].unsqueeze(1).to_broadcast([128, 4, 64]),
                                                    op=ALU.mult), [psB, cmask2], [qz])
                OP("dve", lambda e: e.tensor_tensor(out=apbv, in0=psBv[:, :, 64:128], in1=cmask2[:, 0, 64:128].unsqueeze(1).to_broadcast([128, 4, 64]),
                                                    op=ALU.mult), [psB, cmask2], [apb])
                OP("dve", lambda e: e.tensor_tensor(out=kav, in0=psK[:, :].rearrange("p (c x) -> p c x", c=4),
                                                    in1=cmask2[:, 1, :].unsqueeze(1).to_broadcast([128, 4, 128]), op=ALU.mult), [psK, cmask2], [ka])
                OP("dve", lambda e, qtv=qtv: e.tensor_tensor(out=qtv, in0=psL[:, 0:256].rearrange("p (c x) -> p c x", c=4),
                                                    in1=cmask2[:, 2, 0:64].unsqueeze(1).to_broadcast([128, 4, 64]), op=ALU.mult), [psL, cmask2], [qt])
                OP("pool", lambda e, qzv=qzv: e.tensor_tensor(out=qzv[:, :, 0:64], in0=qzv[:, :, 64:128],
                                                     in1=ident[:, :].rearrange("p (a b) -> p a b", a=2)[:, 0, :].unsqueeze(1).to_broadcast([128, 4, 64])
                                                     if False else identst[:, :].unsqueeze(1).to_broadcast([128, 4, 64]), op=ALU.add), [qz, identst], [qz])
                yield
                bdq, bdt = bs["bdq"], bs["bdt"]
                to_bd(qzv[:, :, 64:128], qz, bdq)
                to_bd(qtv, qt, bdt)
                ps1, ps2 = pB_rot.next(), pB_rot.next()
                for p in range(4):
                    OP("pe", lambda e, p=p, bdt=bdt, qzv=qzv: e.matmul(out=ps1[:, p * 64:(p + 1) * 64], lhsT=f3(bdt, 0)[:, p, :], rhs=qzv[:, p, 64:128], start=True, stop=True),
                       [bdt, qz], [ps1])
                    OP("pe", lambda e, p=p, bdq=bdq, qtv=qtv: e.matmul(out=ps2[:, p * 64:(p + 1) * 64], lhsT=f3(bdq, 0)[:, p, :], rhs=qtv[:, p, :], start=True, stop=True),
                       [bdq, qt], [ps2])
                OP("act", lambda e, qzv=qzv: e.copy(out=qzv[:, :, 64:128], in_=ps1[:, 0:256].rearrange("p (c x) -> p c x", c=4)), [ps1], [qz])
                qt = bs["qt"][1]
                qti = 1
                qtv = f3(qt, 0)
                OP("dve", lambda e, qtv=qtv: e.tensor_copy(out=qtv, in_=ps2[:, 0:256].rearrange("p (c x) -> p c x", c=4)), [ps2], [qt])
                yield
                for lvl in range(1, 6):
                    to_bd(qtv, qt, bdt)
                    if lvl < 5:
                        to_bd(qzv[:, :, 64:128], qz, bdq)
                    N = 128 if lvl < 5 else 64
                    psA = pB_rot.next()
                    for p in range(4):
                        OP("pe", lambda e, p=p, bdt=bdt, qzv=qzv, N=N, psA=psA: e.matmul(out=psA[:, p * 128:p * 128 + N], lhsT=f3(bdt, 0)[:, p, :],
                                                                                      rhs=qzv[:, p, 0:N], start=True, stop=True), [bdt, qz], [psA])
                    if lvl < 5:
                        psC = pB_rot.next()
                        for p in range(4):
                            OP("pe", lambda e, p=p, bdq=bdq, qtv=qtv, psC=psC: e.matmul(out=psC[:, p * 64:(p + 1) * 64], lhsT=f3(bdq, 0)[:, p, :],
                                                                                      rhs=qtv[:, p, :], start=True, stop=True), [bdq, qt], [psC])
                    psAv = psA[:, :].rearrange("p (c x) -> p c x", c=4)
                    if lvl < 5:
                        qzi ^= 1
                        qz_new = bs["qz"][qzi]
                        qznv = f3(qz_new, 0)
                        OP("dve", lambda e, qznv=qznv, qzv=qzv, psAv=psAv: e.tensor_tensor(out=qznv[:, :, 0:64], in0=psAv[:, :, 0:64], in1=qzv[:, :, 0:64],
                                                                                         op=ALU.add), [psA, qz], [qz_new])
                        OP("act", lambda e, qznv=qznv, psAv=psAv: e.copy(out=qznv[:, :, 64:128], in_=psAv[:, :, 64:128]), [psA], [qz_new])
                        qti ^= 1
                        qt_new = bs["qt"][qti]
                        qtnv = f3(qt_new, 0)
                        OP("dve", lambda e, qtnv=qtnv, psC=psC: e.tensor_copy(out=qtnv, in_=psC[:, 0:256].rearrange("p (c x) -> p c x", c=4)), [psC], [qt_new])
                        qz, qzv, qt, qtv = qz_new, qznv, qt_new, qtnv
                        yield
                    else:
                        zb = bs["zb"]
                        zbv = b3(zb, 256)
                        OP("dve", lambda e, zbv=zbv, qzv=qzv, psAv=psAv: e.tensor_tensor(out=zbv, in0=psAv[:, :, 0:64], in1=qzv[:, :, 0:64], op=ALU.add),
                           [psA, qz], [zb])
                ps = pB_rot.next()
                for e2 in range(2):
                    for p in range(4):
                        OP("pe", lambda e, e2=e2, p=p, ps=ps: e.matmul(out=ps[PH[e2], p * 64:(p + 1) * 64], lhsT=fmv[PH[e2], p, 0, :], rhs=Tbv[PH[e2], p, :],
                                                                      start=True, stop=False), [fm, Tbb], [ps])
                        OP("pe", lambda e, e2=e2, p=p, ps=ps: e.matmul(out=ps[PH[e2], p * 64:(p + 1) * 64], lhsT=kav[PH[e2], p, 0:64], rhs=tmv[PH[e2], p, 2, :],
                                                                      start=False, stop=True), [ka, tm], [ps])
                ub = bs["u"]
                ubv = b3(ub, 256)
                OP("act", lambda e, ps=ps: e.copy(out=ubv, in_=ps[:, 0:256].rearrange("p (c x) -> p c x", c=4)), [ps], [ub])
                yield
                ps = pB_rot.next()
                for e2 in range(2):
                    for p in range(4):
                        OP("pe", lambda e, e2=e2, p=p, ps=ps: e.matmul(out=ps[PH[e2], p * 64:(p + 1) * 64], lhsT=zbv[PH[e2], p, :], rhs=ubv[PH[e2], p, :],
                                                                      start=True, stop=True), [zb, ub], [ps])
                pb_ = bs["p"]
                pbv = b3(pb_, 256)
                OP("dve", lambda e, ps=ps: e.tensor_scalar(out=pbv, in0=ps[:, 0:256].rearrange("p (c x) -> p c x", c=4), scalar1=-1.0, scalar2=None, op0=ALU.mult),
                   [ps], [pb_])
                yield
                ps = pB_rot.next()
                for e2 in range(2):
                    for p in range(4):
                        OP("pe", lambda e, e2=e2, p=p, ps=ps: e.matmul(out=ps[PH[e2], p * 64:(p + 1) * 64], lhsT=fmv[PH[e2], p, 1, :], rhs=Tbv[PH[e2], p, :],
                                                                      start=True, stop=False), [fm, Tbb], [ps])
                        OP("pe", lambda e, e2=e2, p=p, ps=ps: e.matmul(out=ps[PH[e2], p * 64:(p + 1) * 64], lhsT=apbv[PH[e2], p, :], rhs=pbv[PH[e2], p, :],
                                                                      start=False, stop=False), [apb, pb_], [ps])
                        OP("pe", lambda e, e2=e2, p=p, ps=ps: e.matmul(out=ps[PH[e2], p * 64:(p + 1) * 64], lhsT=kav[PH[e2], p, 64:128], rhs=tmv[PH[e2], p, 2, :],
                                                                      start=False, stop=True), [ka, tm], [ps])
                yb = bs["y"]
                ybv = f3(yb, 0)
                OP("act", lambda e, ps=ps: e.copy(out=ybv, in_=ps[:, 0:256].rearrange("p (c x) -> p c x", c=4)), [ps], [yb])
                for e2 in range(2):
                    c0 = ch0 + 4 * e2
                    S.dma(YD[c0:c0 + 4, n * 64:(n + 1) * 64, :].rearrange("c s x -> s c x"), ybv[PH[e2], :, :], reads=[yb], writes=[YDB_], owner=yb, queue="act")
                ps = pB_rot.next()
                for e2 in range(2):
                    for p in range(4):
                        OP("pe", lambda e, e2=e2, p=p, ps=ps: e.matmul(out=ps[PH[e2], p * 64:(p + 1) * 64], lhsT=tmv[PH[e2], p, 0, :], rhs=pbv[PH[e2], p, :],
                                                                      start=True, stop=False), [tm, pb_], [ps])
                        OP("pe", lambda e, e2=e2, p=p, ps=ps: e.matmul(out=ps[PH[e2], p * 64:(p + 1) * 64], lhsT=tmv[PH[e2], p, 1, :], rhs=tmv[PH[e2], p, 2, :],
                                                                      start=False, stop=True), [tm], [ps])
                OP("dve", lambda e, ps=ps: e.tensor_tensor(out=Tv, in0=ps[:, 0:256].rearrange("p (c x) -> p c x", c=4), in1=Tv, op=ALU.add), [ps, Tb], [Tb])
                OP("pool", lambda e: e.tensor_tensor(out=Tv, in0=Tv, in1=gm.t[:, 0:4].unsqueeze(2).to_broadcast([128, 4, 64]), op=ALU.mult), [Tb, gm], [Tb])
                OP("act", lambda e: e.copy(out=Tbv, in_=Tv), [Tb], [Tbb])

            tasks = [(ui, u, n) for n in range(16) for ui, u in enumerate(units) if n < u["nch"]]
            slots = [None] * NW
            ti_ = 0
            while ti_ < len(tasks) or any(sl_ is not None for sl_ in slots):
                for w_ in range(NW):
                    if slots[w_] is None and ti_ < len(tasks):
                        ui, u, n = tasks[ti_]
                        if any(sl_ is not None and sl_[1] == ui for sl_ in slots):
                            continue
                        slots[w_] = (unit_chunk(ui, u, n, bsets[w_]), ui)
                        ti_ += 1
                for w_ in range(NW):
                    if slots[w_] is not None:
                        try:
                            next(slots[w_][0])
                        except StopIteration:
                            slots[w_] = None
            for ui, u in enumerate(units):
                if u["smp"]:
                    continue
                Tb = Tst[ui]
                Tv = f3(Tb, 0)
                ps = pP_rot.next()
                for e2 in range(2):
                    for p in range(4):
                        OP("pe", lambda e, e2=e2, p=p, ps=ps, Tv=Tv: e.matmul(out=ps[PH[e2], p * 64:(p + 1) * 64], lhsT=Tv[PH[e2], p, :],
                                                                           rhs=ident[PH[e2], e2 * 64:(e2 + 1) * 64], start=True, stop=True), [Tb, ident], [ps])
                yb = bsets[ui % NW]["y"]
                ybv = f3(yb, 0)
                OP("act", lambda e, ps=ps, ybv=ybv: e.copy(out=ybv, in_=ps[:, 0:256].rearrange("p (c x) -> p c x", c=4)), [ps], [yb])
                for e2 in range(2):
                    h0 = u["h0"] + 4 * e2
                    S.dma(O["st_rwkv"][u["sq"], u["d"], h0:h0 + 4].rearrange("h v k -> v h k"), ybv[PH[e2], :, :], reads=[yb], owner=yb, queue="act")
            S.barrier()
            CK("rwkv_b")

            gnt = bsub(0, 1024, "gnt")
            S.dma(gnt.t, I["rwkv_gn"].partition_broadcast(128), writes=[gnt])
            cs_ = [bsub(1024 + i2 * 1024, 1024, f"cs{i2}") for i2 in range(6)]
            YFb, YBb, Vc, Gc, C1, C2 = cs_
            ogb = bsub(8192, 4096, "ogT")

            def c_group(g):
                tiles, c, smp = g["tiles"], g["c"], g["sample"]
                nt = len(tiles)
                T = nt * 128
                ogT = bview(ogb, 0, [8, T])
                for ti, t in enumerate(tiles):
                    if smp:
                        tk0 = (t - 8) * 128
                        L = 1024
                        S.dma(h16(YFb), YSD[0:16, tk0:tk0 + 128, :].rearrange("h t x -> t h x"), reads=[YSDB], writes=[YFb])
                        S.dma(h16(C1), YSD[16:32, L - 128 - tk0:L - tk0, :].rearrange("h t x -> t h x"), reads=[YSDB], writes=[C1])
                    else:
                        sq = t // 2
                        tk0 = (t % 2) * 128
                        L = 256
                        S.dma(h16(YFb), YPD[sq * 16:(sq + 1) * 16, tk0:tk0 + 128, :].rearrange("h t x -> t h x"), reads=[YPDB], writes=[YFb])
                        S.dma(h16(C1), YPD[(4 + sq) * 16:(5 + sq) * 16, L - 128 - tk0:L - tk0, :].rearrange("h t x -> t h x"),
                              reads=[YPDB], writes=[C1])
                    for hf in range(2):
                        ps = pP_rot.next()
                        OP("pe", lambda e, hf=hf, ps=ps: e.matmul(out=ps[:, :], lhsT=Jm[:], rhs=C1.t[:, hf * 512:(hf + 1) * 512], start=True, stop=True),
                           [Jm, C1], [ps])
                        OP("dve", lambda e, hf=hf, ps=ps: e.tensor_tensor(out=YBb.t[:, hf * 512:(hf + 1) * 512], in0=ps[:, :],
                                                                          in1=YFb.t[:, hf * 512:(hf + 1) * 512], op=ALU.add), [ps, YFb], [YBb])
                    S.dma(Vc.t, RKVG[2][t * 128:(t + 1) * 128, :], reads=[RKVGB[2][t]], writes=[Vc])
                    S.dma(Gc.t, RKVG[3][t * 128:(t + 1) * 128, :], reads=[RKVGB[3][t]], writes=[Gc])
                    nb = tmpB.next()
                    OP("dve", lambda e, nb=nb: e.tensor_reduce(out=nb[:, 0:16], in_=h16(YBb), axis=AX.X, op=ALU.add), [YBb], [nb])
                    OP("dve", lambda e, nb=nb: e.tensor_scalar(out=nb[:, 0:16], in0=nb[:, 0:16], scalar1=-1.0 / 64, scalar2=None, op0=ALU.mult), [nb], [nb])
                    OP("dve", lambda e, nb=nb: e.tensor_tensor(out=h16(YBb), in0=h16(YBb), in1=nb[:, 0:16].unsqueeze(2).to_broadcast([128, 16, 64]),
                                                               op=ALU.add), [YBb, nb], [YBb])
                    OP("pool", lambda e: e.tensor_tensor(out=C2.t, in0=YBb.t, in1=YBb.t, op=ALU.mult), [YBb], [C2])
                    OP("dve", lambda e, nb=nb: e.tensor_reduce(out=nb[:, 16:32], in_=h16(C2), axis=AX.X, op=ALU.add), [C2], [nb])
                    OP("act", lambda e, nb=nb: e.activation(out=nb[:, 32:48], in_=nb[:, 16:32], func=AF.Sqrt, scale=1.0 / 64, bias=GN_EPS), [nb], [nb])
                    OP("dve", lambda e, nb=nb: e.reciprocal(out=nb[:, 48:64], in_=nb[:, 32:48]), [nb], [nb])
                    OP("dve", lambda e, nb=nb: e.tensor_tensor(out=h16(YBb), in0=h16(YBb), in1=nb[:, 48:64].unsqueeze(2).to_broadcast([128, 16, 64]),
                                                               op=ALU.mult), [YBb, nb], [YBb])
                    OP("dve", lambda e: e.tensor_tensor(out=YBb.t, in0=YBb.t, in1=gnt.t, op=ALU.mult), [YBb, gnt], [YBb])
                    OP("dve", lambda e, t=t: e.tensor_tensor(out=h16(C2), in0=h16(Vc), in1=bsum[:, t, :].unsqueeze(2).to_broadcast([128, 16, 64]),
                                                             op=ALU.mult), [Vc, bsum], [C2])
                    OP("dve", lambda e: e.tensor_tensor(out=YBb.t, in0=YBb.t, in1=C2.t, op=ALU.add), [YBb, C2], [YBb])
                    OP("act", lambda e: e.activation(out=Gc.t, in_=Gc.t, func=AF.Silu), [Gc], [Gc])
                    OP("dve", lambda e: e.tensor_tensor(out=YBb.t, in0=YBb.t, in1=Gc.t, op=ALU.mult), [YBb, Gc], [YBb])
                    transpose_blocks(lambda i: YBb.t[:, i * 128:(i + 1) * 128], YBb, 8,
                                     lambda i0, n, ti=ti: (ogT[:, i0:i0 + n, ti * 128:(ti + 1) * 128], ogb))
                ybufs = [bsub(12288, nt * 1024, "yacc")]
                tail(layer, tiles, c, ogT, ogb, 8, Wout, ybufs)

            for g in groups:
                c_group(g)
            S.barrier()

        def final_norm():
            S.dma(gnw[:, 0:1024], I["final_norm_w"].partition_broadcast(128), writes=[gnw])
            for t in range(16):
                xt = xt_rot.next()
                S.dma(xt[:], XB[t].t, reads=[XB[t]], writes=[xt])
                s = st1.next()
                xn = xn_rot.next()
                OP("act", lambda e, xt=xt, xn=xn, s=s: e.activation(out=xn[:], in_=xt[:], func=AF.Square,
                                                                    accum_out=s[:, 0:1]), [xt], [xn, s])
                OP("act", lambda e, s=s: e.activation(out=s[:, 1:2], in_=s[:, 0:1], func=AF.Sqrt, scale=1.0 / 1024,
                                                      bias=EPS), [s], [s])
                OP("dve", lambda e, s=s: e.reciprocal(out=s[:, 2:3], in_=s[:, 1:2]), [s], [s])
                OP("dve", lambda e, xt=xt, xn=xn, s=s: e.scalar_tensor_tensor(out=xn[:], in0=xt[:], scalar=s[:, 2:3],
                                                                              in1=gnw[:, 0:1024], op0=ALU.mult, op1=ALU.mult),
                   [xt, s, gnw], [xn])
                dst = O["yp"] if t < 8 else O["ys"]
                S.dma(dst[(t % 8) * 128:(t % 8 + 1) * 128, :], xn[:], reads=[xn], queue="act")

        LAYERS = {0: layer_ret, 1: layer_rwkv, 2: layer_diff, 3: layer_na}
        try:
            CK("setup")
            for layer in layers:
                mod(layer)
                CK("mod")
                LAYERS[layer](layer)
            if final:
                final_norm()
        except _Stop:
            pass
        S.emit()
        print(f"[build] ops={S.n_ops} waits={S.n_waits} dma_sems={S.ndsem}")
    return nc


def _prep_inputs(inp):
    cst = _consts()
    f = lambda a: np.ascontiguousarray(np.asarray(a, dtype=np.float32))
    shared = {
        "norm_w": f(inp["norm_w"]), "w_mod": f(inp["w_mod"]), "b_mod": f(inp["b_mod"]),
        "final_norm_w": f(inp["final_norm_w"]),
        "ret_w_in": f(inp["ret_w_in"][0]), "ret_decay": f(inp["ret_decay"][0]).reshape(8),
        "ret_gn": f(inp["ret_gn"][0]), "ret_w_out": f(inp["ret_w_out"][0]),
        "rwkv_mu": f(inp["rwkv_mu"][0]), "rwkv_w_in": f(inp["rwkv_w_in"][0]), "rwkv_w0": f(inp["rwkv_w0"][0]),
        "rwkv_wA": f(inp["rwkv_wA"][0]), "rwkv_wB": f(inp["rwkv_wB"][0]), "rwkv_a0": f(inp["rwkv_a0"][0]),
        "rwkv_aA": f(inp["rwkv_aA"][0]), "rwkv_aB": f(inp["rwkv_aB"][0]), "rwkv_kk": f(inp["rwkv_kk"][0]),
        "rwkv_ka": f(inp["rwkv_ka"][0]), "rwkv_rk": f(inp["rwkv_rk"][0]).reshape(1024),
        "rwkv_gn": f(inp["rwkv_gn"][0]), "rwkv_w_out": f(inp["rwkv_w_out"][0]),
        "diff_w_in": f(inp["diff_w_in"][0]), "diff_lambda": f(inp["diff_lambda"][0]).reshape(256),
        "diff_gn": f(inp["diff_gn"][0]), "diff_w_out": f(inp["diff_w_out"][0]),
        "na_w_in": f(inp["na_w_in"][0]), "na_bias_x": _na_bias_expand(f(inp["na_bias"][0])),
        "na_w_out": f(inp["na_w_out"][0]),
    }
    shared.update(cst)
    maps = []
    for c in range(8):
        b = c // 4
        m = dict(shared)
        m["xp"] = f(inp["x_prompt"][4 * c:4 * c + 4]).reshape(1024, 1024)
        m["xs"] = f(inp["x_sample"][b])
        m["cond"] = np.ascontiguousarray(np.stack([f(inp["c_ctx"]), f(inp["c"][b])], 0))
        m["state_ret"] = f(inp["state_ret"][b, 0])
        m["state_rwkv"] = f(inp["state_rwkv"][b, 0])
        m["cache_diff_k"] = f(inp["cache_diff_k"][b, 0])
        m["cache_diff_v"] = f(inp["cache_diff_v"][b, 0])
        m["cache_na_k"] = f(inp["cache_na_k"][b, 0])
        m["cache_na_v"] = f(inp["cache_na_v"][b, 0])
        maps.append(m)
    return maps


_NC_CACHE = {}


def kernel(**inputs):
    maps = _prep_inputs(inputs)
    if "nc" not in _NC_CACHE:
        _NC_CACHE["nc"] = build()
    res = run_bass_kernel_spmd(_NC_CACHE["nc"], maps, core_ids=list(range(8))).results
    y_prompt = np.concatenate([r["yp"].reshape(4, 256, 1024) for r in res], 0)
    y_sample = np.stack([res[0]["ys"], res[4]["ys"]], 0)
    st_ret = np.concatenate([r["st_ret"] for r in res], 0)[:, None]
    st_rwkv = np.concatenate([r["st_rwkv"] for r in res], 0)[:, None]
    dk = np.concatenate([r["dk"] for r in res], 0)[:, None]
    dv = np.concatenate([r["dv"] for r in res], 0)[:, None]
    nk = np.concatenate([r["nk"] for r in res], 0)[:, None]
    nv = np.concatenate([r["nv"] for r in res], 0)[:, None]
    return (y_prompt, y_sample, st_ret, st_rwkv, dk, dv, nk, nv)
```

```python
import math
from contextlib import ExitStack

import numpy as np
import concourse.bass as bass
import concourse.mybir as mybir
from concourse.bass_utils import run_bass_kernel_spmd

F32 = mybir.dt.float32
BF16 = mybir.dt.bfloat16
AF = mybir.ActivationFunctionType
ALU = mybir.AluOpType
AX = mybir.AxisListType

EPS = 1e-6
GN_EPS = 1e-5
NEG = -30000.0


class Buf:
    __slots__ = ("t", "name", "lw", "rd", "dsem", "dcnt", "excl")

    def __init__(self, t, name, excl=False):
        self.excl = excl
        self.t = t
        self.name = name
        self.lw = None
        self.rd = {}
        self.dsem = None
        self.dcnt = 0

    def __getitem__(self, k):
        return self.t[k]


class Sched:
    CE = ("pe", "act", "dve", "pool")
    ALLQ = ("pe", "act", "dve", "pool", "sp")

    def __init__(self, nc, stack):
        self.nc = nc
        self.stack = stack
        self.sems = {}
        self.ecnt = {}
        for e in self.CE:
            self.sems[e] = stack.enter_context(nc.semaphore("es_" + e))
            self.ecnt[e] = 0
        self.q = {e: [] for e in self.ALLQ}
        self.seen = {e: {} for e in self.ALLQ}
        self.nbuf = 0
        self.ndsem = 0
        self.n_ops = 0
        self.n_waits = 0
        self.dma_bufs = {}
        self.CONST = Buf(None, "const")
        self.const_bufs = []
        self.snaps = {}

    def sb(self, shape, dtype=F32, name="b"):
        self.nbuf += 1
        t = self.stack.enter_context(self.nc.sbuf_tensor(f"{name}_{self.nbuf}", list(shape), dtype))
        return Buf(t, name)

    def _waits(self, eng, reads, writes):
        ev = {}
        for b in reads:
            if b.lw is not None:
                k, v = b.lw
                if ev.get(k, 0) < v:
                    ev[k] = v
            if b.excl:
                for k, v in b.rd.items():
                    if k != eng and ev.get(k, 0) < v:
                        ev[k] = v
        for b in writes:
            if b.lw is not None:
                k, v = b.lw
                if ev.get(k, 0) < v:
                    ev[k] = v
            for k, v in b.rd.items():
                if ev.get(k, 0) < v:
                    ev[k] = v
        waits = []
        seen = self.seen[eng]
        for k, v in sorted(ev.items(), key=lambda kv: -kv[1]):
            if eng == "pe" and k == "pe":
                continue
            if seen.get(k, 0) >= v:
                continue
            seen[k] = v
            waits.append((k, v))
            snap = self.snaps.get((k, v))
            if snap is not None:
                for k2, v2 in snap.items():
                    if k2 != eng and seen.get(k2, 0) < v2:
                        seen[k2] = v2
        self.n_waits += len(waits)
        return waits

    def _mark(self, me, reads, writes):
        k, v = me
        for b in reads:
            if b.rd.get(k, 0) < v:
                b.rd[k] = v
        for b in writes:
            b.lw = me
            b.rd = {}

    def op(self, eng, fn, reads=(), writes=()):
        waits = self._waits(eng, reads, writes)
        self.ecnt[eng] += 1
        me = (eng, self.ecnt[eng])
        self.snaps[me] = dict(self.seen[eng])
        self.q[eng].append((waits, fn, eng, 1))
        self._mark(me, reads, writes)
        self.n_ops += 1

    def dma(self, out_ap, in_ap, reads=(), writes=(), owner=None, queue="sp", **kw):
        if owner is self.CONST:
            waits = []
        else:
            waits = self._waits(queue, reads, writes)
        if owner is None:
            owner = writes[0] if writes else reads[0]
        if owner.dsem is None:
            self.ndsem += 1
            key = f"d{self.ndsem}"
            self.sems[key] = self.stack.enter_context(self.nc.semaphore("ds_" + key))
            owner.dsem = key
            self.dma_bufs[key] = owner
        owner.dcnt += 16
        me = (owner.dsem, owner.dcnt)
        if owner is not self.CONST:
            self.snaps[me] = dict(self.seen[queue])
        self.q[queue].append((waits, (lambda e: e.dma_start(out=out_ap, in_=in_ap, **kw)), owner.dsem, 16))
        if owner is self.CONST:
            self.const_bufs.extend(writes)
        else:
            self._mark(me, reads, writes)
        self.n_ops += 1

    def consts_done(self):
        for b in self.const_bufs:
            b.lw = (self.CONST.dsem, self.CONST.dcnt)
        self.const_bufs = []

    def barrier(self):
        tot = [(e, self.ecnt[e]) for e in self.CE] + [(k, b.dcnt) for k, b in self.dma_bufs.items()]
        for eng in self.ALLQ:
            waits = []
            for k, v in tot:
                if k == eng or v == 0:
                    continue
                if self.seen[eng].get(k, 0) >= v:
                    continue
                self.seen[eng][k] = v
                waits.append((k, v))
            if waits:
                self.q[eng].append((waits, None, None, 0))

    def emit(self):
        nc = self.nc
        self.barrier()
        with nc.Block() as block:
            def run(engobj, name):
                import os as _os
                attach = _os.environ.get("KATTACH", "1") == "1"
                for waits, fn, semkey, inc in self.q[name]:
                    if fn is None or not attach or not waits or (name == "pe" and _os.environ.get("KATTACH_PE", "0") != "1"):
                        for k, v in waits:
                            engobj.wait_ge(self.sems[k], v)
                        if fn is not None:
                            fn(engobj).then_inc(self.sems[semkey], inc)
                    else:
                        for k, v in waits[:-1]:
                            engobj.wait_ge(self.sems[k], v)
                        k, v = waits[-1]
                        fn(engobj)._wait_ge(self.sems[k], v).then_inc(self.sems[semkey], inc)

            @block.tensor
            def _(e):
                run(e, "pe")

            @block.scalar
            def _(e):
                run(e, "act")

            @block.vector
            def _(e):
                run(e, "dve")

            @block.gpsimd
            def _(e):
                run(e, "pool")

            @block.sync
            def _(e):
                run(e, "sp")


class Rot:
    def __init__(self, bufs):
        self.bufs = bufs
        self.i = 0

    def next(self):
        b = self.bufs[self.i % len(self.bufs)]
        self.i += 1
        return b


def _rope_tables(d):
    t = np.arange(1024)
    row = (t // 64).astype(np.float32)
    col = (t % 64).astype(np.float32)
    inv = (np.float32(10000.0) ** (-np.arange(0, d, 2, dtype=np.float32) / np.float32(d))).astype(np.float32)
    ang = np.stack([row[:, None] * inv[None, :], col[:, None] * inv[None, :]], axis=1).astype(np.float32)
    return np.cos(ang).astype(np.float32), np.sin(ang).astype(np.float32)


def _consts():
    c = {}
    c0, s0 = _rope_tables(128)
    c2, s2 = _rope_tables(32)
    c["rope0"] = np.ascontiguousarray(np.stack([c0, s0], 0))
    c["rope2"] = np.ascontiguousarray(np.stack([c2, s2], 0))
    kk = np.arange(128)[:, None]
    cols = np.arange(15 * 128)[None, :]
    m = cols // 128 - 7
    qq = cols % 128
    gap = (128 * m + qq - kk).astype(np.float32)
    c["gpn"] = np.ascontiguousarray(np.stack([np.maximum(gap, 0), np.maximum(-gap, 0)], 0))
    p = np.arange(128, dtype=np.float32)
    c["stexp"] = np.ascontiguousarray(np.stack([255.0 - p, 127.0 - p, p, 128.0 + p], 1))
    a_ = np.arange(128)
    c["tri"] = np.ascontiguousarray(((a_[:, None] <= a_[None, :]) & (a_[:, None] // 64 == a_[None, :] // 64)).astype(np.float32))
    j_ = np.arange(64)[:, None]
    t_ = np.arange(64)[None, :]
    su = (j_ < t_).astype(np.float32)
    iu = (j_ <= t_).astype(np.float32)
    sl = (t_ < j_).astype(np.float32)
    cm = np.zeros((64, 3, 128), np.float32)
    cm[:, 0, 0:64] = -su
    cm[:, 0, 64:128] = iu
    cm[:, 1, 0:64] = su
    cm[:, 1, 64:128] = iu
    cm[:, 2, 0:64] = -sl
    c["cmask"] = cm
    c["iota1k"] = np.ascontiguousarray(np.broadcast_to(np.arange(1024, dtype=np.float32)[None, :], (128, 1024)))
    return c


def _na_bias_expand(na_bias):
    out = np.full((16, 5, 128, 576), NEG, np.float32)
    jt = [0, 1, 2, 6, 7]
    qc = np.arange(64)[:, None]
    kc = np.arange(64)[None, :]
    cs = np.clip(qc - 8, 0, 48)
    col_ok = (kc >= cs) & (kc < cs + 16)
    cidx = np.clip(kc - qc, -15, 15) + 15
    for ti, j in enumerate(jt):
        r0 = min(max(2 * j - 4, 0), 8)
        nr = min(9, 16 - r0)
        for a in range(2):
            qr = 2 * j + a
            st = min(max(qr - 4, 0), 8)
            for i in range(nr):
                kr = r0 + i
                if st <= kr < st + 8:
                    dr = kr - qr + 7
                    blk = na_bias[:, dr][:, cidx]
                    blk = np.where(col_ok[None], blk, np.float32(NEG))
                    out[:, ti, a * 64:(a + 1) * 64, i * 64:(i + 1) * 64] = blk
    return out


NA_JT = {0: 0, 1: 1, 2: 2, 3: 2, 4: 2, 5: 2, 6: 3, 7: 4}


class _Stop(Exception):
    pass


def build(layers=(0, 1, 2, 3), final=True, stop=None):
    nc = bass.Bass("TRN2", target_bir_lowering=False)

    def CK(name):
        if stop == name:
            raise _Stop()

    def din(name, shape):
        return nc.dram_tensor(name, list(shape), F32, kind="ExternalInput").ap()

    def dout(name, shape):
        return nc.dram_tensor(name, list(shape), F32, kind="ExternalOutput").ap()

    def dscr(name, shape):
        return nc.dram_tensor(name, list(shape), F32).ap()

    I = {}
    for name, shape in [
        ("xp", (1024, 1024)), ("xs", (1024, 1024)), ("cond", (2, 1024)),
        ("state_ret", (2, 4, 256, 512)), ("state_rwkv", (2, 16, 64, 64)),
        ("cache_diff_k", (8, 256, 128)), ("cache_diff_v", (8, 256, 128)),
        ("cache_na_k", (16, 256, 64)), ("cache_na_v", (16, 256, 64)),
        ("norm_w", (4, 1024)), ("w_mod", (4, 1024, 3072)), ("b_mod", (4, 3072)), ("final_norm_w", (1024,)),
        ("ret_w_in", (1024, 6144)), ("ret_decay", (8,)), ("ret_gn", (2048,)), ("ret_w_out", (2048, 1024)),
        ("rwkv_mu", (6, 1024)), ("rwkv_w_in", (1024, 4096)), ("rwkv_w0", (2, 1024)), ("rwkv_wA", (2, 1024, 64)),
        ("rwkv_wB", (2, 64, 1024)), ("rwkv_a0", (2, 1024)), ("rwkv_aA", (2, 1024, 64)), ("rwkv_aB", (2, 64, 1024)),
        ("rwkv_kk", (1024,)), ("rwkv_ka", (1024,)), ("rwkv_rk", (1024,)), ("rwkv_gn", (1024,)),
        ("rwkv_w_out", (1024, 1024)),
        ("diff_w_in", (1024, 4096)), ("diff_lambda", (256,)), ("diff_gn", (1024,)), ("diff_w_out", (1024, 1024)),
        ("na_w_in", (1024, 4096)), ("na_bias_x", (16, 5, 128, 576)), ("na_w_out", (1024, 1024)),
        ("rope0", (2, 1024, 2, 64)), ("rope2", (2, 1024, 2, 16)), ("gpn", (2, 128, 1920)), ("stexp", (128, 4)),
        ("iota1k", (128, 1024)), ("tri", (128, 128)), ("cmask", (64, 3, 128)),
    ]:
        I[name] = din(name, shape)
    O = {}
    for name, shape in [
        ("yp", (1024, 1024)), ("ys", (1024, 1024)), ("st_ret", (4, 2, 4, 256, 512)),
        ("st_rwkv", (4, 2, 16, 64, 64)), ("dk", (4, 8, 256, 128)), ("dv", (4, 8, 256, 128)),
        ("nk", (4, 16, 256, 64)), ("nv", (4, 16, 256, 64)), ("xd", (2048, 1024)),
    ]:
        O[name] = dout(name, shape)
    XD = O["xd"]

    with ExitStack() as st:
        S = Sched(nc, st)
        OP = S.op

        ident = S.sb([128, 128], F32, "ident")
        PSt = st.enter_context(nc.psum_tensor("ps", [128, 8, 512], F32))
        PB = [Buf(PSt[:, i, :], f"ps{i}", excl=True) for i in range(8)]
        pS_rot = Rot(PB[0:4])
        pP_rot = Rot(PB[4:8])

        wf = Rot([S.sb([128, 8, 256], F32, "wf") for _ in range(2)])
        wb = Rot([S.sb([128, 8, 256], BF16, "wb") for _ in range(2)])
        xt_rot = Rot([S.sb([128, 1024], F32, "xt") for _ in range(2)])
        xn_rot = Rot([S.sb([128, 1024], F32, "xn") for _ in range(1)])
        st1 = Rot([S.sb([128, 8], F32, "st") for _ in range(6)])
        BIGN = 27648
        BIG = st.enter_context(nc.sbuf_tensor("big", [128, BIGN], F32))
        R_H = Buf(BIG[:, 0:4096], "RH")
        R_G = Buf(BIG[:, 4096:12288], "RG")
        ATTN = 15360
        ATT = BIG[:, 12288:27648]
        Jm = S.sb([128, 128], F32, "J")
        identst = S.sb([128, 64], F32, "identst")
        bsum = S.sb([128, 16, 16], F32, "bsum")
        muF = S.sb([128, 6, 8], F32, "muF")

        def bsub(off, n, name):
            assert off + n <= BIGN, (off, n)
            return Buf(BIG[:, off:off + n], name)
        scond = S.sb([128, 8, 2], F32, "scond")
        condF = S.sb([128, 8, 2], F32, "condF")
        normwF = S.sb([128, 4, 8], F32, "normwF")
        bmodF = S.sb([128, 24], F32, "bmodF")
        modF = S.sb([128, 24, 2], F32, "modF")
        scaleF = S.sb([128, 8, 2], F32, "scaleF")
        Gb = [S.sb([128, 1024], F32, "G") for _ in range(2)]
        gbt = Rot([S.sb([128, 128], F32, "gbt") for _ in range(2)])
        gnw = S.sb([128, 2048], F32, "gnw")
        rope0 = S.sb([128, 2, 8, 128], F32, "rope0")
        rope2 = S.sb([128, 2, 8, 32], F32, "rope2")
        rtmp = Rot([S.sb([128, 128], F32, "rtmp") for _ in range(2)])
        tmpA = Rot([S.sb([128, 512], F32, "tmpA") for _ in range(3)])
        tmpB = Rot([S.sb([128, 512], F32, "tmpB") for _ in range(2)])
        lamc = S.sb([128, 8], F32, "lamc")
        dlb = S.sb([128, 256], F32, "dlb")
        lgt = S.sb([128, 16], F32, "lgt")
        stx = S.sb([128, 4], F32, "stx")
        dsc = S.sb([128, 4, 4], F32, "dsc")
        strip = Rot([S.sb([128, 1920], BF16, "strip") for _ in range(2)])
        osb = Rot([S.sb([128, 512], F32, "osb") for _ in range(2)])

        def sub(off, n, name):
            assert off + n <= ATTN, (off, n)
            return Buf(ATT[:, off:off + n], name)

        def bview(buf, off_f, shape):
            n = int(np.prod(shape))
            ap = buf.t[:, off_f:off_f + n // 2].bitcast(BF16)
            if len(shape) == 1:
                return ap
            names = " ".join(f"a{i}" for i in range(len(shape)))
            kw = {f"a{i}": shape[i] for i in range(len(shape) - 1)}
            return ap.rearrange(f"p ({names}) -> p {names}", **kw)

        def fview(buf, off_f, shape):
            n = int(np.prod(shape))
            ap = buf.t[:, off_f:off_f + n]
            if len(shape) == 1:
                return ap
            names = " ".join(f"a{i}" for i in range(len(shape)))
            kw = {f"a{i}": shape[i] for i in range(len(shape) - 1)}
            return ap.rearrange(f"p ({names}) -> p {names}", **kw)

        XB = [Buf(XD[t * 128:(t + 1) * 128, :], f"xd{t}") for t in range(16)]
        first_layer = layers[0]

        def x_src(layer, t):
            if layer == first_layer:
                src = I["xp"] if t < 8 else I["xs"]
                tt = t % 8
                return src[tt * 128:(tt + 1) * 128, :], []
            return XB[t].t, [XB[t]]

        C = S.CONST
        for c2 in range(2):
            S.dma(condF[:, :, c2], I["cond"][c2].rearrange("(k p) -> p k", p=128), writes=[condF], owner=C,
                  allow_slow_non_contiguous=True)
        for l2 in range(4):
            S.dma(normwF[:, l2, :], I["norm_w"][l2].rearrange("(k p) -> p k", p=128), writes=[normwF], owner=C,
                  allow_slow_non_contiguous=True)
        for cs in range(2):
            for d, (rt, hw) in enumerate(((rope0, 64), (rope2, 16))):
                src = I["rope0" if d == 0 else "rope2"][cs].rearrange("(t p) a f -> p t (a f)", p=128)
                S.dma(rt[:, cs, :, :], src, writes=[rt], owner=C)
        S.dma(stx[:], I["stexp"], writes=[stx], owner=C)
        S.consts_done()
        OP("pool", lambda e: e.memset(ident[:], 0.0), [], [ident])
        OP("pool", lambda e: e.affine_select(out=ident[:], in_=ident[:], pattern=[[-1, 128]], compare_op=ALU.not_equal,
                                             fill=1.0, base=0, channel_multiplier=1), [ident], [ident])
        OP("act", lambda e: e.activation(out=scond[:], in_=condF[:], func=AF.Silu), [condF], [scond])
        OP("pool", lambda e: e.tensor_tensor(out=identst[:], in0=ident[:, 0:64], in1=ident[:, 64:128], op=ALU.add), [ident], [identst])
        OP("pool", lambda e: e.memset(Jm[:], 0.0), [], [Jm])
        OP("pool", lambda e: e.affine_select(out=Jm[:], in_=Jm[:], pattern=[[1, 128]], compare_op=ALU.not_equal,
                                             fill=1.0, base=-127, channel_multiplier=1), [Jm], [Jm])

        wctr = [0]

        def load_w(W, k0, c0, ncols=256, cast=True):
            f = wf.next()
            src = W[k0:k0 + 1024, c0:c0 + ncols].rearrange("(k p) n -> p k n", p=128)
            S.dma(f[:, :, 0:ncols], src, writes=[f])
            if not cast:
                return f
            b = wb.next()
            wctr[0] += 1
            if wctr[0] % 2:
                OP("act", lambda e: e.copy(out=b[:, :, 0:ncols], in_=f[:, :, 0:ncols]), [f], [b])
            else:
                OP("dve", lambda e: e.tensor_copy(out=b[:, :, 0:ncols], in_=f[:, :, 0:ncols]), [f], [b])
            return b

        def mod(layer):
            S.dma(bmodF[:], I["b_mod"][layer].rearrange("(m p) -> p m", p=128), writes=[bmodF],
                  allow_slow_non_contiguous=True)
            psm = pP_rot.next()
            for blk in range(12):
                w = load_w(I["w_mod"][layer], 0, blk * 256, cast=False)
                for mm in range(2):
                    m = blk * 2 + mm
                    for k in range(8):
                        OP("pe", lambda e, m=m, mm=mm, k=k, w=w: e.matmul(
                            out=psm[:, m * 2:m * 2 + 2], lhsT=w[:, k, mm * 128:(mm + 1) * 128], rhs=scond[:, k, :],
                            start=(k == 0), stop=(k == 7)), [w, scond], [psm])
            OP("dve", lambda e: e.tensor_tensor(out=modF[:], in0=psm[:, 0:48].rearrange("p (m c) -> p m c", c=2),
                                                in1=bmodF[:].unsqueeze(2).to_broadcast([128, 24, 2]), op=ALU.add),
               [psm, bmodF], [modF])
            OP("dve", lambda e: e.tensor_scalar(out=scaleF[:], in0=modF[:, 8:16, :], scalar1=1.0, scalar2=None,
                                                op0=ALU.add), [modF], [scaleF])
            OP("dve", lambda e: e.tensor_tensor(out=scaleF[:], in0=scaleF[:],
                                                in1=normwF[:, layer, :].unsqueeze(2).to_broadcast([128, 8, 2]),
                                                op=ALU.mult), [scaleF, normwF], [scaleF])
            for c in range(2):
                pg = [pP_rot.next(), pP_rot.next()]
                for k in range(8):
                    g = gbt.next()
                    OP("dve", lambda e, g=g, k=k, c=c: e.tensor_copy(
                        out=g[:], in_=modF[:, 16 + k, c:c + 1].to_broadcast([128, 128])), [modF], [g])
                    OP("pe", lambda e, g=g, k=k, pg=pg: e.matmul(
                        out=pg[k // 4][:, (k % 4) * 128:(k % 4 + 1) * 128], lhsT=g[:], rhs=ident[:],
                        start=True, stop=True), [g, ident], [pg[k // 4]])
                for hlf in range(2):
                    OP("act", lambda e, hlf=hlf, c=c, pg=pg: e.copy(out=Gb[c][:, hlf * 512:(hlf + 1) * 512],
                                                                    in_=pg[hlf][:, :]), [pg[hlf]], [Gb[c]])

        def front(layer, tiles, c, hT_of):
            for ti, t in enumerate(tiles):
                xt = xt_rot.next()
                src, rb = x_src(layer, t)
                S.dma(xt[:], src, reads=rb, writes=[xt])
                s = st1.next()
                xn = xn_rot.next()
                OP("act", lambda e, xt=xt, xn=xn, s=s: e.activation(out=xn[:], in_=xt[:], func=AF.Square,
                                                                    accum_out=s[:, 0:1]), [xt], [xn, s])
                OP("act", lambda e, s=s: e.activation(out=s[:, 1:2], in_=s[:, 0:1], func=AF.Sqrt, scale=1.0 / 1024,
                                                      bias=EPS), [s], [s])
                OP("dve", lambda e, s=s: e.reciprocal(out=s[:, 2:3], in_=s[:, 1:2]), [s], [s])
                OP("dve", lambda e, xt=xt, xn=xn, s=s: e.tensor_scalar(out=xn[:], in0=xt[:], scalar1=s[:, 2:3],
                                                                       scalar2=None, op0=ALU.mult), [xt, s], [xn])
                pp = [pP_rot.next(), pP_rot.next()]
                for k in range(8):
                    OP("pe", lambda e, k=k, xn=xn, pp=pp: e.transpose(
                        out=pp[k // 4][:, (k % 4) * 128:(k % 4 + 1) * 128], in_=xn[:, k * 128:(k + 1) * 128],
                        identity=ident[:]), [xn, ident], [pp[k // 4]])
                for k in range(8):
                    dst, db = hT_of(k, ti)
                    OP("act", lambda e, k=k, dst=dst, pp=pp: e.activation(
                        out=dst, in_=pp[k // 4][:, (k % 4) * 128:(k % 4 + 1) * 128], func=AF.Identity,
                        scale=scaleF[:, k, c:c + 1], bias=modF[:, k, c:c + 1]), [pp[k // 4], scaleF, modF], [db])

        def proj(hT, hbuf, ti, w, ncols, ps):
            for k in range(8):
                OP("pe", lambda e, k=k: e.matmul(out=ps[:, 0:ncols], lhsT=hT[:, k, ti * 128:(ti + 1) * 128],
                                                 rhs=w[:, k, 0:ncols], start=(k == 0), stop=(k == 7)),
                   [hbuf, w], [ps])

        def transpose_blocks(src_ap_of, srcbuf, nblk, dst_of, evac="act", scl=None):
            i = 0
            while i < nblk:
                n = min(4, nblk - i)
                ps = pP_rot.next()
                for j in range(n):
                    OP("pe", lambda e, i=i, j=j, ps=ps: e.transpose(out=ps[:, j * 128:(j + 1) * 128],
                                                                    in_=src_ap_of(i + j), identity=ident[:]),
                       [srcbuf, ident], [ps])
                dst, db = dst_of(i, n)
                if evac == "act" and scl is not None:
                    OP("act", lambda e, dst=dst, ps=ps, n=n: e.activation(
                        out=dst, in_=ps[:, 0:n * 128].rearrange("p (a b) -> p a b", a=n), func=AF.Copy, scale=scl), [ps], [db])
                elif evac == "act":
                    OP("act", lambda e, dst=dst, ps=ps, n=n: e.copy(
                        out=dst, in_=ps[:, 0:n * 128].rearrange("p (a b) -> p a b", a=n)), [ps], [db])
                else:
                    OP("dve", lambda e, dst=dst, ps=ps, n=n: e.tensor_copy(
                        out=dst, in_=ps[:, 0:n * 128].rearrange("p (a b) -> p a b", a=n)), [ps], [db])
                i += n

        def rope(src, dst, table, half, tile, ngrp):
            n = ngrp * 4 * half
            sv = src[:, 0:n].rearrange("p (g a two f) -> p g a two f", g=ngrp, a=2, two=2)
            dv = dst[:, 0:n].rearrange("p (g a two f) -> p g a two f", g=ngrp, a=2, two=2)
            cos = table[:, 0, tile, :].rearrange("p (a f) -> p a f", a=2).unsqueeze(1).to_broadcast([128, ngrp, 2, half])
            sin = table[:, 1, tile, :].rearrange("p (a f) -> p a f", a=2).unsqueeze(1).to_broadcast([128, ngrp, 2, half])
            x1, x2 = sv[:, :, :, 0, :], sv[:, :, :, 1, :]
            o1, o2 = dv[:, :, :, 0, :], dv[:, :, :, 1, :]
            t1 = rtmp.next()
            t2 = rtmp.next()
            m = ngrp * 2 * half
            t1v = t1[:, 0:m].rearrange("p (g a f) -> p g a f", g=ngrp, a=2)
            t2v = t2[:, 0:m].rearrange("p (g a f) -> p g a f", g=ngrp, a=2)
            OP("dve", lambda e: e.tensor_tensor(out=o1, in0=x1, in1=cos, op=ALU.mult), [src, table], [dst])
            OP("pool", lambda e: e.tensor_tensor(out=t1v, in0=x2, in1=sin, op=ALU.mult), [src, table], [t1])
            OP("dve", lambda e: e.tensor_tensor(out=o1, in0=o1, in1=t1v, op=ALU.subtract), [dst, t1], [dst])
            OP("pool", lambda e: e.tensor_tensor(out=t2v, in0=x1, in1=sin, op=ALU.mult), [src, table], [t2])
            OP("dve", lambda e: e.tensor_tensor(out=o2, in0=x2, in1=cos, op=ALU.mult), [src, table], [dst])
            OP("dve", lambda e: e.tensor_tensor(out=o2, in0=o2, in1=t2v, op=ALU.add), [dst, t2], [dst])

        def tail(layer, tiles, c, ogT, ogbuf, KC, Wout, ybufs):
            nt = len(tiles)
            yacc = ATT[:, 0:nt * 1024].rearrange("p (t n) -> p t n", t=nt)
            for cb in range(4):
                ws = [load_w(Wout, kk * 1024, cb * 256) for kk in range(KC // 8)]
                for ti in range(nt):
                    ps = pP_rot.next()
                    for kc in range(KC):
                        w = ws[kc // 8]
                        OP("pe", lambda e, kc=kc, w=w, ti=ti, ps=ps: e.matmul(
                            out=ps[:, 0:256], lhsT=ogT[:, kc, ti * 128:(ti + 1) * 128], rhs=w[:, kc % 8, :],
                            start=(kc == 0), stop=(kc == KC - 1)), [ogbuf, w], [ps])
                    OP("dve", lambda e, ti=ti, cb=cb, ps=ps: e.tensor_tensor(
                        out=yacc[:, ti, cb * 256:(cb + 1) * 256], in0=ps[:, 0:256],
                        in1=Gb[c][:, cb * 256:(cb + 1) * 256], op=ALU.mult), [ps, Gb[c]], ybufs)
            for ti, t in enumerate(tiles):
                xt = xt_rot.next()
                src, rb = x_src(layer, t)
                S.dma(xt[:], src, reads=rb, writes=[xt])
                OP("dve", lambda e, xt=xt, ti=ti: e.tensor_tensor(out=xt[:], in0=xt[:], in1=yacc[:, ti, :], op=ALU.add),
                   [xt] + ybufs, [xt])
                S.dma(XB[t].t, xt[:], reads=[xt], writes=[XB[t]], owner=xt, queue="act")

        def softmax_un(src, srcbufs, scale, Pout, Pbuf):
            s = st1.next()
            ax = AX.XY if len(src.shape) == 3 else AX.X
            if scale == 1.0:
                OP("dve", lambda e: e.tensor_reduce(out=s[:, 1:2], in_=src, axis=ax, op=ALU.max, negate=True), srcbufs, [s])
            else:
                OP("dve", lambda e: e.tensor_reduce(out=s[:, 0:1], in_=src, axis=ax, op=ALU.max), srcbufs, [s])
                OP("dve", lambda e: e.tensor_scalar(out=s[:, 1:2], in0=s[:, 0:1], scalar1=-scale, scalar2=None,
                                                    op0=ALU.mult), [s], [s])
            OP("act", lambda e: e.activation(out=Pout, in_=src, func=AF.Exp, scale=scale, bias=s[:, 1:2],
                                             accum_out=s[:, 2:3]), srcbufs + [s], [Pbuf, s])
            OP("dve", lambda e: e.reciprocal(out=s[:, 3:4], in_=s[:, 2:3]), [s], [s])
            return s

        def pv(pc, pcbuf, kblocks, pcT, pcTbuf, vof, N, pso):
            nb = len(kblocks)
            i = 0
            while i < nb:
                n = min(4, nb - i)
                ps = pP_rot.next()
                for j in range(n):
                    off, nk = kblocks[i + j]
                    OP("pe", lambda e, j=j, off=off, nk=nk, ps=ps: e.transpose(
                        out=ps[0:nk, j * 128:(j + 1) * 128], in_=pc[:, off:off + nk], identity=ident[:]),
                       [pcbuf, ident], [ps])
                full = all(kblocks[i + j][1] == 128 for j in range(n))
                if full:
                    OP("act", lambda e, i=i, n=n, ps=ps: e.copy(
                        out=pcT[:, i:i + n, :], in_=ps[:, 0:n * 128].rearrange("p (a b) -> p a b", a=n)),
                       [ps], [pcTbuf])
                else:
                    for j in range(n):
                        nk = kblocks[i + j][1]
                        OP("act", lambda e, i=i, j=j, nk=nk, ps=ps: e.copy(
                            out=pcT[0:nk, i + j, :], in_=ps[0:nk, j * 128:(j + 1) * 128]), [ps], [pcTbuf])
                i += n
            for i, (off, nk) in enumerate(kblocks):
                vap, vbuf = vof(i)
                OP("pe", lambda e, i=i, nk=nk, vap=vap: e.matmul(out=pso[:, 0:N], lhsT=pcT[0:nk, i, :], rhs=vap,
                                                               start=(i == 0), stop=(i == nb - 1)),
                   [pcTbuf, vbuf], [pso])

        groups = [
            dict(tiles=[0, 1, 2, 3, 4, 5, 6, 7], c=0, seqs=[(0, 2, 0), (2, 2, 1), (4, 2, 2), (6, 2, 3)], sample=False, pair=0),
            dict(tiles=list(range(8, 16)), c=1, seqs=[(0, 8, -1)], sample=True, pair=-1),
        ]

        def layer_diff(layer):
            lam_init = 0.8 - 0.6 * math.exp(-0.3 * layer)
            Win, Wout = I["diff_w_in"], I["diff_w_out"]
            S.dma(gnw[:, 0:1024], I["diff_gn"].partition_broadcast(128), writes=[gnw])
            OP("dve", lambda e: e.tensor_scalar(out=gnw[:, 0:1024], in0=gnw[:, 0:1024], scalar1=1.0 - lam_init,
                                                scalar2=None, op0=ALU.mult), [gnw], [gnw])
            S.dma(dlb[:], I["diff_lambda"].partition_broadcast(128), writes=[dlb])
            dl = dlb[:].rearrange("p (a f) -> p a f", a=4)
            t = tmpA.next()
            for i2 in range(2):
                OP("dve", lambda e, i2=i2: e.tensor_tensor(out=t[:, i2 * 64:(i2 + 1) * 64], in0=dl[:, 2 * i2, :],
                                                           in1=dl[:, 2 * i2 + 1, :], op=ALU.mult), [dlb], [t])
            OP("dve", lambda e: e.tensor_reduce(out=lamc[:, 0:2], in_=t[:, 0:128].rearrange("p (a f) -> p a f", a=2),
                                                axis=AX.X, op=ALU.add), [t], [lamc])
            OP("act", lambda e: e.activation(out=lamc[:, 2:4], in_=lamc[:, 0:2], func=AF.Exp), [lamc], [lamc])
            OP("dve", lambda e: e.tensor_tensor(out=lamc[:, 4:5], in0=lamc[:, 2:3], in1=lamc[:, 3:4], op=ALU.subtract),
               [lamc], [lamc])
            OP("dve", lambda e: e.tensor_scalar(out=lamc[:, 5:6], in0=lamc[:, 4:5], scalar1=lam_init, scalar2=None,
                                                op0=ALU.add), [lamc], [lamc])
            scale = 64 ** -0.5

            def do_group(g):
                S.barrier()
                tiles, c, smp = g["tiles"], g["c"], g["sample"]
                nt = len(tiles)
                T = nt * 128
                NK = T + 256 if smp else 256
                hT = bview(R_H, 0, [8, T])
                ogT = bview(R_G, 0, [8, T])
                qTb = sub(0, 1024, "qT")
                kTb = sub(1024, 1280, "kT")
                vbb = sub(2304, 1280, "vb")
                Psets = [(sub(3584, 1280, "P1a"), sub(4864, 1280, "P2a"), sub(7424, 640, "pcTa")),
                         (sub(6144, 1280, "P1b"), sub(11136, 1280, "P2b"), sub(12416, 640, "pcTb"))]
                obb = sub(8064, 2048, "ob")
                ckb = sub(10112, 1024, "ck")
                qT = bview(qTb, 0, [2, 1024])
                kT = bview(kTb, 0, [2, 1280])
                vb = bview(vbb, 0, [10, 256])
                ob = fview(obb, 0, [8, 256])
                ck = fview(ckb, 0, [2, 512])
                front(layer, tiles, c, lambda k, ti: (hT[:, k, ti * 128:(ti + 1) * 128], R_H))
                CK("front")
                for hb in range(4):
                    for which, dstT, dstb in ((0, qT, qTb), (1, kT, kTb)):
                        w = load_w(Win, 0, which * 1024 + hb * 256)
                        for ti in range(nt):
                            ps = pP_rot.next()
                            proj(hT, R_H, ti, w, 256, ps)
                            ta = tmpA.next()
                            OP("act", lambda e, ta=ta, ps=ps: e.copy(out=ta[:, 0:256], in_=ps[:, 0:256]), [ps], [ta])
                            srcb = ta
                            if smp:
                                tb = tmpB.next()
                                rope(ta, tb, rope2, 16, ti, 4)
                                srcb = tb
                            elif which == 1:
                                sq = g["seqs"][ti // 2][2]
                                tt = ti % 2
                                S.dma(O["dk"][sq, 2 * hb:2 * hb + 2, tt * 128:(tt + 1) * 128, :].rearrange("h t d -> t h d"),
                                      ta[:, 0:256].rearrange("p (h d) -> p h d", h=2), reads=[ta], queue="act")
                            transpose_blocks(lambda i, srcb=srcb: srcb[:, i * 128:(i + 1) * 128], srcb, 2,
                                             lambda i0, n, dstT=dstT, dstb=dstb, ti=ti: (dstT[:, i0:i0 + n, ti * 128:(ti + 1) * 128], dstb),
                                             scl=(scale if which == 0 else None))
                    CK("qk")
                    w = load_w(Win, 0, 2048 + hb * 256)
                    for ti in range(nt):
                        ps = pP_rot.next()
                        proj(hT, R_H, ti, w, 256, ps)
                        ta = tmpA.next()
                        OP("act", lambda e, ta=ta, ps=ps: e.copy(out=ta[:, 0:256], in_=ps[:, 0:256]), [ps], [ta])
                        OP("dve", lambda e, ta=ta, ti=ti: e.tensor_copy(out=vb[:, ti, :], in_=ta[:, 0:256]), [ta], [vbb])
                        if not smp:
                            sq = g["seqs"][ti // 2][2]
                            tt = ti % 2
                            S.dma(O["dv"][sq, 2 * hb:2 * hb + 2, tt * 128:(tt + 1) * 128, :].rearrange("h t d -> t h d"),
                                  ta[:, 0:256].rearrange("p (h d) -> p h d", h=2), reads=[ta], queue="act")
                    if smp:
                        for tt in range(2):
                            S.dma(ck[:, 0, 0:256].rearrange("p (h d) -> p h d", h=2),
                                  I["cache_diff_k"][2 * hb:2 * hb + 2, tt * 128:(tt + 1) * 128, :].rearrange("h t d -> t h d"),
                                  writes=[ckb])
                            transpose_blocks(lambda i: ck[:, 0, i * 128:(i + 1) * 128], ckb, 2,
                                             lambda i0, n, tt=tt: (kT[:, i0:i0 + n, 1024 + tt * 128:1024 + (tt + 1) * 128], kTb))
                            S.dma(ck[:, 1, 0:256].rearrange("p (h d) -> p h d", h=2),
                                  I["cache_diff_v"][2 * hb:2 * hb + 2, tt * 128:(tt + 1) * 128, :].rearrange("h t d -> t h d"),
                                  writes=[ckb])
                            OP("dve", lambda e, tt=tt: e.tensor_copy(out=vb[:, 8 + tt, :], in_=ck[:, 1, 0:256]), [ckb], [vbb])
                    CK("v")
                    nkt = NK // 128
                    nblk, blk = (1, 256) if NK == 256 else (4, 320)

                    def att_s1(t0, hh, qi, k_):
                        P1b_, P2b_ = Psets[k_][0], Psets[k_][1]
                        P1_, P2_ = P1b_.t, P2b_.t
                        tq = t0 + qi
                        stats = []
                        for comp, (Pb, Pap) in enumerate(((P1b_, P1_), (P2b_, P2_))):
                            pr = slice(comp * 64, (comp + 1) * 64)
                            for b in range(nblk):
                                k0 = (t0 * 128 if not smp else 0) + b * blk
                                OP("pe", lambda e, pr=pr, k0=k0, b=b: e.matmul(
                                    out=PB[b][:, 0:blk], lhsT=qT[pr, hh, tq * 128:(tq + 1) * 128],
                                    rhs=kT[pr, hh, k0:k0 + blk], start=True, stop=True), [qTb, kTb], [PB[b]])
                            src = PSt[:, 0:nblk, 0:blk]
                            stats.append(softmax_un(src, PB[0:nblk], 1.0, Pap[:, 0:NK].rearrange("p (a b) -> p a b", a=nblk), Pb))
                        s1, s2 = stats
                        OP("dve", lambda e: e.tensor_scalar(out=P2_[:, 0:NK], in0=P2_[:, 0:NK], scalar1=s2[:, 3:4], scalar2=lamc[:, 5:6], op0=ALU.mult,
                                                            op1=ALU.mult), [P2b_, s2, lamc], [P2b_])
                        OP("dve", lambda e: e.scalar_tensor_tensor(out=P1_[:, 0:NK], in0=P1_[:, 0:NK], scalar=s1[:, 3:4], in1=P2_[:, 0:NK],
                                                                   op0=ALU.mult, op1=ALU.subtract), [P1b_, P2b_, s1], [P1b_])
                        return (t0, hh, qi, k_)

                    def att_s2(ctx):
                        t0, hh, qi, k_ = ctx
                        P1b_, pcTb_ = Psets[k_][0], Psets[k_][2]
                        pcT_ = bview(pcTb_, 0, [10, 128])
                        tq = t0 + qi
                        h = 2 * hb + hh
                        pso = pP_rot.next()
                        vt0 = 0 if smp else t0
                        pv(P1b_.t, P1b_, [(i * 128, 128) for i in range(nkt)], pcT_, pcTb_,
                           lambda i: (vb[:, vt0 + i, hh * 128:(hh + 1) * 128], vbb), 128, pso)
                        s = st1.next()
                        ta = tmpA.next()
                        OP("act", lambda e: e.activation(out=ta[:, 0:128], in_=pso[:, 0:128], func=AF.Square, accum_out=s[:, 0:1]), [pso], [ta, s])
                        OP("act", lambda e: e.activation(out=s[:, 1:2], in_=s[:, 0:1], func=AF.Sqrt, scale=1.0 / 128, bias=EPS), [s], [s])
                        OP("dve", lambda e: e.reciprocal(out=s[:, 2:3], in_=s[:, 1:2]), [s], [s])
                        OP("dve", lambda e: e.scalar_tensor_tensor(out=ob[:, tq, hh * 128:(hh + 1) * 128], in0=pso[:, 0:128], scalar=s[:, 2:3],
                                                                   in1=gnw[:, h * 128:(h + 1) * 128], op0=ALU.mult, op1=ALU.mult), [pso, s, gnw], [obb])

                    its = [(t0, hh, qi) for (t0, ntq, sq) in g["seqs"] for hh in range(2) for qi in range(ntq)]
                    prev = None
                    for ii, (t0, hh, qi) in enumerate(its):
                        ctx = att_s1(t0, hh, qi, ii % 2)
                        if prev is not None:
                            att_s2(prev)
                        prev = ctx
                    att_s2(prev)
                    CK("attn")
                    w = load_w(Win, 0, 3072 + hb * 256)
                    for ti in range(nt):
                        ps = pP_rot.next()
                        proj(hT, R_H, ti, w, 256, ps)
                        ta = tmpA.next()
                        OP("act", lambda e, ta=ta, ps=ps: e.activation(out=ta[:, 0:256], in_=ps[:, 0:256], func=AF.Silu), [ps], [ta])
                        OP("dve", lambda e, ta=ta, ti=ti: e.tensor_tensor(out=ob[:, ti, :], in0=ob[:, ti, :], in1=ta[:, 0:256],
                                                                          op=ALU.mult), [obb, ta], [obb])
                        transpose_blocks(lambda i, ti=ti: ob[:, ti, i * 128:(i + 1) * 128], obb, 2,
                                         lambda i0, n, ti=ti, hb=hb: (ogT[:, 2 * hb + i0:2 * hb + i0 + n, ti * 128:(ti + 1) * 128], R_G))
                S.barrier()
                ybufs = [Buf(ATT[:, 0:nt * 1024], "yacc")]
                tail(layer, tiles, c, ogT, R_G, 8, Wout, ybufs)

            for g in groups:
                do_group(g)
            S.barrier()

        nab_rot = Rot([S.sb([128, 576], F32, "nab") for _ in range(2)])

        def layer_na(layer):
            Win, Wout = I["na_w_in"], I["na_w_out"]
            scale = 64 ** -0.5

            def do_group(g):
                S.barrier()
                tiles, c, smp = g["tiles"], g["c"], g["sample"]
                nt = len(tiles)
                T = nt * 128
                hT = bview(R_H, 0, [8, T])
                ogT = bview(R_G, 0, [8, T])
                qTb = sub(0, 1024, "qT")
                kTb = sub(1024, 1280, "kT")
                vbb = sub(2304, 1280, "vb")
                NAsets = [(sub(3584, 1280, "P1a"), sub(7424, 640, "pcTa")), (sub(6144, 1280, "P1b"), sub(12416, 640, "pcTb"))]
                obb = sub(8064, 2048, "ob")
                ckb = sub(10112, 1024, "ck")
                qT = bview(qTb, 0, [2, 1024])
                kT = bview(kTb, 0, [2, 1280])
                vb = bview(vbb, 0, [10, 256])
                ob = fview(obb, 0, [8, 256])
                ck = fview(ckb, 0, [2, 512])
                front(layer, tiles, c, lambda k, ti: (hT[:, k, ti * 128:(ti + 1) * 128], R_H))
                for hb in range(4):
                    for which, dstT, dstb in ((0, qT, qTb), (1, kT, kTb)):
                        w = load_w(Win, 0, which * 1024 + hb * 256)
                        for ti in range(nt):
                            ps = pP_rot.next()
                            proj(hT, R_H, ti, w, 256, ps)
                            ta = tmpA.next()
                            OP("act", lambda e, ta=ta, ps=ps: e.copy(out=ta[:, 0:256], in_=ps[:, 0:256]), [ps], [ta])
                            if which == 1 and not smp:
                                sq = g["seqs"][ti // 2][2]
                                tt = ti % 2
                                S.dma(O["nk"][sq, 4 * hb:4 * hb + 4, tt * 128:(tt + 1) * 128, :].rearrange("h t d -> t h d"),
                                      ta[:, 0:256].rearrange("p (h d) -> p h d", h=4), reads=[ta], queue="act")
                            transpose_blocks(lambda i, ta=ta: ta[:, i * 128:(i + 1) * 128], ta, 2,
                                             lambda i0, n, dstT=dstT, dstb=dstb, ti=ti: (dstT[:, i0:i0 + n, ti * 128:(ti + 1) * 128], dstb),
                                             scl=(scale if which == 0 else None))
                    w = load_w(Win, 0, 2048 + hb * 256)
                    for ti in range(nt):
                        ps = pP_rot.next()
                        proj(hT, R_H, ti, w, 256, ps)
                        ta = tmpA.next()
                        OP("act", lambda e, ta=ta, ps=ps: e.copy(out=ta[:, 0:256], in_=ps[:, 0:256]), [ps], [ta])
                        OP("dve", lambda e, ta=ta, ti=ti: e.tensor_copy(out=vb[:, ti, :], in_=ta[:, 0:256]), [ta], [vbb])
                        if not smp:
                            sq = g["seqs"][ti // 2][2]
                            tt = ti % 2
                            S.dma(O["nv"][sq, 4 * hb:4 * hb + 4, tt * 128:(tt + 1) * 128, :].rearrange("h t d -> t h d"),
                                  ta[:, 0:256].rearrange("p (h d) -> p h d", h=4), reads=[ta], queue="act")
                    if smp:
                        for tt in range(2):
                            S.dma(ck[:, 0, 0:256].rearrange("p (h d) -> p h d", h=4),
                                  I["cache_na_k"][4 * hb:4 * hb + 4, tt * 128:(tt + 1) * 128, :].rearrange("h t d -> t h d"),
                                  writes=[ckb])
                            transpose_blocks(lambda i: ck[:, 0, i * 128:(i + 1) * 128], ckb, 2,
                                             lambda i0, n, tt=tt: (kT[:, i0:i0 + n, 1024 + tt * 128:1024 + (tt + 1) * 128], kTb))
                            S.dma(ck[:, 1, 0:256].rearrange("p (h d) -> p h d", h=4),
                                  I["cache_na_v"][4 * hb:4 * hb + 4, tt * 128:(tt + 1) * 128, :].rearrange("h t d -> t h d"),
                                  writes=[ckb])
                            OP("dve", lambda e, tt=tt: e.tensor_copy(out=vb[:, 8 + tt, :], in_=ck[:, 1, 0:256]), [ckb], [vbb])
                    def att_s1(t0, hh, qi, k_):
                            P1b, pcTb = NAsets[k_]
                            P1 = P1b.t
                            h = 4 * hb + hh
                            cc = hh // 2
                            pr = slice((hh % 2) * 64, (hh % 2) * 64 + 64)
                            if True:
                                tq = t0 + qi
                                if not smp:
                                    OP("pe", lambda e, pr=pr, cc=cc, tq=tq, t0=t0: e.matmul(
                                        out=PB[0][:, 0:256], lhsT=qT[pr, cc, tq * 128:(tq + 1) * 128],
                                        rhs=kT[pr, cc, t0 * 128:t0 * 128 + 256], start=True, stop=True), [qTb, kTb], [PB[0]])
                                    s1 = softmax_un(PB[0][:, 0:256], [PB[0]], 1.0, P1[:, 0:256], P1b)
                                    NKs = 256
                                    kblocks = [(0, 128), (128, 128)]
                                    vof = lambda i, hh=hh, t0=t0: (vb[:, t0 + i, hh * 64:(hh + 1) * 64], vbb)
                                else:
                                    j = qi
                                    r0 = min(max(2 * j - 4, 0), 8)
                                    nr = min(9, 16 - r0)
                                    nloc = nr * 64
                                    NKs = nloc + 256
                                    blk = NKs // 2
                                    nab = nab_rot.next()
                                    S.dma(nab[:], I["na_bias_x"][h, NA_JT[j]], writes=[nab])
                                    segs = [(r0 * 64, nloc, 0, True), (1024, 256, nloc, False)]
                                    pieces = []
                                    for key0, n, col0, biased in segs:
                                        done = 0
                                        while done < n:
                                            col = col0 + done
                                            b = col // blk
                                            m = min(n - done, (b + 1) * blk - col)
                                            pieces.append((key0 + done, m, b, col - b * blk, col, biased, col0 + done - col0 + (0 if not biased else 0)))
                                            done += m
                                    for (k0, m, b, bc, col, biased, _) in pieces:
                                        OP("pe", lambda e, pr=pr, cc=cc, tq=tq, k0=k0, m=m, b=b, bc=bc: e.matmul(
                                            out=PB[b][:, bc:bc + m], lhsT=qT[pr, cc, tq * 128:(tq + 1) * 128],
                                            rhs=kT[pr, cc, k0:k0 + m], start=True, stop=True), [qTb, kTb], [PB[b]])
                                    for (k0, m, b, bc, col, biased, _) in pieces:
                                        if biased:
                                            OP("dve", lambda e, m=m, b=b, bc=bc, col=col, nab=nab: e.tensor_tensor(
                                                out=P1[:, col:col + m], in0=PB[b][:, bc:bc + m],
                                                in1=nab[:, col:col + m], op=ALU.add), [PB[b], nab], [P1b])
                                        else:
                                            OP("dve", lambda e, m=m, b=b, bc=bc, col=col: e.tensor_copy(
                                                out=P1[:, col:col + m], in_=PB[b][:, bc:bc + m]), [PB[b]], [P1b])
                                    s1 = softmax_un(P1[:, 0:NKs], [P1b], 1.0, P1[:, 0:NKs], P1b)
                                    kblocks = [(i * 128, 128) for i in range(nloc // 128)]
                                    if nloc % 128:
                                        kblocks.append((nloc - 64, 64))
                                    nlb = len(kblocks)
                                    kblocks += [(nloc, 128), (nloc + 128, 128)]

                                    def vof(i, hh=hh, r0=r0, nlb=nlb, kblocks=kblocks):
                                        if i < nlb:
                                            nk = kblocks[i][1]
                                            return vb[0:nk, r0 // 2 + i, hh * 64:(hh + 1) * 64], vbb
                                        return vb[:, 8 + (i - nlb), hh * 64:(hh + 1) * 64], vbb
                                OP("dve", lambda e, s1=s1, NKs=NKs: e.tensor_scalar(out=P1[:, 0:NKs], in0=P1[:, 0:NKs], scalar1=s1[:, 3:4],
                                                                                scalar2=None, op0=ALU.mult), [P1b, s1], [P1b])
                                return (tq, hh, k_, kblocks, vof)

                    def att_s2(ctx):
                        tq, hh, k_, kblocks, vof = ctx
                        P1b, pcTb = NAsets[k_]
                        pcT = bview(pcTb, 0, [10, 128])
                        pso = pP_rot.next()
                        pv(P1b.t, P1b, kblocks, pcT, pcTb, vof, 64, pso)
                        OP("act", lambda e: e.copy(out=ob[:, tq, hh * 64:(hh + 1) * 64], in_=pso[:, 0:64]), [pso], [obb])

                    its = [(t0, hh, qi) for (t0, ntq, sq) in g["seqs"] for hh in range(4) for qi in range(ntq)]
                    prev = None
                    for ii, (t0, hh, qi) in enumerate(its):
                        ctx = att_s1(t0, hh, qi, ii % 2)
                        if prev is not None:
                            att_s2(prev)
                        prev = ctx
                    att_s2(prev)
                    w = load_w(Win, 0, 3072 + hb * 256)
                    for ti in range(nt):
                        ps = pP_rot.next()
                        proj(hT, R_H, ti, w, 256, ps)
                        ta = tmpA.next()
                        OP("act", lambda e, ta=ta, ps=ps: e.activation(out=ta[:, 0:256], in_=ps[:, 0:256], func=AF.Silu), [ps], [ta])
                        OP("dve", lambda e, ta=ta, ti=ti: e.tensor_tensor(out=ob[:, ti, :], in0=ob[:, ti, :], in1=ta[:, 0:256],
                                                                          op=ALU.mult), [obb, ta], [obb])
                        transpose_blocks(lambda i, ti=ti: ob[:, ti, i * 128:(i + 1) * 128], obb, 2,
                                         lambda i0, n, ti=ti, hb=hb: (ogT[:, 2 * hb + i0:2 * hb + i0 + n, ti * 128:(ti + 1) * 128], R_G))
                S.barrier()
                ybufs = [Buf(ATT[:, 0:nt * 1024], "yacc")]
                tail(layer, tiles, c, ogT, R_G, 8, Wout, ybufs)

            for g in groups:
                do_group(g)
            S.barrier()

        def layer_ret(layer):
            Win, Wout = I["ret_w_in"], I["ret_w_out"]
            S.dma(gnw[:, 0:2048], I["ret_gn"].partition_broadcast(128), writes=[gnw])
            S.dma(lgt[:, 0:8], I["ret_decay"].partition_broadcast(128), writes=[lgt])
            OP("act", lambda e: e.activation(out=lgt[:, 0:8], in_=lgt[:, 0:8], func=AF.Exp, scale=-1.0), [lgt], [lgt])
            OP("act", lambda e: e.activation(out=lgt[:, 0:8], in_=lgt[:, 0:8], func=AF.Ln, bias=1.0), [lgt], [lgt])
            OP("dve", lambda e: e.tensor_scalar(out=lgt[:, 0:8], in0=lgt[:, 0:8], scalar1=-1.0, scalar2=None, op0=ALU.mult), [lgt], [lgt])
            OP("dve", lambda e: e.tensor_scalar(out=lgt[:, 8:12], in0=lgt[:, 4:8], scalar1=-1.0, scalar2=None, op0=ALU.mult), [lgt], [lgt])
            OP("dve", lambda e: e.tensor_scalar(out=lgt[:, 12:16], in0=lgt[:, 4:8], scalar1=1024.0, scalar2=None, op0=ALU.mult), [lgt], [lgt])
            for h in range(4):
                OP("act", lambda e, h=h: e.activation(out=dsc[:, h, 0:2], in_=stx[:, 0:2], func=AF.Exp, scale=lgt[:, h:h + 1]), [stx, lgt], [dsc])
                OP("act", lambda e, h=h: e.activation(out=dsc[:, h, 2:4], in_=stx[:, 2:4], func=AF.Exp, scale=lgt[:, 4 + h:5 + h]), [stx, lgt], [dsc])
            OP("dve", lambda e: e.tensor_scalar(out=dsc[:], in0=dsc[:], scalar1=1.0 / 16, scalar2=None, op0=ALU.mult), [dsc], [dsc])

            def do_group(g):
                S.barrier()
                tiles, c, smp = g["tiles"], g["c"], g["sample"]
                nt = len(tiles)
                T = nt * 128
                hT = bview(R_H, 0, [8, T])
                ogT = bview(R_G, 0, [16, T])
                qTb = sub(0, 1024, "qT")
                kTb = sub(1024, 1024, "kT")
                vbb = sub(2048, 2048, "vb")
                atb = sub(4096, 4096, "attT")
                obb = sub(8192, 4096, "ob")
                qT = bview(qTb, 0, [2, 1024])
                kT = bview(kTb, 0, [2, 1024])
                vb = bview(vbb, 0, [8, 512])
                attT = bview(atb, 0, [8, 1024])
                gpn = fview(atb, 0, [2, 1920])
                ob = fview(obb, 0, [8, 512])
                if smp:
                    s0b = sub(12288, 1024, "S0b")
                    qdb = sub(13312, 2048, "qTd")
                    S0 = bview(s0b, 0, [2, 2, 512])
                    qTd = bview(qdb, 0, [2, 2, 1024])
                else:
                    kdb = sub(12288, 2048, "kdec")
                    kdec = bview(kdb, 0, [2, 8, 256])
                front(layer, tiles, c, lambda k, ti: (hT[:, k, ti * 128:(ti + 1) * 128], R_H))
                for h in range(4):
                    stp = strip.next()
                    for i2 in range(2):
                        S.dma(gpn[:, i2, :], I["gpn"][i2], writes=[atb])
                    OP("act", lambda e, h=h: e.activation(out=gpn[:, 0, :], in_=gpn[:, 0, :], func=AF.Exp, scale=lgt[:, h:h + 1]), [atb, lgt], [atb])
                    OP("act", lambda e, h=h: e.activation(out=gpn[:, 1, :], in_=gpn[:, 1, :], func=AF.Exp, scale=lgt[:, 4 + h:5 + h]), [atb, lgt], [atb])
                    OP("dve", lambda e: e.scalar_tensor_tensor(out=gpn[:, 0, :], in0=gpn[:, 0, :], scalar=-1.0, in1=gpn[:, 1, :],
                                                               op0=ALU.add, op1=ALU.add), [atb], [atb])
                    OP("dve", lambda e: e.tensor_tensor(out=gpn[:, 0, 896:1024], in0=gpn[:, 0, 896:1024], in1=ident[:], op=ALU.add),
                       [atb, ident], [atb])
                    OP("dve", lambda e, stp=stp: e.tensor_scalar(out=stp[:], in0=gpn[:, 0, :], scalar1=1.0 / 16, scalar2=None, op0=ALU.mult),
                       [atb], [stp])
                    for which, dstT, dstb in ((0, qT, qTb), (1, kT, kTb)):
                        w = load_w(Win, 0, which * 1024 + h * 256)
                        for ti in range(nt):
                            ps = pP_rot.next()
                            proj(hT, R_H, ti, w, 256, ps)
                            ta = tmpA.next()
                            OP("act", lambda e, ta=ta, ps=ps: e.copy(out=ta[:, 0:256], in_=ps[:, 0:256]), [ps], [ta])
                            srcb = ta
                            if smp:
                                tb = tmpB.next()
                                rope(ta, tb, rope0, 64, ti, 1)
                                srcb = tb
                            elif which == 1:
                                tt = ti % 2
                                for d in range(2):
                                    OP("dve", lambda e, ta=ta, d=d, ti=ti, tt=tt, h=h: e.tensor_scalar(
                                        out=kdec[:, d, ti, :], in0=ta[:, 0:256], scalar1=dsc[:, h, 2 * d + tt:2 * d + tt + 1], scalar2=None,
                                        op0=ALU.mult), [ta, dsc], [kdb])
                            transpose_blocks(lambda i, srcb=srcb: srcb[:, i * 128:(i + 1) * 128], srcb, 2,
                                             lambda i0, n, dstT=dstT, dstb=dstb, ti=ti: (dstT[:, i0:i0 + n, ti * 128:(ti + 1) * 128], dstb))
                    for v2 in range(2):
                        w = load_w(Win, 0, 2048 + h * 512 + v2 * 256)
                        for ti in range(nt):
                            ps = pP_rot.next()
                            proj(hT, R_H, ti, w, 256, ps)
                            OP("act", lambda e, ti=ti, ps=ps, v2=v2: e.copy(out=vb[:, ti, v2 * 256:(v2 + 1) * 256], in_=ps[:, 0:256]), [ps], [vbb])
                    if smp:
                        for d in range(2):
                            for dc in range(2):
                                sb_ = osb.next()
                                S.dma(sb_[:], I["state_ret"][d, h, dc * 128:(dc + 1) * 128, :], writes=[sb_])
                                OP("pool", lambda e, sb_=sb_, d=d, dc=dc: e.tensor_copy(out=S0[:, d, dc, :], in_=sb_[:]), [sb_], [s0b])
                            rd = xt_rot.next()
                            S.dma(rd[:], I["iota1k"], writes=[rd])
                            if d == 0:
                                OP("act", lambda e, rd=rd, h=h: e.activation(out=rd[:], in_=rd[:], func=AF.Exp, scale=lgt[:, h:h + 1],
                                                                             bias=lgt[:, h:h + 1]), [rd, lgt], [rd])
                            else:
                                OP("act", lambda e, rd=rd, h=h: e.activation(out=rd[:], in_=rd[:], func=AF.Exp, scale=lgt[:, 8 + h:9 + h],
                                                                             bias=lgt[:, 12 + h:13 + h]), [rd, lgt], [rd])
                            for dc in range(2):
                                OP("dve", lambda e, rd=rd, d=d, dc=dc: e.tensor_tensor(out=qTd[:, d, dc, :], in0=qT[:, dc, :], in1=rd[:],
                                                                                       op=ALU.mult), [qTb, rd], [qdb])
                    for (t0, ntq, sq) in g["seqs"]:
                        nq = ntq * 128
                        nqb = (nq + 511) // 512
                        N = min(nq, 512)
                        for i in range(ntq):
                            for qh in range(nqb):
                                for dc in range(2):
                                    OP("pe", lambda e, i=i, qh=qh, dc=dc, t0=t0, N=N: e.matmul(
                                        out=PB[qh][:, 0:N], lhsT=kT[:, dc, (t0 + i) * 128:(t0 + i + 1) * 128],
                                        rhs=qT[:, dc, t0 * 128 + qh * 512:t0 * 128 + qh * 512 + N], start=(dc == 0), stop=(dc == 1)),
                                       [qTb, kTb], [PB[qh]])
                                OP("dve", lambda e, i=i, qh=qh, N=N, stp=stp: e.tensor_tensor(
                                    out=attT[:, i, qh * 512:qh * 512 + N], in0=PB[qh][:, 0:N],
                                    in1=stp[:, (7 - i) * 128 + qh * 512:(7 - i) * 128 + qh * 512 + N], op=ALU.mult), [PB[qh], stp], [atb])
                        for j in range(ntq):
                            pso = pP_rot.next()
                            nmm = ntq + (4 if smp else 0)
                            cnt = 0
                            for i in range(ntq):
                                OP("pe", lambda e, i=i, j=j, t0=t0, cnt=cnt, nmm=nmm, pso=pso: e.matmul(
                                    out=pso[:, 0:512], lhsT=attT[:, i, j * 128:(j + 1) * 128], rhs=vb[:, t0 + i, :],
                                    start=(cnt == 0), stop=(cnt == nmm - 1)), [atb, vbb], [pso])
                                cnt += 1
                            if smp:
                                for d in range(2):
                                    for dc in range(2):
                                        OP("pe", lambda e, d=d, dc=dc, j=j, cnt=cnt, nmm=nmm, pso=pso: e.matmul(
                                            out=pso[:, 0:512], lhsT=qTd[:, d, dc, j * 128:(j + 1) * 128], rhs=S0[:, d, dc, :],
                                            start=(cnt == 0), stop=(cnt == nmm - 1)), [qdb, s0b], [pso])
                                        cnt += 1
                            s = st1.next()
                            ta = tmpA.next()
                            OP("dve", lambda e, s=s, pso=pso: e.tensor_reduce(out=s[:, 0:1], in_=pso[:, 0:512], axis=AX.X, op=ALU.add), [pso], [s])
                            OP("dve", lambda e, s=s: e.tensor_scalar(out=s[:, 1:2], in0=s[:, 0:1], scalar1=-1.0 / 512, scalar2=None, op0=ALU.mult), [s], [s])
                            OP("act", lambda e, s=s, ta=ta, pso=pso: e.activation(out=ta[:, 0:512], in_=pso[:, 0:512], func=AF.Identity, bias=s[:, 1:2]),
                               [pso, s], [ta])
                            tb = tmpB.next()
                            OP("act", lambda e, s=s, ta=ta, tb=tb: e.activation(out=tb[:, 0:512], in_=ta[:, 0:512], func=AF.Square, accum_out=s[:, 2:3]),
                               [ta], [tb, s])
                            OP("act", lambda e, s=s: e.activation(out=s[:, 3:4], in_=s[:, 2:3], func=AF.Sqrt, scale=1.0 / 512, bias=GN_EPS), [s], [s])
                            OP("dve", lambda e, s=s: e.reciprocal(out=s[:, 4:5], in_=s[:, 3:4]), [s], [s])
                            OP("dve", lambda e, s=s, ta=ta, j=j, t0=t0, h=h: e.scalar_tensor_tensor(
                                out=ob[:, t0 + j, :], in0=ta[:, 0:512], scalar=s[:, 4:5], in1=gnw[:, h * 512:(h + 1) * 512],
                                op0=ALU.mult, op1=ALU.mult), [ta, s, gnw], [obb])
                        if not smp:
                            for d in range(2):
                                for dc in range(2):
                                    pst = pP_rot.next()
                                    for tt in range(2):
                                        OP("pe", lambda e, d=d, dc=dc, tt=tt, t0=t0, pst=pst: e.matmul(
                                            out=pst[:, 0:512], lhsT=kdec[:, d, t0 + tt, dc * 128:(dc + 1) * 128], rhs=vb[:, t0 + tt, :],
                                            start=(tt == 0), stop=(tt == 1)), [kdb, vbb], [pst])
                                    sb_ = osb.next()
                                    OP("act", lambda e, sb_=sb_, pst=pst: e.copy(out=sb_[:], in_=pst[:, 0:512]), [pst], [sb_])
                                    S.dma(O["st_ret"][sq, d, h, dc * 128:(dc + 1) * 128, :], sb_[:], reads=[sb_], queue="act")
                    for g2 in range(2):
                        w = load_w(Win, 0, 4096 + h * 512 + g2 * 256)
                        for ti in range(nt):
                            ps = pP_rot.next()
                            proj(hT, R_H, ti, w, 256, ps)
                            ta = tmpA.next()
                            OP("act", lambda e, ta=ta, ps=ps: e.activation(out=ta[:, 0:256], in_=ps[:, 0:256], func=AF.Silu), [ps], [ta])
                            OP("dve", lambda e, ta=ta, ti=ti, g2=g2: e.tensor_tensor(out=ob[:, ti, g2 * 256:(g2 + 1) * 256],
                                                                                     in0=ob[:, ti, g2 * 256:(g2 + 1) * 256], in1=ta[:, 0:256],
                                                                                     op=ALU.mult), [obb, ta], [obb])
                    for ti in range(nt):
                        transpose_blocks(lambda i, ti=ti: ob[:, ti, i * 128:(i + 1) * 128], obb, 4,
                                         lambda i0, n, ti=ti, h=h: (ogT[:, 4 * h + i0:4 * h + i0 + n, ti * 128:(ti + 1) * 128], R_G))
                S.barrier()
                ybufs = [Buf(ATT[:, 0:nt * 1024], "yacc")]
                tail(layer, tiles, c, ogT, R_G, 16, Wout, ybufs)

            for g in groups:
                do_group(g)
            S.barrier()

        RKVG = [dscr(f"rkvg{n}", (2048, 1024)) for n in range(4)]
        RKVGB = [[Buf(RKVG[n][t * 128:(t + 1) * 128, :], f"rkvg{n}_{t}") for t in range(16)] for n in range(4)]
        dscrb = lambda name, shape: nc.dram_tensor(name, list(shape), BF16).ap()
        TMP = dscrb("tmp_", (128, 256, 3, 64))
        TMS = dscrb("tms_", (32, 1024, 3, 64))
        FMP = dscrb("fmp_", (128, 4, 64, 256))
        FMS = dscrb("fms_", (32, 4, 64, 1024))
        GMP = dscr("gmp_", (128, 4, 64))
        GMS = dscr("gms_", (32, 16, 64))
        GMPB, GMSB = Buf(GMP, "gmp"), Buf(GMS, "gms")
        cmask2 = S.sb([128, 3, 128], F32, "cmask2")
        YPD = dscr("ypd", (128, 256, 64))
        YSD = dscr("ysd", (32, 1024, 64))
        TMPB, TMSB, FMPB, FMSB = Buf(TMP, "tmp"), Buf(TMS, "tms"), Buf(FMP, "fmp"), Buf(FMS, "fms")
        YPDB = Buf(YPD, "ypd")
        YSDB = Buf(YSD, "ysd")
        tri = S.sb([128, 128], F32, "tri")

        def layer_rwkv(layer):
            Win, Wout = I["rwkv_w_in"], I["rwkv_w_out"]
            S.barrier()
            for n in range(6):
                S.dma(muF[:, n, :], I["rwkv_mu"][n].rearrange("(k p) -> p k", p=128), writes=[muF], allow_slow_non_contiguous=True)
            smallb = bsub(24576, 3072, "rwsmall")
            wAb = bview(smallb, 0, [2, 8, 64])
            aAb = bview(smallb, 512, [2, 8, 64])
            wBb = bview(smallb, 1024, [2, 1024])
            aBb = bview(smallb, 2048, [2, 1024])
            for d in range(2):
                for src, dst in ((I["rwkv_wA"], wAb), (I["rwkv_aA"], aAb)):
                    f = wf.next()
                    S.dma(f[:, :, 0:64], src[d].rearrange("(k p) r -> p k r", p=128), writes=[f])
                    OP("pool", lambda e, f=f, dst=dst, d=d: e.tensor_copy(out=dst[:, d, :, :], in_=f[:, :, 0:64]), [f], [smallb])
                for src, dst in ((I["rwkv_wB"], wBb), (I["rwkv_aB"], aBb)):
                    f = wf.next()
                    fv = f.t[0:64].rearrange("p k n -> p (k n)")[:, 0:1024]
                    S.dma(fv, src[d], writes=[f])
                    OP("pool", lambda e, fv=fv, dst=dst, d=d: e.tensor_copy(out=dst[0:64, d, :], in_=fv), [f], [smallb])
            lorab = bsub(0, 4096, "lora")
            LWT = bview(lorab, 0, [2, 2048])
            LAT = bview(lorab, 2048, [2, 2048])
            OP("dve", lambda e: e.memset(bsum[:], 0.0), [], [bsum])

            def a1_group(g, gi):
                tiles, c, smp = g["tiles"], g["c"], g["sample"]
                nt = len(tiles)
                T = nt * 128
                tok0 = tiles[0] * 128
                hTb = bsub(4096, 8192, "hTf")
                xxb = bsub(12288, 8192, "xxT")
                xnb = bsub(20480, 4096, "xnT")
                hT = fview(hTb, 0, [8, T])
                xx = fview(xxb, 0, [8, T])
                xn = bview(xnb, 0, [8, T])
                front(layer, tiles, c, lambda k, ti: (hT[:, k, ti * 128:(ti + 1) * 128], hTb))
                for (t0, ntq, sq) in g["seqs"]:
                    o = t0 * 128
                    L = ntq * 128
                    OP("dve", lambda e, o=o, L=L: e.tensor_tensor(out=xx[:, :, o + 1:o + L - 1], in0=hT[:, :, o:o + L - 2],
                                                                 in1=hT[:, :, o + 2:o + L], op=ALU.add), [hTb], [xxb])
                    OP("dve", lambda e, o=o, L=L: e.scalar_tensor_tensor(out=xx[:, :, o + 1:o + L - 1], in0=xx[:, :, o + 1:o + L - 1],
                                                                        scalar=0.5, in1=hT[:, :, o + 1:o + L - 1],
                                                                        op0=ALU.mult, op1=ALU.subtract), [xxb, hTb], [xxb])
                    OP("dve", lambda e, o=o: e.scalar_tensor_tensor(out=xx[:, :, o:o + 1], in0=hT[:, :, o + 1:o + 2], scalar=0.5,
                                                                    in1=hT[:, :, o:o + 1], op0=ALU.mult, op1=ALU.subtract), [hTb], [xxb])
                    OP("dve", lambda e, o=o, L=L: e.scalar_tensor_tensor(out=xx[:, :, o + L - 1:o + L], in0=hT[:, :, o + L - 2:o + L - 1],
                                                                        scalar=0.5, in1=hT[:, :, o + L - 1:o + L],
                                                                        op0=ALU.mult, op1=ALU.subtract), [hTb], [xxb])

                def mix(n):
                    for k in range(8):
                        OP("dve", lambda e, k=k, n=n: e.scalar_tensor_tensor(out=xn[:, k, :], in0=xx[:, k, :], scalar=muF[:, n, k:k + 1],
                                                                             in1=hT[:, k, :], op0=ALU.mult, op1=ALU.add),
                           [xxb, hTb, muF], [xnb])
                for pi, n in enumerate((0, 2, 3, 5)):
                    mix(n)
                    for cb in range(4):
                        w = load_w(Win, 0, pi * 1024 + cb * 256)
                        for ti in range(nt):
                            ps = pP_rot.next()
                            proj(xn, xnb, ti, w, 256, ps)
                            ta = tmpA.next()
                            OP("act", lambda e, ta=ta, ps=ps: e.copy(out=ta[:, 0:256], in_=ps[:, 0:256]), [ps], [ta])
                            t = tiles[ti]
                            S.dma(RKVG[pi][t * 128:(t + 1) * 128, cb * 256:(cb + 1) * 256], ta[:, 0:256], reads=[ta],
                                  writes=[RKVGB[pi][t]], owner=ta, queue="act")
                for n, Ab, LT, fn in ((1, wAb, LWT, AF.Tanh), (4, aAb, LAT, AF.Copy)):
                    mix(n)
                    for d in range(2):
                        for c0 in range(0, T, 512):
                            ps = pP_rot.next()
                            for k in range(8):
                                OP("pe", lambda e, k=k, d=d, c0=c0, Ab=Ab, ps=ps: e.matmul(out=ps[0:64, 0:512], lhsT=Ab[:, d, k, :],
                                                                                          rhs=xn[:, k, c0:c0 + 512], start=(k == 0), stop=(k == 7)),
                                   [smallb, xnb], [ps])
                            if fn == AF.Tanh:
                                OP("act", lambda e, d=d, c0=c0, LT=LT, ps=ps: e.activation(out=LT[0:64, d, tok0 + c0:tok0 + c0 + 512],
                                                                                        in_=ps[0:64, 0:512], func=AF.Tanh), [ps], [lorab])
                            else:
                                OP("act", lambda e, d=d, c0=c0, LT=LT, ps=ps: e.copy(out=LT[0:64, d, tok0 + c0:tok0 + c0 + 512],
                                                                                  in_=ps[0:64, 0:512]), [ps], [lorab])

            for gi, g in enumerate(groups):
                a1_group(g, gi)
            S.barrier()

            tabb = bsub(4096, 8192, "tabs")
            TAB = fview(tabb, 0, [8, 1024])
            for i2, src in enumerate((I["rwkv_w0"][0], I["rwkv_w0"][1], I["rwkv_a0"][0], I["rwkv_a0"][1], I["rwkv_kk"],
                                      I["rwkv_ka"], I["rwkv_rk"], I["rwkv_gn"])):
                S.dma(TAB[:, i2, :], src.partition_broadcast(128), writes=[tabb])
            slot = [bsub(12288 + i2 * 1024, 1024, f"slot{i2}") for i2 in range(12)]
            xtb = [Buf(b_.t[:, :], b_.name + "_a2") for b_ in (xt_rot.bufs + xn_rot.bufs)]
            Rb, Kb, Vb, KKb, LWb, ABb, KDb, T1b, FLWb = slot[0:9]
            Frot = Rot([slot[9], slot[10]])
            Hrot = Rot([slot[11], xtb[0]])
            FMrot = Rot([xtb[1], xtb[2]])
            h16 = lambda b: b.t.rearrange("p (h d) -> p h d", h=16)
            S.dma(tri[:], I["tri"], writes=[tri])
            for e2 in range(2):
                S.dma(cmask2[e2 * 64:(e2 + 1) * 64, :, :], I["cmask"], writes=[cmask2])

            def flip(srcb, dstb):
                for hf in range(2):
                    ps = pP_rot.next()
                    OP("pe", lambda e, hf=hf, ps=ps: e.matmul(out=ps[:, :], lhsT=Jm[:], rhs=srcb.t[:, hf * 512:(hf + 1) * 512],
                                                             start=True, stop=True), [Jm, srcb], [ps])
                    OP("act", lambda e, hf=hf, ps=ps: e.copy(out=dstb.t[:, hf * 512:(hf + 1) * 512], in_=ps[:, :]), [ps], [dstb])

            def a2_tile(t):
                smp = t >= 8
                tt_in_seq = (t - 8) if smp else (t % 2)
                for pi, b in ((0, Rb), (1, Kb), (2, Vb)):
                    S.dma(b.t, RKVG[pi][t * 128:(t + 1) * 128, :], reads=[RKVGB[pi][t]], writes=[b])
                OP("dve", lambda e: e.tensor_tensor(out=KKb.t, in0=Kb.t, in1=TAB[:, 4, :], op=ALU.mult), [Kb, tabb], [KKb])
                OP("pool", lambda e: e.tensor_tensor(out=T1b.t, in0=KKb.t, in1=KKb.t, op=ALU.mult), [KKb], [T1b])
                nrm = tmpB.next()
                OP("dve", lambda e, nrm=nrm: e.tensor_reduce(out=nrm[:, 0:16], in_=h16(T1b), axis=AX.X, op=ALU.add), [T1b], [nrm])
                OP("dve", lambda e, nrm=nrm: e.tensor_scalar(out=nrm[:, 0:16], in0=nrm[:, 0:16], scalar1=1e-12, scalar2=None, op0=ALU.max), [nrm], [nrm])
                OP("act", lambda e, nrm=nrm: e.activation(out=nrm[:, 16:32], in_=nrm[:, 0:16], func=AF.Sqrt), [nrm], [nrm])
                OP("dve", lambda e, nrm=nrm: e.reciprocal(out=nrm[:, 32:48], in_=nrm[:, 16:32]), [nrm], [nrm])
                OP("dve", lambda e, nrm=nrm: e.tensor_tensor(out=h16(KKb), in0=h16(KKb), in1=nrm[:, 32:48].unsqueeze(2).to_broadcast([128, 16, 64]),
                                                             op=ALU.mult), [KKb, nrm], [KKb])
                for d in range(2):
                    a2_dir(t, d, smp, tt_in_seq)

            def a2_dir(t, d, smp, tt_in_seq):
                if True:
                    for (LT, Bw, tabi, dstb, post) in ((LWT, wBb, d, LWb, "w"), (LAT, aBb, 2 + d, ABb, "a")):
                        for hf in range(2):
                            ps = pP_rot.next()
                            OP("pe", lambda e, hf=hf, d=d, LT=LT, Bw=Bw, ps=ps: e.matmul(
                                out=ps[:, :], lhsT=LT[0:64, d, t * 128:(t + 1) * 128], rhs=Bw[0:64, d, hf * 512:(hf + 1) * 512],
                                start=True, stop=True), [lorab, smallb], [ps])
                            OP("dve", lambda e, hf=hf, tabi=tabi, dstb=dstb, ps=ps: e.tensor_tensor(
                                out=dstb.t[:, hf * 512:(hf + 1) * 512], in0=ps[:, :], in1=TAB[:, tabi, hf * 512:(hf + 1) * 512], op=ALU.add),
                               [ps, tabb], [dstb])
                        OP("act", lambda e, dstb=dstb: e.activation(out=dstb.t, in_=dstb.t, func=AF.Sigmoid), [dstb], [dstb])
                        if post == "w":
                            OP("act", lambda e, dstb=dstb: e.activation(out=dstb.t, in_=dstb.t, func=AF.Copy, scale=-math.exp(-0.5)), [dstb], [dstb])
                    OP("dve", lambda e: e.scalar_tensor_tensor(out=T1b.t, in0=ABb.t, scalar=-1.0, in1=TAB[:, 5, :], op0=ALU.add, op1=ALU.mult),
                       [ABb, tabb], [T1b])
                    OP("dve", lambda e: e.scalar_tensor_tensor(out=KDb.t, in0=T1b.t, scalar=1.0, in1=Kb.t, op0=ALU.add, op1=ALU.mult),
                       [T1b, Kb], [KDb])
                    OP("pool", lambda e: e.tensor_tensor(out=ABb.t, in0=KKb.t, in1=ABb.t, op=ALU.mult), [KKb, ABb], [ABb])
                    OP("pool", lambda e: e.tensor_tensor(out=T1b.t, in0=Rb.t, in1=KDb.t, op=ALU.mult), [Rb, KDb], [T1b])
                    OP("pool", lambda e: e.tensor_tensor(out=T1b.t, in0=T1b.t, in1=TAB[:, 6, :], op=ALU.mult), [T1b, tabb], [T1b])
                    nb = tmpB.next()
                    OP("dve", lambda e, nb=nb: e.tensor_reduce(out=nb[:, 0:16], in_=h16(T1b), axis=AX.X, op=ALU.add), [T1b], [nb])
                    OP("dve", lambda e, nb=nb: e.tensor_tensor(out=bsum[:, t, :], in0=bsum[:, t, :], in1=nb[:, 0:16], op=ALU.add), [bsum, nb], [bsum])
                    if smp:
                        L, TMD, FMD, TMB_, FMB_, GMD, GMB_ = 1024, TMS, FMS, TMSB, FMSB, GMS, GMSB
                        ch0 = d * 16
                    else:
                        L, TMD, FMD, TMB_, FMB_, GMD, GMB_ = 256, TMP, FMP, TMPB, FMPB, GMP, GMPB
                        ch0 = (d * 4 + t // 2) * 16
                    tk0 = tt_in_seq * 128
                    s0 = tk0 if d == 0 else L - 128 - tk0

                    def chain_order(srcb):
                        if d == 0:
                            return srcb
                        f = Frot.next()
                        flip(srcb, f)
                        return f
                    if d == 0:
                        lwc = LWb
                    else:
                        flip(LWb, FLWb)
                        lwc = FLWb
                    cps = [PB[0], PB[1]]
                    for hf in range(2):
                        OP("pe", lambda e, hf=hf: e.matmul(out=cps[hf][:, :], lhsT=tri[:], rhs=lwc.t[:, hf * 512:(hf + 1) * 512], start=True, stop=True),
                           [tri, lwc], [cps[hf]])

                    def store_tm(hb_, vi):
                        tmb = strip.next()
                        OP("act", lambda e: e.copy(out=tmb[:, 0:1024], in_=hb_.t), [hb_], [tmb])
                        S.dma(TMD[ch0:ch0 + 16, s0:s0 + 128, vi, :].rearrange("h t j -> t h j"), tmb[:, 0:1024].rearrange("p (h d) -> p h d", h=16),
                              reads=[tmb], writes=[TMB_], owner=tmb, queue="act")

                    def store_fm(hb_, vi):
                        fm = FMrot.next()
                        fmv = fm.t[:, 0:512].bitcast(BF16).rearrange("p (a b) -> p a b", a=8)
                        transpose_blocks(lambda i: hb_.t[:, i * 128:(i + 1) * 128], hb_, 8, lambda i0, n: (fmv[:, i0:i0 + n, :], fm), evac="act")
                        for e2 in range(2):
                            S.dma(FMD[ch0 + e2:ch0 + 16:2, vi, :, s0:s0 + 128].rearrange("c k t -> k c t"), fmv[e2 * 64:(e2 + 1) * 64, :, :],
                                  reads=[fm], writes=[FMB_], owner=fm, queue="act")

                    def hat(srcb, kind):
                        hb_ = Hrot.next()
                        for hf in range(2):
                            sl = slice(hf * 512, (hf + 1) * 512)
                            if kind == "prev":
                                OP("dve", lambda e, hf=hf, sl=sl: e.tensor_tensor(out=T1b.t[:, sl], in0=cps[hf][:, :], in1=lwc.t[:, sl], op=ALU.subtract),
                                   [cps[hf], lwc], [T1b])
                                OP("act", lambda e, sl=sl: e.activation(out=T1b.t[:, sl], in_=T1b.t[:, sl], func=AF.Exp), [T1b], [T1b])
                            elif kind == "cur":
                                OP("act", lambda e, hf=hf, sl=sl: e.activation(out=T1b.t[:, sl], in_=cps[hf][:, :], func=AF.Exp), [cps[hf]], [T1b])
                            elif kind == "inv":
                                OP("act", lambda e, hf=hf, sl=sl: e.activation(out=T1b.t[:, sl], in_=cps[hf][:, :], func=AF.Exp, scale=-1.0), [cps[hf]], [T1b])
                        if srcb is None:
                            OP("pool", lambda e: e.tensor_copy(out=hb_.t, in_=T1b.t), [T1b], [hb_])
                        else:
                            OP("pool", lambda e: e.tensor_tensor(out=hb_.t, in0=srcb.t, in1=T1b.t, op=ALU.mult), [srcb, T1b], [hb_])
                        return hb_

                    hb_ = hat(chain_order(KKb), "prev")
                    store_fm(hb_, 0)
                    hb_ = hat(chain_order(Rb), "cur")
                    store_fm(hb_, 1)
                    for cc in range(2):
                        cidx = s0 // 64 + cc
                        S.dma(GMD[ch0:ch0 + 16, cidx:cidx + 1, :].rearrange("h n k -> n h k"),
                              T1b.t[63 + 64 * cc:64 + 64 * cc, :].rearrange("p (h k) -> p h k", h=16), reads=[T1b], writes=[GMB_], owner=T1b, queue="act")
                    hb_ = hat(chain_order(ABb), "inv")
                    store_fm(hb_, 2)
                    store_tm(hb_, 0)
                    hb_ = hat(chain_order(KDb), "inv")
                    store_fm(hb_, 3)
                    store_tm(hb_, 1)
                    store_tm(chain_order(Vb), 2)

            for t in range(16):
                a2_tile(t)
            S.barrier()
            CK("rwkv_a")

            NU = 20
            o = 0

            def carve(size, name):
                nonlocal o
                b = bsub(o, size, name)
                o += size
                return b
            Tst = [carve(256, f"T{u}") for u in range(NU)]
            Tbs = [carve(128, f"Tb{u}") for u in range(NU)]
            NW = 4
            bsets = []
            for w_ in range(NW):
                bsets.append(dict(
                    tm=carve(384, f"tm{w_}"), fm=carve(512, f"fm{w_}"), gm=carve(4, f"gm{w_}"), ka=carve(256, f"ka{w_}"), apb=carve(128, f"apb{w_}"),
                    qz=[carve(512, f"qz{w_}a"), carve(512, f"qz{w_}b")], qt=[carve(256, f"qt{w_}a"), carve(256, f"qt{w_}b")],
                    bdq=carve(512, f"bdq{w_}"), bdt=carve(512, f"bdt{w_}"), zb=carve(128, f"zb{w_}"), u=carve(128, f"u{w_}"),
                    p=carve(128, f"p{w_}"), y=carve(256, f"y{w_}")))
            pB_rot = Rot(PB)
            for bs_ in bsets:
                for b in (bs_["bdq"], bs_["bdt"]):
                    OP("pool", lambda e, b=b: e.memset(b.t, 0.0), [], [b])
            f3 = lambda b, x: b.t.rearrange("p (c x) -> p c x", c=4)
            b3 = lambda b, n: b.t[:, 0:n // 2].bitcast(BF16).rearrange("p (c x) -> p c x", c=4)
            PH = [slice(0, 64), slice(64, 128)]
            ev_rot = Rot(["dve", "act", "pool", "dve", "act"])

            def to_bd(srcv, srcb, bd):
                bdv = f3(bd, 0)
                for e2 in range(2):
                    eng = ev_rot.next()
                    if eng == "act":
                        OP("act", lambda e, e2=e2: e.copy(out=bdv[PH[e2], :, e2 * 64:(e2 + 1) * 64], in_=srcv[PH[e2], :, :]), [srcb], [bd])
                    else:
                        OP(eng, lambda e, e2=e2: e.tensor_copy(out=bdv[PH[e2], :, e2 * 64:(e2 + 1) * 64], in_=srcv[PH[e2], :, :]), [srcb], [bd])

            units = []
            for d in range(2):
                for hh in range(2):
                    units.append(dict(smp=True, ch0=d * 16 + hh * 8, d=d, h0=hh * 8, nch=16))
            for d in range(2):
                for sq in range(4):
                    for hh in range(2):
                        units.append(dict(smp=False, ch0=(d * 4 + sq) * 16 + hh * 8, d=d, sq=sq, h0=hh * 8, nch=4))
            for ui, u in enumerate(units):
                Tb, Tbb = Tst[ui], Tbs[ui]
                if not u["smp"]:
                    OP("pool", lambda e, Tb=Tb: e.memset(Tb.t, 0.0), [], [Tb])
                else:
                    st_ = bsets[ui % NW]["qz"][0]
                    sv = st_.t[:, 0:256].rearrange("p (c x) -> p c x", c=4)
                    for e2 in range(2):
                        h0 = u["h0"] + 4 * e2
                        S.dma(sv[PH[e2], :, :], I["state_rwkv"][u["d"], h0:h0 + 4].rearrange("h v k -> v h k"), writes=[st_])
                    ps = pP_rot.next()
                    for e2 in range(2):
                        for p in range(4):
                            OP("pe", lambda e, e2=e2, p=p, ps=ps, sv=sv: e.matmul(out=ps[PH[e2], p * 64:(p + 1) * 64], lhsT=sv[PH[e2], p, :],
                                                                               rhs=ident[PH[e2], e2 * 64:(e2 + 1) * 64], start=True, stop=True),
                               [st_, ident], [ps])
                    OP("act", lambda e, Tb=Tb, ps=ps: e.copy(out=Tb.t, in_=ps[:, 0:256]), [ps], [Tb])
                OP("dve", lambda e, Tb=Tb, Tbb=Tbb: e.tensor_copy(out=Tbb.t[:, 0:128].bitcast(BF16), in_=Tb.t), [Tb], [Tbb])

            def unit_chunk(ui, u, n, bs):
                smp, ch0 = u["smp"], u["ch0"]
                TMD, FMD, GMD, TMB_, FMB_, GMB_, YD, YDB_ = ((TMS, FMS, GMS, TMSB, FMSB, GMSB, YSD, YSDB) if smp else
                                                             (TMP, FMP, GMP, TMPB, FMPB, GMPB, YPD, YPDB))
                tm, fm, gm = bs["tm"], bs["fm"], bs["gm"]
                tmv = tm.t.bitcast(BF16).rearrange("p (c v j) -> p c v j", c=4, v=3)
                fmv = fm.t.bitcast(BF16).rearrange("p (c v s) -> p c v s", c=4, v=4)
                for e2 in range(2):
                    c0 = ch0 + 4 * e2
                    S.dma(tm.t.bitcast(BF16).rearrange("p (c x) -> p c x", c=4)[PH[e2], :, :],
                          TMD[c0:c0 + 4, n * 64:(n + 1) * 64, :, :].rearrange("c s v j -> s c (v j)"), reads=[TMB_], writes=[tm])
                    S.dma(fm.t.bitcast(BF16).rearrange("p (cv s) -> p cv s", s=64)[PH[e2], :, :],
                          FMD[c0:c0 + 4, :, :, n * 64:(n + 1) * 64].rearrange("c v k s -> k (c v) s"), reads=[FMB_], writes=[fm])
                    S.dma(gm.t[PH[e2], :], GMD[c0:c0 + 4, n, :].rearrange("c k -> k c"), reads=[GMB_], writes=[gm], allow_slow_non_contiguous=True)
                Tb, Tbb = Tst[ui], Tbs[ui]
                Tv = f3(Tb, 0)
                Tbv = b3(Tbb, 256)
                ka, apb = bs["ka"], bs["apb"]
                kav = b3(ka, 512)
                apbv = b3(apb, 256)
                qzi, qti = 0, 0
                qz = bs["qz"][0]
                qzv = f3(qz, 0)
                qt = bs["qt"][0]
                qtv = f3(qt, 0)
                psB, psK, psL = pB_rot.next(), pB_rot.next(), pB_rot.next()
                for e2 in range(2):
                    for p in range(4):
                        OP("pe", lambda e, e2=e2, p=p: e.matmul(out=psB[PH[e2], p * 128:(p + 1) * 128], lhsT=fmv[PH[e2], p, 2, :],
                                                               rhs=fmv[PH[e2], p, 0:2, :], start=True, stop=True), [fm], [psB])
                        OP("pe", lambda e, e2=e2, p=p: e.matmul(out=psK[PH[e2], p * 128:(p + 1) * 128], lhsT=fmv[PH[e2], p, 3, :],
                                                               rhs=fmv[PH[e2], p, 0:2, :], start=True, stop=True), [fm], [psK])
                        OP("pe", lambda e, e2=e2, p=p: e.matmul(out=psL[PH[e2], p * 64:(p + 1) * 64], lhsT=fmv[PH[e2], p, 0, :],
                                                               rhs=fmv[PH[e2], p, 2, :], start=True, stop=True), [fm], [psL])
                psBv = psB[:, :].rearrange("p (c x) -> p c x", c=4)
                OP("dve", lambda e, qzv=qzv: e.tensor_tensor(out=qzv[:, :, 64:128], in0=psBv[:, :, 0:64], in1=cmask2[:, 0, 0:64].unsqueeze(1).to_broadcast([128, 4, 64]),
                                                    op=ALU.mult), [psB, cmask2], [qz])
                OP("dve", lambda e: e.tensor_tensor(out=apbv, in0=psBv[:, :, 64:128], in1=cmask2[:, 0, 64:128].unsqueeze(1).to_broadcast([128, 4, 64]),
                                                    op=ALU.mult), [psB, cmask2], [apb])
                OP("dve", lambda e: e.tensor_tensor(out=kav, in0=psK[:, :].rearrange("p (c x) -> p c x", c=4),
                                                    in1=cmask2[:, 1, :].unsqueeze(1).to_broadcast([128, 4, 128]), op=ALU.mult), [psK, cmask2], [ka])
                OP("dve", lambda e, qtv=qtv: e.tensor_tensor(out=qtv, in0=psL[:, 0:256].rearrange("p (c x) -> p c x", c=4),
                                                    in1=cmask2[:, 2, 0:64].unsqueeze(1).to_broadcast([128, 4, 64]), op=ALU.mult), [psL, cmask2], [qt])
                OP("pool", lambda e, qzv=qzv: e.tensor_tensor(out=qzv[:, :, 0:64], in0=qzv[:, :, 64:128],
                                                     in1=ident[:, :].rearrange("p (a b) -> p a b", a=2)[:, 0, :].unsqueeze(1).to_broadcast([128, 4, 64])
                                                     if False else identst[:, :].unsqueeze(1).to_broadcast([128, 4, 64]), op=ALU.add), [qz, identst], [qz])
                yield
                ps1, ps2 = pB_rot.next(), pB_rot.next()
                for e2 in range(2):
                    for p in range(4):
                        OP("pe", lambda e, e2=e2, p=p, qtv=qtv, qzv=qzv: e.matmul(out=ps1[PH[e2], p * 64:(p + 1) * 64], lhsT=qtv[PH[e2], p, :],
                                                                              rhs=qzv[PH[e2], p, 64:128], start=True, stop=True), [qt, qz], [ps1])
                        OP("pe", lambda e, e2=e2, p=p, qtv=qtv, qzv=qzv: e.matmul(out=ps2[PH[e2], p * 64:(p + 1) * 64], lhsT=qzv[PH[e2], p, 64:128],
                                                                              rhs=qtv[PH[e2], p, :], start=True, stop=True), [qz, qt], [ps2])
                qz1 = bs["qz"][1]
                qz1v = f3(qz1, 0)
                OP("act", lambda e, qz1v=qz1v: e.copy(out=qz1v[:, :, 64:128], in_=ps1[:, 0:256].rearrange("p (c x) -> p c x", c=4)), [ps1], [qz1])
                OP("pool", lambda e, qz1v=qz1v, qzv=qzv: e.tensor_copy(out=qz1v[:, :, 0:64], in_=qzv[:, :, 0:64]), [qz], [qz1])
                qt1 = bs["qt"][1]
                qt1v = f3(qt1, 0)
                OP("dve", lambda e, qt1v=qt1v: e.tensor_copy(out=qt1v, in_=ps2[:, 0:256].rearrange("p (c x) -> p c x", c=4)), [ps2], [qt1])
                qz, qzv, qt, qtv = qz1, qz1v, qt1, qt1v
                qzi, qti = 1, 1
                yield
                for lvl in range(1, 6):
                    N = 128 if lvl < 5 else 64
                    psA = pB_rot.next()
                    for e2 in range(2):
                        for p in range(4):
                            OP("pe", lambda e, e2=e2, p=p, qtv=qtv, qzv=qzv, N=N, psA=psA: e.matmul(out=psA[PH[e2], p * 128:p * 128 + N], lhsT=qtv[PH[e2], p, :],
                                                                                                 rhs=qzv[PH[e2], p, 0:N], start=True, stop=True), [qt, qz], [psA])
                    if lvl < 5:
                        psC = pB_rot.next()
                        for e2 in range(2):
                            for p in range(4):
                                OP("pe", lambda e, e2=e2, p=p, qtv=qtv, qzv=qzv, psC=psC: e.matmul(out=psC[PH[e2], p * 64:(p + 1) * 64], lhsT=qzv[PH[e2], p, 64:128],
                                                                                                rhs=qtv[PH[e2], p, :], start=True, stop=True), [qz, qt], [psC])
                    psAv = psA[:, :].rearrange("p (c x) -> p c x", c=4)
                    if lvl < 5:
                        qzi ^= 1
                        qz_new = bs["qz"][qzi]
                        qznv = f3(qz_new, 0)
                        OP("dve", lambda e, qznv=qznv, qzv=qzv, psAv=psAv: e.tensor_tensor(out=qznv[:, :, 0:64], in0=psAv[:, :, 0:64], in1=qzv[:, :, 0:64],
                                                                                         op=ALU.add), [psA, qz], [qz_new])
                        OP("act", lambda e, qznv=qznv, psAv=psAv: e.copy(out=qznv[:, :, 64:128], in_=psAv[:, :, 64:128]), [psA], [qz_new])
                        qti ^= 1
                        qt_new = bs["qt"][qti]
                        qtnv = f3(qt_new, 0)
                        OP("dve", lambda e, qtnv=qtnv, psC=psC: e.tensor_copy(out=qtnv, in_=psC[:, 0:256].rearrange("p (c x) -> p c x", c=4)), [psC], [qt_new])
                        qz, qzv, qt, qtv = qz_new, qznv, qt_new, qtnv
                        yield
                    else:
                        zb = bs["zb"]
                        zbv = b3(zb, 256)
                        OP("dve", lambda e, zbv=zbv, qzv=qzv, psAv=psAv: e.tensor_tensor(out=zbv, in0=psAv[:, :, 0:64], in1=qzv[:, :, 0:64], op=ALU.add),
                           [psA, qz], [zb])
                ps = pB_rot.next()
                for e2 in range(2):
                    for p in range(4):
                        OP("pe", lambda e, e2=e2, p=p, ps=ps: e.matmul(out=ps[PH[e2], p * 64:(p + 1) * 64], lhsT=fmv[PH[e2], p, 0, :], rhs=Tbv[PH[e2], p, :],
                                                                      start=True, stop=False), [fm, Tbb], [ps])
                        OP("pe", lambda e, e2=e2, p=p, ps=ps: e.matmul(out=ps[PH[e2], p * 64:(p + 1) * 64], lhsT=kav[PH[e2], p, 0:64], rhs=tmv[PH[e2], p, 2, :],
                                                                      start=False, stop=True), [ka, tm], [ps])
                ub = bs["u"]
                ubv = b3(ub, 256)
                OP("act", lambda e, ps=ps: e.copy(out=ubv, in_=ps[:, 0:256].rearrange("p (c x) -> p c x", c=4)), [ps], [ub])
                yield
                ps = pB_rot.next()
                for e2 in range(2):
                    for p in range(4):
                        OP("pe", lambda e, e2=e2, p=p, ps=ps: e.matmul(out=ps[PH[e2], p * 64:(p + 1) * 64], lhsT=zbv[PH[e2], p, :], rhs=ubv[PH[e2], p, :],
                                                                      start=True, stop=True), [zb, ub], [ps])
                pb_ = bs["p"]
                pbv = b3(pb_, 256)
                OP("dve", lambda e, ps=ps: e.tensor_scalar(out=pbv, in0=ps[:, 0:256].rearrange("p (c x) -> p c x", c=4), scalar1=-1.0, scalar2=None, op0=ALU.mult),
                   [ps], [pb_])
                yield
                ps = pB_rot.next()
                for e2 in range(2):
                    for p in range(4):
                        OP("pe", lambda e, e2=e2, p=p, ps=ps: e.matmul(out=ps[PH[e2], p * 64:(p + 1) * 64], lhsT=fmv[PH[e2], p, 1, :], rhs=Tbv[PH[e2], p, :],
                                                                      start=True, stop=False), [fm, Tbb], [ps])
                        OP("pe", lambda e, e2=e2, p=p, ps=ps: e.matmul(out=ps[PH[e2], p * 64:(p + 1) * 64], lhsT=apbv[PH[e2], p, :], rhs=pbv[PH[e2], p, :],
                                                                      start=False, stop=False), [apb, pb_], [ps])
                        OP("pe", lambda e, e2=e2, p=p, ps=ps: e.matmul(out=ps[PH[e2], p * 64:(p + 1) * 64], lhsT=kav[PH[e2], p, 64:128], rhs=tmv[PH[e2], p, 2, :],
                                                                      start=False, stop=True), [ka, tm], [ps])
                yb = bs["y"]
                ybv = f3(yb, 0)
                OP("act", lambda e, ps=ps: e.copy(out=ybv, in_=ps[:, 0:256].rearrange("p (c x) -> p c x", c=4)), [ps], [yb])
                for e2 in range(2):
                    c0 = ch0 + 4 * e2
                    S.dma(YD[c0:c0 + 4, n * 64:(n + 1) * 64, :].rearrange("c s x -> s c x"), ybv[PH[e2], :, :], reads=[yb], writes=[YDB_], owner=yb, queue="act")
                ps = pB_rot.next()
                for e2 in range(2):
                    for p in range(4):
                        OP("pe", lambda e, e2=e2, p=p, ps=ps: e.matmul(out=ps[PH[e2], p * 64:(p + 1) * 64], lhsT=tmv[PH[e2], p, 0, :], rhs=pbv[PH[e2], p, :],
                                                                      start=True, stop=False), [tm, pb_], [ps])
                        OP("pe", lambda e, e2=e2, p=p, ps=ps: e.matmul(out=ps[PH[e2], p * 64:(p + 1) * 64], lhsT=tmv[PH[e2], p, 1, :], rhs=tmv[PH[e2], p, 2, :],
                                                                      start=False, stop=True), [tm], [ps])
                OP("dve", lambda e, ps=ps: e.tensor_tensor(out=Tv, in0=ps[:, 0:256].rearrange("p (c x) -> p c x", c=4), in1=Tv, op=ALU.add), [ps, Tb], [Tb])
                OP("pool", lambda e: e.tensor_tensor(out=Tv, in0=Tv, in1=gm.t[:, 0:4].unsqueeze(2).to_broadcast([128, 4, 64]), op=ALU.mult), [Tb, gm], [Tb])
                OP("act", lambda e: e.copy(out=Tbv, in_=Tv), [Tb], [Tbb])

            tasks = [(ui, u, n) for n in range(16) for ui, u in enumerate(units) if n < u["nch"]]
            slots = [None] * NW
            ti_ = 0
            while ti_ < len(tasks) or any(sl_ is not None for sl_ in slots):
                for w_ in range(NW):
                    if slots[w_] is None and ti_ < len(tasks):
                        ui, u, n = tasks[ti_]
                        if any(sl_ is not None and sl_[1] == ui for sl_ in slots):
                            continue
                        slots[w_] = (unit_chunk(ui, u, n, bsets[w_]), ui)
                        ti_ += 1
                for w_ in range(NW):
                    if slots[w_] is not None:
                        try:
                            next(slots[w_][0])
                        except StopIteration:
                            slots[w_] = None
            for ui, u in enumerate(units):
                if u["smp"]:
                    continue
                Tb = Tst[ui]
                Tv = f3(Tb, 0)
                ps = pP_rot.next()
                for e2 in range(2):
                    for p in range(4):
                        OP("pe", lambda e, e2=e2, p=p, ps=ps, Tv=Tv: e.matmul(out=ps[PH[e2], p * 64:(p + 1) * 64], lhsT=Tv[PH[e2], p, :],
                                                                           rhs=ident[PH[e2], e2 * 64:(e2 + 1) * 64], start=True, stop=True), [Tb, ident], [ps])
                yb = bsets[ui % NW]["y"]
                ybv = f3(yb, 0)
                OP("act", lambda e, ps=ps, ybv=ybv: e.copy(out=ybv, in_=ps[:, 0:256].rearrange("p (c x) -> p c x", c=4)), [ps], [yb])
                for e2 in range(2):
                    h0 = u["h0"] + 4 * e2
                    S.dma(O["st_rwkv"][u["sq"], u["d"], h0:h0 + 4].rearrange("h v k -> v h k"), ybv[PH[e2], :, :], reads=[yb], owner=yb, queue="act")
            S.barrier()
            CK("rwkv_b")

            gnt = bsub(0, 1024, "gnt")
            S.dma(gnt.t, I["rwkv_gn"].partition_broadcast(128), writes=[gnt])
            cs_ = [bsub(1024 + i2 * 1024, 1024, f"cs{i2}") for i2 in range(6)]
            YFb, YBb, Vc, Gc, C1, C2 = cs_
            ogb = bsub(8192, 4096, "ogT")

            def c_group(g):
                tiles, c, smp = g["tiles"], g["c"], g["sample"]
                nt = len(tiles)
                T = nt * 128
                ogT = bview(ogb, 0, [8, T])
                for ti, t in enumerate(tiles):
                    if smp:
                        tk0 = (t - 8) * 128
                        L = 1024
                        S.dma(h16(YFb), YSD[0:16, tk0:tk0 + 128, :].rearrange("h t x -> t h x"), reads=[YSDB], writes=[YFb])
                        S.dma(h16(C1), YSD[16:32, L - 128 - tk0:L - tk0, :].rearrange("h t x -> t h x"), reads=[YSDB], writes=[C1])
                    else:
                        sq = t // 2
                        tk0 = (t % 2) * 128
                        L = 256
                        S.dma(h16(YFb), YPD[sq * 16:(sq + 1) * 16, tk0:tk0 + 128, :].rearrange("h t x -> t h x"), reads=[YPDB], writes=[YFb])
                        S.dma(h16(C1), YPD[(4 + sq) * 16:(5 + sq) * 16, L - 128 - tk0:L - tk0, :].rearrange("h t x -> t h x"),
                              reads=[YPDB], writes=[C1])
                    for hf in range(2):
                        ps = pP_rot.next()
                        OP("pe", lambda e, hf=hf, ps=ps: e.matmul(out=ps[:, :], lhsT=Jm[:], rhs=C1.t[:, hf * 512:(hf + 1) * 512], start=True, stop=True),
                           [Jm, C1], [ps])
                        OP("dve", lambda e, hf=hf, ps=ps: e.tensor_tensor(out=YBb.t[:, hf * 512:(hf + 1) * 512], in0=ps[:, :],
                                                                          in1=YFb.t[:, hf * 512:(hf + 1) * 512], op=ALU.add), [ps, YFb], [YBb])
                    S.dma(Vc.t, RKVG[2][t * 128:(t + 1) * 128, :], reads=[RKVGB[2][t]], writes=[Vc])
                    S.dma(Gc.t, RKVG[3][t * 128:(t + 1) * 128, :], reads=[RKVGB[3][t]], writes=[Gc])
                    nb = tmpB.next()
                    OP("dve", lambda e, nb=nb: e.tensor_reduce(out=nb[:, 0:16], in_=h16(YBb), axis=AX.X, op=ALU.add), [YBb], [nb])
                    OP("dve", lambda e, nb=nb: e.tensor_scalar(out=nb[:, 0:16], in0=nb[:, 0:16], scalar1=-1.0 / 64, scalar2=None, op0=ALU.mult), [nb], [nb])
                    OP("dve", lambda e, nb=nb: e.tensor_tensor(out=h16(YBb), in0=h16(YBb), in1=nb[:, 0:16].unsqueeze(2).to_broadcast([128, 16, 64]),
                                                               op=ALU.add), [YBb, nb], [YBb])
                    OP("pool", lambda e: e.tensor_tensor(out=C2.t, in0=YBb.t, in1=YBb.t, op=ALU.mult), [YBb], [C2])
                    OP("dve", lambda e, nb=nb: e.tensor_reduce(out=nb[:, 16:32], in_=h16(C2), axis=AX.X, op=ALU.add), [C2], [nb])
                    OP("act", lambda e, nb=nb: e.activation(out=nb[:, 32:48], in_=nb[:, 16:32], func=AF.Sqrt, scale=1.0 / 64, bias=GN_EPS), [nb], [nb])
                    OP("dve", lambda e, nb=nb: e.reciprocal(out=nb[:, 48:64], in_=nb[:, 32:48]), [nb], [nb])
                    OP("dve", lambda e, nb=nb: e.tensor_tensor(out=h16(YBb), in0=h16(YBb), in1=nb[:, 48:64].unsqueeze(2).to_broadcast([128, 16, 64]),
                                                               op=ALU.mult), [YBb, nb], [YBb])
                    OP("dve", lambda e: e.tensor_tensor(out=YBb.t, in0=YBb.t, in1=gnt.t, op=ALU.mult), [YBb, gnt], [YBb])
                    OP("dve", lambda e, t=t: e.tensor_tensor(out=h16(C2), in0=h16(Vc), in1=bsum[:, t, :].unsqueeze(2).to_broadcast([128, 16, 64]),
                                                             op=ALU.mult), [Vc, bsum], [C2])
                    OP("dve", lambda e: e.tensor_tensor(out=YBb.t, in0=YBb.t, in1=C2.t, op=ALU.add), [YBb, C2], [YBb])
                    OP("act", lambda e: e.activation(out=Gc.t, in_=Gc.t, func=AF.Silu), [Gc], [Gc])
                    OP("dve", lambda e: e.tensor_tensor(out=YBb.t, in0=YBb.t, in1=Gc.t, op=ALU.mult), [YBb, Gc], [YBb])
                    transpose_blocks(lambda i: YBb.t[:, i * 128:(i + 1) * 128], YBb, 8,
                                     lambda i0, n, ti=ti: (ogT[:, i0:i0 + n, ti * 128:(ti + 1) * 128], ogb))
                ybufs = [bsub(12288, nt * 1024, "yacc")]
                tail(layer, tiles, c, ogT, ogb, 8, Wout, ybufs)

            for g in groups:
                c_group(g)
            S.barrier()

        def final_norm():
            S.dma(gnw[:, 0:1024], I["final_norm_w"].partition_broadcast(128), writes=[gnw])
            for t in range(16):
                xt = xt_rot.next()
                S.dma(xt[:], XB[t].t, reads=[XB[t]], writes=[xt])
                s = st1.next()
                xn = xn_rot.next()
                OP("act", lambda e, xt=xt, xn=xn, s=s: e.activation(out=xn[:], in_=xt[:], func=AF.Square,
                                                                    accum_out=s[:, 0:1]), [xt], [xn, s])
                OP("act", lambda e, s=s: e.activation(out=s[:, 1:2], in_=s[:, 0:1], func=AF.Sqrt, scale=1.0 / 1024,
                                                      bias=EPS), [s], [s])
                OP("dve", lambda e, s=s: e.reciprocal(out=s[:, 2:3], in_=s[:, 1:2]), [s], [s])
                OP("dve", lambda e, xt=xt, xn=xn, s=s: e.scalar_tensor_tensor(out=xn[:], in0=xt[:], scalar=s[:, 2:3],
                                                                              in1=gnw[:, 0:1024], op0=ALU.mult, op1=ALU.mult),
                   [xt, s, gnw], [xn])
                dst = O["yp"] if t < 8 else O["ys"]
                S.dma(dst[(t % 8) * 128:(t % 8 + 1) * 128, :], xn[:], reads=[xn], queue="act")

        LAYERS = {0: layer_ret, 1: layer_rwkv, 2: layer_diff, 3: layer_na}
        try:
            CK("setup")
            for layer in layers:
                mod(layer)
                CK("mod")
                LAYERS[layer](layer)
            if final:
                final_norm()
        except _Stop:
            pass
        S.emit()
        print(f"[build] ops={S.n_ops} waits={S.n_waits} dma_sems={S.ndsem}")
    return nc


def _prep_inputs(inp):
    cst = _consts()
    f = lambda a: np.ascontiguousarray(np.asarray(a, dtype=np.float32))
    shared = {
        "norm_w": f(inp["norm_w"]), "w_mod": f(inp["w_mod"]), "b_mod": f(inp["b_mod"]),
        "final_norm_w": f(inp["final_norm_w"]),
        "ret_w_in": f(inp["ret_w_in"][0]), "ret_decay": f(inp["ret_decay"][0]).reshape(8),
        "ret_gn": f(inp["ret_gn"][0]), "ret_w_out": f(inp["ret_w_out"][0]),
        "rwkv_mu": f(inp["rwkv_mu"][0]), "rwkv_w_in": f(inp["rwkv_w_in"][0]), "rwkv_w0": f(inp["rwkv_w0"][0]),
        "rwkv_wA": f(inp["rwkv_wA"][0]), "rwkv_wB": f(inp["rwkv_wB"][0]), "rwkv_a0": f(inp["rwkv_a0"][0]),
        "rwkv_aA": f(inp["rwkv_aA"][0]), "rwkv_aB": f(inp["rwkv_aB"][0]), "rwkv_kk": f(inp["rwkv_kk"][0]),
        "rwkv_ka": f(inp["rwkv_ka"][0]), "rwkv_rk": f(inp["rwkv_rk"][0]).reshape(1024),
        "rwkv_gn": f(inp["rwkv_gn"][0]), "rwkv_w_out": f(inp["rwkv_w_out"][0]),
        "diff_w_in": f(inp["diff_w_in"][0]), "diff_lambda": f(inp["diff_lambda"][0]).reshape(256),
        "diff_gn": f(inp["diff_gn"][0]), "diff_w_out": f(inp["diff_w_out"][0]),
        "na_w_in": f(inp["na_w_in"][0]), "na_bias_x": _na_bias_expand(f(inp["na_bias"][0])),
        "na_w_out": f(inp["na_w_out"][0]),
    }
    shared.update(cst)
    maps = []
    for c in range(8):
        b = c // 4
        m = dict(shared)
        m["xp"] = f(inp["x_prompt"][4 * c:4 * c + 4]).reshape(1024, 1024)
        m["xs"] = f(inp["x_sample"][b])
        m["cond"] = np.ascontiguousarray(np.stack([f(inp["c_ctx"]), f(inp["c"][b])], 0))
        m["state_ret"] = f(inp["state_ret"][b, 0])
        m["state_rwkv"] = f(inp["state_rwkv"][b, 0])
        m["cache_diff_k"] = f(inp["cache_diff_k"][b, 0])
        m["cache_diff_v"] = f(inp["cache_diff_v"][b, 0])
        m["cache_na_k"] = f(inp["cache_na_k"][b, 0])
        m["cache_na_v"] = f(inp["cache_na_v"][b, 0])
        maps.append(m)
    return maps


_NC_CACHE = {}


def kernel(**inputs):
    maps = _prep_inputs(inputs)
    if "nc" not in _NC_CACHE:
        _NC_CACHE["nc"] = build()
    res = run_bass_kernel_spmd(_NC_CACHE["nc"], maps, core_ids=list(range(8))).results
    y_prompt = np.concatenate([r["yp"].reshape(4, 256, 1024) for r in res], 0)
    y_sample = np.stack([res[0]["ys"], res[4]["ys"]], 0)
    st_ret = np.concatenate([r["st_ret"] for r in res], 0)[:, None]
    st_rwkv = np.concatenate([r["st_rwkv"] for r in res], 0)[:, None]
    dk = np.concatenate([r["dk"] for r in res], 0)[:, None]
    dv = np.concatenate([r["dv"] for r in res], 0)[:, None]
    nk = np.concatenate([r["nk"] for r in res], 0)[:, None]
    nv = np.concatenate([r["nv"] for r in res], 0)[:, None]
    return (y_prompt, y_sample, st_ret, st_rwkv, dk, dv, nk, nv)
```
